# Optimizing a Trainium2 kernel written in Bass

```python
import jax, jax.numpy as jnp
from jax import lax
import numpy as np

D_MODEL = 1024
BATCH = 8
SEQ = 2048
DEPTH = 1
DEC_BATCH = 128
DEC_SEQ = 1
PAST_LEN = 16384
PAGE_SIZE = 128

HEAD_DIM = 64
N_HEADS = D_MODEL // HEAD_DIM
D_RWKV = N_HEADS * HEAD_DIM
D_CONV = D_MODEL // 2
CONV_W = 31
LORA_DECAY = 64
LORA_ICLR = 64
RMS_EPS = 1e-6
GN_EPS = 64e-5
LN_EPS = 1e-5

N_SHIFT = 4 * D_RWKV + LORA_DECAY + LORA_ICLR
N_IN = N_SHIFT + 2 * D_CONV + D_CONV + 2 * D_MODEL

kernel_name = "rwkv7_conformer_gated_parallel_step"


def rms_norm(x, g):
    xf = x.astype(jnp.float32)
    y = xf * lax.rsqrt(jnp.mean(xf * xf, axis=-1, keepdims=True) + RMS_EPS)
    return (y * g.astype(jnp.float32)).astype(x.dtype)


def wkv7_scan(s0, r, w, k, v, a, b):
    def step(s, inp):
        rt, wt, kt, vt, at, bt = inp
        sa = jnp.einsum('bhij,bhj->bhi', s, at)
        s = s * wt[:, :, None, :] + sa[..., None] * bt[:, :, None, :] + vt[..., None] * kt[:, :, None, :]
        yt = jnp.einsum('bhij,bhj->bhi', s, rt)
        return s, yt
    xs = tuple(jnp.swapaxes(t, 0, 1) for t in (r, w, k, v, a, b))
    s_fin, ys = lax.scan(step, s0, xs)
    return jnp.swapaxes(ys, 0, 1), s_fin


def mixer_layer(x, shift_st, wkv_st, conv_st, norm_pre_g, w_in, mu_shift, decay_w0, decay_w2,
                iclr_a0, iclr_a2, k_k, k_a, r_k, gn_g, gn_b, conv_glu_b, conv_w, conv_b,
                ln_c_g, ln_c_b, w_branch_r, w_branch_c, w_out, norm_post_g):
    B, T, _ = x.shape
    f32 = jnp.float32
    h = rms_norm(x, norm_pre_g)
    proj = jnp.einsum('btd,dn->btn', h, w_in)
    ps = proj[..., :N_SHIFT]
    prev_row = jnp.einsum('bd,dn->bn', shift_st, w_in[:, :N_SHIFT])
    ps_prev = jnp.concatenate([prev_row[:, None, :], ps[:, :-1]], axis=1)
    ps = ps + (ps_prev - ps) * mu_shift
    o = 0
    r = ps[..., o:o + D_RWKV]; o += D_RWKV
    k = ps[..., o:o + D_RWKV]; o += D_RWKV
    v = ps[..., o:o + D_RWKV]; o += D_RWKV
    z_r = ps[..., o:o + D_RWKV]; o += D_RWKV
    wl = ps[..., o:o + LORA_DECAY]; o += LORA_DECAY
    al = ps[..., o:o + LORA_ICLR]
    o = N_SHIFT
    u_in = proj[..., o:o + 2 * D_CONV]; o += 2 * D_CONV
    z_c = proj[..., o:o + D_CONV]; o += D_CONV
    g_r = proj[..., o:o + D_MODEL]; o += D_MODEL
    g_c = proj[..., o:o + D_MODEL]

    w_raw = (decay_w0 + jnp.tanh(wl) @ decay_w2).astype(f32)
    w_log = -jax.nn.softplus(-w_raw) - 0.5
    decay = jnp.exp(-jnp.exp(w_log))
    iclr = jax.nn.sigmoid((iclr_a0 + al @ iclr_a2).astype(f32))
    rf, kf, vf = r.astype(f32), k.astype(f32), v.astype(f32)
    hs = lambda t: t.reshape(B, T, N_HEADS, HEAD_DIM)
    kk = hs(kf * k_k.astype(f32))
    kk = kk * lax.rsqrt(jnp.sum(kk * kk, axis=-1, keepdims=True) + 1e-12)
    kf = kf * (1.0 + (iclr - 1.0) * k_a.astype(f32))
    rh, kh, vh, ih = hs(rf), hs(kf), hs(vf), hs(iclr)
    y_w, wkv_new = wkv7_scan(wkv_st.astype(f32), rh, hs(decay), kh, vh, -kk, kk * ih)
    mu = jnp.mean(y_w, axis=-1, keepdims=True)
    var = jnp.mean(jnp.square(y_w - mu), axis=-1, keepdims=True)
    y_gn = ((y_w - mu) * lax.rsqrt(var + GN_EPS)).reshape(B, T, D_RWKV) * gn_g + gn_b
    bonus = jnp.sum(rh * kh * r_k.astype(f32), axis=-1, keepdims=True) * vh
    o_r = (y_gn + bonus.reshape(B, T, D_RWKV)).astype(x.dtype) * jax.nn.silu(z_r)
    branch_r = o_r @ w_branch_r

    u_in = u_in + conv_glu_b
    u = u_in[..., :D_CONV] * jax.nn.sigmoid(u_in[..., D_CONV:])
    buf = jnp.concatenate([conv_st.astype(u.dtype), u], axis=1)
    c = lax.conv_general_dilated(buf, conv_w[:, None, :].astype(u.dtype), window_strides=(1,),
                                 padding='VALID', dimension_numbers=('NWC', 'WIO', 'NWC'),
                                 feature_group_count=D_CONV) + conv_b
    cf = c.astype(f32)
    cm = jnp.mean(cf, axis=-1, keepdims=True)
    cv = jnp.mean(jnp.square(cf - cm), axis=-1, keepdims=True)
    cn = ((cf - cm) * lax.rsqrt(cv + LN_EPS) * ln_c_g + ln_c_b).astype(x.dtype)
    o_c = jax.nn.silu(cn) * jax.nn.silu(z_c)
    branch_c = o_c @ w_branch_c
    conv_new = buf[:, -(CONV_W - 1):]

    merged = jax.nn.sigmoid(g_r) * branch_r + jax.nn.sigmoid(g_c) * branch_c
    out = merged @ w_out
    y = x + rms_norm(out, norm_post_g)
    return y, h[:, -1], wkv_new.astype(wkv_st.dtype), conv_new


def setup_inputs(seed: int = 0) -> dict:
    key = jax.random.key(seed)
    ks = jax.random.split(key, 32)
    nrm = lambda i, shape, s: jax.random.normal(ks[i], shape, jnp.float32) * s
    L = DEPTH
    return {
        "x_prompt": nrm(0, (BATCH, SEQ, D_MODEL), 1.0),
        "x_sample": nrm(1, (DEC_BATCH, DEC_SEQ, D_MODEL), 1.0),
        "state_shift": nrm(2, (L, DEC_BATCH, D_MODEL), 1.0),
        "state_wkv": nrm(3, (L, DEC_BATCH, N_HEADS, HEAD_DIM, HEAD_DIM), 0.3),
        "state_conv": nrm(4, (L, DEC_BATCH, CONV_W - 1, D_CONV), 0.5),
        "norm_pre_g": 1.0 + nrm(5, (L, D_MODEL), 0.02),
        "w_in": nrm(6, (L, D_MODEL, N_IN), D_MODEL ** -0.5),
        "mu_shift": jax.random.uniform(ks[7], (L, N_SHIFT), jnp.float32),
        "decay_w0": jax.random.uniform(ks[8], (L, D_RWKV), jnp.float32, -6.0, -1.0),
        "decay_w2": nrm(9, (L, LORA_DECAY, D_RWKV), 0.5 * LORA_DECAY ** -0.5),
        "iclr_a0": nrm(10, (L, D_RWKV), 0.1),
        "iclr_a2": nrm(11, (L, LORA_ICLR, D_RWKV), LORA_ICLR ** -0.5),
        "k_k": 0.85 + nrm(12, (L, D_RWKV), 0.02),
        "k_a": 1.0 + nrm(13, (L, D_RWKV), 0.02),
        "r_k": nrm(14, (L, N_HEADS, HEAD_DIM), 0.1),
        "gn_g": 1.0 + nrm(15, (L, D_RWKV), 0.02),
        "gn_b": nrm(16, (L, D_RWKV), 0.02),
        "conv_glu_b": nrm(17, (L, 2 * D_CONV), 0.02),
        "conv_w": nrm(18, (L, CONV_W, D_CONV), CONV_W ** -0.5),
        "conv_b": nrm(19, (L, D_CONV), 0.02),
        "ln_c_g": 1.0 + nrm(20, (L, D_CONV), 0.02),
        "ln_c_b": nrm(21, (L, D_CONV), 0.02),
        "w_branch_r": nrm(22, (L, D_RWKV, D_MODEL), D_RWKV ** -0.5),
        "w_branch_c": nrm(23, (L, D_CONV, D_MODEL), D_CONV ** -0.5),
        "w_out": nrm(24, (L, D_MODEL, D_MODEL), D_MODEL ** -0.5),
        "norm_post_g": 1.0 + nrm(25, (L, D_MODEL), 0.02),
    }


def reference(x_prompt, x_sample, state_shift, state_wkv, state_conv, norm_pre_g, w_in, mu_shift,
              decay_w0, decay_w2, iclr_a0, iclr_a2, k_k, k_a, r_k, gn_g, gn_b, conv_glu_b, conv_w,
              conv_b, ln_c_g, ln_c_b, w_branch_r, w_branch_c, w_out, norm_post_g):
    dt = x_prompt.dtype
    xp, xs = x_prompt, x_sample
    sp_shift, sp_wkv, sp_conv = [], [], []
    ss_shift, ss_wkv, ss_conv = [], [], []
    for l in range(DEPTH):
        params = (norm_pre_g[l], w_in[l], mu_shift[l], decay_w0[l], decay_w2[l], iclr_a0[l],
                  iclr_a2[l], k_k[l], k_a[l], r_k[l], gn_g[l], gn_b[l], conv_glu_b[l], conv_w[l],
                  conv_b[l], ln_c_g[l], ln_c_b[l], w_branch_r[l], w_branch_c[l], w_out[l],
                  norm_post_g[l])
        p_shift0 = jnp.zeros((xp.shape[0], D_MODEL), dt)
        p_wkv0 = jnp.zeros((xp.shape[0], N_HEADS, HEAD_DIM, HEAD_DIM), state_wkv.dtype)
        p_conv0 = jnp.zeros((xp.shape[0], CONV_W - 1, D_CONV), state_conv.dtype)
        xp, a1, a2, a3 = mixer_layer(xp, p_shift0, p_wkv0, p_conv0, *params)
        sp_shift.append(a1); sp_wkv.append(a2); sp_conv.append(a3)
        xs, b1, b2, b3 = mixer_layer(xs, state_shift[l], state_wkv[l], state_conv[l], *params)
        ss_shift.append(b1); ss_wkv.append(b2); ss_conv.append(b3)
    new_shift_prompt = jnp.stack(sp_shift)
    new_wkv_prompt = jnp.stack(sp_wkv)
    new_conv_prompt = jnp.stack(sp_conv)
    new_shift_sample = jnp.stack(ss_shift)
    new_wkv_sample = jnp.stack(ss_wkv)
    new_conv_sample = jnp.stack(ss_conv)
    return (xp, xs, new_shift_prompt, new_wkv_prompt, new_conv_prompt,
            new_shift_sample, new_wkv_sample, new_conv_sample)
```

```python
import contextlib
import numpy as np
import concourse.bass as bass
import concourse.mybir as mybir
from concourse.bass_utils import run_bass_kernel_spmd

F32 = mybir.dt.float32
BF16 = mybir.dt.bfloat16
AF = mybir.ActivationFunctionType
ALU = mybir.AluOpType
AX = mybir.AxisListType

D = 1024
NIN = 7808
SEQ = 2048
NS = 16
NCH = 61
CNEG = -0.6065306597126334

O_MU = 0; O_KK = 33; O_KA = 41; O_RK = 49; O_GNG = 57; O_GNB = 65; O_A0 = 73; O_GLUB = 81
O_CB = 89; O_LNG = 93; O_LNB = 97; O_CW = 101; O_GPRE = 225; O_OMM = 233; O_OMKA = 266; NCOL = 274
C_ID = 0; C_SL = 128; C_SU = 256; C_UI = 384; C_TRI = 512; C_TRE = 640; C_BM = 768; C_BO = 896
C_AM = 1024; C_NI = 1152; C_I2 = 1280; NCONST = 1344


class Prog:
    ENG = ("pe", "act", "dve", "pool", "sp")

    def __init__(self, nc):
        self.nc = nc
        self.ops = {e: [] for e in self.ENG}
        self.cnt = {e: 0 for e in self.ENG}
        self.waited = {e: {} for e in self.ENG}
        self.lastw = {}
        self.readers = {}
        self.dcnt = {}
        self.dead = False
        self.phase = ""
        self.annotate = False

    def _need(self, eng, waits, tok):
        if tok is None:
            return
        kind, key, val = tok
        if kind == "e" and key == "pe" and eng == "pe":
            return
        k = (kind, key)
        if self.waited[eng].get(k, 0) >= val:
            return
        if waits.get(k, 0) < val:
            waits[k] = val

    def op(self, eng, fn, reads=(), writes=(), dma=None, tag=None):
        if self.dead:
            return None
        waits = {}
        if eng == "pe":
            prev = getattr(self, "petag", None)
            if tag is not None and prev is not None and tag != prev:
                waits[("e", "pe")] = self.cnt["pe"]
            self.petag = tag
        for r in reads:
            self._need(eng, waits, self.lastw.get(r))
        for w in writes:
            self._need(eng, waits, self.lastw.get(w))
            for rd in self.readers.get(w, ()):
                self._need(eng, waits, rd)
        for k, v in waits.items():
            self.waited[eng][k] = v
        if dma is not None:
            prevc = self.dcnt.get(dma, 0)
            if prevc > 0 and self.waited[eng].get(("d", dma), 0) < prevc:
                waits[("d", dma)] = max(waits.get(("d", dma), 0), prevc)
                self.waited[eng][("d", dma)] = prevc
            self.dcnt[dma] = prevc + 1
            tok = ("d", dma, self.dcnt[dma])
        else:
            self.cnt[eng] += 1
            tok = ("e", eng, self.cnt[eng])
        self.ops[eng].append((waits, fn, tok, self.phase))
        for r in reads:
            self.readers.setdefault(r, []).append(tok)
        for w in writes:
            self.lastw[w] = tok
            self.readers[w] = []
        return tok

    def emit(self):
        nc = self.nc
        with contextlib.ExitStack() as st:
            esem = {e: st.enter_context(nc.semaphore("s_" + e)) for e in self.ENG}
            dsem = {k: st.enter_context(nc.semaphore("d_" + str(k))) for k in self.dcnt}
            block = st.enter_context(nc.Block())

            def run(engname, e):
                for waits, fn, tok, ph in self.ops[engname]:
                    for (kind, key), val in waits.items():
                        if kind == "e":
                            e.wait_ge(esem[key], val)
                        else:
                            e.wait_ge(dsem[key], 16 * val)
                    ins = fn(e)
                    if self.annotate:
                        ins.annotate(ph)
                    if tok[0] == "e":
                        ins.then_inc(esem[tok[1]], 1)
                    else:
                        ins.then_inc(dsem[tok[1]], 16)
                if engname == "sp":
                    for k, c in self.dcnt.items():
                        e.wait_ge(dsem[k], 16 * c)

            @block.tensor
            def _(e):
                run("pe", e)

            @block.scalar
            def _(e):
                run("act", e)

            @block.vector
            def _(e):
                run("dve", e)

            @block.gpsimd
            def _(e):
                run("pool", e)

            @block.sync
            def _(e):
                run("sp", e)


def build():
    nc = bass.Bass("TRN2", target_bir_lowering=False)
    di = lambda n, s: nc.dram_tensor(n, s, F32, kind="ExternalInput").ap()
    do = lambda n, s: nc.dram_tensor(n, s, F32, kind="ExternalOutput").ap()
    xp = di("xp", [SEQ, D]); xs = di("xs", [NS, D]); sshift = di("sshift", [NS, D])
    swkv = di("swkv", [NS, 1024, 64]); sconv = di("sconv", [NS * 30, 512])
    w_in = di("w_in", [D, NIN]); w_br = di("w_br", [D, D]); w_bc = di("w_bc", [512, D]); w_out = di("w_out", [D, D])
    cols_d = di("cols", [128, NCOL]); consts_d = di("consts", [128, NCONST])
    w2ext_d = di("w2ext", [65, 1024]); a2_d = di("a2", [64, 1024])
    npg_d = di("npg", [1, D]); npre_d = di("npre", [1, D])
    yp = do("yp", [SEQ, D]); ys = do("ys", [NS, D]); nsp = do("nsp", [1, D])
    nwp = do("nwp", [1024, 64]); ncp = do("ncp", [30, 512]); nss = do("nss", [NS, D])
    nws = do("nws", [NS, 1024, 64]); ncs = do("ncs", [NS, 30, 512])
    wi_s = nc.dram_tensor("wi_s", [D, NIN], BF16).ap()
    wbr_s = nc.dram_tensor("wbr_s", [D, D], BF16).ap()
    wbc_s = nc.dram_tensor("wbc_s", [512, D], BF16).ap()
    wo_s = nc.dram_tensor("wo_s", [D, D], BF16).ap()

    with contextlib.ExitStack() as st:
        def T(n, s, d=F32):
            return st.enter_context(nc.sbuf_tensor("sb_" + n, s, d))
        P = Prog(nc)
        op = P.op
        cst = T("cst", [128, NCONST]); col = T("col", [128, NCOL])
        idb = T("idb", [128, 128], BF16)
        bob = T("bob", [128, 128], BF16)
        w2e = T("w2e", [65, 1024]); a2b = T("a2b", [128, 1024], BF16)
        wb = [T("wb%d" % i, [128, 8, 512], BF16) for i in range(4)]
        xt = [T("xt%d" % i, [128, D]) for i in range(2)]
        hb = T("hb", [128, D], BF16)
        hT = T("hT", [128, 8, 512], BF16)
        rS = T("rS", [128, 8, 512], BF16); kS = T("kS", [128, 8, 512], BF16)
        vS = T("vS", [128, 8, 512], BF16); zrS = T("zrS", [128, 8, 512], BF16)
        twl = T("twl", [65, 512]); alb = T("alb", [128, 512], BF16)
        ua = T("ua", [128, 4, 512]); uex = T("uex", [128, 4, 542]); ubf = T("ubf", [128, 4, 542], BF16)
        szc = T("szc", [128, 4, 512], BF16)
        orT = T("orT", [128, 8, 512], BF16)
        mT = rS
        ocT = kS
        TT = [T("T%d" % i, [128, 8, 128]) for i in range(8)]
        bon = T("bon", [128, 8, 128], BF16)
        rt_ = T("rt_", [128, 8, 128], BF16); at_ = T("at_", [128, 8, 128], BF16); bt_ = T("bt_", [128, 8, 128], BF16)
        kt_ = T("kt_", [128, 8, 128], BF16); bh_ = T("bh_", [128, 8, 128], BF16); kh_ = T("kh_", [128, 8, 128], BF16)
        Vt = T("Vt", [128, 1024], BF16); Bt = T("Bt", [128, 1024], BF16); Kt = T("Kt", [128, 1024], BF16)
        Ak = [T("Ak%d" % i, [128, 4, 128], BF16) for i in range(2)]
        Nk = [T("Nk%d" % i, [128, 4, 128], BF16) for i in range(2)]
        Qb = T("Qb", [128, 4, 128], BF16)
        LkT = T("LkT", [128, 4, 128], BF16); MbT = T("MbT", [128, 4, 128], BF16); MkT = T("MkT", [128, 4, 128], BF16)
        Xb = T("Xb", [128, 256], BF16); SAb = T("SAb", [128, 256], BF16)
        Xb2 = T("Xb2", [128, 256], BF16); SAb2 = T("SAb2", [128, 256], BF16)
        dummy = T("dummy", [128, 8])
        _w3 = lambda i: wb[3][:, i, :].rearrange("p (a b) -> p a b", b=128)
        SETS = [
            {"Ak": Ak, "Nk": Nk, "Qb": Qb, "LkT": LkT, "MbT": MbT, "MkT": MkT, "Xb": Xb, "SAb": SAb, "banks": (0, 1, 2), "n": "_A"},
            {"Ak": [_w3(0), _w3(1)], "Nk": [_w3(2), _w3(3)], "Qb": _w3(4), "LkT": _w3(5), "MbT": _w3(6), "MkT": _w3(7),
             "Xb": Xb2, "SAb": SAb2, "banks": (3, 4, 5), "n": "_B"},
        ]
        SETB_KEYS = [k + "_B" for k in ("Ak0", "Ak1", "Nk0", "Nk1", "Qb", "LkT", "MbT", "MkT")]
        ALLT4 = ["T4_0", "T4_1", "T4_2", "T4_3"]
        ALLSF = ["Sf0", "Sf1", "Sf2", "Sf3"]
        ALLSB = ["Sb0", "Sb1", "Sb2", "Sb3"]
        Sf = T("Sf", [128, 8, 64]); Sb = T("Sb", [128, 8, 64], BF16)
        gC = T("gC", [128, 8]); tmpb = T("tmpb", [128, 512]); tmpc = T("tmpc", [128, 512])
        sgb = T("sgb", [128, 512], BF16)
        m1 = TT[5][:].rearrange("p a b -> p (a b)")
        pprev = [T("pprev%d" % i, [128, 40]) for i in range(2)]
        small = T("small", [128, 64])
        dg = [T("dg%d" % i, [128, 128], BF16) for i in range(4)]
        wld = xt
        pb = [st.enter_context(nc.psum_tensor("pb%d" % i, [128, 512], F32)) for i in range(7)]
        ptb = st.enter_context(nc.psum_tensor("ptb", [128, 1024], BF16))

        cnt = {"d": 0, "e": 0}
        import os
        STOP = float(os.environ.get("MK_STOP", "1000"))

        def stage(k):
            if k > STOP:
                P.dead = True
        P.annotate = bool(os.environ.get("MK_ANN"))

        def ph(name):
            P.phase = name

        def dma(out, in_, reads, writes, q="sp"):
            cnt[q] = cnt.get(q, 0) + 1
            key = "%s%d" % (q, cnt[q] % (16 if q == "sp" else 8))
            return op(q, lambda e: e.dma_start(out=out, in_=in_), reads, writes, dma=key)

        def mm(out, lhsT, rhs, start, stop, reads, writes):
            b0 = lhsT.base_partition()
            n0 = lhsT.shape[0]
            tag = "lo" if b0 + n0 <= 64 else ("hi" if b0 >= 64 else None)
            op("pe", lambda e: e.matmul(out, lhsT=lhsT, rhs=rhs, start=start, stop=stop), reads, writes, tag=tag)

        def act(out, in_, func, reads, writes, bias=None, scale=None, accum=None):
            kw = {}
            if bias is not None: kw["bias"] = bias
            if scale is not None: kw["scale"] = scale
            if accum is not None: kw["accum_out"] = accum
            op("act", lambda e: e.activation(out=out, in_=in_, func=func, **kw), reads, writes)

        def tt(eng, out, in0, in1, o, reads, writes):
            g = {"dve": "dve", "pool": "pool"}[eng]
            op(g, lambda e: e.tensor_tensor(out=out, in0=in0, in1=in1, op=o), reads, writes)

        def ts(eng, out, in0, s1, s2, o0, o1, reads, writes):
            if s2 is None:
                op(eng, lambda e: e.tensor_scalar(out=out, in0=in0, scalar1=s1, scalar2=None, op0=o0), reads, writes)
            else:
                op(eng, lambda e: e.tensor_scalar(out=out, in0=in0, scalar1=s1, scalar2=s2, op0=o0, op1=o1), reads, writes)

        def stt(eng, out, in0, sc, in1, o0, o1, reads, writes):
            op(eng, lambda e: e.scalar_tensor_tensor(out=out, in0=in0, scalar=sc, in1=in1, op0=o0, op1=o1), reads, writes)

        def cp(eng, out, in_, reads, writes):
            if eng == "act":
                act(out, in_, AF.Copy, reads, writes)
            else:
                op(eng, lambda e: e.tensor_copy(out=out, in_=in_), reads, writes)

        def rsq(out, in_, eps, reads, wkey):
            act(out, in_, AF.Sqrt, reads, [wkey], bias=eps)
            op("dve", lambda e: e.reciprocal(out=out, in_=out), [wkey], [wkey])

        def bc(ap, shape):
            return ap.to_broadcast(shape)

        C = lambda o, n=128: cst[:, o:o + n]

        dma(cst[:], consts_d, [], ["cst"])
        dma(col[:], cols_d, [], ["col"])
        dma(w2e[:], w2ext_d, [], ["w2e"])
        dma(wld[0][64:128, 0:1024], a2_d, [], ["xt0"])
        cp("dve", a2b[64:128, :], wld[0][64:128, 0:1024], ["xt0"], ["a2b"])
        cp("dve", idb[:], C(C_ID), ["cst"], ["idb"])
        cp("dve", bob[:], C(C_BO), ["cst"], ["bob"])
        ts("dve", col[:, O_OMM:O_OMM + 33], col[:, O_MU:O_MU + 33], -1.0, 1.0, ALU.mult, ALU.add, ["col"], ["col"])
        ts("dve", col[:, O_OMKA:O_OMKA + 8], col[:, O_KA:O_KA + 8], -1.0, 1.0, ALU.mult, ALU.add, ["col"], ["col"])
        op("pool", lambda e: e.memset(twl[64:65, :], 1.0), [], ["twl"])
        op("pool", lambda e: e.memset(Sf[:], 0.0), [], ALLSF)
        op("pool", lambda e: e.memset(Sb[:], 0.0), [], ALLSB)
        op("pool", lambda e: e.memset(pprev[0][:], 0.0), [], ["pprev0"])
        op("pool", lambda e: e.memset(pprev[1][:], 0.0), [], ["pprev1"])
        op("pool", lambda e: e.memset(uex[:], 0.0), [], ["uex"])

        stage(1)
        ph("prologue")
        def prologue():
            ph("prologue")
            pieces = []
            for c0 in range(0, NIN, 1024):
                for rc in range(8):
                    pieces.append((w_in, wi_s, rc, c0, min(1024, NIN - c0), "scr_i%d" % (c0 // 1024)))
            for rc in range(8):
                pieces.append((w_br, wbr_s, rc, 0, 1024, "scr_o"))
            for rc in range(4):
                pieces.append((w_bc, wbc_s, rc, 0, 1024, "scr_o"))
            for rc in range(8):
                pieces.append((w_out, wo_s, rc, 0, 1024, "scr_o"))
            fl = lambda t: t[:].rearrange("p a b -> p (a b)")
            sf32 = [(fl(TT[i]), "T%d" % i) for i in range(8)]
            sbf = [(fl(rt_), "rt_0"), (fl(at_), "at_0"), (fl(bt_), "bt_0"), (fl(kt_), "kt_0"), (fl(bh_), "bh_"), (fl(kh_), "kh_"),
                   (Vt[:, :], "Vt_0"), (Bt[:, :], "Bt_0"), (Kt[:, :], "Kt_0")]
            NB = len(sf32); DEPTH = 6
            engs = ["dve", "act"]
            npc = len(pieces)
            for i in range(npc + DEPTH):
                if i < npc:
                    src, dst, rc, c0, n, skey = pieces[i]
                    bf_, kf_ = sf32[i % NB]
                    dma(bf_[:, 0:n], src[rc * 128:(rc + 1) * 128, c0:c0 + n], [], [kf_])
                j = i - DEPTH
                if j >= 0:
                    src, dst, rc, c0, n, skey = pieces[j]
                    bf_, kf_ = sf32[j % NB]
                    bb_, kb_ = sbf[j % len(sbf)]
                    cp(engs[j % 2], bb_[:, 0:n], bf_[:, 0:n], [kf_], [kb_])
                    dma(dst[rc * 128:(rc + 1) * 128, c0:c0 + n], bb_[:, 0:n], [kb_], [skey], q="pool")


        stage(2)
        wi_v = wi_s.rearrange("(dc p) n -> p dc n", p=128)
        wbr_v = wbr_s.rearrange("(dc p) n -> p dc n", p=128)
        wbc_v = wbc_s.rearrange("(dc p) n -> p dc n", p=128)
        wo_v = wo_s.rearrange("(dc p) n -> p dc n", p=128)
        wslot = {"i": 0}

        def wload(view, ndc, c0, n, skeys):
            s = wslot["i"] % 4
            wslot["i"] += 1
            dma(wb[s][:, 0:ndc, 0:n], view[:, :, c0:c0 + n], skeys, ["wb%d" % s])
            return s

        def ikeys(c0, n):
            return ["scr_i%d" % b for b in range(c0 // 1024, (c0 + n - 1) // 1024 + 1)]

        def rmsnorm_tile(xtile, key, npart, dst_cols, want_h_out=None):
            ph("rmsnorm")
            act(hb[0:npart, :], xtile[0:npart, :], AF.Square, [key], ["hb", "small"], accum=small[0:npart, 0:1])
            ts("dve", small[0:npart, 1:2], small[0:npart, 0:1], 1.0 / D, 1e-6, ALU.mult, ALU.add, ["small"], ["small"])
            rsq(small[0:npart, 2:3], small[0:npart, 1:2], 0.0, ["small"], "small")
            ts("dve", hb[0:npart, :], xtile[0:npart, :], small[0:npart, 2:3], None, ALU.mult, None, [key, "small"], ["hb"])
            if want_h_out is not None:
                want_h_out()
            for dc in range(8):
                op("pe", lambda e, dc=dc: e.transpose(ptb[:, dc * 128:dc * 128 + npart], hb[0:npart, dc * 128:(dc + 1) * 128], idb[0:npart, 0:npart]),
                   ["hb", "idb"], ["ptb"])
            for dc in range(8):
                act(hT[:, dc, dst_cols[0]:dst_cols[1]], ptb[:, dc * 128:dc * 128 + npart], AF.Copy, ["ptb", "col"], ["hT"],
                    scale=col[:, O_GPRE + dc:O_GPRE + dc + 1])

        def project(j, wslot_i, jj, NT, bank):
            for dc in range(8):
                mm(pb[bank][:, 0:NT], wb[wslot_i][:, dc, jj * 128:(jj + 1) * 128], hT[:, dc, 0:NT], dc == 0, dc == 7,
                   ["wb%d" % wslot_i, "hT"], ["pb%d" % bank])

        def shiftmix(j, bank, NT, dst, dkey, sample, pp_old, pp_new):
            p = pb[bank]
            mu = col[:, O_MU + j:O_MU + j + 1]
            omm = col[:, O_OMM + j:O_OMM + j + 1]
            bk = "pb%d" % bank
            if sample:
                act(tmpb[:, 0:NS], p[:, 0:NS], AF.Copy, [bk, "col"], ["tmpb"], scale=omm)
                stt("dve", dst, p[:, NS:2 * NS], mu, tmpb[:, 0:NS], ALU.mult, ALU.add, [bk, "tmpb", "col"], [dkey])
            else:
                act(tmpb[:, 0:NT], p[:, 0:NT], AF.Copy, [bk, "col"], ["tmpb"], scale=omm)
                act(pprev[pp_new][:, j:j + 1], p[:, NT - 1:NT], AF.Copy, [bk], ["pprev%d" % pp_new])
                stt("dve", dst[:, 1:NT], p[:, 0:NT - 1], mu, tmpb[:, 1:NT], ALU.mult, ALU.add, [bk, "tmpb", "col"], [dkey])
                stt("dve", dst[:, 0:1], pprev[pp_old][:, j:j + 1], mu, tmpb[:, 0:1], ALU.mult, ALU.add,
                    ["pprev%d" % pp_old, "tmpb", "col"], [dkey])

        def proj_phase(NT, sample, pp_old, pp_new):
            ph("proj")
            nb = 0
            for g0 in range(0, 45, 4):
                ng = min(4, 45 - g0)
                s = wload(wi_v, 8, g0 * 128, ng * 128, ikeys(g0 * 128, ng * 128))
                for jj in range(ng):
                    j = g0 + jj
                    bank = nb % 2
                    nb += 1
                    bk = "pb%d" % bank
                    project(j, s, jj, NT if not sample else 2 * NS, bank)
                    W = NS if sample else NT
                    if j < 8:
                        shiftmix(j, bank, NT, rS[:, j, 0:W], "rS", sample, pp_old, pp_new)
                    elif j < 16:
                        shiftmix(j, bank, NT, kS[:, j - 8, 0:W], "kS", sample, pp_old, pp_new)
                    elif j < 24:
                        shiftmix(j, bank, NT, vS[:, j - 16, 0:W], "vS", sample, pp_old, pp_new)
                    elif j < 32:
                        shiftmix(j, bank, NT, tmpc[:, 0:W], "tmpc", sample, pp_old, pp_new)
                        act(zrS[:, j - 24, 0:W], tmpc[:, 0:W], AF.Silu, ["tmpc"], ["zrS"])
                    elif j == 32:
                        shiftmix(j, bank, NT, tmpc[:, 0:W], "tmpc", sample, pp_old, pp_new)
                        act(twl[0:64, 0:W], tmpc[0:64, 0:W], AF.Tanh, ["tmpc"], ["twl"])
                        cp("pool", alb[64:128, 0:W], tmpc[64:128, 0:W], ["tmpc"], ["alb"])
                    elif j < 37:
                        c = j - 33
                        act(ua[:, c, 0:W], pb[bank][:, 0:W], AF.Identity, [bk, "col"], ["ua"],
                            bias=col[:, O_GLUB + c:O_GLUB + c + 1])
                    elif j < 41:
                        c = j - 37
                        act(tmpc[:, 0:W], pb[bank][:, 0:W], AF.Sigmoid, [bk, "col"], ["tmpc"],
                            bias=col[:, O_GLUB + 4 + c:O_GLUB + 5 + c])
                        if sample:
                            tt("dve", uex[:, c, 0:NS * 31].rearrange("p (n w) -> p n w", w=31)[:, :, 30], ua[:, c, 0:W], tmpc[:, 0:W],
                               ALU.mult, ["ua", "tmpc"], ["uex"])
                        else:
                            tt("dve", uex[:, c, 30:30 + W], ua[:, c, 0:W], tmpc[:, 0:W], ALU.mult, ["ua", "tmpc"], ["uex"])
                    else:
                        c = j - 41
                        act(szc[:, c, 0:W], pb[bank][:, 0:W], AF.Silu, [bk], ["szc"])

        def ln_conv_out(W, cf, ck):
            ph("lnconv")
            for c in range(4):
                mm(pb[2][:, 0:W], C(C_AM), cf[c], c == 0, c == 3, ["cst", ck[c]], ["pb2"])
            for c in range(4):
                tt("dve", cf[c], cf[c], pb[2][:, 0:W], ALU.subtract, [ck[c], "pb2"], [ck[c]])
            for c in range(4):
                tt("pool", ua[:, c, 0:W], cf[c], cf[c], ALU.mult, [ck[c]], ["ua"])
            for c in range(4):
                mm(pb[3][:, 0:W], C(C_AM), ua[:, c, 0:W], c == 0, c == 3, ["cst", "ua"], ["pb3"])
            rsq(tmpc[:, 0:W], pb[3][:, 0:W], 1e-5, ["pb3"], "tmpc")
            for c in range(4):
                tt("dve", cf[c], cf[c], tmpc[:, 0:W], ALU.mult, [ck[c], "tmpc"], [ck[c]])
                act(ua[:, c, 0:W], cf[c], AF.Silu, [ck[c], "col"], ["ua"],
                    bias=col[:, O_LNB + c:O_LNB + c + 1], scale=col[:, O_LNG + c:O_LNG + c + 1])
                tt("pool", ocT[:, c, 0:W], ua[:, c, 0:W], szc[:, c, 0:W], ALU.mult, ["ua", "szc"], ["kS"])

        def prep_gen(cs, W, sample, PSp):
            T0, T1, T2, T3, T4, T5, T6, T7 = TT
            sl = slice(cs, cs + W)
            sfx = PSp["sfx"]
            bonT, gCt = PSp["bon"], PSp["gC"]
            kbon, kgc = "bon" + sfx, "gC" + sfx
            ph("prep")
            sh = [128, 8, W]
            colb = lambda o: bc(col[:, o:o + 8].unsqueeze(2), sh)
            p6 = pb[6]
            p6v = p6[:].rearrange("p (a b) -> p a b", b=128)[:, :, 0:W]
            T0v = T0[:].rearrange("p a b -> p (a b)")
            tt("dve", T5[:, :, 0:W], kS[:, :, sl], colb(O_KK), ALU.mult, ["kS", "col"], ["T5"])
            tt("pool", bh_[:, :, 0:W], T5[:, :, 0:W], T5[:, :, 0:W], ALU.mult, ["T5"], ["bh_"])
            yield
            for hf in range(2):
                mm(p6[0:W, :], twl[0:65, sl], w2e[0:65, hf * 512:(hf + 1) * 512], True, True, ["twl", "w2e"], ["pb6"])
                yield
                act(T0v[0:W, hf * 512:(hf + 1) * 512], p6[0:W, :], AF.Sigmoid, ["pb6"], ["T0"])
                yield
            tri = C(C_TRI) if not sample else cst[0:W, C_NI:C_NI + W]
            tre = C(C_TRE) if not sample else cst[0:W, C_NI + 64:C_NI + 64 + W]
            for hf in range(2):
                hs = slice(hf * 4, hf * 4 + 4)
                for hq in range(4):
                    hh = hf * 4 + hq
                    mm(p6[:, hq * 128:hq * 128 + W], T0v[0:W, hh * 128:(hh + 1) * 128], tri[0:W, 0:W], True, True, ["T0", "cst"], ["pb6"])
                yield
                act(T1[:, hs, 0:W], p6v[:, 0:4, :], AF.Exp, ["pb6"], ["T1"])
                act(T2[:, hs, 0:W], p6v[:, 0:4, :], AF.Exp, ["pb6"], ["T2"], scale=-1.0)
                yield
            cp("pool", gCt[:, :], T1[:, :, W - 1], ["T1"], [kgc])
            for hf in range(2):
                for hq in range(4):
                    hh = hf * 4 + hq
                    mm(p6[:, hq * 128:hq * 128 + W], a2b[64:128, hh * 128:(hh + 1) * 128], alb[64:128, sl], True, True, ["a2b", "alb"], ["pb6"])
                yield
                for hq in range(4):
                    hh = hf * 4 + hq
                    act(T4[:, hh, 0:W], p6[:, hq * 128:hq * 128 + W], AF.Sigmoid, ["pb6", "col"], ALLT4,
                        bias=col[:, O_A0 + hh:O_A0 + hh + 1])
                yield
            for hf in range(2):
                hs = slice(hf * 4, hf * 4 + 4)
                for hq in range(4):
                    hh = hf * 4 + hq
                    mm(p6[:, hq * 128:hq * 128 + W], bob[:], bh_[:, hh, 0:W], True, True, ["bob", "bh_"], ["pb6"])
                yield
                rsq(T7[:, hs, 0:W], p6v[:, 0:4, :], 1e-12, ["pb6"], "T7")
                yield
            stt("dve", T6[:, :, 0:W], T5[:, :, 0:W], -1.0, T7[:, :, 0:W], ALU.mult, ALU.mult, ["T5", "T7"], ["T6"])
            yield
            stt("dve", T7[:, :, 0:W], T6[:, :, 0:W], -1.0, T4[:, :, 0:W], ALU.mult, ALU.mult, ["T6"] + ALLT4, ["T7"])
            yield
            tt("pool", T0[:, :, 0:W], T4[:, :, 0:W], colb(O_KA), ALU.mult, ALLT4 + ["col", "T0"], ["T0"])
            tt("pool", T0[:, :, 0:W], T0[:, :, 0:W], colb(O_OMKA), ALU.add, ["T0", "col"], ["T0"])
            yield
            tt("dve", T5[:, :, 0:W], kS[:, :, sl], T0[:, :, 0:W], ALU.mult, ["kS", "T0"], ["T5"])
            yield
            tt("pool", T0[:, :, 0:W], rS[:, :, sl], T5[:, :, 0:W], ALU.mult, ["rS", "T5"], ["T0"])
            tt("pool", kh_[:, :, 0:W], T0[:, :, 0:W], colb(O_RK), ALU.mult, ["T0", "col"], ["kh_"])
            yield
            for hf in range(2):
                hs = slice(hf * 4, hf * 4 + 4)
                for hq in range(4):
                    hh = hf * 4 + hq
                    mm(p6[:, hq * 128:hq * 128 + W], bob[:], kh_[:, hh, 0:W], True, True, ["bob", "kh_"], ["pb6"])
                yield
                tt("dve", bonT[:, hs, 0:W], p6v[:, 0:4, :], vS[:, hs, sl], ALU.mult, ["pb6", "vS"], [kbon])
                yield
            if sample:
                return
            ph("mults")
            EG, EnG, EGe, k2, aa, bb = T1, T2, T3, T5, T6, T7
            rt, at, bt, kt = PSp["rt"], PSp["at"], PSp["bt"], PSp["kt"]
            krt, kat, kbt, kkt = "rt" + sfx, "at" + sfx, "bt" + sfx, "kt" + sfx
            tt("dve", rt, rS[:, :, sl], EG[:], ALU.mult, ["rS", "T1"], [krt])
            tt("pool", at[:, :, 1:128], aa[:, :, 1:128], EG[:, :, 0:127], ALU.mult, ["T6", "T1"], [kat])
            cp("pool", at[:, :, 0:1], aa[:, :, 0:1], ["T6"], [kat])
            yield
            tt("dve", bt, bb[:], EnG[:], ALU.mult, ["T7", "T2"], [kbt])
            tt("pool", kt, k2[:], EnG[:], ALU.mult, ["T5", "T2"], [kkt])
            yield
            tt("dve", EGe[:], EnG[:], bc(EG[:, :, 127:128], [128, 8, 128]), ALU.mult, ["T2", "T1", kat], ["T3"])
            yield
            tt("pool", bh_[:], bb[:], EGe[:], ALU.mult, ["T7", "T3"], ["bh_"])
            tt("dve", kh_[:], k2[:], EGe[:], ALU.mult, ["T5", "T3"], ["kh_"])
            yield
            ph("transp")
            for src, skey, dst, dkey in ((vS, "vS", PSp["Vt"], "Vt" + sfx), (bh_, "bh_", PSp["Bt"], "Bt" + sfx), (kh_, "kh_", PSp["Kt"], "Kt" + sfx)):
                for hh in range(8):
                    srcap = src[:, hh, sl] if src is vS else src[:, hh, :]
                    op("pe", lambda e, srcap=srcap, hh=hh: e.transpose(ptb[:, hh * 128:(hh + 1) * 128], srcap, idb[:]),
                       [skey, "idb"], ["ptb"])
                yield
                cp("act", dst[:, 0:512], ptb[:, 0:512], ["ptb"], [dkey])
                cp("dve", dst[:, 512:1024], ptb[:, 512:1024], ["ptb"], [dkey])
                yield

        def gn_gen(yT, ykey, cs, W, G, Gk, bonT, kbon):
            ph("gn")
            G1, G2, G3 = G
            k1, k2_, k3 = Gk
            sl = slice(cs, cs + W)
            sh = [128, 8, W]
            colb = lambda o: bc(col[:, o:o + 8].unsqueeze(2), sh)
            p6 = pb[6]
            p6v = p6[:].rearrange("p (a b) -> p a b", b=128)[:, :, 0:W]
            for hf in range(2):
                hs = slice(hf * 4, hf * 4 + 4)
                for hq in range(4):
                    hh = hf * 4 + hq
                    mm(p6[:, hq * 128:hq * 128 + W], C(C_BM), yT[:, hh, 0:W], True, True, ["cst"] + ykey, ["pb6"])
                yield
                tt("dve", G1[:, hs, 0:W], yT[:, hs, 0:W], p6v[:, 0:4, :], ALU.subtract, ykey + ["pb6"], [k1])
                yield
            tt("pool", G2[:, :, 0:W], G1[:, :, 0:W], G1[:, :, 0:W], ALU.mult, [k1], [k2_])
            yield
            for hf in range(2):
                hs = slice(hf * 4, hf * 4 + 4)
                for hq in range(4):
                    hh = hf * 4 + hq
                    mm(p6[:, hq * 128:hq * 128 + W], C(C_BM), G2[:, hh, 0:W], True, True, ["cst", k2_], ["pb6"])
                yield
                rsq(G3[:, hs, 0:W], p6v[:, 0:4, :], 64e-5, ["pb6"], k3)
                yield
            tt("dve", G1[:, :, 0:W], G1[:, :, 0:W], G3[:, :, 0:W], ALU.mult, [k1, k3], [k1])
            yield
            tt("pool", G1[:, :, 0:W], G1[:, :, 0:W], colb(O_GNG), ALU.mult, [k1, "col"], [k1])
            tt("pool", G1[:, :, 0:W], G1[:, :, 0:W], colb(O_GNB), ALU.add, [k1, "col"], [k1])
            yield
            tt("dve", G1[:, :, 0:W], G1[:, :, 0:W], bonT[:, :, 0:W], ALU.add, [k1, kbon], [k1])
            yield
            tt("dve", orT[:, :, sl], G1[:, :, 0:W], zrS[:, :, sl], ALU.mult, [k1, "zrS"], ["orT"])
            yield

        def run_all(gens):
            gens = list(gens)
            while gens:
                for gq in list(gens):
                    try:
                        next(gq)
                    except StopIteration:
                        gens.remove(gq)

        gC2 = T("gC2", [128, 8])
        _fl = lambda t, i: t[:, 2 * i:2 * i + 2, :].rearrange("p a b -> p (a b)")
        _v8 = lambda ap: ap.rearrange("p (a b) -> p a b", b=128)
        PS = [
            {"sfx": "_0", "rt": rt_[:], "at": at_[:], "bt": bt_[:], "kt": kt_[:], "Vt": Vt, "Bt": Bt, "Kt": Kt, "bon": bon, "gC": gC},
            {"sfx": "_1", "rt": _v8(_fl(wb[0], 0)), "at": _v8(_fl(wb[0], 1)), "bt": _v8(_fl(wb[0], 2)), "kt": _v8(_fl(wb[0], 3)),
             "Vt": _fl(wb[1], 0), "Bt": _fl(wb[1], 1), "Kt": _fl(wb[1], 2), "bon": _v8(_fl(wb[1], 3)), "gC": gC2},
        ]
        PS1_KEYS = [k + "_1" for k in ("rt", "at", "bt", "kt", "Vt", "Bt", "Kt", "bon")]

        def scan_group(g, S, PSp, yT):
            Ak_, Nk_, Qb_, LkT_, MbT_, MkT_, Xb_, SAb_ = S["Ak"], S["Nk"], S["Qb"], S["LkT"], S["MbT"], S["MkT"], S["Xb"], S["SAb"]
            b0, b1, b2 = S["banks"]
            kb = lambda i: "pb%d" % i
            n = S["n"]
            sfx = PSp["sfx"]
            rt_, at_, bt_, kt_, Vt, Bt, Kt, gC = PSp["rt"], PSp["at"], PSp["bt"], PSp["kt"], PSp["Vt"], PSp["Bt"], PSp["Kt"], PSp["gC"]
            K = lambda nm: nm + n
            heads = [4 * g + x for x in (0, 2, 1, 3)]
            SbK = "Sb%d" % g; SfK = "Sf%d" % g; yK = "yT%d" % g

            def hp(h):
                hl, hh = h % 2, h // 2
                return slice(hl * 64, hl * 64 + 64), hh
            v4 = lambda p: p[:].rearrange("p (a b) -> p a b", b=128)
            mk = lambda o: bc(cst[:, o:o + 128].unsqueeze(1), [128, 4, 128])
            ph("scores")
            plan = [(b0, "at_", "bt_", Ak_[0], K("Ak0"), C_SL), (b1, "bt_", "at_", Nk_[0], K("Nk0"), C_SU),
                    (b2, "kt_", "at_", LkT_, K("LkT"), C_SU), (b0, "bt_", "rt_", MbT_, K("MbT"), C_UI),
                    (b1, "kt_", "rt_", MkT_, K("MkT"), C_UI)]
            tl = {"at_": at_, "bt_": bt_, "kt_": kt_, "rt_": rt_}
            kn = {"at_": "at" + sfx, "bt_": "bt" + sfx, "kt_": "kt" + sfx, "rt_": "rt" + sfx}
            first_lo = (n == "_A")
            for rnd in (plan[0:3], plan[3:5]):
                for tagsel in ((0, 1) if first_lo else (1, 0)):
                    for (bk, ln, rn, dst, dk, msk) in rnd:
                        for hi, h in enumerate(heads):
                            if (h % 2) != tagsel:
                                continue
                            pr, hh = hp(h)
                            mm(pb[bk][:, hi * 128:(hi + 1) * 128], tl[ln][pr, hh, :], tl[rn][pr, hh, :], True, True, [kn[ln], kn[rn]], [kb(bk)])
                yield
                for (bk, ln, rn, dst, dk, msk) in rnd:
                    tt("dve", dst[:], v4(pb[bk]), mk(msk), ALU.mult, [kb(bk), "cst"], [dk])
                    yield
            ph("doubling")
            tt("pool", Qb_[:], Nk_[0][:], mk(C_ID), ALU.add, [K("Nk0"), "cst"], [K("Qb")])
            yield
            mm(pb[b2][:, :], idb[:], Qb_.rearrange("p a b -> p (a b)"), True, True,
               ["idb", K("Qb")], [kb(b2)])
            yield
            cur = 0
            for lvl in range(6):
                nx = 1 - cur
                for hi in range(4):
                    mm(pb[b0][:, hi * 128:(hi + 1) * 128], Nk_[cur][:, hi, :], Ak_[cur][:, hi, :], True, True,
                       [K("Nk%d" % cur), K("Ak%d" % cur)], [kb(b0)])
                if lvl < 5:
                    for hi in range(4):
                        mm(pb[b1][:, hi * 128:(hi + 1) * 128], Ak_[cur][:, hi, :], Nk_[cur][:, hi, :], True, True,
                           [K("Nk%d" % cur), K("Ak%d" % cur)], [kb(b1)])
                yield
                cp("act", Ak_[nx][:], v4(pb[b0]), [kb(b0)], [K("Ak%d" % nx)])
                if lvl < 5:
                    cp("dve", Nk_[nx][:], v4(pb[b1]), [kb(b1)], [K("Nk%d" % nx)])
                yield
                for hi in range(4):
                    mm(pb[b2][:, hi * 128:(hi + 1) * 128], Ak_[nx][:, hi, :], Qb_[:, hi, :], False, True,
                       [K("Ak%d" % nx), K("Qb")], [kb(b2)])
                yield
                if lvl % 2 == 0:
                    cp("act", Qb_[:], v4(pb[b2]), [kb(b2)], [K("Qb")])
                else:
                    cp("dve", Qb_[:], v4(pb[b2]), [kb(b2)], [K("Qb")])
                yield
                cur = nx
            ph("seq")
            for hi, h in enumerate(heads):
                pr, hh = hp(h)
                o = pb[b0][:, hi * 64:(hi + 1) * 64]
                mm(o, at_[pr, hh, :], Sb[pr, hh, :], True, False, ["at" + sfx, SbK], [kb(b0)])
                mm(o, LkT_[:, hi, :], Vt[:, h * 64:(h + 1) * 64], False, True, [K("LkT"), "Vt" + sfx], [kb(b0)])
            yield
            cp("act", Xb_[:], pb[b0][:, 0:256], [kb(b0)], [K("Xb")])
            yield
            for hi, h in enumerate(heads):
                mm(pb[b1][:, hi * 64:(hi + 1) * 64], Qb_[:, hi, :], Xb_[:, hi * 64:(hi + 1) * 64], True, True, [K("Qb"), K("Xb")], [kb(b1)])
            yield
            cp("dve", SAb_[:], pb[b1][:, 0:256], [kb(b1)], [K("SAb")])
            yield
            for hi, h in enumerate(heads):
                pr, hh = hp(h)
                o = pb[b2][pr, (hh - 2 * g) * 128:(hh - 2 * g) * 128 + 128]
                mm(o, Sb[pr, hh, :], rt_[pr, hh, :], True, False, [SbK, "rt" + sfx], [kb(b2)])
                mm(o, SAb_[:, hi * 64:(hi + 1) * 64], MbT_[:, hi, :], False, False, [K("SAb"), K("MbT")], [kb(b2)])
                mm(o, Vt[:, h * 64:(h + 1) * 64], MkT_[:, hi, :], False, True, ["Vt" + sfx, K("MkT")], [kb(b2)])
            yield
            cp("act", yT[:, 2 * g:2 * g + 2, :], pb[b2][:, 0:256].rearrange("p (a b) -> p a b", b=128), [kb(b2)], [yK])
            for hi, h in enumerate(heads):
                pr, hh = hp(h)
                o = pb[b0][pr, (hh - 2 * g) * 64:(hh - 2 * g) * 64 + 64]
                mm(o, Bt[:, h * 64:(h + 1) * 64], SAb_[:, hi * 64:(hi + 1) * 64], True, False, ["Bt" + sfx, K("SAb")], [kb(b0)])
                mm(o, Kt[:, h * 64:(h + 1) * 64], Vt[:, h * 64:(h + 1) * 64], False, True, ["Kt" + sfx, "Vt" + sfx], [kb(b0)])
            yield
            gs = slice(2 * g, 2 * g + 2)
            tt("dve", Sf[:, gs, :], Sf[:, gs, :], bc(gC[:, gs].unsqueeze(2), [128, 2, 64]), ALU.mult, [SfK, "gC" + sfx], [SfK])
            tt("dve", Sf[:, gs, :], Sf[:, gs, :], pb[b0][:, 0:128].rearrange("p (a b) -> p a b", b=64), ALU.add, [SfK, kb(b0)], [SfK])
            yield
            cp("act", Sb[:, gs, :], Sf[:, gs, :], [SfK], [SbK])
            yield


        def tail(NT, xsrc_tiles, ydst_tiles, nrows):
            ph("tail")
            for q in range(2):
                sgr = wload(wi_v, 8, (45 + 4 * q) * 128, 512, ikeys((45 + 4 * q) * 128, 512))
                sbr = wload(wbr_v, 8, q * 512, 512, ["scr_o"])
                sgc = wload(wi_v, 8, (53 + 4 * q) * 128, 512, ikeys((53 + 4 * q) * 128, 512))
                sbc = wload(wbc_v, 4, q * 512, 512, ["scr_o"])
                for jj in range(4):
                    j = q * 4 + jj
                    project(45 + j, sgr, jj, NT, 0)
                    act(sgb[:, 0:NT], pb[0][:, 0:NT], AF.Sigmoid, ["pb0"], ["sgb"])
                    for fc in range(8):
                        mm(pb[1][:, 0:NT], wb[sbr][:, fc, jj * 128:(jj + 1) * 128], orT[:, fc, 0:NT], fc == 0, fc == 7,
                           ["wb%d" % sbr, "orT"], ["pb1"])
                    tt("dve", m1[:, 0:NT], pb[1][:, 0:NT], sgb[:, 0:NT], ALU.mult, ["pb1", "sgb"], ["T5"])
                    project(53 + j, sgc, jj, NT, 2)
                    act(sgb[:, 0:NT], pb[2][:, 0:NT], AF.Sigmoid, ["pb2"], ["sgb"])
                    for fc in range(4):
                        mm(pb[3][:, 0:NT], wb[sbc][:, fc, jj * 128:(jj + 1) * 128], ocT[:, fc, 0:NT], fc == 0, fc == 3,
                           ["wb%d" % sbc, "kS"], ["pb3"])
                    tt("dve", tmpc[:, 0:NT], pb[3][:, 0:NT], sgb[:, 0:NT], ALU.mult, ["pb3", "sgb"], ["tmpc"])
                    tt("pool", mT[:, j, 0:NT], m1[:, 0:NT], tmpc[:, 0:NT], ALU.add, ["T5", "tmpc"], ["rS"])
            so = [wload(wo_v, 8, 0, 512, ["scr_o"]), wload(wo_v, 8, 512, 512, ["scr_o"])]
            npg = TT[2][:].rearrange("p a b -> p (a b)")
            dma(npg[:, :], npg_d.partition_broadcast(128), [], ["T2"])
            for i, (xsrc, ydst) in enumerate(zip(xsrc_tiles, ydst_tiles)):
                xb = xt[i % 2]
                xk = "xt%d" % (i % 2)
                dma(xb[0:nrows, :], xsrc, [], [xk])
                tsl = slice(i * 128, i * 128 + nrows)
                for hf in range(2):
                    for fc in range(8):
                        mm(pb[4 + hf][0:nrows, :], mT[:, fc, tsl], wb[so[hf]][:, fc, :], fc == 0, fc == 7,
                           ["rS", "wb%d" % so[hf]], ["pb%d" % (4 + hf)])
                for hf in range(2):
                    act(hb[0:nrows, hf * 512:(hf + 1) * 512], pb[4 + hf][0:nrows, :], AF.Square, ["pb%d" % (4 + hf)], ["hb", "small"],
                        accum=small[0:nrows, 8 + hf:9 + hf])
                tt("dve", small[0:nrows, 10:11], small[0:nrows, 8:9], small[0:nrows, 9:10], ALU.add, ["small"], ["small"])
                ts("dve", small[0:nrows, 11:12], small[0:nrows, 10:11], 1.0 / D, 1e-6, ALU.mult, ALU.add, ["small"], ["small"])
                rsq(small[0:nrows, 12:13], small[0:nrows, 11:12], 0.0, ["small"], "small")
                T0v = TT[0][:].rearrange("p a b -> p (a b)")
                for hf in range(2):
                    hsl = slice(hf * 512, (hf + 1) * 512)
                    stt("dve", T0v[0:nrows, hsl], pb[4 + hf][0:nrows, :], small[0:nrows, 12:13], npg[0:nrows, hsl], ALU.mult, ALU.mult,
                        ["pb%d" % (4 + hf), "small", "T2"], ["T0"])
                tt("pool", T0v[0:nrows, :], T0v[0:nrows, :], xb[0:nrows, :], ALU.add, ["T0", xk], ["T0"])
                dma(ydst, T0v[0:nrows, :], ["T0"], [], q="pool")

        def rms_phase(sc):
            t0 = sc * 512
            for i in range(4):
                xb = xt[i % 2]; xk = "xt%d" % (i % 2)
                dma(xb[:], xp[t0 + i * 128:t0 + (i + 1) * 128, :], [], [xk])
                last = (sc == 3 and i == 3)

                def hout(xb=xb, xk=xk):
                    T0v = TT[0][:].rearrange("p a b -> p (a b)")
                    dma(T0v[:, :], npre_d.partition_broadcast(128), [], ["T0"])
                    ts("dve", TT[1][:].rearrange("p a b -> p (a b)"), xb[:], small[:, 2:3], None, ALU.mult, None, [xk, "small"], ["T1"])
                    tt("dve", TT[1][:].rearrange("p a b -> p (a b)"), TT[1][:].rearrange("p a b -> p (a b)"), T0v, ALU.mult, ["T1", "T0"], ["T1"])
                    dma(nsp, TT[1][:].rearrange("p a b -> p (a b)")[127:128, :], ["T1"], [])
                rmsnorm_tile(xb, xk, 128, (i * 128, (i + 1) * 128), hout if last else None)

        rms_phase(0)
        prologue()
        for sc in range(4):
            t0 = sc * 512
            if sc > 0:
                rms_phase(sc)
            stage(3 if sc == 0 else 11)
            if sc > 0:
                cp("pool", tmpc[:, 0:120].rearrange("p (c w) -> p c w", w=30), uex[:, :, 512:542], ["uex"], ["tmpc"])
                cp("pool", uex[:, :, 0:30], tmpc[:, 0:120].rearrange("p (c w) -> p c w", w=30), ["tmpc"], ["uex"])
            proj_phase(512, False, sc % 2, (sc + 1) % 2)
            stage(4 if sc == 0 else 11)
            op("pool", lambda e: e.memset(dummy[:, 0:1], 0.0), [], ["wb3", "wb0", "wb1", "xt0", "xt1", "ua", "uaA", "uaB", "dummy", "yT0", "yT1", "yT2", "yT3"] + SETB_KEYS + PS1_KEYS)
            yTp = xt[0][:, :].rearrange("p (a b) -> p a b", b=128)
            Gp = (xt[1][:, :].rearrange("p (a b) -> p a b", b=128),
                  ua[:, 0:2, :].rearrange("p a b -> p (a b)").rearrange("p (a b) -> p a b", b=128),
                  ua[:, 2:4, :].rearrange("p a b -> p (a b)").rearrange("p (a b) -> p a b", b=128))
            Gpk = ("xt1", "uaA", "uaB")

            def chunk_scan(c4):
                PSp = PS[c4 % 2]
                for pair in ((0, 1), (2, 3)):
                    gens = [scan_group(pair[0], SETS[0], PSp, yTp), scan_group(pair[1], SETS[1], PSp, yTp)]
                    while gens:
                        for gq in list(gens):
                            try:
                                next(gq)
                                yield
                            except StopIteration:
                                gens.remove(gq)
                yield from gn_gen(yTp, ["yT0", "yT1", "yT2", "yT3"], c4 * 128, 128, Gp, Gpk, PSp["bon"], "bon" + PSp["sfx"])

            run_all([prep_gen(0, 128, False, PS[0])])
            for c4 in range(4):
                main = chunk_scan(c4)
                side = prep_gen((c4 + 1) * 128, 128, False, PS[(c4 + 1) % 2]) if c4 < 3 else None
                RATIO = int(os.environ.get("MK_RATIO", "4"))
                done = False
                while not done:
                    for _ in range(RATIO):
                        try:
                            next(main)
                        except StopIteration:
                            done = True
                            break
                    if side is not None:
                        try:
                            next(side)
                        except StopIteration:
                            side = None
                if side is not None:
                    run_all([side])
            op("pool", lambda e: e.memset(dummy[:, 1:2], 0.0), [], ["wb3", "wb0", "wb1", "xt0", "xt1", "ua", "uaA", "uaB", "dummy", "yT0", "yT1", "yT2", "yT3"] + SETB_KEYS + PS1_KEYS)
            stage(8 if sc == 0 else 11)
            ph("conv")
            cp("pool", ubf[:], uex[:], ["uex"], ["ubf"])
            for c in range(4):
                for w in range(31):
                    s = (c * 31 + w) % 4
                    if w % 2 == 0:
                        act(dg[s][:], idb[:], AF.Copy, ["idb", "col"], ["dg%d" % s], scale=col[:, O_CW + c * 31 + w:O_CW + c * 31 + w + 1])
                    else:
                        ts("dve", dg[s][:], idb[:], col[:, O_CW + c * 31 + w:O_CW + c * 31 + w + 1], None, ALU.mult, None,
                           ["idb", "col"], ["dg%d" % s])
                    mm(pb[6][:, :], dg[s][:], ubf[:, c, w:w + 512], w == 0, w == 30, ["dg%d" % s, "ubf"], ["pb6"])
                act(TT[c][:].rearrange("p a b -> p (a b)")[:, 0:512], pb[6][:, :], AF.Identity, ["pb6", "col"], ["T%d" % c],
                    bias=col[:, O_CB + c:O_CB + c + 1])
            ln_conv_out(512, [TT[c][:].rearrange("p a b -> p (a b)")[:, 0:512] for c in range(4)], ["T0", "T1", "T2", "T3"])
            if sc == 3:
                for c in range(4):
                    mm(pb[6][0:30, c * 128:(c + 1) * 128], uex[:, c, 512:542], C(C_ID), True, True, ["uex", "cst"], ["pb6"])
                cp("dve", tmpc[0:30, :], pb[6][0:30, :], ["pb6"], ["tmpc"])
                dma(ncp, tmpc[0:30, :], ["tmpc"], [])
            stage(9 if sc == 0 else 11)
            tail(512, [xp[t0 + i * 128:t0 + (i + 1) * 128, :] for i in range(4)],
                 [yp[t0 + i * 128:t0 + (i + 1) * 128, :] for i in range(4)], 128)

        stage(12)
        for h in range(16):
            hl, hh = h % 2, h // 2
            pr = slice(hl * 64, hl * 64 + 64)
            mm(pb[0][pr, hh * 64:(hh + 1) * 64], Sf[pr, hh, :], cst[pr, C_ID + hl * 64:C_ID + hl * 64 + 64], True, True, ALLSF + ["cst"], ["pb0"])
        cp("dve", tmpc[:, :], pb[0][:, :], ["pb0"], ["tmpc"])
        dma(nwp.rearrange("(hh p) j -> p hh j", p=128), tmpc[:, :].rearrange("p (a b) -> p a b", b=64), ["tmpc"], [])

        stage(13)
        ph("sample")
        xb = xt[0]
        dma(xb[0:NS, :], xs, [], ["xt0"])

        def hout_s():
            T0v = TT[0][:].rearrange("p a b -> p (a b)")
            T1v = TT[1][:].rearrange("p a b -> p (a b)")
            dma(T0v[0:NS, :], npre_d.partition_broadcast(NS), [], ["T0"])
            ts("dve", T1v[0:NS, :], xb[0:NS, :], small[0:NS, 2:3], None, ALU.mult, None, ["xt0", "small"], ["T1"])
            tt("dve", T1v[0:NS, :], T1v[0:NS, :], T0v[0:NS, :], ALU.mult, ["T1", "T0"], ["T1"])
            dma(nss, T1v[0:NS, :], ["T1"], [])
        rmsnorm_tile(xb, "xt0", NS, (0, NS), hout_s)
        dma(xt[1][0:NS, :], sshift, [], ["xt1"])
        cp("dve", hb[0:NS, :], xt[1][0:NS, :], ["xt1"], ["hb"])
        for dc in range(8):
            op("pe", lambda e, dc=dc: e.transpose(ptb[:, dc * 128:dc * 128 + NS], hb[0:NS, dc * 128:(dc + 1) * 128], idb[0:NS, 0:NS]),
               ["hb", "idb"], ["ptb"])
        cp("act", hT[:, :, NS:2 * NS], ptb[:, :].rearrange("p (a b) -> p a b", b=128)[:, :, 0:NS], ["ptb"], ["hT"])
        uv = [uex[:, c, 0:NS * 31].rearrange("p (n w) -> p n w", w=31) for c in range(4)]
        for q in range(4):
            dma(xt[1][0:120, 0:512], sconv[q * 120:(q + 1) * 120, :], [], ["xt1"])
            for c in range(4):
                mm(pb[6][:, c * 120:(c + 1) * 120], xt[1][0:120, c * 128:(c + 1) * 128], cst[0:120, C_ID:C_ID + 120], True, True,
                   ["xt1", "cst"], ["pb6"])
            for c in range(4):
                cp("dve", uv[c][:, q * 4:(q + 1) * 4, 0:30], pb[6][:, c * 120:(c + 1) * 120].rearrange("p (n w) -> p n w", w=30),
                   ["pb6"], ["uex"])
        dma(ncs[:, 0:29, :], sconv.rearrange("(n w) c -> n w c", w=30)[:, 1:30, :], [], [])
        proj_phase(2 * NS, True, 0, 0)
        stage(14)
        run_all([prep_gen(0, NS, True, PS[0])])
        EG, EnG, EGe, k2, aa, bb = TT[1], TT[2], TT[3], TT[5], TT[6], TT[7]
        SW = [ua[:, i, :].rearrange("p (a b) -> p a b", b=64) for i in range(2)]
        Dxs = [tmpb[:, :], xt[1][:, 0:512], xt[1][:, 512:1024], xt[0][:, 0:512], xt[0][:, 512:1024]]
        Dxk = ["tmpb", "xt1", "xt1", "xt0", "xt0"]
        yTs = TT[4]
        i2b = bc(cst[:, C_I2:C_I2 + 64].unsqueeze(1), [128, 8, 64])
        for n in range(NS):
            Sw = SW[n % 2]; sk = "ua"
            dma(Sw, swkv[n].rearrange("(hh p) j -> p hh j", p=128), [], [sk])
            vecs = [(aa, "T6"), (EG, "T1"), (bb, "T7"), (k2, "T5"), (rS, "rS")]
            for vi, (vt_, vk) in enumerate(vecs):
                tt("pool", Dxs[vi].rearrange("p (a b) -> p a b", b=64), i2b, bc(vt_[:, :, n:n + 1], [128, 8, 64]), ALU.mult,
                   ["cst", vk], [Dxk[vi]])
                mm(pb[vi][:, :], C(C_BO), Dxs[vi], True, True, ["cst", Dxk[vi]], ["pb%d" % vi])
            v8 = lambda p: p[:].rearrange("p (a b) -> p a b", b=64)
            W3 = TT[0][:, :, 0:64]
            tt("dve", W3, Sw, v8(pb[0]), ALU.mult, [sk, "pb0"], ["T0"])
            op("dve", lambda e: e.tensor_reduce(out=small[:, 16:24], in_=TT[0][:, :, 0:64], axis=AX.X, op=ALU.add), ["T0"], ["small"])
            tt("dve", Sw, Sw, v8(pb[1]), ALU.mult, [sk, "pb1"], [sk])
            tt("dve", W3, v8(pb[2]), bc(small[:, 16:24].unsqueeze(2), [128, 8, 64]), ALU.mult, ["pb2", "small"], ["T0"])
            tt("pool", Sw, Sw, W3, ALU.add, [sk, "T0"], [sk])
            cp("dve", small[:, 24:32], vS[:, :, n], ["vS"], ["small"])
            tt("dve", W3, v8(pb[3]), bc(small[:, 24:32].unsqueeze(2), [128, 8, 64]), ALU.mult, ["pb3", "small"], ["T0"])
            tt("pool", Sw, Sw, W3, ALU.add, [sk, "T0"], [sk])
            dma(nws[n].rearrange("(hh p) j -> p hh j", p=128), Sw, [sk], [])
            tt("dve", W3, Sw, v8(pb[4]), ALU.mult, [sk, "pb4"], ["T0"])
            op("dve", lambda e, n=n: e.tensor_reduce(out=yTs[:, :, n], in_=TT[0][:, :, 0:64], axis=AX.X, op=ALU.add), ["T0"], ["T4_0"])
        stage(15)
        run_all([gn_gen(yTs, ALLT4, 0, NS, (TT[1], TT[2], TT[3]), ("T1", "T2", "T3"), bon, "bon_0")])
        stage(16)
        cf = []
        for c in range(4):
            cwb = bc(col[:, O_CW + c * 31:O_CW + (c + 1) * 31].unsqueeze(1), [128, NS, 31])
            tt("dve", tmpc[:, 0:NS * 31].rearrange("p (n w) -> p n w", w=31), uv[c], cwb, ALU.mult, ["uex", "col"], ["tmpc"])
            cfc = TT[c][:].rearrange("p a b -> p (a b)")[:, 0:NS]
            op("dve", lambda e, cfc=cfc: e.tensor_reduce(out=cfc, in_=tmpc[:, 0:NS * 31].rearrange("p (n w) -> p n w", w=31),
                                                        axis=AX.X, op=ALU.add), ["tmpc"], ["T%d" % c])
            ts("dve", cfc, cfc, col[:, O_CB + c:O_CB + c + 1], None, ALU.add, None, ["T%d" % c, "col"], ["T%d" % c])
            cf.append(cfc)
        for c in range(4):
            cp("dve", tmpb[:, c * NS:(c + 1) * NS], uv[c][:, :, 30], ["uex"], ["tmpb"])
        for c in range(4):
            mm(pb[6][0:NS, c * 128:(c + 1) * 128], tmpb[:, c * NS:(c + 1) * NS], C(C_ID), True, True, ["tmpb", "cst"], ["pb6"])
        cp("dve", m1[0:NS, 0:512], pb[6][0:NS, :], ["pb6"], ["T5"])
        dma(ncs[:, 29, :], m1[0:NS, 0:512], ["T5"], [])
        ln_conv_out(NS, cf, ["T0", "T1", "T2", "T3"])
        stage(17)
        tail(NS, [xs], [ys], NS)
        P.emit()
    return nc


def _prep_consts():
    c = np.zeros((128, NCONST), np.float32)
    idx = np.arange(128)
    c[:, C_ID:C_ID + 128] = np.eye(128)
    c[:, C_SL:C_SL + 128] = (idx[None, :] < idx[:, None])
    c[:, C_SU:C_SU + 128] = (idx[:, None] < idx[None, :])
    c[:, C_UI:C_UI + 128] = (idx[:, None] <= idx[None, :])
    c[:, C_TRI:C_TRI + 128] = (idx[:, None] <= idx[None, :]) * CNEG
    c[:, C_TRE:C_TRE + 128] = (idx[:, None] < idx[None, :]) * CNEG
    blk = (idx[:, None] // 64 == idx[None, :] // 64).astype(np.float32)
    c[:, C_BM:C_BM + 128] = blk / 64.0
    c[:, C_BO:C_BO + 128] = blk
    c[:, C_AM:C_AM + 128] = 1.0 / 512.0
    c[:, C_NI:C_NI + 64] = np.eye(128)[:, :64] * CNEG
    c[:, C_I2:C_I2 + 64] = (idx[:, None] % 64 == np.arange(64)[None, :])
    return c


_NC = None


def kernel(x_prompt, x_sample, state_shift, state_wkv, state_conv, norm_pre_g, w_in, mu_shift,
           decay_w0, decay_w2, iclr_a0, iclr_a2, k_k, k_a, r_k, gn_g, gn_b, conv_glu_b, conv_w,
           conv_b, ln_c_g, ln_c_b, w_branch_r, w_branch_c, w_out, norm_post_g):
    global _NC
    f = lambda a: np.ascontiguousarray(np.asarray(a, dtype=np.float32))
    colv = lambda v, n: f(v).reshape(n, 128).T
    cols = np.zeros((128, NCOL), np.float32)
    cols[:, O_MU:O_MU + 33] = colv(mu_shift[0], 33)
    cols[:, O_KK:O_KK + 8] = colv(k_k[0], 8)
    cols[:, O_KA:O_KA + 8] = colv(k_a[0], 8)
    cols[:, O_RK:O_RK + 8] = colv(np.asarray(r_k[0]).reshape(-1), 8)
    cols[:, O_GNG:O_GNG + 8] = colv(gn_g[0], 8)
    cols[:, O_GNB:O_GNB + 8] = colv(gn_b[0], 8)
    cols[:, O_A0:O_A0 + 8] = colv(iclr_a0[0], 8)
    cols[:, O_GLUB:O_GLUB + 8] = colv(conv_glu_b[0], 8)
    cols[:, O_CB:O_CB + 4] = colv(conv_b[0], 4)
    cols[:, O_LNG:O_LNG + 4] = colv(ln_c_g[0], 4)
    cols[:, O_LNB:O_LNB + 4] = colv(ln_c_b[0], 4)
    cw = f(conv_w[0])
    cols[:, O_CW:O_CW + 124] = cw.reshape(31, 4, 128).transpose(2, 1, 0).reshape(128, 124)
    cols[:, O_GPRE:O_GPRE + 8] = colv(norm_pre_g[0], 8)
    consts = _prep_consts()
    w2ext = np.concatenate([f(decay_w2[0]), f(decay_w0[0])[None, :]], axis=0)
    shared = {
        "w_in": f(w_in[0]), "w_br": f(w_branch_r[0]), "w_bc": f(w_branch_c[0]), "w_out": f(w_out[0]),
        "cols": cols, "consts": consts, "w2ext": f(w2ext), "a2": f(iclr_a2[0]),
        "npg": f(norm_post_g[0])[None, :], "npre": f(norm_pre_g[0])[None, :],
    }
    xpf = f(x_prompt); xsf = f(x_sample).reshape(128, D); ssf = f(state_shift[0])
    swf = f(state_wkv[0]).reshape(128, 1024, 64); scf = f(state_conv[0]).reshape(128 * 30, 512)
    in_maps = []
    for c in range(8):
        m = dict(shared)
        m["xp"] = xpf[c]
        m["xs"] = xsf[c * NS:(c + 1) * NS]
        m["sshift"] = ssf[c * NS:(c + 1) * NS]
        m["swkv"] = swf[c * NS:(c + 1) * NS]
        m["sconv"] = scf[c * NS * 30:(c + 1) * NS * 30]
        in_maps.append(m)
    if _NC is None:
        _NC = build()
    res = run_bass_kernel_spmd(_NC, in_maps, core_ids=list(range(8)))
    R = res.results
    y_prompt = np.stack([R[c]["yp"] for c in range(8)]).astype(np.float32)
    y_sample = np.concatenate([R[c]["ys"] for c in range(8)]).reshape(128, 1, D).astype(np.float32)
    nsp_ = np.concatenate([R[c]["nsp"] for c in range(8)]).reshape(1, 8, D).astype(np.float32)
    nwp_ = np.stack([R[c]["nwp"] for c in range(8)]).reshape(1, 8, 16, 64, 64).astype(np.float32)
    ncp_ = np.stack([R[c]["ncp"] for c in range(8)]).reshape(1, 8, 30, 512).astype(np.float32)
    nss_ = np.concatenate([R[c]["nss"] for c in range(8)]).reshape(1, 128, D).astype(np.float32)
    nws_ = np.concatenate([R[c]["nws"] for c in range(8)]).reshape(1, 128, 16, 64, 64).astype(np.float32)
    ncs_ = np.concatenate([R[c]["ncs"] for c in range(8)]).reshape(1, 128, 30, 512).astype(np.float32)
    return (y_prompt, y_sample, nsp_, nwp_, ncp_, nss_, nws_, ncs_)
```

```python
import contextlib
import numpy as np
import concourse.bass as bass
import concourse.mybir as mybir
from concourse.bass_utils import run_bass_kernel_spmd

F32 = mybir.dt.float32
BF16 = mybir.dt.bfloat16
AF = mybir.ActivationFunctionType
ALU = mybir.AluOpType
AX = mybir.AxisListType

D = 1024
NIN = 7808
SEQ = 2048
NS = 16
NCH = 61
CNEG = -0.6065306597126334

O_MU = 0; O_KK = 33; O_KA = 41; O_RK = 49; O_GNG = 57; O_GNB = 65; O_A0 = 73; O_GLUB = 81
O_CB = 89; O_LNG = 93; O_LNB = 97; O_CW = 101; O_GPRE = 225; O_OMM = 233; O_OMKA = 266; NCOL = 274
C_ID = 0; C_SL = 128; C_SU = 256; C_UI = 384; C_TRI = 512; C_TRE = 640; C_BM = 768; C_BO = 896
C_AM = 1024; C_NI = 1152; C_I2 = 1280; NCONST = 1344


class Prog:
    ENG = ("pe", "act", "dve", "pool", "sp")

    def __init__(self, nc):
        self.nc = nc
        self.ops = {e: [] for e in self.ENG}
        self.cnt = {e: 0 for e in self.ENG}
        self.waited = {e: {} for e in self.ENG}
        self.lastw = {}
        self.readers = {}
        self.dcnt = {}
        self.dead = False
        self.phase = ""
        self.annotate = False

    def _need(self, eng, waits, tok):
        if tok is None:
            return
        kind, key, val = tok
        if kind == "e" and key == "pe" and eng == "pe":
            return
        k = (kind, key)
        if self.waited[eng].get(k, 0) >= val:
            return
        if waits.get(k, 0) < val:
            waits[k] = val

    def op(self, eng, fn, reads=(), writes=(), dma=None, tag=None):
        if self.dead:
            return None
        waits = {}
        if eng == "pe":
            prev = getattr(self, "petag", None)
            if tag is not None and prev is not None and tag != prev:
                waits[("e", "pe")] = self.cnt["pe"]
            self.petag = tag
        for r in reads:
            self._need(eng, waits, self.lastw.get(r))
        for w in writes:
            self._need(eng, waits, self.lastw.get(w))
            for rd in self.readers.get(w, ()):
                self._need(eng, waits, rd)
        for k, v in waits.items():
            self.waited[eng][k] = v
        if dma is not None:
            prevc = self.dcnt.get(dma, 0)
            if prevc > 0 and self.waited[eng].get(("d", dma), 0) < prevc:
                waits[("d", dma)] = max(waits.get(("d", dma), 0), prevc)
                self.waited[eng][("d", dma)] = prevc
            self.dcnt[dma] = prevc + 1
            tok = ("d", dma, self.dcnt[dma])
        else:
            self.cnt[eng] += 1
            tok = ("e", eng, self.cnt[eng])
        self.ops[eng].append((waits, fn, tok, self.phase))
        for r in reads:
            self.readers.setdefault(r, []).append(tok)
        for w in writes:
            self.lastw[w] = tok
            self.readers[w] = []
        return tok

    def emit(self):
        nc = self.nc
        with contextlib.ExitStack() as st:
            esem = {e: st.enter_context(nc.semaphore("s_" + e)) for e in self.ENG}
            dsem = {k: st.enter_context(nc.semaphore("d_" + str(k))) for k in self.dcnt}
            block = st.enter_context(nc.Block())

            def run(engname, e):
                for waits, fn, tok, ph in self.ops[engname]:
                    for (kind, key), val in waits.items():
                        if kind == "e":
                            e.wait_ge(esem[key], val)
                        else:
                            e.wait_ge(dsem[key], 16 * val)
                    ins = fn(e)
                    if self.annotate:
                        ins.annotate(ph)
                    if tok[0] == "e":
                        ins.then_inc(esem[tok[1]], 1)
                    else:
                        ins.then_inc(dsem[tok[1]], 16)
                if engname == "sp":
                    for k, c in self.dcnt.items():
                        e.wait_ge(dsem[k], 16 * c)

            @block.tensor
            def _(e):
                run("pe", e)

            @block.scalar
            def _(e):
                run("act", e)

            @block.vector
            def _(e):
                run("dve", e)

            @block.gpsimd
            def _(e):
                run("pool", e)

            @block.sync
            def _(e):
                run("sp", e)


def build():
    nc = bass.Bass("TRN2", target_bir_lowering=False)
    di = lambda n, s: nc.dram_tensor(n, s, F32, kind="ExternalInput").ap()
    do = lambda n, s: nc.dram_tensor(n, s, F32, kind="ExternalOutput").ap()
    xp = di("xp", [SEQ, D]); xs = di("xs", [NS, D]); sshift = di("sshift", [NS, D])
    swkv = di("swkv", [NS, 1024, 64]); sconv = di("sconv", [NS * 30, 512])
    w_in = di("w_in", [D, NIN]); w_br = di("w_br", [D, D]); w_bc = di("w_bc", [512, D]); w_out = di("w_out", [D, D])
    cols_d = di("cols", [128, NCOL]); consts_d = di("consts", [128, NCONST])
    w2ext_d = di("w2ext", [65, 1024]); a2_d = di("a2", [64, 1024])
    npg_d = di("npg", [1, D]); npre_d = di("npre", [1, D])
    yp = do("yp", [SEQ, D]); ys = do("ys", [NS, D]); nsp = do("nsp", [1, D])
    nwp = do("nwp", [1024, 64]); ncp = do("ncp", [30, 512]); nss = do("nss", [NS, D])
    nws = do("nws", [NS, 1024, 64]); ncs = do("ncs", [NS, 30, 512])
    wi_s = nc.dram_tensor("wi_s", [D, NIN], BF16).ap()
    wbr_s = nc.dram_tensor("wbr_s", [D, D], BF16).ap()
    wbc_s = nc.dram_tensor("wbc_s", [512, D], BF16).ap()
    wo_s = nc.dram_tensor("wo_s", [D, D], BF16).ap()

    with contextlib.ExitStack() as st:
        def T(n, s, d=F32):
            return st.enter_context(nc.sbuf_tensor("sb_" + n, s, d))
        P = Prog(nc)
        op = P.op
        cst = T("cst", [128, NCONST]); col = T("col", [128, NCOL])
        idb = T("idb", [128, 128], BF16)
        bob = T("bob", [128, 128], BF16)
        w2e = T("w2e", [65, 1024]); a2b = T("a2b", [128, 1024], BF16)
        wb = [T("wb%d" % i, [128, 8, 512], BF16) for i in range(4)]
        xt = [T("xt%d" % i, [128, D]) for i in range(2)]
        hb = T("hb", [128, D], BF16)
        hT = T("hT", [128, 8, 512], BF16)
        rS = T("rS", [128, 8, 512], BF16); kS = T("kS", [128, 8, 512], BF16)
        vS = T("vS", [128, 8, 512], BF16); zrS = T("zrS", [128, 8, 512], BF16)
        twl = T("twl", [65, 512]); alb = T("alb", [128, 512], BF16)
        ua = T("ua", [128, 4, 512]); uex = T("uex", [128, 4, 542]); ubf = T("ubf", [128, 4, 542], BF16)
        szc = T("szc", [128, 4, 512], BF16)
        orT = T("orT", [128, 8, 512], BF16)
        mT = rS
        ocT = kS
        TT = [T("T%d" % i, [128, 8, 128]) for i in range(8)]
        bon = T("bon", [128, 8, 128], BF16)
        rt_ = T("rt_", [128, 8, 128], BF16); at_ = T("at_", [128, 8, 128], BF16); bt_ = T("bt_", [128, 8, 128], BF16)
        kt_ = T("kt_", [128, 8, 128], BF16); bh_ = T("bh_", [128, 8, 128], BF16); kh_ = T("kh_", [128, 8, 128], BF16)
        Vt = T("Vt", [128, 1024], BF16); Bt = T("Bt", [128, 1024], BF16); Kt = T("Kt", [128, 1024], BF16)
        Ak = [T("Ak%d" % i, [128, 4, 128], BF16) for i in range(2)]
        Nk = [T("Nk%d" % i, [128, 4, 128], BF16) for i in range(2)]
        Qb = T("Qb", [128, 4, 128], BF16)
        LkT = T("LkT", [128, 4, 128], BF16); MbT = T("MbT", [128, 4, 128], BF16); MkT = T("MkT", [128, 4, 128], BF16)
        Xb = T("Xb", [128, 256], BF16); SAb = T("SAb", [128, 256], BF16)
        Xb2 = T("Xb2", [128, 256], BF16); SAb2 = T("SAb2", [128, 256], BF16)
        dummy = T("dummy", [128, 8])
        _w3 = lambda i: wb[3][:, i, :].rearrange("p (a b) -> p a b", b=128)
        SETS = [
            {"Ak": Ak, "Nk": Nk, "Qb": Qb, "LkT": LkT, "MbT": MbT, "MkT": MkT, "Xb": Xb, "SAb": SAb, "banks": (0, 1, 2), "n": "_A"},
            {"Ak": [_w3(0), _w3(1)], "Nk": [_w3(2), _w3(3)], "Qb": _w3(4), "LkT": _w3(5), "MbT": _w3(6), "MkT": _w3(7),
             "Xb": Xb2, "SAb": SAb2, "banks": (3, 4, 5), "n": "_B"},
        ]
        SETB_KEYS = [k + "_B" for k in ("Ak0", "Ak1", "Nk0", "Nk1", "Qb", "LkT", "MbT", "MkT")]
        ALLT4 = ["T4_0", "T4_1", "T4_2", "T4_3"]
        ALLSF = ["Sf0", "Sf1", "Sf2", "Sf3"]
        ALLSB = ["Sb0", "Sb1", "Sb2", "Sb3"]
        Sf = T("Sf", [128, 8, 64]); Sb = T("Sb", [128, 8, 64], BF16)
        gC = T("gC", [128, 8]); tmpb = T("tmpb", [128, 512]); tmpc = T("tmpc", [128, 512])
        sgb = T("sgb", [128, 512], BF16)
        m1 = TT[5][:].rearrange("p a b -> p (a b)")
        pprev = [T("pprev%d" % i, [128, 40]) for i in range(2)]
        small = T("small", [128, 64])
        dg = [T("dg%d" % i, [128, 128], BF16) for i in range(4)]
        wld = xt
        pb = [st.enter_context(nc.psum_tensor("pb%d" % i, [128, 512], F32)) for i in range(7)]
        ptb = st.enter_context(nc.psum_tensor("ptb", [128, 1024], BF16))

        cnt = {"d": 0, "e": 0}
        import os
        STOP = float(os.environ.get("MK_STOP", "1000"))

        def stage(k):
            if k > STOP:
                P.dead = True
        P.annotate = bool(os.environ.get("MK_ANN"))

        def ph(name):
            P.phase = name

        def dma(out, in_, reads, writes, q="sp"):
            cnt[q] = cnt.get(q, 0) + 1
            key = "%s%d" % (q, cnt[q] % (16 if q == "sp" else 8))
            if q == "act":
                return op("act", lambda e: e.dma_start(out=out, in_=in_), reads, writes, dma=key)
            return op(q, lambda e: e.dma_start(out=out, in_=in_), reads, writes, dma=key)

        def mm(out, lhsT, rhs, start, stop, reads, writes):
            b0 = lhsT.base_partition()
            n0 = lhsT.shape[0]
            tag = "lo" if b0 + n0 <= 64 else ("hi" if b0 >= 64 else None)
            op("pe", lambda e: e.matmul(out, lhsT=lhsT, rhs=rhs, start=start, stop=stop), reads, writes, tag=tag)

        def act(out, in_, func, reads, writes, bias=None, scale=None, accum=None):
            kw = {}
            if bias is not None: kw["bias"] = bias
            if scale is not None: kw["scale"] = scale
            if accum is not None: kw["accum_out"] = accum
            op("act", lambda e: e.activation(out=out, in_=in_, func=func, **kw), reads, writes)

        def tt(eng, out, in0, in1, o, reads, writes):
            g = {"dve": "dve", "pool": "pool"}[eng]
            op(g, lambda e: e.tensor_tensor(out=out, in0=in0, in1=in1, op=o), reads, writes)

        def ts(eng, out, in0, s1, s2, o0, o1, reads, writes):
            if s2 is None:
                op(eng, lambda e: e.tensor_scalar(out=out, in0=in0, scalar1=s1, scalar2=None, op0=o0), reads, writes)
            else:
                op(eng, lambda e: e.tensor_scalar(out=out, in0=in0, scalar1=s1, scalar2=s2, op0=o0, op1=o1), reads, writes)

        def stt(eng, out, in0, sc, in1, o0, o1, reads, writes):
            op(eng, lambda e: e.scalar_tensor_tensor(out=out, in0=in0, scalar=sc, in1=in1, op0=o0, op1=o1), reads, writes)

        def cp(eng, out, in_, reads, writes):
            if eng == "act":
                act(out, in_, AF.Copy, reads, writes)
            else:
                op(eng, lambda e: e.tensor_copy(out=out, in_=in_), reads, writes)

        def rsq(out, in_, eps, reads, wkey):
            act(out, in_, AF.Sqrt, reads, [wkey], bias=eps)
            op("dve", lambda e: e.reciprocal(out=out, in_=out), [wkey], [wkey])

        def bc(ap, shape):
            return ap.to_broadcast(shape)

        C = lambda o, n=128: cst[:, o:o + n]

        dma(cst[:], consts_d, [], ["cst"])
        dma(col[:], cols_d, [], ["col"])
        dma(w2e[:], w2ext_d, [], ["w2e"])
        dma(wld[0][64:128, 0:1024], a2_d, [], ["xt0"])
        cp("dve", a2b[64:128, :], wld[0][64:128, 0:1024], ["xt0"], ["a2b"])
        cp("dve", idb[:], C(C_ID), ["cst"], ["idb"])
        cp("dve", bob[:], C(C_BO), ["cst"], ["bob"])
        ts("dve", col[:, O_OMM:O_OMM + 33], col[:, O_MU:O_MU + 33], -1.0, 1.0, ALU.mult, ALU.add, ["col"], ["col"])
        ts("dve", col[:, O_OMKA:O_OMKA + 8], col[:, O_KA:O_KA + 8], -1.0, 1.0, ALU.mult, ALU.add, ["col"], ["col"])
        op("pool", lambda e: e.memset(twl[64:65, :], 1.0), [], ["twl"])
        op("pool", lambda e: e.memset(Sf[:], 0.0), [], ALLSF)
        op("pool", lambda e: e.memset(Sb[:], 0.0), [], ALLSB)
        op("pool", lambda e: e.memset(pprev[0][:], 0.0), [], ["pprev0"])
        op("pool", lambda e: e.memset(pprev[1][:], 0.0), [], ["pprev1"])
        op("pool", lambda e: e.memset(uex[:], 0.0), [], ["uex"])

        stage(1)
        ph("prologue")
        def prologue_gen(part):
            ph("prologue")
            pieces = []
            for c0 in range(0, NIN, 1024):
                for rc in range(8):
                    pieces.append((w_in, wi_s, rc, c0, min(1024, NIN - c0), "scr_i%d" % (c0 // 1024)))
            for rc in range(8):
                pieces.append((w_br, wbr_s, rc, 0, 1024, "scr_o"))
            for rc in range(4):
                pieces.append((w_bc, wbc_s, rc, 0, 1024, "scr_o"))
            for rc in range(8):
                pieces.append((w_out, wo_s, rc, 0, 1024, "scr_o"))
            fl = lambda t: t[:].rearrange("p a b -> p (a b)")
            orv = lambda i: orT[:, 2 * i:2 * i + 2, :].rearrange("p a b -> p (a b)")
            if part == 1:
                pieces = pieces[0:48]
                sf32 = [(fl(TT[i]), "T%d" % i) for i in range(8)]
                sbf = [(fl(rt_), "rt_0"), (fl(at_), "at_0"), (fl(bt_), "bt_0"), (fl(kt_), "kt_0"), (fl(bh_), "bh_"), (fl(kh_), "kh_"),
                       (Vt[:, :], "Vt_0"), (Bt[:, :], "Bt_0"), (Kt[:, :], "Kt_0")]
                DEPTH = 6
            else:
                pieces = pieces[48:]
                sf32 = [(fl(TT[i]), "T%d" % i) for i in (3, 4, 6, 7)]
                sbf = [(orv(0), "orT"), (orv(1), "orT"), (orv(2), "orT"), (orv(3), "orT"),
                       (fl(kh_), "kh_"), (Vt[:, :], "Vt_0"), (Bt[:, :], "Bt_0"), (Kt[:, :], "Kt_0")]
                DEPTH = 3
            NB = len(sf32)
            engs = ["dve", "act"]
            npc = len(pieces)
            for i in range(npc + DEPTH):
                if i < npc:
                    src, dst, rc, c0, n, skey = pieces[i]
                    bf_, kf_ = sf32[i % NB]
                    dma(bf_[:, 0:n], src[rc * 128:(rc + 1) * 128, c0:c0 + n], [], [kf_],
                        q=("act" if (os.environ.get("MK_ACTQ") and i % 2 == 1) else "sp"))
                j = i - DEPTH
                if j >= 0:
                    src, dst, rc, c0, n, skey = pieces[j]
                    bf_, kf_ = sf32[j % NB]
                    bb_, kb_ = sbf[j % len(sbf)]
                    cp(engs[j % 2], bb_[:, 0:n], bf_[:, 0:n], [kf_], [kb_])
                    dma(dst[rc * 128:(rc + 1) * 128, c0:c0 + n], bb_[:, 0:n], [kb_], [skey], q="pool")
                yield

        stage(2)
        wi_v = wi_s.rearrange("(dc p) n -> p dc n", p=128)
        wbr_v = wbr_s.rearrange("(dc p) n -> p dc n", p=128)
        wbc_v = wbc_s.rearrange("(dc p) n -> p dc n", p=128)
        wo_v = wo_s.rearrange("(dc p) n -> p dc n", p=128)
        wslot = {"i": 0}

        def wload(view, ndc, c0, n, skeys):
            s = wslot["i"] % 4
            wslot["i"] += 1
            dma(wb[s][:, 0:ndc, 0:n], view[:, :, c0:c0 + n], skeys, ["wb%d" % s])
            return s

        def ikeys(c0, n):
            return ["scr_i%d" % b for b in range(c0 // 1024, (c0 + n - 1) // 1024 + 1)]

        def rmsnorm_tile(xtile, key, npart, dst_cols, want_h_out=None):
            ph("rmsnorm")
            act(hb[0:npart, :], xtile[0:npart, :], AF.Square, [key], ["hb", "small"], accum=small[0:npart, 0:1])
            ts("dve", small[0:npart, 1:2], small[0:npart, 0:1], 1.0 / D, 1e-6, ALU.mult, ALU.add, ["small"], ["small"])
            rsq(small[0:npart, 2:3], small[0:npart, 1:2], 0.0, ["small"], "small")
            ts("dve", hb[0:npart, :], xtile[0:npart, :], small[0:npart, 2:3], None, ALU.mult, None, [key, "small"], ["hb"])
            if want_h_out is not None:
                want_h_out()
            for dc in range(8):
                op("pe", lambda e, dc=dc: e.transpose(ptb[:, dc * 128:dc * 128 + npart], hb[0:npart, dc * 128:(dc + 1) * 128], idb[0:npart, 0:npart]),
                   ["hb", "idb"], ["ptb"])
            for dc in range(8):
                act(hT[:, dc, dst_cols[0]:dst_cols[1]], ptb[:, dc * 128:dc * 128 + npart], AF.Copy, ["ptb", "col"], ["hT"],
                    scale=col[:, O_GPRE + dc:O_GPRE + dc + 1])

        def project(j, wslot_i, jj, NT, bank):
            for dc in range(8):
                mm(pb[bank][:, 0:NT], wb[wslot_i][:, dc, jj * 128:(jj + 1) * 128], hT[:, dc, 0:NT], dc == 0, dc == 7,
                   ["wb%d" % wslot_i, "hT"], ["pb%d" % bank])

        def shiftmix(j, bank, NT, dst, dkey, sample, pp_old, pp_new):
            p = pb[bank]
            mu = col[:, O_MU + j:O_MU + j + 1]
            omm = col[:, O_OMM + j:O_OMM + j + 1]
            bk = "pb%d" % bank
            if sample:
                act(tmpb[:, 0:NS], p[:, 0:NS], AF.Copy, [bk, "col"], ["tmpb"], scale=omm)
                stt("dve", dst, p[:, NS:2 * NS], mu, tmpb[:, 0:NS], ALU.mult, ALU.add, [bk, "tmpb", "col"], [dkey])
            else:
                act(tmpb[:, 0:NT], p[:, 0:NT], AF.Copy, [bk, "col"], ["tmpb"], scale=omm)
                act(pprev[pp_new][:, j:j + 1], p[:, NT - 1:NT], AF.Copy, [bk], ["pprev%d" % pp_new])
                stt("dve", dst[:, 1:NT], p[:, 0:NT - 1], mu, tmpb[:, 1:NT], ALU.mult, ALU.add, [bk, "tmpb", "col"], [dkey])
                stt("dve", dst[:, 0:1], pprev[pp_old][:, j:j + 1], mu, tmpb[:, 0:1], ALU.mult, ALU.add,
                    ["pprev%d" % pp_old, "tmpb", "col"], [dkey])

        def proj_phase(NT, sample, pp_old, pp_new):
            ph("proj")
            nb = 0
            for g0 in range(0, 45, 4):
                ng = min(4, 45 - g0)
                s = wload(wi_v, 8, g0 * 128, ng * 128, ikeys(g0 * 128, ng * 128))
                for jj in range(ng):
                    j = g0 + jj
                    bank = nb % 2
                    nb += 1
                    bk = "pb%d" % bank
                    project(j, s, jj, NT if not sample else 2 * NS, bank)
                    W = NS if sample else NT
                    if j < 8:
                        shiftmix(j, bank, NT, rS[:, j, 0:W], "rS", sample, pp_old, pp_new)
                    elif j < 16:
                        shiftmix(j, bank, NT, kS[:, j - 8, 0:W], "kS", sample, pp_old, pp_new)
                    elif j < 24:
                        shiftmix(j, bank, NT, vS[:, j - 16, 0:W], "vS", sample, pp_old, pp_new)
                    elif j < 32:
                        shiftmix(j, bank, NT, tmpc[:, 0:W], "tmpc", sample, pp_old, pp_new)
                        act(zrS[:, j - 24, 0:W], tmpc[:, 0:W], AF.Silu, ["tmpc"], ["zrS"])
                    elif j == 32:
                        shiftmix(j, bank, NT, tmpc[:, 0:W], "tmpc", sample, pp_old, pp_new)
                        act(twl[0:64, 0:W], tmpc[0:64, 0:W], AF.Tanh, ["tmpc"], ["twl"])
                        cp("pool", alb[64:128, 0:W], tmpc[64:128, 0:W], ["tmpc"], ["alb"])
                    elif j < 37:
                        c = j - 33
                        act(ua[:, c, 0:W], pb[bank][:, 0:W], AF.Identity, [bk, "col"], ["ua"],
                            bias=col[:, O_GLUB + c:O_GLUB + c + 1])
                    elif j < 41:
                        c = j - 37
                        act(tmpc[:, 0:W], pb[bank][:, 0:W], AF.Sigmoid, [bk, "col"], ["tmpc"],
                            bias=col[:, O_GLUB + 4 + c:O_GLUB + 5 + c])
                        if sample:
                            tt("dve", uex[:, c, 0:NS * 31].rearrange("p (n w) -> p n w", w=31)[:, :, 30], ua[:, c, 0:W], tmpc[:, 0:W],
                               ALU.mult, ["ua", "tmpc"], ["uex"])
                        else:
                            tt("dve", uex[:, c, 30:30 + W], ua[:, c, 0:W], tmpc[:, 0:W], ALU.mult, ["ua", "tmpc"], ["uex"])
                    else:
                        c = j - 41
                        act(szc[:, c, 0:W], pb[bank][:, 0:W], AF.Silu, [bk], ["szc"])
                    yield

        def ln_conv_out(W, cf, ck):
            ph("lnconv")
            for c in range(4):
                mm(pb[2][:, 0:W], C(C_AM), cf[c], c == 0, c == 3, ["cst", ck[c]], ["pb2"])
            for c in range(4):
                tt("dve", cf[c], cf[c], pb[2][:, 0:W], ALU.subtract, [ck[c], "pb2"], [ck[c]])
            for c in range(4):
                tt("pool", ua[:, c, 0:W], cf[c], cf[c], ALU.mult, [ck[c]], ["ua"])
            for c in range(4):
                mm(pb[3][:, 0:W], C(C_AM), ua[:, c, 0:W], c == 0, c == 3, ["cst", "ua"], ["pb3"])
            rsq(tmpc[:, 0:W], pb[3][:, 0:W], 1e-5, ["pb3"], "tmpc")
            for c in range(4):
                tt("dve", cf[c], cf[c], tmpc[:, 0:W], ALU.mult, [ck[c], "tmpc"], [ck[c]])
                act(ua[:, c, 0:W], cf[c], AF.Silu, [ck[c], "col"], ["ua"],
                    bias=col[:, O_LNB + c:O_LNB + c + 1], scale=col[:, O_LNG + c:O_LNG + c + 1])
                tt("pool", ocT[:, c, 0:W], ua[:, c, 0:W], szc[:, c, 0:W], ALU.mult, ["ua", "szc"], ["kS"])

        def prep_gen(cs, W, sample, PSp):
            T0, T1, T2, T3, T4, T5, T6, T7 = TT
            sl = slice(cs, cs + W)
            sfx = PSp["sfx"]
            bonT, gCt = PSp["bon"], PSp["gC"]
            kbon, kgc = "bon" + sfx, "gC" + sfx
            ph("prep")
            sh = [128, 8, W]
            colb = lambda o: bc(col[:, o:o + 8].unsqueeze(2), sh)
            p6 = pb[6]
            p6v = p6[:].rearrange("p (a b) -> p a b", b=128)[:, :, 0:W]
            T0v = T0[:].rearrange("p a b -> p (a b)")
            tt("dve", T5[:, :, 0:W], kS[:, :, sl], colb(O_KK), ALU.mult, ["kS", "col"], ["T5"])
            tt("pool", bh_[:, :, 0:W], T5[:, :, 0:W], T5[:, :, 0:W], ALU.mult, ["T5"], ["bh_"])
            yield
            for hf in range(2):
                mm(p6[0:W, :], twl[0:65, sl], w2e[0:65, hf * 512:(hf + 1) * 512], True, True, ["twl", "w2e"], ["pb6"])
                yield
                act(T0v[0:W, hf * 512:(hf + 1) * 512], p6[0:W, :], AF.Sigmoid, ["pb6"], ["T0"])
                yield
            tri = C(C_TRI) if not sample else cst[0:W, C_NI:C_NI + W]
            tre = C(C_TRE) if not sample else cst[0:W, C_NI + 64:C_NI + 64 + W]
            for hf in range(2):
                hs = slice(hf * 4, hf * 4 + 4)
                for hq in range(4):
                    hh = hf * 4 + hq
                    mm(p6[:, hq * 128:hq * 128 + W], T0v[0:W, hh * 128:(hh + 1) * 128], tri[0:W, 0:W], True, True, ["T0", "cst"], ["pb6"])
                yield
                act(T1[:, hs, 0:W], p6v[:, 0:4, :], AF.Exp, ["pb6"], ["T1"])
                act(T2[:, hs, 0:W], p6v[:, 0:4, :], AF.Exp, ["pb6"], ["T2"], scale=-1.0)
                yield
            cp("pool", gCt[:, :], T1[:, :, W - 1], ["T1"], [kgc])
            for hf in range(2):
                for hq in range(4):
                    hh = hf * 4 + hq
                    mm(p6[:, hq * 128:hq * 128 + W], a2b[64:128, hh * 128:(hh + 1) * 128], alb[64:128, sl], True, True, ["a2b", "alb"], ["pb6"])
                yield
                for hq in range(4):
                    hh = hf * 4 + hq
                    act(T4[:, hh, 0:W], p6[:, hq * 128:hq * 128 + W], AF.Sigmoid, ["pb6", "col"], ALLT4,
                        bias=col[:, O_A0 + hh:O_A0 + hh + 1])
                yield
            for hf in range(2):
                hs = slice(hf * 4, hf * 4 + 4)
                for hq in range(4):
                    hh = hf * 4 + hq
                    mm(p6[:, hq * 128:hq * 128 + W], bob[:], bh_[:, hh, 0:W], True, True, ["bob", "bh_"], ["pb6"])
                yield
                rsq(T7[:, hs, 0:W], p6v[:, 0:4, :], 1e-12, ["pb6"], "T7")
                yield
            stt("dve", T6[:, :, 0:W], T5[:, :, 0:W], -1.0, T7[:, :, 0:W], ALU.mult, ALU.mult, ["T5", "T7"], ["T6"])
            yield
            stt("dve", T7[:, :, 0:W], T6[:, :, 0:W], -1.0, T4[:, :, 0:W], ALU.mult, ALU.mult, ["T6"] + ALLT4, ["T7"])
            yield
            tt("pool", T0[:, :, 0:W], T4[:, :, 0:W], colb(O_KA), ALU.mult, ALLT4 + ["col", "T0"], ["T0"])
            tt("pool", T0[:, :, 0:W], T0[:, :, 0:W], colb(O_OMKA), ALU.add, ["T0", "col"], ["T0"])
            yield
            tt("dve", T5[:, :, 0:W], kS[:, :, sl], T0[:, :, 0:W], ALU.mult, ["kS", "T0"], ["T5"])
            yield
            tt("pool", T0[:, :, 0:W], rS[:, :, sl], T5[:, :, 0:W], ALU.mult, ["rS", "T5"], ["T0"])
            tt("pool", kh_[:, :, 0:W], T0[:, :, 0:W], colb(O_RK), ALU.mult, ["T0", "col"], ["kh_"])
            yield
            for hf in range(2):
                hs = slice(hf * 4, hf * 4 + 4)
                for hq in range(4):
                    hh = hf * 4 + hq
                    mm(p6[:, hq * 128:hq * 128 + W], bob[:], kh_[:, hh, 0:W], True, True, ["bob", "kh_"], ["pb6"])
                yield
                tt("dve", bonT[:, hs, 0:W], p6v[:, 0:4, :], vS[:, hs, sl], ALU.mult, ["pb6", "vS"], [kbon])
                yield
            if sample:
                return
            ph("mults")
            EG, EnG, EGe, k2, aa, bb = T1, T2, T3, T5, T6, T7
            rt, at, bt, kt = PSp["rt"], PSp["at"], PSp["bt"], PSp["kt"]
            krt, kat, kbt, kkt = "rt" + sfx, "at" + sfx, "bt" + sfx, "kt" + sfx
            tt("dve", rt, rS[:, :, sl], EG[:], ALU.mult, ["rS", "T1"], [krt])
            tt("pool", at[:, :, 1:128], aa[:, :, 1:128], EG[:, :, 0:127], ALU.mult, ["T6", "T1"], [kat])
            cp("pool", at[:, :, 0:1], aa[:, :, 0:1], ["T6"], [kat])
            yield
            tt("dve", bt, bb[:], EnG[:], ALU.mult, ["T7", "T2"], [kbt])
            tt("pool", kt, k2[:], EnG[:], ALU.mult, ["T5", "T2"], [kkt])
            yield
            tt("dve", EGe[:], EnG[:], bc(EG[:, :, 127:128], [128, 8, 128]), ALU.mult, ["T2", "T1", kat], ["T3"])
            yield
            tt("pool", bh_[:], bb[:], EGe[:], ALU.mult, ["T7", "T3"], ["bh_"])
            tt("dve", kh_[:], k2[:], EGe[:], ALU.mult, ["T5", "T3"], ["kh_"])
            yield
            ph("transp")
            for src, skey, dst, dkey in ((vS, "vS", PSp["Vt"], "Vt" + sfx), (bh_, "bh_", PSp["Bt"], "Bt" + sfx), (kh_, "kh_", PSp["Kt"], "Kt" + sfx)):
                for hh in range(8):
                    srcap = src[:, hh, sl] if src is vS else src[:, hh, :]
                    op("pe", lambda e, srcap=srcap, hh=hh: e.transpose(ptb[:, hh * 128:(hh + 1) * 128], srcap, idb[:]),
                       [skey, "idb"], ["ptb"])
                yield
                cp("act", dst[:, 0:512], ptb[:, 0:512], ["ptb"], [dkey])
                cp("dve", dst[:, 512:1024], ptb[:, 512:1024], ["ptb"], [dkey])
                yield

        def gn_gen(yT, ykey, cs, W, G, Gk, bonT, kbon):
            ph("gn")
            G1, G2, G3 = G
            k1, k2_, k3 = Gk
            sl = slice(cs, cs + W)
            sh = [128, 8, W]
            colb = lambda o: bc(col[:, o:o + 8].unsqueeze(2), sh)
            p6 = pb[6]
            p6v = p6[:].rearrange("p (a b) -> p a b", b=128)[:, :, 0:W]
            for hf in range(2):
                hs = slice(hf * 4, hf * 4 + 4)
                for hq in range(4):
                    hh = hf * 4 + hq
                    mm(p6[:, hq * 128:hq * 128 + W], C(C_BM), yT[:, hh, 0:W], True, True, ["cst"] + ykey, ["pb6"])
                yield
                tt("dve", G1[:, hs, 0:W], yT[:, hs, 0:W], p6v[:, 0:4, :], ALU.subtract, ykey + ["pb6"], [k1])
                yield
            tt("pool", G2[:, :, 0:W], G1[:, :, 0:W], G1[:, :, 0:W], ALU.mult, [k1], [k2_])
            yield
            for hf in range(2):
                hs = slice(hf * 4, hf * 4 + 4)
                for hq in range(4):
                    hh = hf * 4 + hq
                    mm(p6[:, hq * 128:hq * 128 + W], C(C_BM), G2[:, hh, 0:W], True, True, ["cst", k2_], ["pb6"])
                yield
                rsq(G3[:, hs, 0:W], p6v[:, 0:4, :], 64e-5, ["pb6"], k3)
                yield
            tt("dve", G1[:, :, 0:W], G1[:, :, 0:W], G3[:, :, 0:W], ALU.mult, [k1, k3], [k1])
            yield
            tt("pool", G1[:, :, 0:W], G1[:, :, 0:W], colb(O_GNG), ALU.mult, [k1, "col"], [k1])
            tt("pool", G1[:, :, 0:W], G1[:, :, 0:W], colb(O_GNB), ALU.add, [k1, "col"], [k1])
            yield
            tt("dve", G1[:, :, 0:W], G1[:, :, 0:W], bonT[:, :, 0:W], ALU.add, [k1, kbon], [k1])
            yield
            tt("dve", orT[:, :, sl], G1[:, :, 0:W], zrS[:, :, sl], ALU.mult, [k1, "zrS"], ["orT"])
            yield

        def run_all(gens):
            gens = list(gens)
            while gens:
                for gq in list(gens):
                    try:
                        next(gq)
                    except StopIteration:
                        gens.remove(gq)

        gC2 = T("gC2", [128, 8])
        _fl = lambda t, i: t[:, 2 * i:2 * i + 2, :].rearrange("p a b -> p (a b)")
        _v8 = lambda ap: ap.rearrange("p (a b) -> p a b", b=128)
        PS = [
            {"sfx": "_0", "rt": rt_[:], "at": at_[:], "bt": bt_[:], "kt": kt_[:], "Vt": Vt, "Bt": Bt, "Kt": Kt, "bon": bon, "gC": gC},
            {"sfx": "_1", "rt": _v8(_fl(wb[0], 0)), "at": _v8(_fl(wb[0], 1)), "bt": _v8(_fl(wb[0], 2)), "kt": _v8(_fl(wb[0], 3)),
             "Vt": _fl(wb[1], 0), "Bt": _fl(wb[1], 1), "Kt": _fl(wb[1], 2), "bon": _v8(_fl(wb[1], 3)), "gC": gC2},
        ]
        PS1_KEYS = [k + "_1" for k in ("rt", "at", "bt", "kt", "Vt", "Bt", "Kt", "bon")]

        def scan_group(g, S, PSp, yT):
            Ak_, Nk_, Qb_, LkT_, MbT_, MkT_, Xb_, SAb_ = S["Ak"], S["Nk"], S["Qb"], S["LkT"], S["MbT"], S["MkT"], S["Xb"], S["SAb"]
            b0, b1, b2 = S["banks"]
            kb = lambda i: "pb%d" % i
            n = S["n"]
            sfx = PSp["sfx"]
            rt_, at_, bt_, kt_, Vt, Bt, Kt, gC = PSp["rt"], PSp["at"], PSp["bt"], PSp["kt"], PSp["Vt"], PSp["Bt"], PSp["Kt"], PSp["gC"]
            K = lambda nm: nm + n
            heads = [4 * g + x for x in (0, 2, 1, 3)]
            SbK = "Sb%d" % g; SfK = "Sf%d" % g; yK = "yT%d" % g

            def hp(h):
                hl, hh = h % 2, h // 2
                return slice(hl * 64, hl * 64 + 64), hh
            v4 = lambda p: p[:].rearrange("p (a b) -> p a b", b=128)
            mk = lambda o: bc(cst[:, o:o + 128].unsqueeze(1), [128, 4, 128])
            ph("scores")
            plan = [(b0, "at_", "bt_", Ak_[0], K("Ak0"), C_SL), (b1, "bt_", "at_", Nk_[0], K("Nk0"), C_SU),
                    (b2, "kt_", "at_", LkT_, K("LkT"), C_SU), (b0, "bt_", "rt_", MbT_, K("MbT"), C_UI),
                    (b1, "kt_", "rt_", MkT_, K("MkT"), C_UI)]
            tl = {"at_": at_, "bt_": bt_, "kt_": kt_, "rt_": rt_}
            kn = {"at_": "at" + sfx, "bt_": "bt" + sfx, "kt_": "kt" + sfx, "rt_": "rt" + sfx}
            first_lo = (n == "_A")
            for rnd in (plan[0:3], plan[3:5]):
                for tagsel in ((0, 1) if first_lo else (1, 0)):
                    for (bk, ln, rn, dst, dk, msk) in rnd:
                        for hi, h in enumerate(heads):
                            if (h % 2) != tagsel:
                                continue
                            pr, hh = hp(h)
                            mm(pb[bk][:, hi * 128:(hi + 1) * 128], tl[ln][pr, hh, :], tl[rn][pr, hh, :], True, True, [kn[ln], kn[rn]], [kb(bk)])
                yield
                for (bk, ln, rn, dst, dk, msk) in rnd:
                    tt("dve", dst[:], v4(pb[bk]), mk(msk), ALU.mult, [kb(bk), "cst"], [dk])
                    yield
            ph("doubling")
            tt("pool", Qb_[:], Nk_[0][:], mk(C_ID), ALU.add, [K("Nk0"), "cst"], [K("Qb")])
            yield
            mm(pb[b2][:, :], idb[:], Qb_.rearrange("p a b -> p (a b)"), True, True,
               ["idb", K("Qb")], [kb(b2)])
            yield
            cur = 0
            for lvl in range(6):
                nx = 1 - cur
                for hi in range(4):
                    mm(pb[b0][:, hi * 128:(hi + 1) * 128], Nk_[cur][:, hi, :], Ak_[cur][:, hi, :], True, True,
                       [K("Nk%d" % cur), K("Ak%d" % cur)], [kb(b0)])
                if lvl < 5:
                    for hi in range(4):
                        mm(pb[b1][:, hi * 128:(hi + 1) * 128], Ak_[cur][:, hi, :], Nk_[cur][:, hi, :], True, True,
                           [K("Nk%d" % cur), K("Ak%d" % cur)], [kb(b1)])
                yield
                cp("act", Ak_[nx][:], v4(pb[b0]), [kb(b0)], [K("Ak%d" % nx)])
                if lvl < 5:
                    cp("dve", Nk_[nx][:], v4(pb[b1]), [kb(b1)], [K("Nk%d" % nx)])
                yield
                for hi in range(4):
                    mm(pb[b2][:, hi * 128:(hi + 1) * 128], Ak_[nx][:, hi, :], Qb_[:, hi, :], False, True,
                       [K("Ak%d" % nx), K("Qb")], [kb(b2)])
                yield
                if lvl % 2 == 0:
                    cp("act", Qb_[:], v4(pb[b2]), [kb(b2)], [K("Qb")])
                else:
                    cp("dve", Qb_[:], v4(pb[b2]), [kb(b2)], [K("Qb")])
                yield
                cur = nx
            ph("seq")
            for hi, h in enumerate(heads):
                pr, hh = hp(h)
                o = pb[b0][:, hi * 64:(hi + 1) * 64]
                mm(o, at_[pr, hh, :], Sb[pr, hh, :], True, False, ["at" + sfx, SbK], [kb(b0)])
                mm(o, LkT_[:, hi, :], Vt[:, h * 64:(h + 1) * 64], False, True, [K("LkT"), "Vt" + sfx], [kb(b0)])
            yield
            cp("act", Xb_[:], pb[b0][:, 0:256], [kb(b0)], [K("Xb")])
            yield
            for hi, h in enumerate(heads):
                mm(pb[b1][:, hi * 64:(hi + 1) * 64], Qb_[:, hi, :], Xb_[:, hi * 64:(hi + 1) * 64], True, True, [K("Qb"), K("Xb")], [kb(b1)])
            yield
            cp("dve", SAb_[:], pb[b1][:, 0:256], [kb(b1)], [K("SAb")])
            yield
            for hi, h in enumerate(heads):
                pr, hh = hp(h)
                o = pb[b2][pr, (hh - 2 * g) * 128:(hh - 2 * g) * 128 + 128]
                mm(o, Sb[pr, hh, :], rt_[pr, hh, :], True, False, [SbK, "rt" + sfx], [kb(b2)])
                mm(o, SAb_[:, hi * 64:(hi + 1) * 64], MbT_[:, hi, :], False, False, [K("SAb"), K("MbT")], [kb(b2)])
                mm(o, Vt[:, h * 64:(h + 1) * 64], MkT_[:, hi, :], False, True, ["Vt" + sfx, K("MkT")], [kb(b2)])
            yield
            cp("act", yT[:, 2 * g:2 * g + 2, :], pb[b2][:, 0:256].rearrange("p (a b) -> p a b", b=128), [kb(b2)], [yK])
            for hi, h in enumerate(heads):
                pr, hh = hp(h)
                o = pb[b0][pr, (hh - 2 * g) * 64:(hh - 2 * g) * 64 + 64]
                mm(o, Bt[:, h * 64:(h + 1) * 64], SAb_[:, hi * 64:(hi + 1) * 64], True, False, ["Bt" + sfx, K("SAb")], [kb(b0)])
                mm(o, Kt[:, h * 64:(h + 1) * 64], Vt[:, h * 64:(h + 1) * 64], False, True, ["Kt" + sfx, "Vt" + sfx], [kb(b0)])
            yield
            gs = slice(2 * g, 2 * g + 2)
            tt("dve", Sf[:, gs, :], Sf[:, gs, :], bc(gC[:, gs].unsqueeze(2), [128, 2, 64]), ALU.mult, [SfK, "gC" + sfx], [SfK])
            tt("dve", Sf[:, gs, :], Sf[:, gs, :], pb[b0][:, 0:128].rearrange("p (a b) -> p a b", b=64), ALU.add, [SfK, kb(b0)], [SfK])
            yield
            cp("act", Sb[:, gs, :], Sf[:, gs, :], [SfK], [SbK])
            yield


        def tail(NT, xsrc_tiles, ydst_tiles, nrows):
            ph("tail")
            for q in range(2):
                sgr = wload(wi_v, 8, (45 + 4 * q) * 128, 512, ikeys((45 + 4 * q) * 128, 512))
                sbr = wload(wbr_v, 8, q * 512, 512, ["scr_o"])
                sgc = wload(wi_v, 8, (53 + 4 * q) * 128, 512, ikeys((53 + 4 * q) * 128, 512))
                sbc = wload(wbc_v, 4, q * 512, 512, ["scr_o"])
                for jj in range(4):
                    j = q * 4 + jj
                    project(45 + j, sgr, jj, NT, 0)
                    act(sgb[:, 0:NT], pb[0][:, 0:NT], AF.Sigmoid, ["pb0"], ["sgb"])
                    for fc in range(8):
                        mm(pb[1][:, 0:NT], wb[sbr][:, fc, jj * 128:(jj + 1) * 128], orT[:, fc, 0:NT], fc == 0, fc == 7,
                           ["wb%d" % sbr, "orT"], ["pb1"])
                    tt("dve", m1[:, 0:NT], pb[1][:, 0:NT], sgb[:, 0:NT], ALU.mult, ["pb1", "sgb"], ["T5"])
                    project(53 + j, sgc, jj, NT, 2)
                    act(sgb[:, 0:NT], pb[2][:, 0:NT], AF.Sigmoid, ["pb2"], ["sgb"])
                    for fc in range(4):
                        mm(pb[3][:, 0:NT], wb[sbc][:, fc, jj * 128:(jj + 1) * 128], ocT[:, fc, 0:NT], fc == 0, fc == 3,
                           ["wb%d" % sbc, "kS"], ["pb3"])
                    tt("dve", tmpc[:, 0:NT], pb[3][:, 0:NT], sgb[:, 0:NT], ALU.mult, ["pb3", "sgb"], ["tmpc"])
                    tt("pool", mT[:, j, 0:NT], m1[:, 0:NT], tmpc[:, 0:NT], ALU.add, ["T5", "tmpc"], ["rS"])
            so = [wload(wo_v, 8, 0, 512, ["scr_o"]), wload(wo_v, 8, 512, 512, ["scr_o"])]
            npg = TT[2][:].rearrange("p a b -> p (a b)")
            dma(npg[:, :], npg_d.partition_broadcast(128), [], ["T2"])
            for i, (xsrc, ydst) in enumerate(zip(xsrc_tiles, ydst_tiles)):
                xb = xt[i % 2]
                xk = "xt%d" % (i % 2)
                dma(xb[0:nrows, :], xsrc, [], [xk])
                tsl = slice(i * 128, i * 128 + nrows)
                for hf in range(2):
                    for fc in range(8):
                        mm(pb[4 + hf][0:nrows, :], mT[:, fc, tsl], wb[so[hf]][:, fc, :], fc == 0, fc == 7,
                           ["rS", "wb%d" % so[hf]], ["pb%d" % (4 + hf)])
                for hf in range(2):
                    act(hb[0:nrows, hf * 512:(hf + 1) * 512], pb[4 + hf][0:nrows, :], AF.Square, ["pb%d" % (4 + hf)], ["hb", "small"],
                        accum=small[0:nrows, 8 + hf:9 + hf])
                tt("dve", small[0:nrows, 10:11], small[0:nrows, 8:9], small[0:nrows, 9:10], ALU.add, ["small"], ["small"])
                ts("dve", small[0:nrows, 11:12], small[0:nrows, 10:11], 1.0 / D, 1e-6, ALU.mult, ALU.add, ["small"], ["small"])
                rsq(small[0:nrows, 12:13], small[0:nrows, 11:12], 0.0, ["small"], "small")
                T0v = TT[0][:].rearrange("p a b -> p (a b)")
                for hf in range(2):
                    hsl = slice(hf * 512, (hf + 1) * 512)
                    stt("dve", T0v[0:nrows, hsl], pb[4 + hf][0:nrows, :], small[0:nrows, 12:13], npg[0:nrows, hsl], ALU.mult, ALU.mult,
                        ["pb%d" % (4 + hf), "small", "T2"], ["T0"])
                tt("pool", T0v[0:nrows, :], T0v[0:nrows, :], xb[0:nrows, :], ALU.add, ["T0", xk], ["T0"])
                dma(ydst, T0v[0:nrows, :], ["T0"], [], q="pool")

        def rms_phase(sc):
            t0 = sc * 512
            for i in range(4):
                xb = xt[i % 2]; xk = "xt%d" % (i % 2)
                dma(xb[:], xp[t0 + i * 128:t0 + (i + 1) * 128, :], [], [xk])
                last = (sc == 3 and i == 3)

                def hout(xb=xb, xk=xk):
                    T0v = TT[0][:].rearrange("p a b -> p (a b)")
                    dma(T0v[:, :], npre_d.partition_broadcast(128), [], ["T0"])
                    ts("dve", TT[1][:].rearrange("p a b -> p (a b)"), xb[:], small[:, 2:3], None, ALU.mult, None, [xk, "small"], ["T1"])
                    tt("dve", TT[1][:].rearrange("p a b -> p (a b)"), TT[1][:].rearrange("p a b -> p (a b)"), T0v, ALU.mult, ["T1", "T0"], ["T1"])
                    dma(nsp, TT[1][:].rearrange("p a b -> p (a b)")[127:128, :], ["T1"], [])
                rmsnorm_tile(xb, xk, 128, (i * 128, (i + 1) * 128), hout if last else None)

        rms_phase(0)
        run_all([prologue_gen(1)])
        for sc in range(4):
            t0 = sc * 512
            if sc > 0:
                rms_phase(sc)
            stage(3 if sc == 0 else 11)
            if sc > 0:
                cp("pool", tmpc[:, 0:120].rearrange("p (c w) -> p c w", w=30), uex[:, :, 512:542], ["uex"], ["tmpc"])
                cp("pool", uex[:, :, 0:30], tmpc[:, 0:120].rearrange("p (c w) -> p c w", w=30), ["tmpc"], ["uex"])
            if sc == 0:
                run_all([proj_phase(512, False, sc % 2, (sc + 1) % 2), prologue_gen(2)])
            else:
                run_all([proj_phase(512, False, sc % 2, (sc + 1) % 2)])
            stage(4 if sc == 0 else 11)
            op("pool", lambda e: e.memset(dummy[:, 0:1], 0.0), [], ["wb3", "wb0", "wb1", "xt0", "xt1", "ua", "uaA", "uaB", "dummy", "yT0", "yT1", "yT2", "yT3"] + SETB_KEYS + PS1_KEYS)
            yTp = xt[0][:, :].rearrange("p (a b) -> p a b", b=128)
            Gp = (xt[1][:, :].rearrange("p (a b) -> p a b", b=128),
                  ua[:, 0:2, :].rearrange("p a b -> p (a b)").rearrange("p (a b) -> p a b", b=128),
                  ua[:, 2:4, :].rearrange("p a b -> p (a b)").rearrange("p (a b) -> p a b", b=128))
            Gpk = ("xt1", "uaA", "uaB")

            def chunk_scan(c4):
                PSp = PS[c4 % 2]
                for pair in ((0, 1), (2, 3)):
                    gens = [scan_group(pair[0], SETS[0], PSp, yTp), scan_group(pair[1], SETS[1], PSp, yTp)]
                    while gens:
                        for gq in list(gens):
                            try:
                                next(gq)
                                yield
                            except StopIteration:
                                gens.remove(gq)
                yield from gn_gen(yTp, ["yT0", "yT1", "yT2", "yT3"], c4 * 128, 128, Gp, Gpk, PSp["bon"], "bon" + PSp["sfx"])

            run_all([prep_gen(0, 128, False, PS[0])])
            for c4 in range(4):
                main = chunk_scan(c4)
                side = prep_gen((c4 + 1) * 128, 128, False, PS[(c4 + 1) % 2]) if c4 < 3 else None
                RATIO = int(os.environ.get("MK_RATIO", "4"))
                done = False
                while not done:
                    for _ in range(RATIO):
                        try:
                            next(main)
                        except StopIteration:
                            done = True
                            break
                    if side is not None:
                        try:
                            next(side)
                        except StopIteration:
                            side = None
                if side is not None:
                    run_all([side])
            op("pool", lambda e: e.memset(dummy[:, 1:2], 0.0), [], ["wb3", "wb0", "wb1", "xt0", "xt1", "ua", "uaA", "uaB", "dummy", "yT0", "yT1", "yT2", "yT3"] + SETB_KEYS + PS1_KEYS)
            stage(8 if sc == 0 else 11)
            ph("conv")
            cp("pool", ubf[:], uex[:], ["uex"], ["ubf"])
            for c in range(4):
                for w in range(31):
                    s = (c * 31 + w) % 4
                    if w % 2 == 0:
                        act(dg[s][:], idb[:], AF.Copy, ["idb", "col"], ["dg%d" % s], scale=col[:, O_CW + c * 31 + w:O_CW + c * 31 + w + 1])
                    else:
                        ts("dve", dg[s][:], idb[:], col[:, O_CW + c * 31 + w:O_CW + c * 31 + w + 1], None, ALU.mult, None,
                           ["idb", "col"], ["dg%d" % s])
                    mm(pb[6][:, :], dg[s][:], ubf[:, c, w:w + 512], w == 0, w == 30, ["dg%d" % s, "ubf"], ["pb6"])
                act(TT[c][:].rearrange("p a b -> p (a b)")[:, 0:512], pb[6][:, :], AF.Identity, ["pb6", "col"], ["T%d" % c],
                    bias=col[:, O_CB + c:O_CB + c + 1])
            ln_conv_out(512, [TT[c][:].rearrange("p a b -> p (a b)")[:, 0:512] for c in range(4)], ["T0", "T1", "T2", "T3"])
            if sc == 3:
                for c in range(4):
                    mm(pb[6][0:30, c * 128:(c + 1) * 128], uex[:, c, 512:542], C(C_ID), True, True, ["uex", "cst"], ["pb6"])
                cp("dve", tmpc[0:30, :], pb[6][0:30, :], ["pb6"], ["tmpc"])
                dma(ncp, tmpc[0:30, :], ["tmpc"], [])
            stage(9 if sc == 0 else 11)
            tail(512, [xp[t0 + i * 128:t0 + (i + 1) * 128, :] for i in range(4)],
                 [yp[t0 + i * 128:t0 + (i + 1) * 128, :] for i in range(4)], 128)

        stage(12)
        for h in range(16):
            hl, hh = h % 2, h // 2
            pr = slice(hl * 64, hl * 64 + 64)
            mm(pb[0][pr, hh * 64:(hh + 1) * 64], Sf[pr, hh, :], cst[pr, C_ID + hl * 64:C_ID + hl * 64 + 64], True, True, ALLSF + ["cst"], ["pb0"])
        cp("dve", tmpc[:, :], pb[0][:, :], ["pb0"], ["tmpc"])
        dma(nwp.rearrange("(hh p) j -> p hh j", p=128), tmpc[:, :].rearrange("p (a b) -> p a b", b=64), ["tmpc"], [])

        stage(13)
        ph("sample")
        xb = xt[0]
        dma(xb[0:NS, :], xs, [], ["xt0"])

        def hout_s():
            T0v = TT[0][:].rearrange("p a b -> p (a b)")
            T1v = TT[1][:].rearrange("p a b -> p (a b)")
            dma(T0v[0:NS, :], npre_d.partition_broadcast(NS), [], ["T0"])
            ts("dve", T1v[0:NS, :], xb[0:NS, :], small[0:NS, 2:3], None, ALU.mult, None, ["xt0", "small"], ["T1"])
            tt("dve", T1v[0:NS, :], T1v[0:NS, :], T0v[0:NS, :], ALU.mult, ["T1", "T0"], ["T1"])
            dma(nss, T1v[0:NS, :], ["T1"], [])
        rmsnorm_tile(xb, "xt0", NS, (0, NS), hout_s)
        dma(xt[1][0:NS, :], sshift, [], ["xt1"])
        cp("dve", hb[0:NS, :], xt[1][0:NS, :], ["xt1"], ["hb"])
        for dc in range(8):
            op("pe", lambda e, dc=dc: e.transpose(ptb[:, dc * 128:dc * 128 + NS], hb[0:NS, dc * 128:(dc + 1) * 128], idb[0:NS, 0:NS]),
               ["hb", "idb"], ["ptb"])
        cp("act", hT[:, :, NS:2 * NS], ptb[:, :].rearrange("p (a b) -> p a b", b=128)[:, :, 0:NS], ["ptb"], ["hT"])
        uv = [uex[:, c, 0:NS * 31].rearrange("p (n w) -> p n w", w=31) for c in range(4)]
        for q in range(4):
            dma(xt[1][0:120, 0:512], sconv[q * 120:(q + 1) * 120, :], [], ["xt1"])
            for c in range(4):
                mm(pb[6][:, c * 120:(c + 1) * 120], xt[1][0:120, c * 128:(c + 1) * 128], cst[0:120, C_ID:C_ID + 120], True, True,
                   ["xt1", "cst"], ["pb6"])
            for c in range(4):
                cp("dve", uv[c][:, q * 4:(q + 1) * 4, 0:30], pb[6][:, c * 120:(c + 1) * 120].rearrange("p (n w) -> p n w", w=30),
                   ["pb6"], ["uex"])
        dma(ncs[:, 0:29, :], sconv.rearrange("(n w) c -> n w c", w=30)[:, 1:30, :], [], [])
        run_all([proj_phase(2 * NS, True, 0, 0)])
        stage(14)
        run_all([prep_gen(0, NS, True, PS[0])])
        EG, EnG, EGe, k2, aa, bb = TT[1], TT[2], TT[3], TT[5], TT[6], TT[7]
        SW = [ua[:, i, :].rearrange("p (a b) -> p a b", b=64) for i in range(2)]
        Dxs = [tmpb[:, :], xt[1][:, 0:512], xt[1][:, 512:1024], xt[0][:, 0:512], xt[0][:, 512:1024]]
        Dxk = ["tmpb", "xt1", "xt1", "xt0", "xt0"]
        yTs = TT[4]
        i2b = bc(cst[:, C_I2:C_I2 + 64].unsqueeze(1), [128, 8, 64])
        for n in range(NS):
            Sw = SW[n % 2]; sk = "ua"
            dma(Sw, swkv[n].rearrange("(hh p) j -> p hh j", p=128), [], [sk])
            vecs = [(aa, "T6"), (EG, "T1"), (bb, "T7"), (k2, "T5"), (rS, "rS")]
            for vi, (vt_, vk) in enumerate(vecs):
                tt("pool", Dxs[vi].rearrange("p (a b) -> p a b", b=64), i2b, bc(vt_[:, :, n:n + 1], [128, 8, 64]), ALU.mult,
                   ["cst", vk], [Dxk[vi]])
                mm(pb[vi][:, :], C(C_BO), Dxs[vi], True, True, ["cst", Dxk[vi]], ["pb%d" % vi])
            v8 = lambda p: p[:].rearrange("p (a b) -> p a b", b=64)
            W3 = TT[0][:, :, 0:64]
            tt("dve", W3, Sw, v8(pb[0]), ALU.mult, [sk, "pb0"], ["T0"])
            op("dve", lambda e: e.tensor_reduce(out=small[:, 16:24], in_=TT[0][:, :, 0:64], axis=AX.X, op=ALU.add), ["T0"], ["small"])
            tt("dve", Sw, Sw, v8(pb[1]), ALU.mult, [sk, "pb1"], [sk])
            tt("dve", W3, v8(pb[2]), bc(small[:, 16:24].unsqueeze(2), [128, 8, 64]), ALU.mult, ["pb2", "small"], ["T0"])
            tt("pool", Sw, Sw, W3, ALU.add, [sk, "T0"], [sk])
            cp("dve", small[:, 24:32], vS[:, :, n], ["vS"], ["small"])
            tt("dve", W3, v8(pb[3]), bc(small[:, 24:32].unsqueeze(2), [128, 8, 64]), ALU.mult, ["pb3", "small"], ["T0"])
            tt("pool", Sw, Sw, W3, ALU.add, [sk, "T0"], [sk])
            dma(nws[n].rearrange("(hh p) j -> p hh j", p=128), Sw, [sk], [])
            tt("dve", W3, Sw, v8(pb[4]), ALU.mult, [sk, "pb4"], ["T0"])
            op("dve", lambda e, n=n: e.tensor_reduce(out=yTs[:, :, n], in_=TT[0][:, :, 0:64], axis=AX.X, op=ALU.add), ["T0"], ["T4_0"])
        stage(15)
        run_all([gn_gen(yTs, ALLT4, 0, NS, (TT[1], TT[2], TT[3]), ("T1", "T2", "T3"), bon, "bon_0")])
        stage(16)
        cf = []
        for c in range(4):
            cwb = bc(col[:, O_CW + c * 31:O_CW + (c + 1) * 31].unsqueeze(1), [128, NS, 31])
            tt("dve", tmpc[:, 0:NS * 31].rearrange("p (n w) -> p n w", w=31), uv[c], cwb, ALU.mult, ["uex", "col"], ["tmpc"])
            cfc = TT[c][:].rearrange("p a b -> p (a b)")[:, 0:NS]
            op("dve", lambda e, cfc=cfc: e.tensor_reduce(out=cfc, in_=tmpc[:, 0:NS * 31].rearrange("p (n w) -> p n w", w=31),
                                                        axis=AX.X, op=ALU.add), ["tmpc"], ["T%d" % c])
            ts("dve", cfc, cfc, col[:, O_CB + c:O_CB + c + 1], None, ALU.add, None, ["T%d" % c, "col"], ["T%d" % c])
            cf.append(cfc)
        for c in range(4):
            cp("dve", tmpb[:, c * NS:(c + 1) * NS], uv[c][:, :, 30], ["uex"], ["tmpb"])
        for c in range(4):
            mm(pb[6][0:NS, c * 128:(c + 1) * 128], tmpb[:, c * NS:(c + 1) * NS], C(C_ID), True, True, ["tmpb", "cst"], ["pb6"])
        cp("dve", m1[0:NS, 0:512], pb[6][0:NS, :], ["pb6"], ["T5"])
        dma(ncs[:, 29, :], m1[0:NS, 0:512], ["T5"], [])
        ln_conv_out(NS, cf, ["T0", "T1", "T2", "T3"])
        stage(17)
        tail(NS, [xs], [ys], NS)
        P.emit()
    return nc


def _prep_consts():
    c = np.zeros((128, NCONST), np.float32)
    idx = np.arange(128)
    c[:, C_ID:C_ID + 128] = np.eye(128)
    c[:, C_SL:C_SL + 128] = (idx[None, :] < idx[:, None])
    c[:, C_SU:C_SU + 128] = (idx[:, None] < idx[None, :])
    c[:, C_UI:C_UI + 128] = (idx[:, None] <= idx[None, :])
    c[:, C_TRI:C_TRI + 128] = (idx[:, None] <= idx[None, :]) * CNEG
    c[:, C_TRE:C_TRE + 128] = (idx[:, None] < idx[None, :]) * CNEG
    blk = (idx[:, None] // 64 == idx[None, :] // 64).astype(np.float32)
    c[:, C_BM:C_BM + 128] = blk / 64.0
    c[:, C_BO:C_BO + 128] = blk
    c[:, C_AM:C_AM + 128] = 1.0 / 512.0
    c[:, C_NI:C_NI + 64] = np.eye(128)[:, :64] * CNEG
    c[:, C_I2:C_I2 + 64] = (idx[:, None] % 64 == np.arange(64)[None, :])
    return c


_NC = None


def kernel(x_prompt, x_sample, state_shift, state_wkv, state_conv, norm_pre_g, w_in, mu_shift,
           decay_w0, decay_w2, iclr_a0, iclr_a2, k_k, k_a, r_k, gn_g, gn_b, conv_glu_b, conv_w,
           conv_b, ln_c_g, ln_c_b, w_branch_r, w_branch_c, w_out, norm_post_g):
    global _NC
    f = lambda a: np.ascontiguousarray(np.asarray(a, dtype=np.float32))
    colv = lambda v, n: f(v).reshape(n, 128).T
    cols = np.zeros((128, NCOL), np.float32)
    cols[:, O_MU:O_MU + 33] = colv(mu_shift[0], 33)
    cols[:, O_KK:O_KK + 8] = colv(k_k[0], 8)
    cols[:, O_KA:O_KA + 8] = colv(k_a[0], 8)
    cols[:, O_RK:O_RK + 8] = colv(np.asarray(r_k[0]).reshape(-1), 8)
    cols[:, O_GNG:O_GNG + 8] = colv(gn_g[0], 8)
    cols[:, O_GNB:O_GNB + 8] = colv(gn_b[0], 8)
    cols[:, O_A0:O_A0 + 8] = colv(iclr_a0[0], 8)
    cols[:, O_GLUB:O_GLUB + 8] = colv(conv_glu_b[0], 8)
    cols[:, O_CB:O_CB + 4] = colv(conv_b[0], 4)
    cols[:, O_LNG:O_LNG + 4] = colv(ln_c_g[0], 4)
    cols[:, O_LNB:O_LNB + 4] = colv(ln_c_b[0], 4)
    cw = f(conv_w[0])
    cols[:, O_CW:O_CW + 124] = cw.reshape(31, 4, 128).transpose(2, 1, 0).reshape(128, 124)
    cols[:, O_GPRE:O_GPRE + 8] = colv(norm_pre_g[0], 8)
    consts = _prep_consts()
    w2ext = np.concatenate([f(decay_w2[0]), f(decay_w0[0])[None, :]], axis=0)
    shared = {
        "w_in": f(w_in[0]), "w_br": f(w_branch_r[0]), "w_bc": f(w_branch_c[0]), "w_out": f(w_out[0]),
        "cols": cols, "consts": consts, "w2ext": f(w2ext), "a2": f(iclr_a2[0]),
        "npg": f(norm_post_g[0])[None, :], "npre": f(norm_pre_g[0])[None, :],
    }
    xpf = f(x_prompt); xsf = f(x_sample).reshape(128, D); ssf = f(state_shift[0])
    swf = f(state_wkv[0]).reshape(128, 1024, 64); scf = f(state_conv[0]).reshape(128 * 30, 512)
    in_maps = []
    for c in range(8):
        m = dict(shared)
        m["xp"] = xpf[c]
        m["xs"] = xsf[c * NS:(c + 1) * NS]
        m["sshift"] = ssf[c * NS:(c + 1) * NS]
        m["swkv"] = swf[c * NS:(c + 1) * NS]
        m["sconv"] = scf[c * NS * 30:(c + 1) * NS * 30]
        in_maps.append(m)
    if _NC is None:
        _NC = build()
    res = run_bass_kernel_spmd(_NC, in_maps, core_ids=list(range(8)))
    R = res.results
    y_prompt = np.stack([R[c]["yp"] for c in range(8)]).astype(np.float32)
    y_sample = np.concatenate([R[c]["ys"] for c in range(8)]).reshape(128, 1, D).astype(np.float32)
    nsp_ = np.concatenate([R[c]["nsp"] for c in range(8)]).reshape(1, 8, D).astype(np.float32)
    nwp_ = np.stack([R[c]["nwp"] for c in range(8)]).reshape(1, 8, 16, 64, 64).astype(np.float32)
    ncp_ = np.stack([R[c]["ncp"] for c in range(8)]).reshape(1, 8, 30, 512).astype(np.float32)
    nss_ = np.concatenate([R[c]["nss"] for c in range(8)]).reshape(1, 128, D).astype(np.float32)
    nws_ = np.concatenate([R[c]["nws"] for c in range(8)]).reshape(1, 128, 16, 64, 64).astype(np.float32)
    ncs_ = np.concatenate([R[c]["ncs"] for c in range(8)]).reshape(1, 128, 30, 512).astype(np.float32)
    return (y_prompt, y_sample, nsp_, nwp_, ncp_, nss_, nws_, ncs_)
```

```python
import contextlib
import numpy as np
import concourse.bass as bass
import concourse.mybir as mybir
from concourse.bass_utils import run_bass_kernel_spmd

F32 = mybir.dt.float32
BF16 = mybir.dt.bfloat16
AF = mybir.ActivationFunctionType
ALU = mybir.AluOpType
AX = mybir.AxisListType

D = 1024
NIN = 7808
SEQ = 2048
NS = 16
NCH = 61
CNEG = -0.6065306597126334

O_MU = 0; O_KK = 33; O_KA = 41; O_RK = 49; O_GNG = 57; O_GNB = 65; O_A0 = 73; O_GLUB = 81
O_CB = 89; O_LNG = 93; O_LNB = 97; O_CW = 101; O_GPRE = 225; O_OMM = 233; O_OMKA = 266; NCOL = 274
C_ID = 0; C_SL = 128; C_SU = 256; C_UI = 384; C_TRI = 512; C_TRE = 640; C_BM = 768; C_BO = 896
C_AM = 1024; C_NI = 1152; C_I2 = 1280; NCONST = 1344


class Prog:
    ENG = ("pe", "act", "dve", "pool", "sp")

    def __init__(self, nc):
        self.nc = nc
        self.ops = {e: [] for e in self.ENG}
        self.cnt = {e: 0 for e in self.ENG}
        self.waited = {e: {} for e in self.ENG}
        self.lastw = {}
        self.readers = {}
        self.dcnt = {}
        self.dead = False
        self.phase = ""
        self.annotate = False

    def _need(self, eng, waits, tok):
        if tok is None:
            return
        kind, key, val = tok
        if kind == "e" and key == "pe" and eng == "pe":
            return
        k = (kind, key)
        if self.waited[eng].get(k, 0) >= val:
            return
        if waits.get(k, 0) < val:
            waits[k] = val

    def op(self, eng, fn, reads=(), writes=(), dma=None, tag=None):
        if self.dead:
            return None
        waits = {}
        if eng == "pe":
            prev = getattr(self, "petag", None)
            if tag is not None and prev is not None and tag != prev:
                waits[("e", "pe")] = self.cnt["pe"]
            self.petag = tag
        for r in reads:
            self._need(eng, waits, self.lastw.get(r))
        for w in writes:
            self._need(eng, waits, self.lastw.get(w))
            for rd in self.readers.get(w, ()):
                self._need(eng, waits, rd)
        for k, v in waits.items():
            self.waited[eng][k] = v
        if dma is not None:
            prevc = self.dcnt.get(dma, 0)
            if prevc > 0 and self.waited[eng].get(("d", dma), 0) < prevc:
                waits[("d", dma)] = max(waits.get(("d", dma), 0), prevc)
                self.waited[eng][("d", dma)] = prevc
            self.dcnt[dma] = prevc + 1
            tok = ("d", dma, self.dcnt[dma])
        else:
            self.cnt[eng] += 1
            tok = ("e", eng, self.cnt[eng])
        self.ops[eng].append((waits, fn, tok, self.phase))
        for r in reads:
            self.readers.setdefault(r, []).append(tok)
        for w in writes:
            self.lastw[w] = tok
            self.readers[w] = []
        return tok

    def emit(self):
        nc = self.nc
        with contextlib.ExitStack() as st:
            esem = {e: st.enter_context(nc.semaphore("s_" + e)) for e in self.ENG}
            dsem = {k: st.enter_context(nc.semaphore("d_" + str(k))) for k in self.dcnt}
            block = st.enter_context(nc.Block())

            def run(engname, e):
                for waits, fn, tok, ph in self.ops[engname]:
                    for (kind, key), val in waits.items():
                        if kind == "e":
                            e.wait_ge(esem[key], val)
                        else:
                            e.wait_ge(dsem[key], 16 * val)
                    ins = fn(e)
                    if self.annotate:
                        ins.annotate(ph)
                    if tok[0] == "e":
                        ins.then_inc(esem[tok[1]], 1)
                    else:
                        ins.then_inc(dsem[tok[1]], 16)
                if engname == "sp":
                    for k, c in self.dcnt.items():
                        e.wait_ge(dsem[k], 16 * c)

            @block.tensor
            def _(e):
                run("pe", e)

            @block.scalar
            def _(e):
                run("act", e)

            @block.vector
            def _(e):
                run("dve", e)

            @block.gpsimd
            def _(e):
                run("pool", e)

            @block.sync
            def _(e):
                run("sp", e)


def build():
    nc = bass.Bass("TRN2", target_bir_lowering=False)
    di = lambda n, s: nc.dram_tensor(n, s, F32, kind="ExternalInput").ap()
    do = lambda n, s: nc.dram_tensor(n, s, F32, kind="ExternalOutput").ap()
    xp = di("xp", [SEQ, D]); xs = di("xs", [NS, D]); sshift = di("sshift", [NS, D])
    swkv = di("swkv", [NS, 1024, 64]); sconv = di("sconv", [NS * 30, 512])
    w_in = di("w_in", [D, NIN]); w_br = di("w_br", [D, D]); w_bc = di("w_bc", [512, D]); w_out = di("w_out", [D, D])
    cols_d = di("cols", [128, NCOL]); consts_d = di("consts", [128, NCONST])
    w2ext_d = di("w2ext", [65, 1024]); a2_d = di("a2", [64, 1024])
    npg_d = di("npg", [1, D]); npre_d = di("npre", [1, D])
    yp = do("yp", [SEQ, D]); ys = do("ys", [NS, D]); nsp = do("nsp", [1, D])
    nwp = do("nwp", [1024, 64]); ncp = do("ncp", [30, 512]); nss = do("nss", [NS, D])
    nws = do("nws", [NS, 1024, 64]); ncs = do("ncs", [NS, 30, 512])
    wi_s = nc.dram_tensor("wi_s", [D, NIN], BF16).ap()
    wbr_s = nc.dram_tensor("wbr_s", [D, D], BF16).ap()
    wbc_s = nc.dram_tensor("wbc_s", [512, D], BF16).ap()
    wo_s = nc.dram_tensor("wo_s", [D, D], BF16).ap()

    with contextlib.ExitStack() as st:
        def T(n, s, d=F32):
            return st.enter_context(nc.sbuf_tensor("sb_" + n, s, d))
        P = Prog(nc)
        op = P.op
        cst = T("cst", [128, NCONST]); col = T("col", [128, NCOL])
        idb = T("idb", [128, 128], BF16)
        bob = T("bob", [128, 128], BF16)
        bmb = T("bmb", [128, 128], BF16)
        w2e = T("w2e", [65, 1024]); a2b = T("a2b", [128, 1024], BF16)
        wb = [T("wb%d" % i, [128, 8, 512], BF16) for i in range(4)]
        xt = [T("xt%d" % i, [128, D]) for i in range(2)]
        hb = T("hb", [128, D], BF16)
        hT = T("hT", [128, 8, 512], BF16)
        rS = T("rS", [128, 8, 512], BF16); kS = T("kS", [128, 8, 512], BF16)
        vS = T("vS", [128, 8, 512], BF16); zrS = T("zrS", [128, 8, 512], BF16)
        twl = T("twl", [65, 512]); alb = T("alb", [128, 512], BF16)
        ua = T("ua", [128, 4, 512]); uex = T("uex", [128, 4, 542]); ubf = T("ubf", [128, 4, 542], BF16)
        szc = T("szc", [128, 4, 512], BF16)
        orT = T("orT", [128, 8, 512], BF16)
        mT = rS
        ocT = kS
        TT = [T("T%d" % i, [128, 8, 128]) for i in range(8)]
        bon = T("bon", [128, 8, 128], BF16)
        rt_ = T("rt_", [128, 8, 128], BF16); at_ = T("at_", [128, 8, 128], BF16); bt_ = T("bt_", [128, 8, 128], BF16)
        kt_ = T("kt_", [128, 8, 128], BF16); bh_ = T("bh_", [128, 8, 128], BF16); kh_ = T("kh_", [128, 8, 128], BF16)
        Vt = T("Vt", [128, 1024], BF16); Bt = T("Bt", [128, 1024], BF16); Kt = T("Kt", [128, 1024], BF16)
        Ak = [T("Ak%d" % i, [128, 4, 128], BF16) for i in range(2)]
        Nk = [T("Nk%d" % i, [128, 4, 128], BF16) for i in range(2)]
        Qb = T("Qb", [128, 4, 128], BF16)
        LkT = T("LkT", [128, 4, 128], BF16); MbT = T("MbT", [128, 4, 128], BF16); MkT = T("MkT", [128, 4, 128], BF16)
        Xb = T("Xb", [128, 256], BF16); SAb = T("SAb", [128, 256], BF16)
        Xb2 = T("Xb2", [128, 256], BF16); SAb2 = T("SAb2", [128, 256], BF16)
        dummy = T("dummy", [128, 8])
        _w3 = lambda i: wb[3][:, i, :].rearrange("p (a b) -> p a b", b=128)
        SETS = [
            {"Ak": Ak, "Nk": Nk, "Qb": Qb, "LkT": LkT, "MbT": MbT, "MkT": MkT, "Xb": Xb, "SAb": SAb, "banks": (0, 1, 2), "n": "_A"},
            {"Ak": [_w3(0), _w3(1)], "Nk": [_w3(2), _w3(3)], "Qb": _w3(4), "LkT": _w3(5), "MbT": _w3(6), "MkT": _w3(7),
             "Xb": Xb2, "SAb": SAb2, "banks": (3, 4, 5), "n": "_B"},
        ]
        SETB_KEYS = [k + "_B" for k in ("Ak0", "Ak1", "Nk0", "Nk1", "Qb", "LkT", "MbT", "MkT")]
        ALLT4 = ["T4_0", "T4_1", "T4_2", "T4_3"]
        ALLSF = ["Sf0", "Sf1", "Sf2", "Sf3"]
        ALLSB = ["Sb0", "Sb1", "Sb2", "Sb3"]
        Sf = T("Sf", [128, 8, 64]); Sb = T("Sb", [128, 8, 64], BF16)
        gC = T("gC", [128, 8]); tmpb = T("tmpb", [128, 512]); tmpc = T("tmpc", [128, 512])
        sgb = T("sgb", [128, 512], BF16)
        m1 = TT[5][:].rearrange("p a b -> p (a b)")
        pprev = [T("pprev%d" % i, [128, 40]) for i in range(2)]
        small = T("small", [128, 64])
        dg = [T("dg%d" % i, [128, 128], BF16) for i in range(4)]
        wld = xt
        pb = [st.enter_context(nc.psum_tensor("pb%d" % i, [128, 512], F32)) for i in range(7)]
        ptb = st.enter_context(nc.psum_tensor("ptb", [128, 1024], BF16))

        cnt = {"d": 0, "e": 0}
        import os
        STOP = float(os.environ.get("MK_STOP", "1000"))

        def stage(k):
            if k > STOP:
                P.dead = True
        P.annotate = bool(os.environ.get("MK_ANN"))

        def ph(name):
            P.phase = name

        def dma(out, in_, reads, writes, q="sp"):
            cnt[q] = cnt.get(q, 0) + 1
            key = "%s%d" % (q, cnt[q] % (16 if q == "sp" else 8))
            if q == "act":
                return op("act", lambda e: e.dma_start(out=out, in_=in_), reads, writes, dma=key)
            return op(q, lambda e: e.dma_start(out=out, in_=in_), reads, writes, dma=key)

        def mm(out, lhsT, rhs, start, stop, reads, writes):
            b0 = lhsT.base_partition()
            n0 = lhsT.shape[0]
            tag = "lo" if b0 + n0 <= 64 else ("hi" if b0 >= 64 else None)
            op("pe", lambda e: e.matmul(out, lhsT=lhsT, rhs=rhs, start=start, stop=stop), reads, writes, tag=tag)

        def act(out, in_, func, reads, writes, bias=None, scale=None, accum=None):
            kw = {}
            if bias is not None: kw["bias"] = bias
            if scale is not None: kw["scale"] = scale
            if accum is not None: kw["accum_out"] = accum
            op("act", lambda e: e.activation(out=out, in_=in_, func=func, **kw), reads, writes)

        def tt(eng, out, in0, in1, o, reads, writes):
            g = {"dve": "dve", "pool": "pool"}[eng]
            op(g, lambda e: e.tensor_tensor(out=out, in0=in0, in1=in1, op=o), reads, writes)

        def ts(eng, out, in0, s1, s2, o0, o1, reads, writes):
            if s2 is None:
                op(eng, lambda e: e.tensor_scalar(out=out, in0=in0, scalar1=s1, scalar2=None, op0=o0), reads, writes)
            else:
                op(eng, lambda e: e.tensor_scalar(out=out, in0=in0, scalar1=s1, scalar2=s2, op0=o0, op1=o1), reads, writes)

        def stt(eng, out, in0, sc, in1, o0, o1, reads, writes):
            op(eng, lambda e: e.scalar_tensor_tensor(out=out, in0=in0, scalar=sc, in1=in1, op0=o0, op1=o1), reads, writes)

        def cp(eng, out, in_, reads, writes):
            if eng == "act":
                act(out, in_, AF.Copy, reads, writes)
            else:
                op(eng, lambda e: e.tensor_copy(out=out, in_=in_), reads, writes)

        def rsq(out, in_, eps, reads, wkey):
            act(out, in_, AF.Sqrt, reads, [wkey], bias=eps)
            op("dve", lambda e: e.reciprocal(out=out, in_=out), [wkey], [wkey])

        def bc(ap, shape):
            return ap.to_broadcast(shape)

        C = lambda o, n=128: cst[:, o:o + n]

        dma(cst[:], consts_d, [], ["cst"])
        dma(col[:], cols_d, [], ["col"])
        dma(w2e[:], w2ext_d, [], ["w2e"])
        dma(wld[0][64:128, 0:1024], a2_d, [], ["xt0"])
        cp("dve", a2b[64:128, :], wld[0][64:128, 0:1024], ["xt0"], ["a2b"])
        cp("dve", idb[:], C(C_ID), ["cst"], ["idb"])
        cp("dve", bob[:], C(C_BO), ["cst"], ["bob"])
        cp("dve", bmb[:], C(C_BM), ["cst"], ["bmb"])
        ts("dve", col[:, O_OMM:O_OMM + 33], col[:, O_MU:O_MU + 33], -1.0, 1.0, ALU.mult, ALU.add, ["col"], ["col"])
        ts("dve", col[:, O_OMKA:O_OMKA + 8], col[:, O_KA:O_KA + 8], -1.0, 1.0, ALU.mult, ALU.add, ["col"], ["col"])
        op("pool", lambda e: e.memset(twl[64:65, :], 1.0), [], ["twl"])
        op("pool", lambda e: e.memset(Sf[:], 0.0), [], ALLSF)
        op("pool", lambda e: e.memset(Sb[:], 0.0), [], ALLSB)
        op("pool", lambda e: e.memset(pprev[0][:], 0.0), [], ["pprev0"])
        op("pool", lambda e: e.memset(pprev[1][:], 0.0), [], ["pprev1"])
        op("pool", lambda e: e.memset(uex[:], 0.0), [], ["uex"])

        stage(1)
        ph("prologue")
        def prologue_gen(part):
            ph("prologue")
            pieces = []
            for c0 in range(0, NIN, 1024):
                for rc in range(8):
                    pieces.append((w_in, wi_s, rc, c0, min(1024, NIN - c0), "scr_i%d" % (c0 // 1024)))
            for rc in range(8):
                pieces.append((w_br, wbr_s, rc, 0, 1024, "scr_o"))
            for rc in range(4):
                pieces.append((w_bc, wbc_s, rc, 0, 1024, "scr_o"))
            for rc in range(8):
                pieces.append((w_out, wo_s, rc, 0, 1024, "scr_o"))
            fl = lambda t: t[:].rearrange("p a b -> p (a b)")
            orv = lambda i: orT[:, 2 * i:2 * i + 2, :].rearrange("p a b -> p (a b)")
            if part == 1:
                pieces = pieces[0:48]
                sf32 = [(fl(TT[i]), "T%d" % i) for i in range(8)]
                sbf = [(fl(rt_), "rt_0"), (fl(at_), "at_0"), (fl(bt_), "bt_0"), (fl(kt_), "kt_0"), (fl(bh_), "bh_"), (fl(kh_), "kh_"),
                       (Vt[:, :], "Vt_0"), (Bt[:, :], "Bt_0"), (Kt[:, :], "Kt_0")]
                DEPTH = 6
            else:
                pieces = pieces[48:]
                sf32 = [(fl(TT[i]), "T%d" % i) for i in (3, 4, 6, 7)]
                sbf = [(orv(0), "orT"), (orv(1), "orT"), (orv(2), "orT"), (orv(3), "orT"),
                       (fl(kh_), "kh_"), (Vt[:, :], "Vt_0"), (Bt[:, :], "Bt_0"), (Kt[:, :], "Kt_0")]
                DEPTH = 3
            NB = len(sf32)
            engs = ["dve", "act"]
            npc = len(pieces)
            for i in range(npc + DEPTH):
                if i < npc:
                    src, dst, rc, c0, n, skey = pieces[i]
                    bf_, kf_ = sf32[i % NB]
                    dma(bf_[:, 0:n], src[rc * 128:(rc + 1) * 128, c0:c0 + n], [], [kf_],
                        q=("act" if (os.environ.get("MK_ACTQ") and i % 2 == 1) else "sp"))
                j = i - DEPTH
                if j >= 0:
                    src, dst, rc, c0, n, skey = pieces[j]
                    bf_, kf_ = sf32[j % NB]
                    bb_, kb_ = sbf[j % len(sbf)]
                    cp(engs[j % 2], bb_[:, 0:n], bf_[:, 0:n], [kf_], [kb_])
                    dma(dst[rc * 128:(rc + 1) * 128, c0:c0 + n], bb_[:, 0:n], [kb_], [skey], q="pool")
                yield

        stage(2)
        wi_v = wi_s.rearrange("(dc p) n -> p dc n", p=128)
        wbr_v = wbr_s.rearrange("(dc p) n -> p dc n", p=128)
        wbc_v = wbc_s.rearrange("(dc p) n -> p dc n", p=128)
        wo_v = wo_s.rearrange("(dc p) n -> p dc n", p=128)
        wslot = {"i": 0}

        def wload(view, ndc, c0, n, skeys):
            s = wslot["i"] % 4
            wslot["i"] += 1
            dma(wb[s][:, 0:ndc, 0:n], view[:, :, c0:c0 + n], skeys, ["wb%d" % s])
            return s

        def ikeys(c0, n):
            return ["scr_i%d" % b for b in range(c0 // 1024, (c0 + n - 1) // 1024 + 1)]

        def rmsnorm_tile(xtile, key, npart, dst_cols, want_h_out=None):
            ph("rmsnorm")
            act(hb[0:npart, :], xtile[0:npart, :], AF.Square, [key], ["hb", "small"], accum=small[0:npart, 0:1])
            ts("dve", small[0:npart, 1:2], small[0:npart, 0:1], 1.0 / D, 1e-6, ALU.mult, ALU.add, ["small"], ["small"])
            rsq(small[0:npart, 2:3], small[0:npart, 1:2], 0.0, ["small"], "small")
            ts("dve", hb[0:npart, :], xtile[0:npart, :], small[0:npart, 2:3], None, ALU.mult, None, [key, "small"], ["hb"])
            if want_h_out is not None:
                want_h_out()
            for dc in range(8):
                op("pe", lambda e, dc=dc: e.transpose(ptb[:, dc * 128:dc * 128 + npart], hb[0:npart, dc * 128:(dc + 1) * 128], idb[0:npart, 0:npart]),
                   ["hb", "idb"], ["ptb"])
            for dc in range(8):
                act(hT[:, dc, dst_cols[0]:dst_cols[1]], ptb[:, dc * 128:dc * 128 + npart], AF.Copy, ["ptb", "col"], ["hT"],
                    scale=col[:, O_GPRE + dc:O_GPRE + dc + 1])

        def project(j, wslot_i, jj, NT, bank):
            for dc in range(8):
                mm(pb[bank][:, 0:NT], wb[wslot_i][:, dc, jj * 128:(jj + 1) * 128], hT[:, dc, 0:NT], dc == 0, dc == 7,
                   ["wb%d" % wslot_i, "hT"], ["pb%d" % bank])

        def shiftmix(j, bank, NT, dst, dkey, sample, pp_old, pp_new):
            p = pb[bank]
            mu = col[:, O_MU + j:O_MU + j + 1]
            omm = col[:, O_OMM + j:O_OMM + j + 1]
            bk = "pb%d" % bank
            if sample:
                act(tmpb[:, 0:NS], p[:, 0:NS], AF.Copy, [bk, "col"], ["tmpb"], scale=omm)
                stt("dve", dst, p[:, NS:2 * NS], mu, tmpb[:, 0:NS], ALU.mult, ALU.add, [bk, "tmpb", "col"], [dkey])
            else:
                act(tmpb[:, 0:NT], p[:, 0:NT], AF.Copy, [bk, "col"], ["tmpb"], scale=omm)
                act(pprev[pp_new][:, j:j + 1], p[:, NT - 1:NT], AF.Copy, [bk], ["pprev%d" % pp_new])
                stt("dve", dst[:, 1:NT], p[:, 0:NT - 1], mu, tmpb[:, 1:NT], ALU.mult, ALU.add, [bk, "tmpb", "col"], [dkey])
                stt("dve", dst[:, 0:1], pprev[pp_old][:, j:j + 1], mu, tmpb[:, 0:1], ALU.mult, ALU.add,
                    ["pprev%d" % pp_old, "tmpb", "col"], [dkey])

        def proj_phase(NT, sample, pp_old, pp_new):
            ph("proj")
            nb = 0
            for g0 in range(0, 45, 4):
                ng = min(4, 45 - g0)
                s = wload(wi_v, 8, g0 * 128, ng * 128, ikeys(g0 * 128, ng * 128))
                for jj in range(ng):
                    j = g0 + jj
                    bank = nb % 2
                    nb += 1
                    bk = "pb%d" % bank
                    project(j, s, jj, NT if not sample else 2 * NS, bank)
                    W = NS if sample else NT
                    if j < 8:
                        shiftmix(j, bank, NT, rS[:, j, 0:W], "rS", sample, pp_old, pp_new)
                    elif j < 16:
                        shiftmix(j, bank, NT, kS[:, j - 8, 0:W], "kS", sample, pp_old, pp_new)
                    elif j < 24:
                        shiftmix(j, bank, NT, vS[:, j - 16, 0:W], "vS", sample, pp_old, pp_new)
                    elif j < 32:
                        shiftmix(j, bank, NT, tmpc[:, 0:W], "tmpc", sample, pp_old, pp_new)
                        act(zrS[:, j - 24, 0:W], tmpc[:, 0:W], AF.Silu, ["tmpc"], ["zrS"])
                    elif j == 32:
                        shiftmix(j, bank, NT, tmpc[:, 0:W], "tmpc", sample, pp_old, pp_new)
                        act(twl[0:64, 0:W], tmpc[0:64, 0:W], AF.Tanh, ["tmpc"], ["twl"])
                        cp("pool", alb[64:128, 0:W], tmpc[64:128, 0:W], ["tmpc"], ["alb"])
                    elif j < 37:
                        c = j - 33
                        act(ua[:, c, 0:W], pb[bank][:, 0:W], AF.Identity, [bk, "col"], ["ua"],
                            bias=col[:, O_GLUB + c:O_GLUB + c + 1])
                    elif j < 41:
                        c = j - 37
                        act(tmpc[:, 0:W], pb[bank][:, 0:W], AF.Sigmoid, [bk, "col"], ["tmpc"],
                            bias=col[:, O_GLUB + 4 + c:O_GLUB + 5 + c])
                        if sample:
                            tt("dve", uex[:, c, 0:NS * 31].rearrange("p (n w) -> p n w", w=31)[:, :, 30], ua[:, c, 0:W], tmpc[:, 0:W],
                               ALU.mult, ["ua", "tmpc"], ["uex"])
                        else:
                            tt("dve", uex[:, c, 30:30 + W], ua[:, c, 0:W], tmpc[:, 0:W], ALU.mult, ["ua", "tmpc"], ["uex"])
                    else:
                        c = j - 41
                        act(szc[:, c, 0:W], pb[bank][:, 0:W], AF.Silu, [bk], ["szc"])
                    yield

        def ln_conv_out(W, cf, ck):
            ph("lnconv")
            for c in range(4):
                mm(pb[2][:, 0:W], C(C_AM), cf[c], c == 0, c == 3, ["cst", ck[c]], ["pb2"])
            for c in range(4):
                tt("dve", cf[c], cf[c], pb[2][:, 0:W], ALU.subtract, [ck[c], "pb2"], [ck[c]])
            for c in range(4):
                tt("pool", ua[:, c, 0:W], cf[c], cf[c], ALU.mult, [ck[c]], ["ua"])
            for c in range(4):
                mm(pb[3][:, 0:W], C(C_AM), ua[:, c, 0:W], c == 0, c == 3, ["cst", "ua"], ["pb3"])
            rsq(tmpc[:, 0:W], pb[3][:, 0:W], 1e-5, ["pb3"], "tmpc")
            for c in range(4):
                tt("dve", cf[c], cf[c], tmpc[:, 0:W], ALU.mult, [ck[c], "tmpc"], [ck[c]])
                act(ua[:, c, 0:W], cf[c], AF.Silu, [ck[c], "col"], ["ua"],
                    bias=col[:, O_LNB + c:O_LNB + c + 1], scale=col[:, O_LNG + c:O_LNG + c + 1])
                tt("pool", ocT[:, c, 0:W], ua[:, c, 0:W], szc[:, c, 0:W], ALU.mult, ["ua", "szc"], ["kS"])

        def prep_gen(cs, W, sample, PSp):
            T0, T1, T2, T3, T4, T5, T6, T7 = TT
            sl = slice(cs, cs + W)
            sfx = PSp["sfx"]
            bonT, gCt = PSp["bon"], PSp["gC"]
            kbon, kgc = "bon" + sfx, "gC" + sfx
            ph("prep")
            sh = [128, 8, W]
            colb = lambda o: bc(col[:, o:o + 8].unsqueeze(2), sh)
            p6 = pb[6]
            p6v = p6[:].rearrange("p (a b) -> p a b", b=128)[:, :, 0:W]
            T0v = T0[:].rearrange("p a b -> p (a b)")
            tt("dve", T5[:, :, 0:W], kS[:, :, sl], colb(O_KK), ALU.mult, ["kS", "col"], ["T5"])
            tt("pool", bh_[:, :, 0:W], T5[:, :, 0:W], T5[:, :, 0:W], ALU.mult, ["T5"], ["bh_"])
            yield
            for hf in range(2):
                mm(p6[0:W, :], twl[0:65, sl], w2e[0:65, hf * 512:(hf + 1) * 512], True, True, ["twl", "w2e"], ["pb6"])
                yield
                act(T0v[0:W, hf * 512:(hf + 1) * 512], p6[0:W, :], AF.Sigmoid, ["pb6"], ["T0"])
                yield
            tri = C(C_TRI) if not sample else cst[0:W, C_NI:C_NI + W]
            tre = C(C_TRE) if not sample else cst[0:W, C_NI + 64:C_NI + 64 + W]
            for hf in range(2):
                hs = slice(hf * 4, hf * 4 + 4)
                for hq in range(4):
                    hh = hf * 4 + hq
                    mm(p6[:, hq * 128:hq * 128 + W], T0v[0:W, hh * 128:(hh + 1) * 128], tri[0:W, 0:W], True, True, ["T0", "cst"], ["pb6"])
                yield
                act(T1[:, hs, 0:W], p6v[:, 0:4, :], AF.Exp, ["pb6"], ["T1"])
                act(T2[:, hs, 0:W], p6v[:, 0:4, :], AF.Exp, ["pb6"], ["T2"], scale=-1.0)
                yield
            cp("pool", gCt[:, :], T1[:, :, W - 1], ["T1"], [kgc])
            for hf in range(2):
                for hq in range(4):
                    hh = hf * 4 + hq
                    mm(p6[:, hq * 128:hq * 128 + W], a2b[64:128, hh * 128:(hh + 1) * 128], alb[64:128, sl], True, True, ["a2b", "alb"], ["pb6"])
                yield
                for hq in range(4):
                    hh = hf * 4 + hq
                    act(T4[:, hh, 0:W], p6[:, hq * 128:hq * 128 + W], AF.Sigmoid, ["pb6", "col"], ALLT4,
                        bias=col[:, O_A0 + hh:O_A0 + hh + 1])
                yield
            for hf in range(2):
                hs = slice(hf * 4, hf * 4 + 4)
                for hq in range(4):
                    hh = hf * 4 + hq
                    mm(p6[:, hq * 128:hq * 128 + W], bob[:], bh_[:, hh, 0:W], True, True, ["bob", "bh_"], ["pb6"])
                yield
                rsq(T7[:, hs, 0:W], p6v[:, 0:4, :], 1e-12, ["pb6"], "T7")
                yield
            stt("dve", T6[:, :, 0:W], T5[:, :, 0:W], -1.0, T7[:, :, 0:W], ALU.mult, ALU.mult, ["T5", "T7"], ["T6"])
            yield
            stt("dve", T7[:, :, 0:W], T6[:, :, 0:W], -1.0, T4[:, :, 0:W], ALU.mult, ALU.mult, ["T6"] + ALLT4, ["T7"])
            yield
            tt("pool", T0[:, :, 0:W], T4[:, :, 0:W], colb(O_KA), ALU.mult, ALLT4 + ["col", "T0"], ["T0"])
            tt("pool", T0[:, :, 0:W], T0[:, :, 0:W], colb(O_OMKA), ALU.add, ["T0", "col"], ["T0"])
            yield
            tt("dve", T5[:, :, 0:W], kS[:, :, sl], T0[:, :, 0:W], ALU.mult, ["kS", "T0"], ["T5"])
            yield
            tt("pool", T0[:, :, 0:W], rS[:, :, sl], T5[:, :, 0:W], ALU.mult, ["rS", "T5"], ["T0"])
            tt("pool", kh_[:, :, 0:W], T0[:, :, 0:W], colb(O_RK), ALU.mult, ["T0", "col"], ["kh_"])
            yield
            for hf in range(2):
                hs = slice(hf * 4, hf * 4 + 4)
                for hq in range(4):
                    hh = hf * 4 + hq
                    mm(p6[:, hq * 128:hq * 128 + W], bob[:], kh_[:, hh, 0:W], True, True, ["bob", "kh_"], ["pb6"])
                yield
                tt("dve", bonT[:, hs, 0:W], p6v[:, 0:4, :], vS[:, hs, sl], ALU.mult, ["pb6", "vS"], [kbon])
                yield
            if sample:
                return
            ph("mults")
            EG, EnG, EGe, k2, aa, bb = T1, T2, T3, T5, T6, T7
            rt, at, bt, kt = PSp["rt"], PSp["at"], PSp["bt"], PSp["kt"]
            krt, kat, kbt, kkt = "rt" + sfx, "at" + sfx, "bt" + sfx, "kt" + sfx
            tt("dve", rt, rS[:, :, sl], EG[:], ALU.mult, ["rS", "T1"], [krt])
            tt("pool", at[:, :, 1:128], aa[:, :, 1:128], EG[:, :, 0:127], ALU.mult, ["T6", "T1"], [kat])
            cp("pool", at[:, :, 0:1], aa[:, :, 0:1], ["T6"], [kat])
            yield
            tt("dve", bt, bb[:], EnG[:], ALU.mult, ["T7", "T2"], [kbt])
            tt("pool", kt, k2[:], EnG[:], ALU.mult, ["T5", "T2"], [kkt])
            yield
            tt("dve", EGe[:], EnG[:], bc(EG[:, :, 127:128], [128, 8, 128]), ALU.mult, ["T2", "T1", kat], ["T3"])
            yield
            tt("pool", bh_[:], bb[:], EGe[:], ALU.mult, ["T7", "T3"], ["bh_"])
            tt("dve", kh_[:], k2[:], EGe[:], ALU.mult, ["T5", "T3"], ["kh_"])
            yield
            ph("transp")
            for src, skey, dst, dkey in ((vS, "vS", PSp["Vt"], "Vt" + sfx), (bh_, "bh_", PSp["Bt"], "Bt" + sfx), (kh_, "kh_", PSp["Kt"], "Kt" + sfx)):
                for hh in range(8):
                    srcap = src[:, hh, sl] if src is vS else src[:, hh, :]
                    op("pe", lambda e, srcap=srcap, hh=hh: e.transpose(ptb[:, hh * 128:(hh + 1) * 128], srcap, idb[:]),
                       [skey, "idb"], ["ptb"])
                yield
                cp("act", dst[:, 0:512], ptb[:, 0:512], ["ptb"], [dkey])
                cp("dve", dst[:, 512:1024], ptb[:, 512:1024], ["ptb"], [dkey])
                yield

        def gn_gen(yT, ykey, cs, W, G, Gk, bonT, kbon):
            ph("gn")
            G1, G2, G3 = G
            k1, k2_, k3 = Gk
            sl = slice(cs, cs + W)
            sh = [128, 8, W]
            colb = lambda o: bc(col[:, o:o + 8].unsqueeze(2), sh)
            p6 = pb[6]
            p6v = p6[:].rearrange("p (a b) -> p a b", b=128)[:, :, 0:W]
            for hf in range(2):
                hs = slice(hf * 4, hf * 4 + 4)
                for hq in range(4):
                    hh = hf * 4 + hq
                    mm(p6[:, hq * 128:hq * 128 + W], C(C_BM), yT[:, hh, 0:W], True, True, ["cst"] + ykey, ["pb6"])
                yield
                tt("dve", G1[:, hs, 0:W], yT[:, hs, 0:W], p6v[:, 0:4, :], ALU.subtract, ykey + ["pb6"], [k1])
                yield
            hbv = hb[:, :].rearrange("p (a b) -> p a b", b=128)
            tt("pool", hbv[:, :, 0:W], G1[:, :, 0:W], G1[:, :, 0:W], ALU.mult, [k1], ["hb"])
            yield
            for hf in range(2):
                hs = slice(hf * 4, hf * 4 + 4)
                for hq in range(4):
                    hh = hf * 4 + hq
                    mm(p6[:, hq * 128:hq * 128 + W], bmb[:], hbv[:, hh, 0:W], True, True, ["bmb", "hb"], ["pb6"])
                yield
                rsq(G3[:, hs, 0:W], p6v[:, 0:4, :], 64e-5, ["pb6"], k3)
                yield
            tt("dve", G1[:, :, 0:W], G1[:, :, 0:W], G3[:, :, 0:W], ALU.mult, [k1, k3], [k1])
            yield
            tt("pool", G1[:, :, 0:W], G1[:, :, 0:W], colb(O_GNG), ALU.mult, [k1, "col"], [k1])
            tt("pool", G1[:, :, 0:W], G1[:, :, 0:W], colb(O_GNB), ALU.add, [k1, "col"], [k1])
            yield
            tt("dve", G1[:, :, 0:W], G1[:, :, 0:W], bonT[:, :, 0:W], ALU.add, [k1, kbon], [k1])
            yield
            tt("dve", orT[:, :, sl], G1[:, :, 0:W], zrS[:, :, sl], ALU.mult, [k1, "zrS"], ["orT"])
            yield

        def run_all(gens):
            gens = list(gens)
            while gens:
                for gq in list(gens):
                    try:
                        next(gq)
                    except StopIteration:
                        gens.remove(gq)

        gC2 = T("gC2", [128, 8])
        _fl = lambda t, i: t[:, 2 * i:2 * i + 2, :].rearrange("p a b -> p (a b)")
        _v8 = lambda ap: ap.rearrange("p (a b) -> p a b", b=128)
        PS = [
            {"sfx": "_0", "rt": rt_[:], "at": at_[:], "bt": bt_[:], "kt": kt_[:], "Vt": Vt, "Bt": Bt, "Kt": Kt, "bon": bon, "gC": gC},
            {"sfx": "_1", "rt": _v8(_fl(wb[0], 0)), "at": _v8(_fl(wb[0], 1)), "bt": _v8(_fl(wb[0], 2)), "kt": _v8(_fl(wb[0], 3)),
             "Vt": _fl(wb[1], 0), "Bt": _fl(wb[1], 1), "Kt": _fl(wb[1], 2), "bon": _v8(_fl(wb[1], 3)), "gC": gC2},
        ]
        PS1_KEYS = [k + "_1" for k in ("rt", "at", "bt", "kt", "Vt", "Bt", "Kt", "bon")]

        def scan_group(g, S, PSp, yT):
            Ak_, Nk_, Qb_, LkT_, MbT_, MkT_, Xb_, SAb_ = S["Ak"], S["Nk"], S["Qb"], S["LkT"], S["MbT"], S["MkT"], S["Xb"], S["SAb"]
            b0, b1, b2 = S["banks"]
            kb = lambda i: "pb%d" % i
            n = S["n"]
            sfx = PSp["sfx"]
            rt_, at_, bt_, kt_, Vt, Bt, Kt, gC = PSp["rt"], PSp["at"], PSp["bt"], PSp["kt"], PSp["Vt"], PSp["Bt"], PSp["Kt"], PSp["gC"]
            K = lambda nm: nm + n
            heads = [4 * g + x for x in (0, 2, 1, 3)]
            SbK = "Sb%d" % g; SfK = "Sf%d" % g; yK = "yT%d" % g

            def hp(h):
                hl, hh = h % 2, h // 2
                return slice(hl * 64, hl * 64 + 64), hh
            v4 = lambda p: p[:].rearrange("p (a b) -> p a b", b=128)
            mk = lambda o: bc(cst[:, o:o + 128].unsqueeze(1), [128, 4, 128])
            ph("scores")
            plan = [(b0, "at_", "bt_", Ak_[0], K("Ak0"), C_SL), (b1, "bt_", "at_", Nk_[0], K("Nk0"), C_SU),
                    (b2, "kt_", "at_", LkT_, K("LkT"), C_SU), (b0, "bt_", "rt_", MbT_, K("MbT"), C_UI),
                    (b1, "kt_", "rt_", MkT_, K("MkT"), C_UI)]
            tl = {"at_": at_, "bt_": bt_, "kt_": kt_, "rt_": rt_}
            kn = {"at_": "at" + sfx, "bt_": "bt" + sfx, "kt_": "kt" + sfx, "rt_": "rt" + sfx}
            first_lo = (n == "_A")
            for rnd in (plan[0:3], plan[3:5]):
                for tagsel in ((0, 1) if first_lo else (1, 0)):
                    for (bk, ln, rn, dst, dk, msk) in rnd:
                        for hi, h in enumerate(heads):
                            if (h % 2) != tagsel:
                                continue
                            pr, hh = hp(h)
                            mm(pb[bk][:, hi * 128:(hi + 1) * 128], tl[ln][pr, hh, :], tl[rn][pr, hh, :], True, True, [kn[ln], kn[rn]], [kb(bk)])
                yield
                for (bk, ln, rn, dst, dk, msk) in rnd:
                    tt("dve", dst[:], v4(pb[bk]), mk(msk), ALU.mult, [kb(bk), "cst"], [dk])
                    yield
            ph("doubling")
            tt("pool", Qb_[:], Nk_[0][:], mk(C_ID), ALU.add, [K("Nk0"), "cst"], [K("Qb")])
            yield
            mm(pb[b2][:, :], idb[:], Qb_.rearrange("p a b -> p (a b)"), True, True,
               ["idb", K("Qb")], [kb(b2)])
            yield
            cur = 0
            for lvl in range(6):
                nx = 1 - cur
                for hi in range(4):
                    mm(pb[b0][:, hi * 128:(hi + 1) * 128], Nk_[cur][:, hi, :], Ak_[cur][:, hi, :], True, True,
                       [K("Nk%d" % cur), K("Ak%d" % cur)], [kb(b0)])
                if lvl < 5:
                    for hi in range(4):
                        mm(pb[b1][:, hi * 128:(hi + 1) * 128], Ak_[cur][:, hi, :], Nk_[cur][:, hi, :], True, True,
                           [K("Nk%d" % cur), K("Ak%d" % cur)], [kb(b1)])
                yield
                cp("act", Ak_[nx][:], v4(pb[b0]), [kb(b0)], [K("Ak%d" % nx)])
                if lvl < 5:
                    cp("dve", Nk_[nx][:], v4(pb[b1]), [kb(b1)], [K("Nk%d" % nx)])
                yield
                for hi in range(4):
                    mm(pb[b2][:, hi * 128:(hi + 1) * 128], Ak_[nx][:, hi, :], Qb_[:, hi, :], False, True,
                       [K("Ak%d" % nx), K("Qb")], [kb(b2)])
                yield
                if lvl % 2 == 0:
                    cp("act", Qb_[:], v4(pb[b2]), [kb(b2)], [K("Qb")])
                else:
                    cp("dve", Qb_[:], v4(pb[b2]), [kb(b2)], [K("Qb")])
                yield
                cur = nx
            ph("seq")
            for hi, h in enumerate(heads):
                pr, hh = hp(h)
                o = pb[b0][:, hi * 64:(hi + 1) * 64]
                mm(o, at_[pr, hh, :], Sb[pr, hh, :], True, False, ["at" + sfx, SbK], [kb(b0)])
                mm(o, LkT_[:, hi, :], Vt[:, h * 64:(h + 1) * 64], False, True, [K("LkT"), "Vt" + sfx], [kb(b0)])
            yield
            cp("act", Xb_[:], pb[b0][:, 0:256], [kb(b0)], [K("Xb")])
            yield
            for hi, h in enumerate(heads):
                mm(pb[b1][:, hi * 64:(hi + 1) * 64], Qb_[:, hi, :], Xb_[:, hi * 64:(hi + 1) * 64], True, True, [K("Qb"), K("Xb")], [kb(b1)])
            yield
            cp("dve", SAb_[:], pb[b1][:, 0:256], [kb(b1)], [K("SAb")])
            yield
            for hi, h in enumerate(heads):
                pr, hh = hp(h)
                o = pb[b2][pr, (hh - 2 * g) * 128:(hh - 2 * g) * 128 + 128]
                mm(o, Sb[pr, hh, :], rt_[pr, hh, :], True, False, [SbK, "rt" + sfx], [kb(b2)])
                mm(o, SAb_[:, hi * 64:(hi + 1) * 64], MbT_[:, hi, :], False, False, [K("SAb"), K("MbT")], [kb(b2)])
                mm(o, Vt[:, h * 64:(h + 1) * 64], MkT_[:, hi, :], False, True, ["Vt" + sfx, K("MkT")], [kb(b2)])
            yield
            cp("act", yT[:, 2 * g:2 * g + 2, :], pb[b2][:, 0:256].rearrange("p (a b) -> p a b", b=128), [kb(b2)], [yK])
            for hi, h in enumerate(heads):
                pr, hh = hp(h)
                o = pb[b0][pr, (hh - 2 * g) * 64:(hh - 2 * g) * 64 + 64]
                mm(o, Bt[:, h * 64:(h + 1) * 64], SAb_[:, hi * 64:(hi + 1) * 64], True, False, ["Bt" + sfx, K("SAb")], [kb(b0)])
                mm(o, Kt[:, h * 64:(h + 1) * 64], Vt[:, h * 64:(h + 1) * 64], False, True, ["Kt" + sfx, "Vt" + sfx], [kb(b0)])
            yield
            gs = slice(2 * g, 2 * g + 2)
            tt("dve", Sf[:, gs, :], Sf[:, gs, :], bc(gC[:, gs].unsqueeze(2), [128, 2, 64]), ALU.mult, [SfK, "gC" + sfx], [SfK])
            tt("dve", Sf[:, gs, :], Sf[:, gs, :], pb[b0][:, 0:128].rearrange("p (a b) -> p a b", b=64), ALU.add, [SfK, kb(b0)], [SfK])
            yield
            cp("act", Sb[:, gs, :], Sf[:, gs, :], [SfK], [SbK])
            yield


        def tail(NT, xsrc_tiles, ydst_tiles, nrows):
            ph("tail")
            for q in range(2):
                sgr = wload(wi_v, 8, (45 + 4 * q) * 128, 512, ikeys((45 + 4 * q) * 128, 512))
                sbr = wload(wbr_v, 8, q * 512, 512, ["scr_o"])
                sgc = wload(wi_v, 8, (53 + 4 * q) * 128, 512, ikeys((53 + 4 * q) * 128, 512))
                sbc = wload(wbc_v, 4, q * 512, 512, ["scr_o"])
                for jj in range(4):
                    j = q * 4 + jj
                    project(45 + j, sgr, jj, NT, 0)
                    act(sgb[:, 0:NT], pb[0][:, 0:NT], AF.Sigmoid, ["pb0"], ["sgb"])
                    for fc in range(8):
                        mm(pb[1][:, 0:NT], wb[sbr][:, fc, jj * 128:(jj + 1) * 128], orT[:, fc, 0:NT], fc == 0, fc == 7,
                           ["wb%d" % sbr, "orT"], ["pb1"])
                    tt("dve", m1[:, 0:NT], pb[1][:, 0:NT], sgb[:, 0:NT], ALU.mult, ["pb1", "sgb"], ["T5"])
                    project(53 + j, sgc, jj, NT, 2)
                    act(sgb[:, 0:NT], pb[2][:, 0:NT], AF.Sigmoid, ["pb2"], ["sgb"])
                    for fc in range(4):
                        mm(pb[3][:, 0:NT], wb[sbc][:, fc, jj * 128:(jj + 1) * 128], ocT[:, fc, 0:NT], fc == 0, fc == 3,
                           ["wb%d" % sbc, "kS"], ["pb3"])
                    tt("dve", tmpc[:, 0:NT], pb[3][:, 0:NT], sgb[:, 0:NT], ALU.mult, ["pb3", "sgb"], ["tmpc"])
                    tt("pool", mT[:, j, 0:NT], m1[:, 0:NT], tmpc[:, 0:NT], ALU.add, ["T5", "tmpc"], ["rS"])
            so = [wload(wo_v, 8, 0, 512, ["scr_o"]), wload(wo_v, 8, 512, 512, ["scr_o"])]
            npg = TT[2][:].rearrange("p a b -> p (a b)")
            dma(npg[:, :], npg_d.partition_broadcast(128), [], ["T2"])
            for i, (xsrc, ydst) in enumerate(zip(xsrc_tiles, ydst_tiles)):
                xb = xt[i % 2]
                xk = "xt%d" % (i % 2)
                dma(xb[0:nrows, :], xsrc, [], [xk])
                tsl = slice(i * 128, i * 128 + nrows)
                for hf in range(2):
                    for fc in range(8):
                        mm(pb[4 + hf][0:nrows, :], mT[:, fc, tsl], wb[so[hf]][:, fc, :], fc == 0, fc == 7,
                           ["rS", "wb%d" % so[hf]], ["pb%d" % (4 + hf)])
                for hf in range(2):
                    act(hb[0:nrows, hf * 512:(hf + 1) * 512], pb[4 + hf][0:nrows, :], AF.Square, ["pb%d" % (4 + hf)], ["hb", "small"],
                        accum=small[0:nrows, 8 + hf:9 + hf])
                tt("dve", small[0:nrows, 10:11], small[0:nrows, 8:9], small[0:nrows, 9:10], ALU.add, ["small"], ["small"])
                ts("dve", small[0:nrows, 11:12], small[0:nrows, 10:11], 1.0 / D, 1e-6, ALU.mult, ALU.add, ["small"], ["small"])
                rsq(small[0:nrows, 12:13], small[0:nrows, 11:12], 0.0, ["small"], "small")
                T0v = TT[0][:].rearrange("p a b -> p (a b)")
                for hf in range(2):
                    hsl = slice(hf * 512, (hf + 1) * 512)
                    stt("dve", T0v[0:nrows, hsl], pb[4 + hf][0:nrows, :], small[0:nrows, 12:13], npg[0:nrows, hsl], ALU.mult, ALU.mult,
                        ["pb%d" % (4 + hf), "small", "T2"], ["T0"])
                tt("pool", T0v[0:nrows, :], T0v[0:nrows, :], xb[0:nrows, :], ALU.add, ["T0", xk], ["T0"])
                dma(ydst, T0v[0:nrows, :], ["T0"], [], q="pool")

        def rms_phase(sc):
            t0 = sc * 512
            for i in range(4):
                xb = xt[i % 2]; xk = "xt%d" % (i % 2)
                dma(xb[:], xp[t0 + i * 128:t0 + (i + 1) * 128, :], [], [xk])
                last = (sc == 3 and i == 3)

                def hout(xb=xb, xk=xk):
                    T0v = TT[0][:].rearrange("p a b -> p (a b)")
                    dma(T0v[:, :], npre_d.partition_broadcast(128), [], ["T0"])
                    ts("dve", TT[1][:].rearrange("p a b -> p (a b)"), xb[:], small[:, 2:3], None, ALU.mult, None, [xk, "small"], ["T1"])
                    tt("dve", TT[1][:].rearrange("p a b -> p (a b)"), TT[1][:].rearrange("p a b -> p (a b)"), T0v, ALU.mult, ["T1", "T0"], ["T1"])
                    dma(nsp, TT[1][:].rearrange("p a b -> p (a b)")[127:128, :], ["T1"], [])
                rmsnorm_tile(xb, xk, 128, (i * 128, (i + 1) * 128), hout if last else None)

        rms_phase(0)
        run_all([prologue_gen(1)])
        for sc in range(4):
            t0 = sc * 512
            if sc > 0:
                rms_phase(sc)
            stage(3 if sc == 0 else 11)
            if sc > 0:
                cp("pool", tmpc[:, 0:120].rearrange("p (c w) -> p c w", w=30), uex[:, :, 512:542], ["uex"], ["tmpc"])
                cp("pool", uex[:, :, 0:30], tmpc[:, 0:120].rearrange("p (c w) -> p c w", w=30), ["tmpc"], ["uex"])
            if sc == 0:
                run_all([proj_phase(512, False, sc % 2, (sc + 1) % 2), prologue_gen(2)])
            else:
                run_all([proj_phase(512, False, sc % 2, (sc + 1) % 2)])
            stage(4 if sc == 0 else 11)
            op("pool", lambda e: e.memset(dummy[:, 0:1], 0.0), [], ["wb3", "wb0", "wb1", "xt0", "xt1", "ua", "uaA", "uaB", "dummy", "yT0", "yT1", "yT2", "yT3"] + SETB_KEYS + PS1_KEYS)
            yTp = xt[0][:, :].rearrange("p (a b) -> p a b", b=128)
            Gp = (xt[1][:, :].rearrange("p (a b) -> p a b", b=128),
                  ua[:, 0:2, :].rearrange("p a b -> p (a b)").rearrange("p (a b) -> p a b", b=128),
                  ua[:, 2:4, :].rearrange("p a b -> p (a b)").rearrange("p (a b) -> p a b", b=128))
            Gpk = ("xt1", "uaA", "uaB")

            def chunk_scan(c4):
                PSp = PS[c4 % 2]
                for pair in ((0, 1), (2, 3)):
                    gens = [scan_group(pair[0], SETS[0], PSp, yTp), scan_group(pair[1], SETS[1], PSp, yTp)]
                    while gens:
                        for gq in list(gens):
                            try:
                                next(gq)
                                yield
                            except StopIteration:
                                gens.remove(gq)
                yield from gn_gen(yTp, ["yT0", "yT1", "yT2", "yT3"], c4 * 128, 128, Gp, Gpk, PSp["bon"], "bon" + PSp["sfx"])

            run_all([prep_gen(0, 128, False, PS[0])])
            for c4 in range(4):
                main = chunk_scan(c4)
                side = prep_gen((c4 + 1) * 128, 128, False, PS[(c4 + 1) % 2]) if c4 < 3 else None
                RATIO = int(os.environ.get("MK_RATIO", "4"))
                done = False
                while not done:
                    for _ in range(RATIO):
                        try:
                            next(main)
                        except StopIteration:
                            done = True
                            break
                    if side is not None:
                        try:
                            next(side)
                        except StopIteration:
                            side = None
                if side is not None:
                    run_all([side])
            op("pool", lambda e: e.memset(dummy[:, 1:2], 0.0), [], ["wb3", "wb0", "wb1", "xt0", "xt1", "ua", "uaA", "uaB", "dummy", "yT0", "yT1", "yT2", "yT3"] + SETB_KEYS + PS1_KEYS)
            stage(8 if sc == 0 else 11)
            ph("conv")
            cp("pool", ubf[:], uex[:], ["uex"], ["ubf"])
            for c in range(4):
                for w in range(31):
                    s = (c * 31 + w) % 4
                    if w % 2 == 0:
                        act(dg[s][:], idb[:], AF.Copy, ["idb", "col"], ["dg%d" % s], scale=col[:, O_CW + c * 31 + w:O_CW + c * 31 + w + 1])
                    else:
                        ts("dve", dg[s][:], idb[:], col[:, O_CW + c * 31 + w:O_CW + c * 31 + w + 1], None, ALU.mult, None,
                           ["idb", "col"], ["dg%d" % s])
                    mm(pb[6][:, :], dg[s][:], ubf[:, c, w:w + 512], w == 0, w == 30, ["dg%d" % s, "ubf"], ["pb6"])
                act(TT[c][:].rearrange("p a b -> p (a b)")[:, 0:512], pb[6][:, :], AF.Identity, ["pb6", "col"], ["T%d" % c],
                    bias=col[:, O_CB + c:O_CB + c + 1])
            ln_conv_out(512, [TT[c][:].rearrange("p a b -> p (a b)")[:, 0:512] for c in range(4)], ["T0", "T1", "T2", "T3"])
            if sc == 3:
                for c in range(4):
                    mm(pb[6][0:30, c * 128:(c + 1) * 128], uex[:, c, 512:542], C(C_ID), True, True, ["uex", "cst"], ["pb6"])
                cp("dve", tmpc[0:30, :], pb[6][0:30, :], ["pb6"], ["tmpc"])
                dma(ncp, tmpc[0:30, :], ["tmpc"], [])
            stage(9 if sc == 0 else 11)
            tail(512, [xp[t0 + i * 128:t0 + (i + 1) * 128, :] for i in range(4)],
                 [yp[t0 + i * 128:t0 + (i + 1) * 128, :] for i in range(4)], 128)

        stage(12)
        for h in range(16):
            hl, hh = h % 2, h // 2
            pr = slice(hl * 64, hl * 64 + 64)
            mm(pb[0][pr, hh * 64:(hh + 1) * 64], Sf[pr, hh, :], cst[pr, C_ID + hl * 64:C_ID + hl * 64 + 64], True, True, ALLSF + ["cst"], ["pb0"])
        cp("dve", tmpc[:, :], pb[0][:, :], ["pb0"], ["tmpc"])
        dma(nwp.rearrange("(hh p) j -> p hh j", p=128), tmpc[:, :].rearrange("p (a b) -> p a b", b=64), ["tmpc"], [])

        stage(13)
        ph("sample")
        xb = xt[0]
        dma(xb[0:NS, :], xs, [], ["xt0"])

        def hout_s():
            T0v = TT[0][:].rearrange("p a b -> p (a b)")
            T1v = TT[1][:].rearrange("p a b -> p (a b)")
            dma(T0v[0:NS, :], npre_d.partition_broadcast(NS), [], ["T0"])
            ts("dve", T1v[0:NS, :], xb[0:NS, :], small[0:NS, 2:3], None, ALU.mult, None, ["xt0", "small"], ["T1"])
            tt("dve", T1v[0:NS, :], T1v[0:NS, :], T0v[0:NS, :], ALU.mult, ["T1", "T0"], ["T1"])
            dma(nss, T1v[0:NS, :], ["T1"], [])
        rmsnorm_tile(xb, "xt0", NS, (0, NS), hout_s)
        dma(xt[1][0:NS, :], sshift, [], ["xt1"])
        cp("dve", hb[0:NS, :], xt[1][0:NS, :], ["xt1"], ["hb"])
        for dc in range(8):
            op("pe", lambda e, dc=dc: e.transpose(ptb[:, dc * 128:dc * 128 + NS], hb[0:NS, dc * 128:(dc + 1) * 128], idb[0:NS, 0:NS]),
               ["hb", "idb"], ["ptb"])
        cp("act", hT[:, :, NS:2 * NS], ptb[:, :].rearrange("p (a b) -> p a b", b=128)[:, :, 0:NS], ["ptb"], ["hT"])
        uv = [uex[:, c, 0:NS * 31].rearrange("p (n w) -> p n w", w=31) for c in range(4)]
        for q in range(4):
            dma(xt[1][0:120, 0:512], sconv[q * 120:(q + 1) * 120, :], [], ["xt1"])
            for c in range(4):
                mm(pb[6][:, c * 120:(c + 1) * 120], xt[1][0:120, c * 128:(c + 1) * 128], cst[0:120, C_ID:C_ID + 120], True, True,
                   ["xt1", "cst"], ["pb6"])
            for c in range(4):
                cp("dve", uv[c][:, q * 4:(q + 1) * 4, 0:30], pb[6][:, c * 120:(c + 1) * 120].rearrange("p (n w) -> p n w", w=30),
                   ["pb6"], ["uex"])
        dma(ncs[:, 0:29, :], sconv.rearrange("(n w) c -> n w c", w=30)[:, 1:30, :], [], [])
        run_all([proj_phase(2 * NS, True, 0, 0)])
        stage(14)
        run_all([prep_gen(0, NS, True, PS[0])])
        EG, EnG, EGe, k2, aa, bb = TT[1], TT[2], TT[3], TT[5], TT[6], TT[7]
        SW = [ua[:, i, :].rearrange("p (a b) -> p a b", b=64) for i in range(2)]
        Dxs = [Vt[:, 0:512], tmpb[:, :], Bt[:, 0:512], Kt[:, 0:512], Vt[:, 512:1024]]
        Dxk = ["Vt_0", "tmpb", "Bt_0", "Kt_0", "Vt_0"]
        Dxo = [bob[:], C(C_BO), bob[:], bob[:], bob[:]]
        Dxok = ["bob", "cst", "bob", "bob", "bob"]
        yTs = TT[4]
        i2b = bc(cst[:, C_I2:C_I2 + 64].unsqueeze(1), [128, 8, 64])
        op("pool", lambda e: e.memset(dummy[:, 2:3], 0.0), [], ["ua", "uaA", "uaB", "dummy"])
        for n in range(NS):
            Sw = SW[n % 2]; sk = ("uaA", "uaB")[n % 2]
            dma(Sw, swkv[n].rearrange("(hh p) j -> p hh j", p=128), [], [sk])
            vecs = [(aa, "T6"), (EG, "T1"), (bb, "T7"), (k2, "T5"), (rS, "rS")]
            for vi, (vt_, vk) in enumerate(vecs):
                tt("pool", Dxs[vi].rearrange("p (a b) -> p a b", b=64), i2b, bc(vt_[:, :, n:n + 1], [128, 8, 64]), ALU.mult,
                   ["cst", vk], [Dxk[vi]])
                mm(pb[vi][:, :], Dxo[vi], Dxs[vi], True, True, [Dxok[vi], Dxk[vi]], ["pb%d" % vi])
            v8 = lambda p: p[:].rearrange("p (a b) -> p a b", b=64)
            W3 = TT[0][:, :, 0:64]
            tt("dve", W3, Sw, v8(pb[0]), ALU.mult, [sk, "pb0"], ["T0"])
            op("dve", lambda e: e.tensor_reduce(out=small[:, 16:24], in_=TT[0][:, :, 0:64], axis=AX.X, op=ALU.add), ["T0"], ["small"])
            tt("dve", Sw, Sw, v8(pb[1]), ALU.mult, [sk, "pb1"], [sk])
            tt("dve", W3, v8(pb[2]), bc(small[:, 16:24].unsqueeze(2), [128, 8, 64]), ALU.mult, ["pb2", "small"], ["T0"])
            tt("pool", Sw, Sw, W3, ALU.add, [sk, "T0"], [sk])
            cp("dve", small[:, 24:32], vS[:, :, n], ["vS"], ["small"])
            tt("dve", W3, v8(pb[3]), bc(small[:, 24:32].unsqueeze(2), [128, 8, 64]), ALU.mult, ["pb3", "small"], ["T0"])
            tt("pool", Sw, Sw, W3, ALU.add, [sk, "T0"], [sk])
            dma(nws[n].rearrange("(hh p) j -> p hh j", p=128), Sw, [sk], [])
            tt("dve", W3, Sw, v8(pb[4]), ALU.mult, [sk, "pb4"], ["T0"])
            op("dve", lambda e, n=n: e.tensor_reduce(out=yTs[:, :, n], in_=TT[0][:, :, 0:64], axis=AX.X, op=ALU.add), ["T0"], ["T4_0"])
        stage(15)
        op("pool", lambda e: e.memset(dummy[:, 3:4], 0.0), [], ["ua", "uaA", "uaB", "dummy"])
        run_all([gn_gen(yTs, ALLT4, 0, NS, (TT[1], TT[2], TT[3]), ("T1", "T2", "T3"), bon, "bon_0")])
        stage(16)
        cf = []
        for c in range(4):
            cwb = bc(col[:, O_CW + c * 31:O_CW + (c + 1) * 31].unsqueeze(1), [128, NS, 31])
            tt("dve", tmpc[:, 0:NS * 31].rearrange("p (n w) -> p n w", w=31), uv[c], cwb, ALU.mult, ["uex", "col"], ["tmpc"])
            cfc = TT[c][:].rearrange("p a b -> p (a b)")[:, 0:NS]
            op("dve", lambda e, cfc=cfc: e.tensor_reduce(out=cfc, in_=tmpc[:, 0:NS * 31].rearrange("p (n w) -> p n w", w=31),
                                                        axis=AX.X, op=ALU.add), ["tmpc"], ["T%d" % c])
            ts("dve", cfc, cfc, col[:, O_CB + c:O_CB + c + 1], None, ALU.add, None, ["T%d" % c, "col"], ["T%d" % c])
            cf.append(cfc)
        for c in range(4):
            cp("dve", tmpb[:, c * NS:(c + 1) * NS], uv[c][:, :, 30], ["uex"], ["tmpb"])
        for c in range(4):
            mm(pb[6][0:NS, c * 128:(c + 1) * 128], tmpb[:, c * NS:(c + 1) * NS], C(C_ID), True, True, ["tmpb", "cst"], ["pb6"])
        cp("dve", m1[0:NS, 0:512], pb[6][0:NS, :], ["pb6"], ["T5"])
        dma(ncs[:, 29, :], m1[0:NS, 0:512], ["T5"], [])
        ln_conv_out(NS, cf, ["T0", "T1", "T2", "T3"])
        stage(17)
        tail(NS, [xs], [ys], NS)
        P.emit()
    return nc


def _prep_consts():
    c = np.zeros((128, NCONST), np.float32)
    idx = np.arange(128)
    c[:, C_ID:C_ID + 128] = np.eye(128)
    c[:, C_SL:C_SL + 128] = (idx[None, :] < idx[:, None])
    c[:, C_SU:C_SU + 128] = (idx[:, None] < idx[None, :])
    c[:, C_UI:C_UI + 128] = (idx[:, None] <= idx[None, :])
    c[:, C_TRI:C_TRI + 128] = (idx[:, None] <= idx[None, :]) * CNEG
    c[:, C_TRE:C_TRE + 128] = (idx[:, None] < idx[None, :]) * CNEG
    blk = (idx[:, None] // 64 == idx[None, :] // 64).astype(np.float32)
    c[:, C_BM:C_BM + 128] = blk / 64.0
    c[:, C_BO:C_BO + 128] = blk
    c[:, C_AM:C_AM + 128] = 1.0 / 512.0
    c[:, C_NI:C_NI + 64] = np.eye(128)[:, :64] * CNEG
    c[:, C_I2:C_I2 + 64] = (idx[:, None] % 64 == np.arange(64)[None, :])
    return c


_NC = None


def kernel(x_prompt, x_sample, state_shift, state_wkv, state_conv, norm_pre_g, w_in, mu_shift,
           decay_w0, decay_w2, iclr_a0, iclr_a2, k_k, k_a, r_k, gn_g, gn_b, conv_glu_b, conv_w,
           conv_b, ln_c_g, ln_c_b, w_branch_r, w_branch_c, w_out, norm_post_g):
    global _NC
    f = lambda a: np.ascontiguousarray(np.asarray(a, dtype=np.float32))
    colv = lambda v, n: f(v).reshape(n, 128).T
    cols = np.zeros((128, NCOL), np.float32)
    cols[:, O_MU:O_MU + 33] = colv(mu_shift[0], 33)
    cols[:, O_KK:O_KK + 8] = colv(k_k[0], 8)
    cols[:, O_KA:O_KA + 8] = colv(k_a[0], 8)
    cols[:, O_RK:O_RK + 8] = colv(np.asarray(r_k[0]).reshape(-1), 8)
    cols[:, O_GNG:O_GNG + 8] = colv(gn_g[0], 8)
    cols[:, O_GNB:O_GNB + 8] = colv(gn_b[0], 8)
    cols[:, O_A0:O_A0 + 8] = colv(iclr_a0[0], 8)
    cols[:, O_GLUB:O_GLUB + 8] = colv(conv_glu_b[0], 8)
    cols[:, O_CB:O_CB + 4] = colv(conv_b[0], 4)
    cols[:, O_LNG:O_LNG + 4] = colv(ln_c_g[0], 4)
    cols[:, O_LNB:O_LNB + 4] = colv(ln_c_b[0], 4)
    cw = f(conv_w[0])
    cols[:, O_CW:O_CW + 124] = cw.reshape(31, 4, 128).transpose(2, 1, 0).reshape(128, 124)
    cols[:, O_GPRE:O_GPRE + 8] = colv(norm_pre_g[0], 8)
    consts = _prep_consts()
    w2ext = np.concatenate([f(decay_w2[0]), f(decay_w0[0])[None, :]], axis=0)
    shared = {
        "w_in": f(w_in[0]), "w_br": f(w_branch_r[0]), "w_bc": f(w_branch_c[0]), "w_out": f(w_out[0]),
        "cols": cols, "consts": consts, "w2ext": f(w2ext), "a2": f(iclr_a2[0]),
        "npg": f(norm_post_g[0])[None, :], "npre": f(norm_pre_g[0])[None, :],
    }
    xpf = f(x_prompt); xsf = f(x_sample).reshape(128, D); ssf = f(state_shift[0])
    swf = f(state_wkv[0]).reshape(128, 1024, 64); scf = f(state_conv[0]).reshape(128 * 30, 512)
    in_maps = []
    for c in range(8):
        m = dict(shared)
        m["xp"] = xpf[c]
        m["xs"] = xsf[c * NS:(c + 1) * NS]
        m["sshift"] = ssf[c * NS:(c + 1) * NS]
        m["swkv"] = swf[c * NS:(c + 1) * NS]
        m["sconv"] = scf[c * NS * 30:(c + 1) * NS * 30]
        in_maps.append(m)
    if _NC is None:
        _NC = build()
    res = run_bass_kernel_spmd(_NC, in_maps, core_ids=list(range(8)))
    R = res.results
    y_prompt = np.stack([R[c]["yp"] for c in range(8)]).astype(np.float32)
    y_sample = np.concatenate([R[c]["ys"] for c in range(8)]).reshape(128, 1, D).astype(np.float32)
    nsp_ = np.concatenate([R[c]["nsp"] for c in range(8)]).reshape(1, 8, D).astype(np.float32)
    nwp_ = np.stack([R[c]["nwp"] for c in range(8)]).reshape(1, 8, 16, 64, 64).astype(np.float32)
    ncp_ = np.stack([R[c]["ncp"] for c in range(8)]).reshape(1, 8, 30, 512).astype(np.float32)
    nss_ = np.concatenate([R[c]["nss"] for c in range(8)]).reshape(1, 128, D).astype(np.float32)
    nws_ = np.concatenate([R[c]["nws"] for c in range(8)]).reshape(1, 128, 16, 64, 64).astype(np.float32)
    ncs_ = np.concatenate([R[c]["ncs"] for c in range(8)]).reshape(1, 128, 30, 512).astype(np.float32)
    return (y_prompt, y_sample, nsp_, nwp_, ncp_, nss_, nws_, ncs_)
```

```python
import contextlib
import numpy as np
import concourse.bass as bass
import concourse.mybir as mybir
from concourse.bass_utils import run_bass_kernel_spmd

F32 = mybir.dt.float32
BF16 = mybir.dt.bfloat16
AF = mybir.ActivationFunctionType
ALU = mybir.AluOpType
AX = mybir.AxisListType

D = 1024
NIN = 7808
SEQ = 2048
NS = 16
NCH = 61
CNEG = -0.6065306597126334

O_MU = 0; O_KK = 33; O_KA = 41; O_RK = 49; O_GNG = 57; O_GNB = 65; O_A0 = 73; O_GLUB = 81
O_CB = 89; O_LNG = 93; O_LNB = 97; O_CW = 101; O_GPRE = 225; O_OMM = 233; O_OMKA = 266; NCOL = 274
C_ID = 0; C_SL = 128; C_SU = 256; C_UI = 384; C_TRI = 512; C_TRE = 640; C_BM = 768; C_BO = 896
C_AM = 1024; C_NI = 1152; C_I2 = 1280; NCONST = 1344


class Prog:
    ENG = ("pe", "act", "dve", "pool", "sp")

    def __init__(self, nc):
        self.nc = nc
        self.ops = {e: [] for e in self.ENG}
        self.cnt = {e: 0 for e in self.ENG}
        self.waited = {e: {} for e in self.ENG}
        self.lastw = {}
        self.readers = {}
        self.dcnt = {}
        self.dead = False
        self.phase = ""
        self.annotate = False

    def _need(self, eng, waits, tok):
        if tok is None:
            return
        kind, key, val = tok
        if kind == "e" and key == "pe" and eng == "pe":
            return
        k = (kind, key)
        if self.waited[eng].get(k, 0) >= val:
            return
        if waits.get(k, 0) < val:
            waits[k] = val

    def op(self, eng, fn, reads=(), writes=(), dma=None, tag=None):
        if self.dead:
            return None
        waits = {}
        if eng == "pe":
            prev = getattr(self, "petag", None)
            if tag is not None and prev is not None and tag != prev:
                waits[("e", "pe")] = self.cnt["pe"]
            self.petag = tag
        for r in reads:
            self._need(eng, waits, self.lastw.get(r))
        for w in writes:
            self._need(eng, waits, self.lastw.get(w))
            for rd in self.readers.get(w, ()):
                self._need(eng, waits, rd)
        for k, v in waits.items():
            self.waited[eng][k] = v
        if dma is not None:
            prevc = self.dcnt.get(dma, 0)
            if prevc > 0 and self.waited[eng].get(("d", dma), 0) < prevc:
                waits[("d", dma)] = max(waits.get(("d", dma), 0), prevc)
                self.waited[eng][("d", dma)] = prevc
            self.dcnt[dma] = prevc + 1
            tok = ("d", dma, self.dcnt[dma])
        else:
            self.cnt[eng] += 1
            tok = ("e", eng, self.cnt[eng])
        self.ops[eng].append((waits, fn, tok, self.phase))
        for r in reads:
            self.readers.setdefault(r, []).append(tok)
        for w in writes:
            self.lastw[w] = tok
            self.readers[w] = []
        return tok

    def emit(self):
        nc = self.nc
        with contextlib.ExitStack() as st:
            esem = {e: st.enter_context(nc.semaphore("s_" + e)) for e in self.ENG}
            dsem = {k: st.enter_context(nc.semaphore("d_" + str(k))) for k in self.dcnt}
            block = st.enter_context(nc.Block())

            def run(engname, e):
                for waits, fn, tok, ph in self.ops[engname]:
                    for (kind, key), val in waits.items():
                        if kind == "e":
                            e.wait_ge(esem[key], val)
                        else:
                            e.wait_ge(dsem[key], 16 * val)
                    ins = fn(e)
                    if self.annotate:
                        ins.annotate(ph)
                    if tok[0] == "e":
                        ins.then_inc(esem[tok[1]], 1)
                    else:
                        ins.then_inc(dsem[tok[1]], 16)
                if engname == "sp":
                    for k, c in self.dcnt.items():
                        e.wait_ge(dsem[k], 16 * c)

            @block.tensor
            def _(e):
                run("pe", e)

            @block.scalar
            def _(e):
                run("act", e)

            @block.vector
            def _(e):
                run("dve", e)

            @block.gpsimd
            def _(e):
                run("pool", e)

            @block.sync
            def _(e):
                run("sp", e)


def build():
    nc = bass.Bass("TRN2", target_bir_lowering=False)
    di = lambda n, s: nc.dram_tensor(n, s, F32, kind="ExternalInput").ap()
    do = lambda n, s: nc.dram_tensor(n, s, F32, kind="ExternalOutput").ap()
    xp = di("xp", [SEQ, D]); xs = di("xs", [NS, D]); sshift = di("sshift", [NS, D])
    swkv = di("swkv", [NS, 1024, 64]); sconv = di("sconv", [NS * 30, 512])
    w_in = di("w_in", [D, NIN]); w_br = di("w_br", [D, D]); w_bc = di("w_bc", [512, D]); w_out = di("w_out", [D, D])
    cols_d = di("cols", [128, NCOL]); consts_d = di("consts", [128, NCONST])
    w2ext_d = di("w2ext", [65, 1024]); a2_d = di("a2", [64, 1024])
    npg_d = di("npg", [1, D]); npre_d = di("npre", [1, D])
    yp = do("yp", [SEQ, D]); ys = do("ys", [NS, D]); nsp = do("nsp", [1, D])
    nwp = do("nwp", [1024, 64]); ncp = do("ncp", [30, 512]); nss = do("nss", [NS, D])
    nws = do("nws", [NS, 1024, 64]); ncs = do("ncs", [NS, 30, 512])
    wi_s = nc.dram_tensor("wi_s", [D, NIN], BF16).ap()
    wbr_s = nc.dram_tensor("wbr_s", [D, D], BF16).ap()
    wbc_s = nc.dram_tensor("wbc_s", [512, D], BF16).ap()
    wo_s = nc.dram_tensor("wo_s", [D, D], BF16).ap()

    with contextlib.ExitStack() as st:
        def T(n, s, d=F32):
            return st.enter_context(nc.sbuf_tensor("sb_" + n, s, d))
        P = Prog(nc)
        op = P.op
        cst = T("cst", [128, NCONST]); col = T("col", [128, NCOL])
        idb = T("idb", [128, 128], BF16)
        bob = T("bob", [128, 128], BF16)
        bmb = T("bmb", [128, 128], BF16)
        w2e = T("w2e", [65, 1024]); a2b = T("a2b", [128, 1024], BF16)
        wb = [T("wb%d" % i, [128, 8, 512], BF16) for i in range(4)]
        xt = [T("xt%d" % i, [128, D]) for i in range(2)]
        hb = T("hb", [128, D], BF16)
        hT = T("hT", [128, 8, 512], BF16)
        rS = T("rS", [128, 8, 512], BF16); kS = T("kS", [128, 8, 512], BF16)
        vS = T("vS", [128, 8, 512], BF16); zrS = T("zrS", [128, 8, 512], BF16)
        twl = T("twl", [65, 512]); alb = T("alb", [128, 512], BF16)
        ua = T("ua", [128, 4, 512]); uex = T("uex", [128, 4, 542]); ubf = T("ubf", [128, 4, 542], BF16)
        szc = T("szc", [128, 4, 512], BF16)
        orT = T("orT", [128, 8, 512], BF16)
        mT = rS
        ocT = kS
        TT = [T("T%d" % i, [128, 8, 128]) for i in range(8)]
        bon = T("bon", [128, 8, 128], BF16)
        rt_ = T("rt_", [128, 8, 128], BF16); at_ = T("at_", [128, 8, 128], BF16); bt_ = T("bt_", [128, 8, 128], BF16)
        kt_ = T("kt_", [128, 8, 128], BF16); bh_ = T("bh_", [128, 8, 128], BF16); kh_ = T("kh_", [128, 8, 128], BF16)
        Vt = T("Vt", [128, 1024], BF16); Bt = T("Bt", [128, 1024], BF16); Kt = T("Kt", [128, 1024], BF16)
        Ak = [T("Ak%d" % i, [128, 4, 128], BF16) for i in range(2)]
        Nk = [T("Nk%d" % i, [128, 4, 128], BF16) for i in range(2)]
        Qb = T("Qb", [128, 4, 128], BF16)
        LkT = T("LkT", [128, 4, 128], BF16); MbT = T("MbT", [128, 4, 128], BF16); MkT = T("MkT", [128, 4, 128], BF16)
        Xb = T("Xb", [128, 256], BF16); SAb = T("SAb", [128, 256], BF16)
        Xb2 = T("Xb2", [128, 256], BF16); SAb2 = T("SAb2", [128, 256], BF16)
        dummy = T("dummy", [128, 8])
        _w3 = lambda i: wb[3][:, i, :].rearrange("p (a b) -> p a b", b=128)
        SETS = [
            {"Ak": Ak, "Nk": Nk, "Qb": Qb, "LkT": LkT, "MbT": MbT, "MkT": MkT, "Xb": Xb, "SAb": SAb, "banks": (0, 1, 2), "n": "_A"},
            {"Ak": [_w3(0), _w3(1)], "Nk": [_w3(2), _w3(3)], "Qb": _w3(4), "LkT": _w3(5), "MbT": _w3(6), "MkT": _w3(7),
             "Xb": Xb2, "SAb": SAb2, "banks": (3, 4, 5), "n": "_B"},
        ]
        SETB_KEYS = [k + "_B" for k in ("Ak0", "Ak1", "Nk0", "Nk1", "Qb", "LkT", "MbT", "MkT")]
        ALLT4 = ["T4_0", "T4_1", "T4_2", "T4_3"]
        ALLSF = ["Sf0", "Sf1", "Sf2", "Sf3"]
        ALLSB = ["Sb0", "Sb1", "Sb2", "Sb3"]
        Sf = T("Sf", [128, 8, 64]); Sb = T("Sb", [128, 8, 64], BF16)
        gC = T("gC", [128, 8]); tmpb = T("tmpb", [128, 512]); tmpc = T("tmpc", [128, 512])
        sgb = T("sgb", [128, 512], BF16)
        m1 = TT[5][:].rearrange("p a b -> p (a b)")
        pprev = [T("pprev%d" % i, [128, 40]) for i in range(2)]
        small = T("small", [128, 64])
        dg = [T("dg%d" % i, [128, 128], BF16) for i in range(4)]
        wld = xt
        pb = [st.enter_context(nc.psum_tensor("pb%d" % i, [128, 512], F32)) for i in range(7)]
        ptb = st.enter_context(nc.psum_tensor("ptb", [128, 1024], BF16))

        cnt = {"d": 0, "e": 0}
        import os
        STOP = float(os.environ.get("MK_STOP", "1000"))

        def stage(k):
            if k > STOP:
                P.dead = True
        P.annotate = bool(os.environ.get("MK_ANN"))

        def ph(name):
            P.phase = name

        def dma(out, in_, reads, writes, q="sp"):
            cnt[q] = cnt.get(q, 0) + 1
            key = "%s%d" % (q, cnt[q] % (16 if q == "sp" else 8))
            if q == "act":
                return op("act", lambda e: e.dma_start(out=out, in_=in_), reads, writes, dma=key)
            return op(q, lambda e: e.dma_start(out=out, in_=in_), reads, writes, dma=key)

        def mm(out, lhsT, rhs, start, stop, reads, writes):
            b0 = lhsT.base_partition()
            n0 = lhsT.shape[0]
            tag = "lo" if b0 + n0 <= 64 else ("hi" if b0 >= 64 else None)
            op("pe", lambda e: e.matmul(out, lhsT=lhsT, rhs=rhs, start=start, stop=stop), reads, writes, tag=tag)

        def act(out, in_, func, reads, writes, bias=None, scale=None, accum=None):
            kw = {}
            if bias is not None: kw["bias"] = bias
            if scale is not None: kw["scale"] = scale
            if accum is not None: kw["accum_out"] = accum
            op("act", lambda e: e.activation(out=out, in_=in_, func=func, **kw), reads, writes)

        def tt(eng, out, in0, in1, o, reads, writes):
            g = {"dve": "dve", "pool": "pool"}[eng]
            op(g, lambda e: e.tensor_tensor(out=out, in0=in0, in1=in1, op=o), reads, writes)

        def ts(eng, out, in0, s1, s2, o0, o1, reads, writes):
            if s2 is None:
                op(eng, lambda e: e.tensor_scalar(out=out, in0=in0, scalar1=s1, scalar2=None, op0=o0), reads, writes)
            else:
                op(eng, lambda e: e.tensor_scalar(out=out, in0=in0, scalar1=s1, scalar2=s2, op0=o0, op1=o1), reads, writes)

        def stt(eng, out, in0, sc, in1, o0, o1, reads, writes):
            op(eng, lambda e: e.scalar_tensor_tensor(out=out, in0=in0, scalar=sc, in1=in1, op0=o0, op1=o1), reads, writes)

        def cp(eng, out, in_, reads, writes):
            if eng == "act":
                act(out, in_, AF.Copy, reads, writes)
            else:
                op(eng, lambda e: e.tensor_copy(out=out, in_=in_), reads, writes)

        def rsq(out, in_, eps, reads, wkey):
            act(out, in_, AF.Sqrt, reads, [wkey], bias=eps)
            op("dve", lambda e: e.reciprocal(out=out, in_=out), [wkey], [wkey])

        def bc(ap, shape):
            return ap.to_broadcast(shape)

        C = lambda o, n=128: cst[:, o:o + n]

        dma(cst[:], consts_d, [], ["cst"])
        dma(col[:], cols_d, [], ["col"])
        dma(w2e[:], w2ext_d, [], ["w2e"])
        dma(wld[0][64:128, 0:1024], a2_d, [], ["xt0"])
        cp("dve", a2b[64:128, :], wld[0][64:128, 0:1024], ["xt0"], ["a2b"])
        cp("dve", idb[:], C(C_ID), ["cst"], ["idb"])
        cp("dve", bob[:], C(C_BO), ["cst"], ["bob"])
        cp("dve", bmb[:], C(C_BM), ["cst"], ["bmb"])
        ts("dve", col[:, O_OMM:O_OMM + 33], col[:, O_MU:O_MU + 33], -1.0, 1.0, ALU.mult, ALU.add, ["col"], ["col"])
        ts("dve", col[:, O_OMKA:O_OMKA + 8], col[:, O_KA:O_KA + 8], -1.0, 1.0, ALU.mult, ALU.add, ["col"], ["col"])
        op("pool", lambda e: e.memset(twl[64:65, :], 1.0), [], ["twl"])
        op("pool", lambda e: e.memset(Sf[:], 0.0), [], ALLSF)
        op("pool", lambda e: e.memset(Sb[:], 0.0), [], ALLSB)
        op("pool", lambda e: e.memset(pprev[0][:], 0.0), [], ["pprev0"])
        op("pool", lambda e: e.memset(pprev[1][:], 0.0), [], ["pprev1"])
        op("pool", lambda e: e.memset(uex[:], 0.0), [], ["uex"])

        stage(1)
        ph("prologue")
        def prologue_gen(part):
            ph("prologue")
            pieces = []
            for c0 in range(0, NIN, 1024):
                for rc in range(8):
                    pieces.append((w_in, wi_s, rc, c0, min(1024, NIN - c0), "scr_i%d" % (c0 // 1024)))
            for rc in range(8):
                pieces.append((w_br, wbr_s, rc, 0, 1024, "scr_o"))
            for rc in range(4):
                pieces.append((w_bc, wbc_s, rc, 0, 1024, "scr_o"))
            for rc in range(8):
                pieces.append((w_out, wo_s, rc, 0, 1024, "scr_o"))
            fl = lambda t: t[:].rearrange("p a b -> p (a b)")
            orv = lambda i: orT[:, 2 * i:2 * i + 2, :].rearrange("p a b -> p (a b)")
            if part == 1:
                pieces = pieces[0:48]
                sf32 = [(fl(TT[i]), "T%d" % i) for i in range(8)]
                sbf = [(fl(rt_), "rt_0"), (fl(at_), "at_0"), (fl(bt_), "bt_0"), (fl(kt_), "kt_0"), (fl(bh_), "bh_"), (fl(kh_), "kh_"),
                       (Vt[:, :], "Vt_0"), (Bt[:, :], "Bt_0"), (Kt[:, :], "Kt_0")]
                DEPTH = 6
            else:
                pieces = pieces[48:]
                sf32 = [(fl(TT[i]), "T%d" % i) for i in (3, 4, 6, 7)]
                sbf = [(orv(0), "orT"), (orv(1), "orT"), (orv(2), "orT"), (orv(3), "orT"),
                       (fl(kh_), "kh_"), (Vt[:, :], "Vt_0"), (Bt[:, :], "Bt_0"), (Kt[:, :], "Kt_0")]
                DEPTH = 3
            NB = len(sf32)
            engs = ["dve", "act"]
            npc = len(pieces)
            for i in range(npc + DEPTH):
                if i < npc:
                    src, dst, rc, c0, n, skey = pieces[i]
                    bf_, kf_ = sf32[i % NB]
                    dma(bf_[:, 0:n], src[rc * 128:(rc + 1) * 128, c0:c0 + n], [], [kf_],
                        q=("act" if (os.environ.get("MK_ACTQ") and i % 2 == 1) else "sp"))
                j = i - DEPTH
                if j >= 0:
                    src, dst, rc, c0, n, skey = pieces[j]
                    bf_, kf_ = sf32[j % NB]
                    bb_, kb_ = sbf[j % len(sbf)]
                    cp(engs[j % 2], bb_[:, 0:n], bf_[:, 0:n], [kf_], [kb_])
                    dma(dst[rc * 128:(rc + 1) * 128, c0:c0 + n], bb_[:, 0:n], [kb_], [skey], q="pool")
                yield

        stage(2)
        wi_v = wi_s.rearrange("(dc p) n -> p dc n", p=128)
        wbr_v = wbr_s.rearrange("(dc p) n -> p dc n", p=128)
        wbc_v = wbc_s.rearrange("(dc p) n -> p dc n", p=128)
        wo_v = wo_s.rearrange("(dc p) n -> p dc n", p=128)
        wslot = {"i": 0}

        def wload(view, ndc, c0, n, skeys):
            s = wslot["i"] % 4
            wslot["i"] += 1
            dma(wb[s][:, 0:ndc, 0:n], view[:, :, c0:c0 + n], skeys, ["wb%d" % s])
            return s

        def ikeys(c0, n):
            return ["scr_i%d" % b for b in range(c0 // 1024, (c0 + n - 1) // 1024 + 1)]

        def rmsnorm_tile(xtile, key, npart, dst_cols, want_h_out=None):
            ph("rmsnorm")
            act(hb[0:npart, :], xtile[0:npart, :], AF.Square, [key], ["hb", "small"], accum=small[0:npart, 0:1])
            ts("dve", small[0:npart, 1:2], small[0:npart, 0:1], 1.0 / D, 1e-6, ALU.mult, ALU.add, ["small"], ["small"])
            rsq(small[0:npart, 2:3], small[0:npart, 1:2], 0.0, ["small"], "small")
            ts("dve", hb[0:npart, :], xtile[0:npart, :], small[0:npart, 2:3], None, ALU.mult, None, [key, "small"], ["hb"])
            if want_h_out is not None:
                want_h_out()
            for dc in range(8):
                op("pe", lambda e, dc=dc: e.transpose(ptb[:, dc * 128:dc * 128 + npart], hb[0:npart, dc * 128:(dc + 1) * 128], idb[0:npart, 0:npart]),
                   ["hb", "idb"], ["ptb"])
            for dc in range(8):
                act(hT[:, dc, dst_cols[0]:dst_cols[1]], ptb[:, dc * 128:dc * 128 + npart], AF.Copy, ["ptb", "col"], ["hT"],
                    scale=col[:, O_GPRE + dc:O_GPRE + dc + 1])

        def project(j, wslot_i, jj, NT, bank):
            for dc in range(8):
                mm(pb[bank][:, 0:NT], wb[wslot_i][:, dc, jj * 128:(jj + 1) * 128], hT[:, dc, 0:NT], dc == 0, dc == 7,
                   ["wb%d" % wslot_i, "hT"], ["pb%d" % bank])

        def shiftmix(j, bank, NT, dst, dkey, sample, pp_old, pp_new):
            p = pb[bank]
            mu = col[:, O_MU + j:O_MU + j + 1]
            omm = col[:, O_OMM + j:O_OMM + j + 1]
            bk = "pb%d" % bank
            if sample:
                act(tmpb[:, 0:NS], p[:, 0:NS], AF.Copy, [bk, "col"], ["tmpb"], scale=omm)
                stt("dve", dst, p[:, NS:2 * NS], mu, tmpb[:, 0:NS], ALU.mult, ALU.add, [bk, "tmpb", "col"], [dkey])
            else:
                act(tmpb[:, 0:NT], p[:, 0:NT], AF.Copy, [bk, "col"], ["tmpb"], scale=omm)
                act(pprev[pp_new][:, j:j + 1], p[:, NT - 1:NT], AF.Copy, [bk], ["pprev%d" % pp_new])
                stt("dve", dst[:, 1:NT], p[:, 0:NT - 1], mu, tmpb[:, 1:NT], ALU.mult, ALU.add, [bk, "tmpb", "col"], [dkey])
                stt("dve", dst[:, 0:1], pprev[pp_old][:, j:j + 1], mu, tmpb[:, 0:1], ALU.mult, ALU.add,
                    ["pprev%d" % pp_old, "tmpb", "col"], [dkey])

        def proj_phase(NT, sample, pp_old, pp_new):
            ph("proj")
            nb = 0
            for g0 in range(0, 45, 4):
                ng = min(4, 45 - g0)
                s = wload(wi_v, 8, g0 * 128, ng * 128, ikeys(g0 * 128, ng * 128))
                for jj in range(ng):
                    j = g0 + jj
                    bank = nb % 2
                    nb += 1
                    bk = "pb%d" % bank
                    project(j, s, jj, NT if not sample else 2 * NS, bank)
                    W = NS if sample else NT
                    if j < 8:
                        shiftmix(j, bank, NT, rS[:, j, 0:W], "rS", sample, pp_old, pp_new)
                    elif j < 16:
                        shiftmix(j, bank, NT, kS[:, j - 8, 0:W], "kS", sample, pp_old, pp_new)
                    elif j < 24:
                        shiftmix(j, bank, NT, vS[:, j - 16, 0:W], "vS", sample, pp_old, pp_new)
                    elif j < 32:
                        shiftmix(j, bank, NT, tmpc[:, 0:W], "tmpc", sample, pp_old, pp_new)
                        act(zrS[:, j - 24, 0:W], tmpc[:, 0:W], AF.Silu, ["tmpc"], ["zrS"])
                    elif j == 32:
                        shiftmix(j, bank, NT, tmpc[:, 0:W], "tmpc", sample, pp_old, pp_new)
                        act(twl[0:64, 0:W], tmpc[0:64, 0:W], AF.Tanh, ["tmpc"], ["twl"])
                        cp("pool", alb[64:128, 0:W], tmpc[64:128, 0:W], ["tmpc"], ["alb"])
                    elif j < 37:
                        c = j - 33
                        act(ua[:, c, 0:W], pb[bank][:, 0:W], AF.Identity, [bk, "col"], ["ua"],
                            bias=col[:, O_GLUB + c:O_GLUB + c + 1])
                    elif j < 41:
                        c = j - 37
                        act(tmpc[:, 0:W], pb[bank][:, 0:W], AF.Sigmoid, [bk, "col"], ["tmpc"],
                            bias=col[:, O_GLUB + 4 + c:O_GLUB + 5 + c])
                        if sample:
                            tt("dve", uex[:, c, 0:NS * 31].rearrange("p (n w) -> p n w", w=31)[:, :, 30], ua[:, c, 0:W], tmpc[:, 0:W],
                               ALU.mult, ["ua", "tmpc"], ["uex"])
                        else:
                            tt("dve", uex[:, c, 30:30 + W], ua[:, c, 0:W], tmpc[:, 0:W], ALU.mult, ["ua", "tmpc"], ["uex"])
                    else:
                        c = j - 41
                        act(szc[:, c, 0:W], pb[bank][:, 0:W], AF.Silu, [bk], ["szc"])
                    yield

        def ln_conv_out(W, cf, ck):
            ph("lnconv")
            for c in range(4):
                mm(pb[2][:, 0:W], C(C_AM), cf[c], c == 0, c == 3, ["cst", ck[c]], ["pb2"])
            for c in range(4):
                tt("dve", cf[c], cf[c], pb[2][:, 0:W], ALU.subtract, [ck[c], "pb2"], [ck[c]])
            for c in range(4):
                tt("pool", ua[:, c, 0:W], cf[c], cf[c], ALU.mult, [ck[c]], ["ua"])
            for c in range(4):
                mm(pb[3][:, 0:W], C(C_AM), ua[:, c, 0:W], c == 0, c == 3, ["cst", "ua"], ["pb3"])
            rsq(tmpc[:, 0:W], pb[3][:, 0:W], 1e-5, ["pb3"], "tmpc")
            for c in range(4):
                tt("dve", cf[c], cf[c], tmpc[:, 0:W], ALU.mult, [ck[c], "tmpc"], [ck[c]])
                act(ua[:, c, 0:W], cf[c], AF.Silu, [ck[c], "col"], ["ua"],
                    bias=col[:, O_LNB + c:O_LNB + c + 1], scale=col[:, O_LNG + c:O_LNG + c + 1])
                tt("pool", ocT[:, c, 0:W], ua[:, c, 0:W], szc[:, c, 0:W], ALU.mult, ["ua", "szc"], ["kS"])

        def prep_gen(cs, W, sample, PSp):
            T0, T1, T2, T3, T4, T5, T6, T7 = TT
            sl = slice(cs, cs + W)
            sfx = PSp["sfx"]
            bonT, gCt = PSp["bon"], PSp["gC"]
            kbon, kgc = "bon" + sfx, "gC" + sfx
            ph("prep")
            sh = [128, 8, W]
            colb = lambda o: bc(col[:, o:o + 8].unsqueeze(2), sh)
            p6 = pb[6]
            p6v = p6[:].rearrange("p (a b) -> p a b", b=128)[:, :, 0:W]
            T0v = T0[:].rearrange("p a b -> p (a b)")
            tt("dve", T5[:, :, 0:W], kS[:, :, sl], colb(O_KK), ALU.mult, ["kS", "col"], ["T5"])
            tt("pool", bh_[:, :, 0:W], T5[:, :, 0:W], T5[:, :, 0:W], ALU.mult, ["T5"], ["bh_"])
            yield
            for hf in range(2):
                mm(p6[0:W, :], twl[0:65, sl], w2e[0:65, hf * 512:(hf + 1) * 512], True, True, ["twl", "w2e"], ["pb6"])
                yield
                act(T0v[0:W, hf * 512:(hf + 1) * 512], p6[0:W, :], AF.Sigmoid, ["pb6"], ["T0"])
                yield
            tri = C(C_TRI) if not sample else cst[0:W, C_NI:C_NI + W]
            tre = C(C_TRE) if not sample else cst[0:W, C_NI + 64:C_NI + 64 + W]
            for hf in range(2):
                hs = slice(hf * 4, hf * 4 + 4)
                for hq in range(4):
                    hh = hf * 4 + hq
                    mm(p6[:, hq * 128:hq * 128 + W], T0v[0:W, hh * 128:(hh + 1) * 128], tri[0:W, 0:W], True, True, ["T0", "cst"], ["pb6"])
                yield
                act(T1[:, hs, 0:W], p6v[:, 0:4, :], AF.Exp, ["pb6"], ["T1"])
                act(T2[:, hs, 0:W], p6v[:, 0:4, :], AF.Exp, ["pb6"], ["T2"], scale=-1.0)
                yield
            cp("pool", gCt[:, :], T1[:, :, W - 1], ["T1"], [kgc])
            for hf in range(2):
                for hq in range(4):
                    hh = hf * 4 + hq
                    mm(p6[:, hq * 128:hq * 128 + W], a2b[64:128, hh * 128:(hh + 1) * 128], alb[64:128, sl], True, True, ["a2b", "alb"], ["pb6"])
                yield
                for hq in range(4):
                    hh = hf * 4 + hq
                    act(T4[:, hh, 0:W], p6[:, hq * 128:hq * 128 + W], AF.Sigmoid, ["pb6", "col"], ALLT4,
                        bias=col[:, O_A0 + hh:O_A0 + hh + 1])
                yield
            for hf in range(2):
                hs = slice(hf * 4, hf * 4 + 4)
                for hq in range(4):
                    hh = hf * 4 + hq
                    mm(p6[:, hq * 128:hq * 128 + W], bob[:], bh_[:, hh, 0:W], True, True, ["bob", "bh_"], ["pb6"])
                yield
                rsq(T7[:, hs, 0:W], p6v[:, 0:4, :], 1e-12, ["pb6"], "T7")
                yield
            stt("dve", T6[:, :, 0:W], T5[:, :, 0:W], -1.0, T7[:, :, 0:W], ALU.mult, ALU.mult, ["T5", "T7"], ["T6"])
            yield
            stt("dve", T7[:, :, 0:W], T6[:, :, 0:W], -1.0, T4[:, :, 0:W], ALU.mult, ALU.mult, ["T6"] + ALLT4, ["T7"])
            yield
            tt("pool", T0[:, :, 0:W], T4[:, :, 0:W], colb(O_KA), ALU.mult, ALLT4 + ["col", "T0"], ["T0"])
            tt("pool", T0[:, :, 0:W], T0[:, :, 0:W], colb(O_OMKA), ALU.add, ["T0", "col"], ["T0"])
            yield
            tt("dve", T5[:, :, 0:W], kS[:, :, sl], T0[:, :, 0:W], ALU.mult, ["kS", "T0"], ["T5"])
            yield
            tt("pool", T0[:, :, 0:W], rS[:, :, sl], T5[:, :, 0:W], ALU.mult, ["rS", "T5"], ["T0"])
            tt("pool", kh_[:, :, 0:W], T0[:, :, 0:W], colb(O_RK), ALU.mult, ["T0", "col"], ["kh_"])
            yield
            for hf in range(2):
                hs = slice(hf * 4, hf * 4 + 4)
                for hq in range(4):
                    hh = hf * 4 + hq
                    mm(p6[:, hq * 128:hq * 128 + W], bob[:], kh_[:, hh, 0:W], True, True, ["bob", "kh_"], ["pb6"])
                yield
                tt("dve", bonT[:, hs, 0:W], p6v[:, 0:4, :], vS[:, hs, sl], ALU.mult, ["pb6", "vS"], [kbon])
                yield
            if sample:
                return
            ph("mults")
            EG, EnG, EGe, k2, aa, bb = T1, T2, T3, T5, T6, T7
            rt, at, bt, kt = PSp["rt"], PSp["at"], PSp["bt"], PSp["kt"]
            krt, kat, kbt, kkt = "rt" + sfx, "at" + sfx, "bt" + sfx, "kt" + sfx
            tt("dve", rt, rS[:, :, sl], EG[:], ALU.mult, ["rS", "T1"], [krt])
            tt("pool", at[:, :, 1:128], aa[:, :, 1:128], EG[:, :, 0:127], ALU.mult, ["T6", "T1"], [kat])
            cp("pool", at[:, :, 0:1], aa[:, :, 0:1], ["T6"], [kat])
            yield
            tt("dve", bt, bb[:], EnG[:], ALU.mult, ["T7", "T2"], [kbt])
            tt("pool", kt, k2[:], EnG[:], ALU.mult, ["T5", "T2"], [kkt])
            yield
            tt("dve", EGe[:], EnG[:], bc(EG[:, :, 127:128], [128, 8, 128]), ALU.mult, ["T2", "T1", kat], ["T3"])
            yield
            tt("pool", bh_[:], bb[:], EGe[:], ALU.mult, ["T7", "T3"], ["bh_"])
            tt("dve", kh_[:], k2[:], EGe[:], ALU.mult, ["T5", "T3"], ["kh_"])
            yield
            ph("transp")
            for src, skey, dst, dkey in ((vS, "vS", PSp["Vt"], "Vt" + sfx), (bh_, "bh_", PSp["Bt"], "Bt" + sfx), (kh_, "kh_", PSp["Kt"], "Kt" + sfx)):
                for hh in range(8):
                    srcap = src[:, hh, sl] if src is vS else src[:, hh, :]
                    op("pe", lambda e, srcap=srcap, hh=hh: e.transpose(ptb[:, hh * 128:(hh + 1) * 128], srcap, idb[:]),
                       [skey, "idb"], ["ptb"])
                yield
                cp("act", dst[:, 0:512], ptb[:, 0:512], ["ptb"], [dkey])
                cp("dve", dst[:, 512:1024], ptb[:, 512:1024], ["ptb"], [dkey])
                yield

        def gn_gen(yT, ykey, cs, W, G, Gk, bonT, kbon):
            ph("gn")
            G1, G2, G3 = G
            k1, k2_, k3 = Gk
            sl = slice(cs, cs + W)
            sh = [128, 8, W]
            colb = lambda o: bc(col[:, o:o + 8].unsqueeze(2), sh)
            p6 = pb[6]
            p6v = p6[:].rearrange("p (a b) -> p a b", b=128)[:, :, 0:W]
            for hf in range(2):
                hs = slice(hf * 4, hf * 4 + 4)
                for hq in range(4):
                    hh = hf * 4 + hq
                    mm(p6[:, hq * 128:hq * 128 + W], C(C_BM), yT[:, hh, 0:W], True, True, ["cst"] + ykey, ["pb6"])
                yield
                tt("dve", G1[:, hs, 0:W], yT[:, hs, 0:W], p6v[:, 0:4, :], ALU.subtract, ykey + ["pb6"], [k1])
                yield
            hbv = hb[:, :].rearrange("p (a b) -> p a b", b=128)
            tt("pool", hbv[:, :, 0:W], G1[:, :, 0:W], G1[:, :, 0:W], ALU.mult, [k1], ["hb"])
            yield
            for hf in range(2):
                hs = slice(hf * 4, hf * 4 + 4)
                for hq in range(4):
                    hh = hf * 4 + hq
                    mm(p6[:, hq * 128:hq * 128 + W], bmb[:], hbv[:, hh, 0:W], True, True, ["bmb", "hb"], ["pb6"])
                yield
                rsq(G3[:, hs, 0:W], p6v[:, 0:4, :], 64e-5, ["pb6"], k3)
                yield
            tt("dve", G1[:, :, 0:W], G1[:, :, 0:W], G3[:, :, 0:W], ALU.mult, [k1, k3], [k1])
            yield
            tt("pool", G1[:, :, 0:W], G1[:, :, 0:W], colb(O_GNG), ALU.mult, [k1, "col"], [k1])
            tt("pool", G1[:, :, 0:W], G1[:, :, 0:W], colb(O_GNB), ALU.add, [k1, "col"], [k1])
            yield
            tt("dve", G1[:, :, 0:W], G1[:, :, 0:W], bonT[:, :, 0:W], ALU.add, [k1, kbon], [k1])
            yield
            tt("dve", orT[:, :, sl], G1[:, :, 0:W], zrS[:, :, sl], ALU.mult, [k1, "zrS"], ["orT"])
            yield

        def run_all(gens):
            gens = list(gens)
            while gens:
                for gq in list(gens):
                    try:
                        next(gq)
                    except StopIteration:
                        gens.remove(gq)

        gC2 = T("gC2", [128, 8])
        _fl = lambda t, i: t[:, 2 * i:2 * i + 2, :].rearrange("p a b -> p (a b)")
        _v8 = lambda ap: ap.rearrange("p (a b) -> p a b", b=128)
        PS = [
            {"sfx": "_0", "rt": rt_[:], "at": at_[:], "bt": bt_[:], "kt": kt_[:], "Vt": Vt, "Bt": Bt, "Kt": Kt, "bon": bon, "gC": gC},
            {"sfx": "_1", "rt": _v8(_fl(wb[0], 0)), "at": _v8(_fl(wb[0], 1)), "bt": _v8(_fl(wb[0], 2)), "kt": _v8(_fl(wb[0], 3)),
             "Vt": _fl(wb[1], 0), "Bt": _fl(wb[1], 1), "Kt": _fl(wb[1], 2), "bon": _v8(_fl(wb[1], 3)), "gC": gC2},
        ]
        PS1_KEYS = [k + "_1" for k in ("rt", "at", "bt", "kt", "Vt", "Bt", "Kt", "bon")]

        def scan_group(g, S, PSp, yT):
            Ak_, Nk_, Qb_, LkT_, MbT_, MkT_, Xb_, SAb_ = S["Ak"], S["Nk"], S["Qb"], S["LkT"], S["MbT"], S["MkT"], S["Xb"], S["SAb"]
            b0, b1, b2 = S["banks"]
            kb = lambda i: "pb%d" % i
            n = S["n"]
            sfx = PSp["sfx"]
            rt_, at_, bt_, kt_, Vt, Bt, Kt, gC = PSp["rt"], PSp["at"], PSp["bt"], PSp["kt"], PSp["Vt"], PSp["Bt"], PSp["Kt"], PSp["gC"]
            K = lambda nm: nm + n
            heads = [4 * g + x for x in (0, 2, 1, 3)]
            SbK = "Sb%d" % g; SfK = "Sf%d" % g; yK = "yT%d" % g

            def hp(h):
                hl, hh = h % 2, h // 2
                return slice(hl * 64, hl * 64 + 64), hh
            v4 = lambda p: p[:].rearrange("p (a b) -> p a b", b=128)
            mk = lambda o: bc(cst[:, o:o + 128].unsqueeze(1), [128, 4, 128])
            ph("scores")
            plan = [(b0, "at_", "bt_", Ak_[0], K("Ak0"), C_SL), (b1, "bt_", "at_", Nk_[0], K("Nk0"), C_SU),
                    (b2, "kt_", "at_", LkT_, K("LkT"), C_SU), (b0, "bt_", "rt_", MbT_, K("MbT"), C_UI),
                    (b1, "kt_", "rt_", MkT_, K("MkT"), C_UI)]
            tl = {"at_": at_, "bt_": bt_, "kt_": kt_, "rt_": rt_}
            kn = {"at_": "at" + sfx, "bt_": "bt" + sfx, "kt_": "kt" + sfx, "rt_": "rt" + sfx}
            first_lo = (n == "_A")
            for rnd in (plan[0:3], plan[3:5]):
                for tagsel in ((0, 1) if first_lo else (1, 0)):
                    for (bk, ln, rn, dst, dk, msk) in rnd:
                        for hi, h in enumerate(heads):
                            if (h % 2) != tagsel:
                                continue
                            pr, hh = hp(h)
                            mm(pb[bk][:, hi * 128:(hi + 1) * 128], tl[ln][pr, hh, :], tl[rn][pr, hh, :], True, True, [kn[ln], kn[rn]], [kb(bk)])
                yield
                for (bk, ln, rn, dst, dk, msk) in rnd:
                    tt("dve", dst[:], v4(pb[bk]), mk(msk), ALU.mult, [kb(bk), "cst"], [dk])
                    yield
            ph("doubling")
            tt("pool", Qb_[:], Nk_[0][:], mk(C_ID), ALU.add, [K("Nk0"), "cst"], [K("Qb")])
            yield
            mm(pb[b2][:, :], idb[:], Qb_.rearrange("p a b -> p (a b)"), True, True,
               ["idb", K("Qb")], [kb(b2)])
            yield
            cur = 0
            for lvl in range(6):
                nx = 1 - cur
                for hi in range(4):
                    mm(pb[b0][:, hi * 128:(hi + 1) * 128], Nk_[cur][:, hi, :], Ak_[cur][:, hi, :], True, True,
                       [K("Nk%d" % cur), K("Ak%d" % cur)], [kb(b0)])
                if lvl < 5:
                    for hi in range(4):
                        mm(pb[b1][:, hi * 128:(hi + 1) * 128], Ak_[cur][:, hi, :], Nk_[cur][:, hi, :], True, True,
                           [K("Nk%d" % cur), K("Ak%d" % cur)], [kb(b1)])
                yield
                cp("act", Ak_[nx][:], v4(pb[b0]), [kb(b0)], [K("Ak%d" % nx)])
                if lvl < 5:
                    cp("dve", Nk_[nx][:], v4(pb[b1]), [kb(b1)], [K("Nk%d" % nx)])
                yield
                for hi in range(4):
                    mm(pb[b2][:, hi * 128:(hi + 1) * 128], Ak_[nx][:, hi, :], Qb_[:, hi, :], False, True,
                       [K("Ak%d" % nx), K("Qb")], [kb(b2)])
                yield
                if lvl % 2 == 0 or os.environ.get("MK_QACT"):
                    cp("act", Qb_[:], v4(pb[b2]), [kb(b2)], [K("Qb")])
                else:
                    cp("dve", Qb_[:], v4(pb[b2]), [kb(b2)], [K("Qb")])
                yield
                cur = nx
            ph("seq")
            for hi, h in enumerate(heads):
                pr, hh = hp(h)
                o = pb[b0][:, hi * 64:(hi + 1) * 64]
                mm(o, at_[pr, hh, :], Sb[pr, hh, :], True, False, ["at" + sfx, SbK], [kb(b0)])
                mm(o, LkT_[:, hi, :], Vt[:, h * 64:(h + 1) * 64], False, True, [K("LkT"), "Vt" + sfx], [kb(b0)])
            yield
            cp("act", Xb_[:], pb[b0][:, 0:256], [kb(b0)], [K("Xb")])
            yield
            for hi, h in enumerate(heads):
                mm(pb[b1][:, hi * 64:(hi + 1) * 64], Qb_[:, hi, :], Xb_[:, hi * 64:(hi + 1) * 64], True, True, [K("Qb"), K("Xb")], [kb(b1)])
            yield
            cp("dve", SAb_[:], pb[b1][:, 0:256], [kb(b1)], [K("SAb")])
            yield
            for hi, h in enumerate(heads):
                pr, hh = hp(h)
                o = pb[b2][pr, (hh - 2 * g) * 128:(hh - 2 * g) * 128 + 128]
                mm(o, Sb[pr, hh, :], rt_[pr, hh, :], True, False, [SbK, "rt" + sfx], [kb(b2)])
                mm(o, SAb_[:, hi * 64:(hi + 1) * 64], MbT_[:, hi, :], False, False, [K("SAb"), K("MbT")], [kb(b2)])
                mm(o, Vt[:, h * 64:(h + 1) * 64], MkT_[:, hi, :], False, True, ["Vt" + sfx, K("MkT")], [kb(b2)])
            yield
            cp("act", yT[:, 2 * g:2 * g + 2, :], pb[b2][:, 0:256].rearrange("p (a b) -> p a b", b=128), [kb(b2)], [yK])
            for hi, h in enumerate(heads):
                pr, hh = hp(h)
                o = pb[b0][pr, (hh - 2 * g) * 64:(hh - 2 * g) * 64 + 64]
                mm(o, Bt[:, h * 64:(h + 1) * 64], SAb_[:, hi * 64:(hi + 1) * 64], True, False, ["Bt" + sfx, K("SAb")], [kb(b0)])
                mm(o, Kt[:, h * 64:(h + 1) * 64], Vt[:, h * 64:(h + 1) * 64], False, True, ["Kt" + sfx, "Vt" + sfx], [kb(b0)])
            yield
            gs = slice(2 * g, 2 * g + 2)
            tt("dve", Sf[:, gs, :], Sf[:, gs, :], bc(gC[:, gs].unsqueeze(2), [128, 2, 64]), ALU.mult, [SfK, "gC" + sfx], [SfK])
            tt("dve", Sf[:, gs, :], Sf[:, gs, :], pb[b0][:, 0:128].rearrange("p (a b) -> p a b", b=64), ALU.add, [SfK, kb(b0)], [SfK])
            yield
            cp("act", Sb[:, gs, :], Sf[:, gs, :], [SfK], [SbK])
            yield


        def tail(NT, xsrc_tiles, ydst_tiles, nrows):
            ph("tail")
            for q in range(2):
                sgr = wload(wi_v, 8, (45 + 4 * q) * 128, 512, ikeys((45 + 4 * q) * 128, 512))
                sbr = wload(wbr_v, 8, q * 512, 512, ["scr_o"])
                sgc = wload(wi_v, 8, (53 + 4 * q) * 128, 512, ikeys((53 + 4 * q) * 128, 512))
                sbc = wload(wbc_v, 4, q * 512, 512, ["scr_o"])
                for jj in range(4):
                    j = q * 4 + jj
                    od = j % 2
                    bA, bB, bC, bD = (0, 1, 2, 3) if od == 0 else (4, 5, 6, 3)
                    sg1, sg1k = (sgb, "sgb")
                    sg2, sg2k = (alb, "alb")
                    m1_ = TT[5 + od][:].rearrange("p a b -> p (a b)")
                    m1k = "T%d" % (5 + od)
                    t2_, t2k = ((tmpc, "tmpc"), (tmpb, "tmpb"))[od]
                    project(45 + j, sgr, jj, NT, bA)
                    act(sg1[:, 0:NT], pb[bA][:, 0:NT], AF.Sigmoid, ["pb%d" % bA], [sg1k])
                    for fc in range(8):
                        mm(pb[bB][:, 0:NT], wb[sbr][:, fc, jj * 128:(jj + 1) * 128], orT[:, fc, 0:NT], fc == 0, fc == 7,
                           ["wb%d" % sbr, "orT"], ["pb%d" % bB])
                    tt("dve", m1_[:, 0:NT], pb[bB][:, 0:NT], sg1[:, 0:NT], ALU.mult, ["pb%d" % bB, sg1k], [m1k])
                    project(53 + j, sgc, jj, NT, bC)
                    act(sg2[:, 0:NT], pb[bC][:, 0:NT], AF.Sigmoid, ["pb%d" % bC], [sg2k])
                    for fc in range(4):
                        mm(pb[bD][:, 0:NT], wb[sbc][:, fc, jj * 128:(jj + 1) * 128], ocT[:, fc, 0:NT], fc == 0, fc == 3,
                           ["wb%d" % sbc, "kS"], ["pb%d" % bD])
                    tt("dve", t2_[:, 0:NT], pb[bD][:, 0:NT], sg2[:, 0:NT], ALU.mult, ["pb%d" % bD, sg2k], [t2k])
                    tt("pool", mT[:, j, 0:NT], m1_[:, 0:NT], t2_[:, 0:NT], ALU.add, [m1k, t2k], ["rS"])
            so = [wload(wo_v, 8, 0, 512, ["scr_o"]), wload(wo_v, 8, 512, 512, ["scr_o"])]
            npg = TT[2][:].rearrange("p a b -> p (a b)")
            dma(npg[:, :], npg_d.partition_broadcast(128), [], ["T2"])
            for i, (xsrc, ydst) in enumerate(zip(xsrc_tiles, ydst_tiles)):
                xb = xt[i % 2]
                xk = "xt%d" % (i % 2)
                dma(xb[0:nrows, :], xsrc, [], [xk])
                tsl = slice(i * 128, i * 128 + nrows)
                bks = [(4, 5), (0, 1), (2, 3)][i % 3]
                so_ = 32 + 8 * (i % 3)
                stg_i = (0, 1, 3)[i % 3]
                stg = TT[stg_i][:].rearrange("p a b -> p (a b)")
                sk_ = "T%d" % stg_i
                for hf in range(2):
                    for fc in range(8):
                        mm(pb[bks[hf]][0:nrows, :], mT[:, fc, tsl], wb[so[hf]][:, fc, :], fc == 0, fc == 7,
                           ["rS", "wb%d" % so[hf]], ["pb%d" % bks[hf]])
                for hf in range(2):
                    act(hb[0:nrows, hf * 512:(hf + 1) * 512], pb[bks[hf]][0:nrows, :], AF.Square, ["pb%d" % bks[hf]], ["hb", "small"],
                        accum=small[0:nrows, so_ + hf:so_ + hf + 1])
                tt("dve", small[0:nrows, so_ + 2:so_ + 3], small[0:nrows, so_:so_ + 1], small[0:nrows, so_ + 1:so_ + 2], ALU.add, ["small"], ["small"])
                ts("dve", small[0:nrows, so_ + 3:so_ + 4], small[0:nrows, so_ + 2:so_ + 3], 1.0 / D, 1e-6, ALU.mult, ALU.add, ["small"], ["small"])
                rsq(small[0:nrows, so_ + 4:so_ + 5], small[0:nrows, so_ + 3:so_ + 4], 0.0, ["small"], "small")
                for hf in range(2):
                    hsl = slice(hf * 512, (hf + 1) * 512)
                    stt("dve", stg[0:nrows, hsl], pb[bks[hf]][0:nrows, :], small[0:nrows, so_ + 4:so_ + 5], npg[0:nrows, hsl], ALU.mult, ALU.mult,
                        ["pb%d" % bks[hf], "small", "T2"], [sk_])
                tt("pool", stg[0:nrows, :], stg[0:nrows, :], xb[0:nrows, :], ALU.add, [sk_, xk], [sk_])
                dma(ydst, stg[0:nrows, :], [sk_], [], q="pool")

        def rms_phase(sc):
            t0 = sc * 512
            for i in range(4):
                xb = xt[i % 2]; xk = "xt%d" % (i % 2)
                dma(xb[:], xp[t0 + i * 128:t0 + (i + 1) * 128, :], [], [xk])
                last = (sc == 3 and i == 3)

                def hout(xb=xb, xk=xk):
                    T0v = TT[0][:].rearrange("p a b -> p (a b)")
                    dma(T0v[:, :], npre_d.partition_broadcast(128), [], ["T0"])
                    ts("dve", TT[1][:].rearrange("p a b -> p (a b)"), xb[:], small[:, 2:3], None, ALU.mult, None, [xk, "small"], ["T1"])
                    tt("dve", TT[1][:].rearrange("p a b -> p (a b)"), TT[1][:].rearrange("p a b -> p (a b)"), T0v, ALU.mult, ["T1", "T0"], ["T1"])
                    dma(nsp, TT[1][:].rearrange("p a b -> p (a b)")[127:128, :], ["T1"], [])
                rmsnorm_tile(xb, xk, 128, (i * 128, (i + 1) * 128), hout if last else None)

        rms_phase(0)
        run_all([prologue_gen(1)])
        for sc in range(4):
            t0 = sc * 512
            if sc > 0:
                rms_phase(sc)
            stage(3 if sc == 0 else 11)
            if sc > 0:
                cp("pool", tmpc[:, 0:120].rearrange("p (c w) -> p c w", w=30), uex[:, :, 512:542], ["uex"], ["tmpc"])
                cp("pool", uex[:, :, 0:30], tmpc[:, 0:120].rearrange("p (c w) -> p c w", w=30), ["tmpc"], ["uex"])
            if sc == 0:
                run_all([proj_phase(512, False, sc % 2, (sc + 1) % 2), prologue_gen(2)])
            else:
                run_all([proj_phase(512, False, sc % 2, (sc + 1) % 2)])
            stage(4 if sc == 0 else 11)
            op("pool", lambda e: e.memset(dummy[:, 0:1], 0.0), [], ["wb3", "wb0", "wb1", "xt0", "xt1", "ua", "uaA", "uaB", "dummy", "yT0", "yT1", "yT2", "yT3"] + SETB_KEYS + PS1_KEYS)
            yTp = xt[0][:, :].rearrange("p (a b) -> p a b", b=128)
            Gp = (xt[1][:, :].rearrange("p (a b) -> p a b", b=128),
                  ua[:, 0:2, :].rearrange("p a b -> p (a b)").rearrange("p (a b) -> p a b", b=128),
                  ua[:, 2:4, :].rearrange("p a b -> p (a b)").rearrange("p (a b) -> p a b", b=128))
            Gpk = ("xt1", "uaA", "uaB")

            def chunk_scan(c4):
                PSp = PS[c4 % 2]
                for pair in ((0, 1), (2, 3)):
                    gens = [scan_group(pair[0], SETS[0], PSp, yTp), scan_group(pair[1], SETS[1], PSp, yTp)]
                    while gens:
                        for gq in list(gens):
                            try:
                                next(gq)
                                yield
                            except StopIteration:
                                gens.remove(gq)
                yield from gn_gen(yTp, ["yT0", "yT1", "yT2", "yT3"], c4 * 128, 128, Gp, Gpk, PSp["bon"], "bon" + PSp["sfx"])

            run_all([prep_gen(0, 128, False, PS[0])])
            for c4 in range(4):
                main = chunk_scan(c4)
                side = prep_gen((c4 + 1) * 128, 128, False, PS[(c4 + 1) % 2]) if c4 < 3 else None
                RATIO = int(os.environ.get("MK_RATIO", "4"))
                done = False
                while not done:
                    for _ in range(RATIO):
                        try:
                            next(main)
                        except StopIteration:
                            done = True
                            break
                    if side is not None:
                        try:
                            next(side)
                        except StopIteration:
                            side = None
                if side is not None:
                    run_all([side])
            op("pool", lambda e: e.memset(dummy[:, 1:2], 0.0), [], ["wb3", "wb0", "wb1", "xt0", "xt1", "ua", "uaA", "uaB", "dummy", "yT0", "yT1", "yT2", "yT3"] + SETB_KEYS + PS1_KEYS)
            stage(8 if sc == 0 else 11)
            ph("conv")
            cp("pool", ubf[:], uex[:], ["uex"], ["ubf"])
            for c in range(4):
                for w in range(31):
                    s = (c * 31 + w) % 4
                    if w % 2 == 0:
                        act(dg[s][:], idb[:], AF.Copy, ["idb", "col"], ["dg%d" % s], scale=col[:, O_CW + c * 31 + w:O_CW + c * 31 + w + 1])
                    else:
                        ts("dve", dg[s][:], idb[:], col[:, O_CW + c * 31 + w:O_CW + c * 31 + w + 1], None, ALU.mult, None,
                           ["idb", "col"], ["dg%d" % s])
                    mm(pb[6][:, :], dg[s][:], ubf[:, c, w:w + 512], w == 0, w == 30, ["dg%d" % s, "ubf"], ["pb6"])
                act(TT[c][:].rearrange("p a b -> p (a b)")[:, 0:512], pb[6][:, :], AF.Identity, ["pb6", "col"], ["T%d" % c],
                    bias=col[:, O_CB + c:O_CB + c + 1])
            ln_conv_out(512, [TT[c][:].rearrange("p a b -> p (a b)")[:, 0:512] for c in range(4)], ["T0", "T1", "T2", "T3"])
            if sc == 3:
                for c in range(4):
                    mm(pb[6][0:30, c * 128:(c + 1) * 128], uex[:, c, 512:542], C(C_ID), True, True, ["uex", "cst"], ["pb6"])
                cp("dve", tmpc[0:30, :], pb[6][0:30, :], ["pb6"], ["tmpc"])
                dma(ncp, tmpc[0:30, :], ["tmpc"], [])
            stage(9 if sc == 0 else 11)
            tail(512, [xp[t0 + i * 128:t0 + (i + 1) * 128, :] for i in range(4)],
                 [yp[t0 + i * 128:t0 + (i + 1) * 128, :] for i in range(4)], 128)

        stage(12)
        for h in range(16):
            hl, hh = h % 2, h // 2
            pr = slice(hl * 64, hl * 64 + 64)
            mm(pb[0][pr, hh * 64:(hh + 1) * 64], Sf[pr, hh, :], cst[pr, C_ID + hl * 64:C_ID + hl * 64 + 64], True, True, ALLSF + ["cst"], ["pb0"])
        cp("dve", tmpc[:, :], pb[0][:, :], ["pb0"], ["tmpc"])
        dma(nwp.rearrange("(hh p) j -> p hh j", p=128), tmpc[:, :].rearrange("p (a b) -> p a b", b=64), ["tmpc"], [])

        stage(13)
        ph("sample")
        xb = xt[0]
        dma(xb[0:NS, :], xs, [], ["xt0"])

        def hout_s():
            T0v = TT[0][:].rearrange("p a b -> p (a b)")
            T1v = TT[1][:].rearrange("p a b -> p (a b)")
            dma(T0v[0:NS, :], npre_d.partition_broadcast(NS), [], ["T0"])
            ts("dve", T1v[0:NS, :], xb[0:NS, :], small[0:NS, 2:3], None, ALU.mult, None, ["xt0", "small"], ["T1"])
            tt("dve", T1v[0:NS, :], T1v[0:NS, :], T0v[0:NS, :], ALU.mult, ["T1", "T0"], ["T1"])
            dma(nss, T1v[0:NS, :], ["T1"], [])
        rmsnorm_tile(xb, "xt0", NS, (0, NS), hout_s)
        dma(xt[1][0:NS, :], sshift, [], ["xt1"])
        cp("dve", hb[0:NS, :], xt[1][0:NS, :], ["xt1"], ["hb"])
        for dc in range(8):
            op("pe", lambda e, dc=dc: e.transpose(ptb[:, dc * 128:dc * 128 + NS], hb[0:NS, dc * 128:(dc + 1) * 128], idb[0:NS, 0:NS]),
               ["hb", "idb"], ["ptb"])
        cp("act", hT[:, :, NS:2 * NS], ptb[:, :].rearrange("p (a b) -> p a b", b=128)[:, :, 0:NS], ["ptb"], ["hT"])
        uv = [uex[:, c, 0:NS * 31].rearrange("p (n w) -> p n w", w=31) for c in range(4)]
        for q in range(4):
            dma(xt[1][0:120, 0:512], sconv[q * 120:(q + 1) * 120, :], [], ["xt1"])
            for c in range(4):
                mm(pb[6][:, c * 120:(c + 1) * 120], xt[1][0:120, c * 128:(c + 1) * 128], cst[0:120, C_ID:C_ID + 120], True, True,
                   ["xt1", "cst"], ["pb6"])
            for c in range(4):
                cp("dve", uv[c][:, q * 4:(q + 1) * 4, 0:30], pb[6][:, c * 120:(c + 1) * 120].rearrange("p (n w) -> p n w", w=30),
                   ["pb6"], ["uex"])
        dma(ncs[:, 0:29, :], sconv.rearrange("(n w) c -> n w c", w=30)[:, 1:30, :], [], [])
        run_all([proj_phase(2 * NS, True, 0, 0)])
        stage(14)
        run_all([prep_gen(0, NS, True, PS[0])])
        EG, EnG, EGe, k2, aa, bb = TT[1], TT[2], TT[3], TT[5], TT[6], TT[7]
        SW = [ua[:, i, :].rearrange("p (a b) -> p a b", b=64) for i in range(2)]
        Dxs = [Vt[:, 0:512], tmpb[:, :], Bt[:, 0:512], Kt[:, 0:512], Vt[:, 512:1024]]
        Dxk = ["Vt_0", "tmpb", "Bt_0", "Kt_0", "Vt_0"]
        Dxo = [bob[:], C(C_BO), bob[:], bob[:], bob[:]]
        Dxok = ["bob", "cst", "bob", "bob", "bob"]
        yTs = TT[4]
        i2b = bc(cst[:, C_I2:C_I2 + 64].unsqueeze(1), [128, 8, 64])
        op("pool", lambda e: e.memset(dummy[:, 2:3], 0.0), [], ["ua", "uaA", "uaB", "dummy"])
        for n in range(NS):
            Sw = SW[n % 2]; sk = ("uaA", "uaB")[n % 2]
            dma(Sw, swkv[n].rearrange("(hh p) j -> p hh j", p=128), [], [sk])
            vecs = [(aa, "T6"), (EG, "T1"), (bb, "T7"), (k2, "T5"), (rS, "rS")]
            for vi, (vt_, vk) in enumerate(vecs):
                tt("pool", Dxs[vi].rearrange("p (a b) -> p a b", b=64), i2b, bc(vt_[:, :, n:n + 1], [128, 8, 64]), ALU.mult,
                   ["cst", vk], [Dxk[vi]])
                mm(pb[vi][:, :], Dxo[vi], Dxs[vi], True, True, [Dxok[vi], Dxk[vi]], ["pb%d" % vi])
            v8 = lambda p: p[:].rearrange("p (a b) -> p a b", b=64)
            W3 = TT[0][:, :, 0:64]
            tt("dve", W3, Sw, v8(pb[0]), ALU.mult, [sk, "pb0"], ["T0"])
            op("dve", lambda e: e.tensor_reduce(out=small[:, 16:24], in_=TT[0][:, :, 0:64], axis=AX.X, op=ALU.add), ["T0"], ["small"])
            tt("dve", Sw, Sw, v8(pb[1]), ALU.mult, [sk, "pb1"], [sk])
            tt("dve", W3, v8(pb[2]), bc(small[:, 16:24].unsqueeze(2), [128, 8, 64]), ALU.mult, ["pb2", "small"], ["T0"])
            tt("pool", Sw, Sw, W3, ALU.add, [sk, "T0"], [sk])
            cp("dve", small[:, 24:32], vS[:, :, n], ["vS"], ["small"])
            tt("dve", W3, v8(pb[3]), bc(small[:, 24:32].unsqueeze(2), [128, 8, 64]), ALU.mult, ["pb3", "small"], ["T0"])
            tt("pool", Sw, Sw, W3, ALU.add, [sk, "T0"], [sk])
            dma(nws[n].rearrange("(hh p) j -> p hh j", p=128), Sw, [sk], [])
            tt("dve", W3, Sw, v8(pb[4]), ALU.mult, [sk, "pb4"], ["T0"])
            op("dve", lambda e, n=n: e.tensor_reduce(out=yTs[:, :, n], in_=TT[0][:, :, 0:64], axis=AX.X, op=ALU.add), ["T0"], ["T4_0"])
        stage(15)
        op("pool", lambda e: e.memset(dummy[:, 3:4], 0.0), [], ["ua", "uaA", "uaB", "dummy"])
        run_all([gn_gen(yTs, ALLT4, 0, NS, (TT[1], TT[2], TT[3]), ("T1", "T2", "T3"), bon, "bon_0")])
        stage(16)
        cf = []
        for c in range(4):
            cwb = bc(col[:, O_CW + c * 31:O_CW + (c + 1) * 31].unsqueeze(1), [128, NS, 31])
            tt("dve", tmpc[:, 0:NS * 31].rearrange("p (n w) -> p n w", w=31), uv[c], cwb, ALU.mult, ["uex", "col"], ["tmpc"])
            cfc = TT[c][:].rearrange("p a b -> p (a b)")[:, 0:NS]
            op("dve", lambda e, cfc=cfc: e.tensor_reduce(out=cfc, in_=tmpc[:, 0:NS * 31].rearrange("p (n w) -> p n w", w=31),
                                                        axis=AX.X, op=ALU.add), ["tmpc"], ["T%d" % c])
            ts("dve", cfc, cfc, col[:, O_CB + c:O_CB + c + 1], None, ALU.add, None, ["T%d" % c, "col"], ["T%d" % c])
            cf.append(cfc)
        for c in range(4):
            cp("dve", tmpb[:, c * NS:(c + 1) * NS], uv[c][:, :, 30], ["uex"], ["tmpb"])
        for c in range(4):
            mm(pb[6][0:NS, c * 128:(c + 1) * 128], tmpb[:, c * NS:(c + 1) * NS], C(C_ID), True, True, ["tmpb", "cst"], ["pb6"])
        cp("dve", m1[0:NS, 0:512], pb[6][0:NS, :], ["pb6"], ["T5"])
        dma(ncs[:, 29, :], m1[0:NS, 0:512], ["T5"], [])
        ln_conv_out(NS, cf, ["T0", "T1", "T2", "T3"])
        stage(17)
        tail(NS, [xs], [ys], NS)
        P.emit()
    return nc


def _prep_consts():
    c = np.zeros((128, NCONST), np.float32)
    idx = np.arange(128)
    c[:, C_ID:C_ID + 128] = np.eye(128)
    c[:, C_SL:C_SL + 128] = (idx[None, :] < idx[:, None])
    c[:, C_SU:C_SU + 128] = (idx[:, None] < idx[None, :])
    c[:, C_UI:C_UI + 128] = (idx[:, None] <= idx[None, :])
    c[:, C_TRI:C_TRI + 128] = (idx[:, None] <= idx[None, :]) * CNEG
    c[:, C_TRE:C_TRE + 128] = (idx[:, None] < idx[None, :]) * CNEG
    blk = (idx[:, None] // 64 == idx[None, :] // 64).astype(np.float32)
    c[:, C_BM:C_BM + 128] = blk / 64.0
    c[:, C_BO:C_BO + 128] = blk
    c[:, C_AM:C_AM + 128] = 1.0 / 512.0
    c[:, C_NI:C_NI + 64] = np.eye(128)[:, :64] * CNEG
    c[:, C_I2:C_I2 + 64] = (idx[:, None] % 64 == np.arange(64)[None, :])
    return c


_NC = None


def kernel(x_prompt, x_sample, state_shift, state_wkv, state_conv, norm_pre_g, w_in, mu_shift,
           decay_w0, decay_w2, iclr_a0, iclr_a2, k_k, k_a, r_k, gn_g, gn_b, conv_glu_b, conv_w,
           conv_b, ln_c_g, ln_c_b, w_branch_r, w_branch_c, w_out, norm_post_g):
    global _NC
    f = lambda a: np.ascontiguousarray(np.asarray(a, dtype=np.float32))
    colv = lambda v, n: f(v).reshape(n, 128).T
    cols = np.zeros((128, NCOL), np.float32)
    cols[:, O_MU:O_MU + 33] = colv(mu_shift[0], 33)
    cols[:, O_KK:O_KK + 8] = colv(k_k[0], 8)
    cols[:, O_KA:O_KA + 8] = colv(k_a[0], 8)
    cols[:, O_RK:O_RK + 8] = colv(np.asarray(r_k[0]).reshape(-1), 8)
    cols[:, O_GNG:O_GNG + 8] = colv(gn_g[0], 8)
    cols[:, O_GNB:O_GNB + 8] = colv(gn_b[0], 8)
    cols[:, O_A0:O_A0 + 8] = colv(iclr_a0[0], 8)
    cols[:, O_GLUB:O_GLUB + 8] = colv(conv_glu_b[0], 8)
    cols[:, O_CB:O_CB + 4] = colv(conv_b[0], 4)
    cols[:, O_LNG:O_LNG + 4] = colv(ln_c_g[0], 4)
    cols[:, O_LNB:O_LNB + 4] = colv(ln_c_b[0], 4)
    cw = f(conv_w[0])
    cols[:, O_CW:O_CW + 124] = cw.reshape(31, 4, 128).transpose(2, 1, 0).reshape(128, 124)
    cols[:, O_GPRE:O_GPRE + 8] = colv(norm_pre_g[0], 8)
    consts = _prep_consts()
    w2ext = np.concatenate([f(decay_w2[0]), f(decay_w0[0])[None, :]], axis=0)
    shared = {
        "w_in": f(w_in[0]), "w_br": f(w_branch_r[0]), "w_bc": f(w_branch_c[0]), "w_out": f(w_out[0]),
        "cols": cols, "consts": consts, "w2ext": f(w2ext), "a2": f(iclr_a2[0]),
        "npg": f(norm_post_g[0])[None, :], "npre": f(norm_pre_g[0])[None, :],
    }
    xpf = f(x_prompt); xsf = f(x_sample).reshape(128, D); ssf = f(state_shift[0])
    swf = f(state_wkv[0]).reshape(128, 1024, 64); scf = f(state_conv[0]).reshape(128 * 30, 512)
    in_maps = []
    for c in range(8):
        m = dict(shared)
        m["xp"] = xpf[c]
        m["xs"] = xsf[c * NS:(c + 1) * NS]
        m["sshift"] = ssf[c * NS:(c + 1) * NS]
        m["swkv"] = swf[c * NS:(c + 1) * NS]
        m["sconv"] = scf[c * NS * 30:(c + 1) * NS * 30]
        in_maps.append(m)
    if _NC is None:
        _NC = build()
    res = run_bass_kernel_spmd(_NC, in_maps, core_ids=list(range(8)))
    R = res.results
    y_prompt = np.stack([R[c]["yp"] for c in range(8)]).astype(np.float32)
    y_sample = np.concatenate([R[c]["ys"] for c in range(8)]).reshape(128, 1, D).astype(np.float32)
    nsp_ = np.concatenate([R[c]["nsp"] for c in range(8)]).reshape(1, 8, D).astype(np.float32)
    nwp_ = np.stack([R[c]["nwp"] for c in range(8)]).reshape(1, 8, 16, 64, 64).astype(np.float32)
    ncp_ = np.stack([R[c]["ncp"] for c in range(8)]).reshape(1, 8, 30, 512).astype(np.float32)
    nss_ = np.concatenate([R[c]["nss"] for c in range(8)]).reshape(1, 128, D).astype(np.float32)
    nws_ = np.concatenate([R[c]["nws"] for c in range(8)]).reshape(1, 128, 16, 64, 64).astype(np.float32)
    ncs_ = np.concatenate([R[c]["ncs"] for c in range(8)]).reshape(1, 128, 30, 512).astype(np.float32)
    return (y_prompt, y_sample, nsp_, nwp_, ncp_, nss_, nws_, ncs_)
```

```python
import contextlib
import numpy as np
import concourse.bass as bass
import concourse.mybir as mybir
from concourse.bass_utils import run_bass_kernel_spmd

F32 = mybir.dt.float32
BF16 = mybir.dt.bfloat16
AF = mybir.ActivationFunctionType
ALU = mybir.AluOpType
AX = mybir.AxisListType

D = 1024
NIN = 7808
SEQ = 2048
NS = 16
NCH = 61
CNEG = -0.6065306597126334

O_MU = 0; O_KK = 33; O_KA = 41; O_RK = 49; O_GNG = 57; O_GNB = 65; O_A0 = 73; O_GLUB = 81
O_CB = 89; O_LNG = 93; O_LNB = 97; O_CW = 101; O_GPRE = 225; O_OMM = 233; O_OMKA = 266; NCOL = 274
C_ID = 0; C_SL = 128; C_SU = 256; C_UI = 384; C_TRI = 512; C_TRE = 640; C_BM = 768; C_BO = 896
C_AM = 1024; C_NI = 1152; C_I2 = 1280; NCONST = 1344


class Prog:
    ENG = ("pe", "act", "dve", "pool", "sp")

    def __init__(self, nc):
        self.nc = nc
        self.ops = {e: [] for e in self.ENG}
        self.cnt = {e: 0 for e in self.ENG}
        self.waited = {e: {} for e in self.ENG}
        self.lastw = {}
        self.readers = {}
        self.dcnt = {}
        self.dead = False
        self.phase = ""
        self.annotate = False

    def _need(self, eng, waits, tok):
        if tok is None:
            return
        kind, key, val = tok
        if kind == "e" and key == "pe" and eng == "pe":
            return
        k = (kind, key)
        if self.waited[eng].get(k, 0) >= val:
            return
        if waits.get(k, 0) < val:
            waits[k] = val

    def op(self, eng, fn, reads=(), writes=(), dma=None, tag=None):
        if self.dead:
            return None
        waits = {}
        if eng == "pe":
            prev = getattr(self, "petag", None)
            if tag is not None and prev is not None and tag != prev:
                waits[("e", "pe")] = self.cnt["pe"]
            self.petag = tag
        for r in reads:
            self._need(eng, waits, self.lastw.get(r))
        for w in writes:
            self._need(eng, waits, self.lastw.get(w))
            for rd in self.readers.get(w, ()):
                self._need(eng, waits, rd)
        for k, v in waits.items():
            self.waited[eng][k] = v
        if dma is not None:
            prevc = self.dcnt.get(dma, 0)
            if prevc > 0 and self.waited[eng].get(("d", dma), 0) < prevc:
                waits[("d", dma)] = max(waits.get(("d", dma), 0), prevc)
                self.waited[eng][("d", dma)] = prevc
            self.dcnt[dma] = prevc + 1
            tok = ("d", dma, self.dcnt[dma])
        else:
            self.cnt[eng] += 1
            tok = ("e", eng, self.cnt[eng])
        self.ops[eng].append((waits, fn, tok, self.phase))
        for r in reads:
            self.readers.setdefault(r, []).append(tok)
        for w in writes:
            self.lastw[w] = tok
            self.readers[w] = []
        return tok

    def emit(self):
        nc = self.nc
        with contextlib.ExitStack() as st:
            esem = {e: st.enter_context(nc.semaphore("s_" + e)) for e in self.ENG}
            dsem = {k: st.enter_context(nc.semaphore("d_" + str(k))) for k in self.dcnt}
            block = st.enter_context(nc.Block())

            def run(engname, e):
                for waits, fn, tok, ph in self.ops[engname]:
                    for (kind, key), val in waits.items():
                        if kind == "e":
                            e.wait_ge(esem[key], val)
                        else:
                            e.wait_ge(dsem[key], 16 * val)
                    ins = fn(e)
                    if self.annotate:
                        ins.annotate(ph)
                    if tok[0] == "e":
                        ins.then_inc(esem[tok[1]], 1)
                    else:
                        ins.then_inc(dsem[tok[1]], 16)
                if engname == "sp":
                    for k, c in self.dcnt.items():
                        e.wait_ge(dsem[k], 16 * c)

            @block.tensor
            def _(e):
                run("pe", e)

            @block.scalar
            def _(e):
                run("act", e)

            @block.vector
            def _(e):
                run("dve", e)

            @block.gpsimd
            def _(e):
                run("pool", e)

            @block.sync
            def _(e):
                run("sp", e)


def build():
    nc = bass.Bass("TRN2", target_bir_lowering=False)
    di = lambda n, s: nc.dram_tensor(n, s, F32, kind="ExternalInput").ap()
    do = lambda n, s: nc.dram_tensor(n, s, F32, kind="ExternalOutput").ap()
    xp = di("xp", [SEQ, D]); xs = di("xs", [NS, D]); sshift = di("sshift", [NS, D])
    swkv = di("swkv", [NS, 1024, 64]); sconv = di("sconv", [NS * 30, 512])
    w_in = di("w_in", [D, NIN]); w_br = di("w_br", [D, D]); w_bc = di("w_bc", [512, D]); w_out = di("w_out", [D, D])
    cols_d = di("cols", [128, NCOL]); consts_d = di("consts", [128, NCONST])
    w2ext_d = di("w2ext", [65, 1024]); a2_d = di("a2", [64, 1024])
    npg_d = di("npg", [1, D]); npre_d = di("npre", [1, D])
    yp = do("yp", [SEQ, D]); ys = do("ys", [NS, D]); nsp = do("nsp", [1, D])
    nwp = do("nwp", [1024, 64]); ncp = do("ncp", [30, 512]); nss = do("nss", [NS, D])
    nws = do("nws", [NS, 1024, 64]); ncs = do("ncs", [NS, 30, 512])
    wi_s = nc.dram_tensor("wi_s", [D, NIN], BF16).ap()
    wbr_s = nc.dram_tensor("wbr_s", [D, D], BF16).ap()
    wbc_s = nc.dram_tensor("wbc_s", [512, D], BF16).ap()
    wo_s = nc.dram_tensor("wo_s", [D, D], BF16).ap()

    with contextlib.ExitStack() as st:
        def T(n, s, d=F32):
            return st.enter_context(nc.sbuf_tensor("sb_" + n, s, d))
        P = Prog(nc)
        op = P.op
        cst = T("cst", [128, NCONST]); col = T("col", [128, NCOL])
        idb = T("idb", [128, 128], BF16)
        bob = T("bob", [128, 128], BF16)
        bmb = T("bmb", [128, 128], BF16)
        w2e = T("w2e", [65, 1024]); a2b = T("a2b", [128, 1024], BF16)
        wb = [T("wb%d" % i, [128, 8, 512], BF16) for i in range(4)]
        xt = [T("xt%d" % i, [128, D]) for i in range(2)]
        hb = T("hb", [128, D], BF16)
        hT = T("hT", [128, 8, 512], BF16)
        rS = T("rS", [128, 8, 512], BF16); kS = T("kS", [128, 8, 512], BF16)
        vS = T("vS", [128, 8, 512], BF16); zrS = T("zrS", [128, 8, 512], BF16)
        twl = T("twl", [65, 512]); alb = T("alb", [128, 512], BF16)
        ua = T("ua", [128, 4, 512]); uex = T("uex", [128, 4, 542]); ubf = T("ubf", [128, 4, 542], BF16)
        szc = T("szc", [128, 4, 512], BF16)
        orT = T("orT", [128, 8, 512], BF16)
        mT = rS
        ocT = kS
        TT = [T("T%d" % i, [128, 8, 128]) for i in range(8)]
        bon = T("bon", [128, 8, 128], BF16)
        rt_ = T("rt_", [128, 8, 128], BF16); at_ = T("at_", [128, 8, 128], BF16); bt_ = T("bt_", [128, 8, 128], BF16)
        kt_ = T("kt_", [128, 8, 128], BF16); bh_ = T("bh_", [128, 8, 128], BF16); kh_ = T("kh_", [128, 8, 128], BF16)
        Vt = T("Vt", [128, 1024], BF16); Bt = T("Bt", [128, 1024], BF16); Kt = T("Kt", [128, 1024], BF16)
        Ak = [T("Ak%d" % i, [128, 4, 128], BF16) for i in range(2)]
        Nk = [T("Nk%d" % i, [128, 4, 128], BF16) for i in range(2)]
        Qb = T("Qb", [128, 4, 128], BF16)
        LkT = T("LkT", [128, 4, 128], BF16); MbT = T("MbT", [128, 4, 128], BF16); MkT = T("MkT", [128, 4, 128], BF16)
        Xb = T("Xb", [128, 256], BF16); SAb = T("SAb", [128, 256], BF16)
        Xb2 = T("Xb2", [128, 256], BF16); SAb2 = T("SAb2", [128, 256], BF16)
        dummy = T("dummy", [128, 8])
        _w3 = lambda i: wb[3][:, i, :].rearrange("p (a b) -> p a b", b=128)
        SETS = [
            {"Ak": Ak, "Nk": Nk, "Qb": Qb, "LkT": LkT, "MbT": MbT, "MkT": MkT, "Xb": Xb, "SAb": SAb, "banks": (0, 1, 2), "n": "_A"},
            {"Ak": [_w3(0), _w3(1)], "Nk": [_w3(2), _w3(3)], "Qb": _w3(4), "LkT": _w3(5), "MbT": _w3(6), "MkT": _w3(7),
             "Xb": Xb2, "SAb": SAb2, "banks": (3, 4, 5), "n": "_B"},
        ]
        SETB_KEYS = [k + "_B" for k in ("Ak0", "Ak1", "Nk0", "Nk1", "Qb", "LkT", "MbT", "MkT")]
        ALLT4 = ["T4_0", "T4_1", "T4_2", "T4_3"]
        ALLSF = ["Sf0", "Sf1", "Sf2", "Sf3"]
        ALLSB = ["Sb0", "Sb1", "Sb2", "Sb3"]
        Sf = T("Sf", [128, 8, 64]); Sb = T("Sb", [128, 8, 64], BF16)
        gC = T("gC", [128, 8]); tmpb = T("tmpb", [128, 512]); tmpc = T("tmpc", [128, 512])
        sgb = T("sgb", [128, 512], BF16)
        m1 = TT[5][:].rearrange("p a b -> p (a b)")
        pprev = [T("pprev%d" % i, [128, 40]) for i in range(2)]
        small = T("small", [128, 64])
        dg = [T("dg%d" % i, [128, 128], BF16) for i in range(4)]
        wld = xt
        pb = [st.enter_context(nc.psum_tensor("pb%d" % i, [128, 512], F32)) for i in range(7)]
        ptb = st.enter_context(nc.psum_tensor("ptb", [128, 1024], BF16))

        cnt = {"d": 0, "e": 0}
        import os
        STOP = float(os.environ.get("MK_STOP", "1000"))

        def stage(k):
            if k > STOP:
                P.dead = True
        P.annotate = bool(os.environ.get("MK_ANN"))

        def ph(name):
            P.phase = name

        def dma(out, in_, reads, writes, q="sp"):
            cnt[q] = cnt.get(q, 0) + 1
            key = "%s%d" % (q, cnt[q] % (16 if q == "sp" else 8))
            if q == "act":
                return op("act", lambda e: e.dma_start(out=out, in_=in_), reads, writes, dma=key)
            return op(q, lambda e: e.dma_start(out=out, in_=in_), reads, writes, dma=key)

        def mm(out, lhsT, rhs, start, stop, reads, writes):
            b0 = lhsT.base_partition()
            n0 = lhsT.shape[0]
            tag = "lo" if b0 + n0 <= 64 else ("hi" if b0 >= 64 else None)
            op("pe", lambda e: e.matmul(out, lhsT=lhsT, rhs=rhs, start=start, stop=stop), reads, writes, tag=tag)

        def act(out, in_, func, reads, writes, bias=None, scale=None, accum=None):
            kw = {}
            if bias is not None: kw["bias"] = bias
            if scale is not None: kw["scale"] = scale
            if accum is not None: kw["accum_out"] = accum
            op("act", lambda e: e.activation(out=out, in_=in_, func=func, **kw), reads, writes)

        def tt(eng, out, in0, in1, o, reads, writes):
            g = {"dve": "dve", "pool": "pool"}[eng]
            op(g, lambda e: e.tensor_tensor(out=out, in0=in0, in1=in1, op=o), reads, writes)

        def ts(eng, out, in0, s1, s2, o0, o1, reads, writes):
            if s2 is None:
                op(eng, lambda e: e.tensor_scalar(out=out, in0=in0, scalar1=s1, scalar2=None, op0=o0), reads, writes)
            else:
                op(eng, lambda e: e.tensor_scalar(out=out, in0=in0, scalar1=s1, scalar2=s2, op0=o0, op1=o1), reads, writes)

        def stt(eng, out, in0, sc, in1, o0, o1, reads, writes):
            op(eng, lambda e: e.scalar_tensor_tensor(out=out, in0=in0, scalar=sc, in1=in1, op0=o0, op1=o1), reads, writes)

        def cp(eng, out, in_, reads, writes):
            if eng == "act":
                act(out, in_, AF.Copy, reads, writes)
            else:
                op(eng, lambda e: e.tensor_copy(out=out, in_=in_), reads, writes)

        def rsq(out, in_, eps, reads, wkey):
            act(out, in_, AF.Sqrt, reads, [wkey], bias=eps)
            op("dve", lambda e: e.reciprocal(out=out, in_=out), [wkey], [wkey])

        def bc(ap, shape):
            return ap.to_broadcast(shape)

        C = lambda o, n=128: cst[:, o:o + n]

        dma(cst[:], consts_d, [], ["cst"])
        dma(col[:], cols_d, [], ["col"])
        dma(w2e[:], w2ext_d, [], ["w2e"])
        dma(wld[0][64:128, 0:1024], a2_d, [], ["xt0"])
        cp("dve", a2b[64:128, :], wld[0][64:128, 0:1024], ["xt0"], ["a2b"])
        cp("dve", idb[:], C(C_ID), ["cst"], ["idb"])
        cp("dve", bob[:], C(C_BO), ["cst"], ["bob"])
        cp("dve", bmb[:], C(C_BM), ["cst"], ["bmb"])
        ts("dve", col[:, O_OMM:O_OMM + 33], col[:, O_MU:O_MU + 33], -1.0, 1.0, ALU.mult, ALU.add, ["col"], ["col"])
        ts("dve", col[:, O_OMKA:O_OMKA + 8], col[:, O_KA:O_KA + 8], -1.0, 1.0, ALU.mult, ALU.add, ["col"], ["col"])
        op("pool", lambda e: e.memset(twl[64:65, :], 1.0), [], ["twl"])
        op("pool", lambda e: e.memset(Sf[:], 0.0), [], ALLSF)
        op("pool", lambda e: e.memset(Sb[:], 0.0), [], ALLSB)
        op("pool", lambda e: e.memset(pprev[0][:], 0.0), [], ["pprev0"])
        op("pool", lambda e: e.memset(pprev[1][:], 0.0), [], ["pprev1"])
        op("pool", lambda e: e.memset(uex[:], 0.0), [], ["uex"])

        stage(1)
        ph("prologue")
        def prologue_gen(part):
            ph("prologue")
            pieces = []
            for c0 in range(0, NIN, 1024):
                for rc in range(8):
                    pieces.append((w_in, wi_s, rc, c0, min(1024, NIN - c0), "scr_i%d" % (c0 // 1024)))
            for rc in range(8):
                pieces.append((w_br, wbr_s, rc, 0, 1024, "scr_o"))
            for rc in range(4):
                pieces.append((w_bc, wbc_s, rc, 0, 1024, "scr_o"))
            for rc in range(8):
                pieces.append((w_out, wo_s, rc, 0, 1024, "scr_o"))
            fl = lambda t: t[:].rearrange("p a b -> p (a b)")
            orv = lambda i: orT[:, 2 * i:2 * i + 2, :].rearrange("p a b -> p (a b)")
            if part == 1:
                pieces = pieces[0:48]
                sf32 = [(fl(TT[i]), "T%d" % i) for i in range(8)]
                sbf = [(fl(rt_), "rt_0"), (fl(at_), "at_0"), (fl(bt_), "bt_0"), (fl(kt_), "kt_0"), (fl(bh_), "bh_"), (fl(kh_), "kh_"),
                       (Vt[:, :], "Vt_0"), (Bt[:, :], "Bt_0"), (Kt[:, :], "Kt_0")]
                DEPTH = 6
            else:
                pieces = pieces[48:]
                sf32 = [(fl(TT[i]), "T%d" % i) for i in (3, 4, 6, 7)]
                sbf = [(orv(0), "orT"), (orv(1), "orT"), (orv(2), "orT"), (orv(3), "orT"),
                       (fl(kh_), "kh_"), (Vt[:, :], "Vt_0"), (Bt[:, :], "Bt_0"), (Kt[:, :], "Kt_0")]
                DEPTH = 3
            NB = len(sf32)
            engs = ["dve", "act"]
            npc = len(pieces)
            for i in range(npc + DEPTH):
                if i < npc:
                    src, dst, rc, c0, n, skey = pieces[i]
                    bf_, kf_ = sf32[i % NB]
                    dma(bf_[:, 0:n], src[rc * 128:(rc + 1) * 128, c0:c0 + n], [], [kf_],
                        q=("act" if (os.environ.get("MK_ACTQ") and i % 2 == 1) else "sp"))
                j = i - DEPTH
                if j >= 0:
                    src, dst, rc, c0, n, skey = pieces[j]
                    bf_, kf_ = sf32[j % NB]
                    bb_, kb_ = sbf[j % len(sbf)]
                    cp(engs[j % 2], bb_[:, 0:n], bf_[:, 0:n], [kf_], [kb_])
                    dma(dst[rc * 128:(rc + 1) * 128, c0:c0 + n], bb_[:, 0:n], [kb_], [skey], q="pool")
                yield

        stage(2)
        wi_v = wi_s.rearrange("(dc p) n -> p dc n", p=128)
        wbr_v = wbr_s.rearrange("(dc p) n -> p dc n", p=128)
        wbc_v = wbc_s.rearrange("(dc p) n -> p dc n", p=128)
        wo_v = wo_s.rearrange("(dc p) n -> p dc n", p=128)
        wslot = {"i": 0}

        def wload(view, ndc, c0, n, skeys):
            s = wslot["i"] % 4
            wslot["i"] += 1
            dma(wb[s][:, 0:ndc, 0:n], view[:, :, c0:c0 + n], skeys, ["wb%d" % s])
            return s

        def ikeys(c0, n):
            return ["scr_i%d" % b for b in range(c0 // 1024, (c0 + n - 1) // 1024 + 1)]

        def rmsnorm_tile(xtile, key, npart, dst_cols, want_h_out=None):
            ph("rmsnorm")
            act(hb[0:npart, :], xtile[0:npart, :], AF.Square, [key], ["hb", "small"], accum=small[0:npart, 0:1])
            ts("dve", small[0:npart, 1:2], small[0:npart, 0:1], 1.0 / D, 1e-6, ALU.mult, ALU.add, ["small"], ["small"])
            rsq(small[0:npart, 2:3], small[0:npart, 1:2], 0.0, ["small"], "small")
            ts("dve", hb[0:npart, :], xtile[0:npart, :], small[0:npart, 2:3], None, ALU.mult, None, [key, "small"], ["hb"])
            if want_h_out is not None:
                want_h_out()
            for dc in range(8):
                op("pe", lambda e, dc=dc: e.transpose(ptb[:, dc * 128:dc * 128 + npart], hb[0:npart, dc * 128:(dc + 1) * 128], idb[0:npart, 0:npart]),
                   ["hb", "idb"], ["ptb"])
            for dc in range(8):
                act(hT[:, dc, dst_cols[0]:dst_cols[1]], ptb[:, dc * 128:dc * 128 + npart], AF.Copy, ["ptb", "col"], ["hT"],
                    scale=col[:, O_GPRE + dc:O_GPRE + dc + 1])

        def project(j, wslot_i, jj, NT, bank):
            for dc in range(8):
                mm(pb[bank][:, 0:NT], wb[wslot_i][:, dc, jj * 128:(jj + 1) * 128], hT[:, dc, 0:NT], dc == 0, dc == 7,
                   ["wb%d" % wslot_i, "hT"], ["pb%d" % bank])

        def shiftmix(j, bank, NT, dst, dkey, sample, pp_old, pp_new):
            p = pb[bank]
            mu = col[:, O_MU + j:O_MU + j + 1]
            omm = col[:, O_OMM + j:O_OMM + j + 1]
            bk = "pb%d" % bank
            if sample:
                act(tmpb[:, 0:NS], p[:, 0:NS], AF.Copy, [bk, "col"], ["tmpb"], scale=omm)
                stt("dve", dst, p[:, NS:2 * NS], mu, tmpb[:, 0:NS], ALU.mult, ALU.add, [bk, "tmpb", "col"], [dkey])
            else:
                act(tmpb[:, 0:NT], p[:, 0:NT], AF.Copy, [bk, "col"], ["tmpb"], scale=omm)
                act(pprev[pp_new][:, j:j + 1], p[:, NT - 1:NT], AF.Copy, [bk], ["pprev%d" % pp_new])
                stt("dve", dst[:, 1:NT], p[:, 0:NT - 1], mu, tmpb[:, 1:NT], ALU.mult, ALU.add, [bk, "tmpb", "col"], [dkey])
                stt("dve", dst[:, 0:1], pprev[pp_old][:, j:j + 1], mu, tmpb[:, 0:1], ALU.mult, ALU.add,
                    ["pprev%d" % pp_old, "tmpb", "col"], [dkey])

        def proj_phase(NT, sample, pp_old, pp_new):
            ph("proj")
            nb = 0
            for g0 in range(0, 45, 4):
                ng = min(4, 45 - g0)
                s = wload(wi_v, 8, g0 * 128, ng * 128, ikeys(g0 * 128, ng * 128))
                for jj in range(ng):
                    j = g0 + jj
                    bank = nb % 2
                    nb += 1
                    bk = "pb%d" % bank
                    project(j, s, jj, NT if not sample else 2 * NS, bank)
                    W = NS if sample else NT
                    if j < 8:
                        shiftmix(j, bank, NT, rS[:, j, 0:W], "rS", sample, pp_old, pp_new)
                    elif j < 16:
                        shiftmix(j, bank, NT, kS[:, j - 8, 0:W], "kS", sample, pp_old, pp_new)
                    elif j < 24:
                        shiftmix(j, bank, NT, vS[:, j - 16, 0:W], "vS", sample, pp_old, pp_new)
                    elif j < 32:
                        shiftmix(j, bank, NT, tmpc[:, 0:W], "tmpc", sample, pp_old, pp_new)
                        act(zrS[:, j - 24, 0:W], tmpc[:, 0:W], AF.Silu, ["tmpc"], ["zrS"])
                    elif j == 32:
                        shiftmix(j, bank, NT, tmpc[:, 0:W], "tmpc", sample, pp_old, pp_new)
                        act(twl[0:64, 0:W], tmpc[0:64, 0:W], AF.Tanh, ["tmpc"], ["twl"])
                        cp("pool", alb[64:128, 0:W], tmpc[64:128, 0:W], ["tmpc"], ["alb"])
                    elif j < 37:
                        c = j - 33
                        act(ua[:, c, 0:W], pb[bank][:, 0:W], AF.Identity, [bk, "col"], ["ua"],
                            bias=col[:, O_GLUB + c:O_GLUB + c + 1])
                    elif j < 41:
                        c = j - 37
                        act(tmpc[:, 0:W], pb[bank][:, 0:W], AF.Sigmoid, [bk, "col"], ["tmpc"],
                            bias=col[:, O_GLUB + 4 + c:O_GLUB + 5 + c])
                        if sample:
                            tt("dve", uex[:, c, 0:NS * 31].rearrange("p (n w) -> p n w", w=31)[:, :, 30], ua[:, c, 0:W], tmpc[:, 0:W],
                               ALU.mult, ["ua", "tmpc"], ["uex"])
                        else:
                            tt("dve", uex[:, c, 30:30 + W], ua[:, c, 0:W], tmpc[:, 0:W], ALU.mult, ["ua", "tmpc"], ["uex"])
                    else:
                        c = j - 41
                        act(szc[:, c, 0:W], pb[bank][:, 0:W], AF.Silu, [bk], ["szc"])
                    yield

        def ln_conv_out(W, cf, ck):
            ph("lnconv")
            for c in range(4):
                mm(pb[2][:, 0:W], C(C_AM), cf[c], c == 0, c == 3, ["cst", ck[c]], ["pb2"])
            for c in range(4):
                tt("dve", cf[c], cf[c], pb[2][:, 0:W], ALU.subtract, [ck[c], "pb2"], [ck[c]])
            for c in range(4):
                tt("pool", ua[:, c, 0:W], cf[c], cf[c], ALU.mult, [ck[c]], ["ua"])
            for c in range(4):
                mm(pb[3][:, 0:W], C(C_AM), ua[:, c, 0:W], c == 0, c == 3, ["cst", "ua"], ["pb3"])
            rsq(tmpc[:, 0:W], pb[3][:, 0:W], 1e-5, ["pb3"], "tmpc")
            for c in range(4):
                tt("dve", cf[c], cf[c], tmpc[:, 0:W], ALU.mult, [ck[c], "tmpc"], [ck[c]])
                act(ua[:, c, 0:W], cf[c], AF.Silu, [ck[c], "col"], ["ua"],
                    bias=col[:, O_LNB + c:O_LNB + c + 1], scale=col[:, O_LNG + c:O_LNG + c + 1])
                tt("pool", ocT[:, c, 0:W], ua[:, c, 0:W], szc[:, c, 0:W], ALU.mult, ["ua", "szc"], ["kS"])

        def prep_gen(cs, W, sample, PSp):
            T0, T1, T2, T3, T4, T5, T6, T7 = TT
            sl = slice(cs, cs + W)
            sfx = PSp["sfx"]
            bonT, gCt = PSp["bon"], PSp["gC"]
            kbon, kgc = "bon" + sfx, "gC" + sfx
            ph("prep")
            sh = [128, 8, W]
            colb = lambda o: bc(col[:, o:o + 8].unsqueeze(2), sh)
            p6 = pb[6]
            p6v = p6[:].rearrange("p (a b) -> p a b", b=128)[:, :, 0:W]
            T0v = T0[:].rearrange("p a b -> p (a b)")
            tt("dve", T5[:, :, 0:W], kS[:, :, sl], colb(O_KK), ALU.mult, ["kS", "col"], ["T5"])
            tt("pool", bh_[:, :, 0:W], T5[:, :, 0:W], T5[:, :, 0:W], ALU.mult, ["T5"], ["bh_"])
            yield
            for hf in range(2):
                mm(p6[0:W, :], twl[0:65, sl], w2e[0:65, hf * 512:(hf + 1) * 512], True, True, ["twl", "w2e"], ["pb6"])
                yield
                act(T0v[0:W, hf * 512:(hf + 1) * 512], p6[0:W, :], AF.Sigmoid, ["pb6"], ["T0"])
                yield
            tri = C(C_TRI) if not sample else cst[0:W, C_NI:C_NI + W]
            tre = C(C_TRE) if not sample else cst[0:W, C_NI + 64:C_NI + 64 + W]
            for hf in range(2):
                hs = slice(hf * 4, hf * 4 + 4)
                for hq in range(4):
                    hh = hf * 4 + hq
                    mm(p6[:, hq * 128:hq * 128 + W], T0v[0:W, hh * 128:(hh + 1) * 128], tri[0:W, 0:W], True, True, ["T0", "cst"], ["pb6"])
                yield
                act(T1[:, hs, 0:W], p6v[:, 0:4, :], AF.Exp, ["pb6"], ["T1"])
                act(T2[:, hs, 0:W], p6v[:, 0:4, :], AF.Exp, ["pb6"], ["T2"], scale=-1.0)
                yield
            cp("pool", gCt[:, :], T1[:, :, W - 1], ["T1"], [kgc])
            for hf in range(2):
                for hq in range(4):
                    hh = hf * 4 + hq
                    mm(p6[:, hq * 128:hq * 128 + W], a2b[64:128, hh * 128:(hh + 1) * 128], alb[64:128, sl], True, True, ["a2b", "alb"], ["pb6"])
                yield
                for hq in range(4):
                    hh = hf * 4 + hq
                    act(T4[:, hh, 0:W], p6[:, hq * 128:hq * 128 + W], AF.Sigmoid, ["pb6", "col"], ALLT4,
                        bias=col[:, O_A0 + hh:O_A0 + hh + 1])
                yield
            for hf in range(2):
                hs = slice(hf * 4, hf * 4 + 4)
                for hq in range(4):
                    hh = hf * 4 + hq
                    mm(p6[:, hq * 128:hq * 128 + W], bob[:], bh_[:, hh, 0:W], True, True, ["bob", "bh_"], ["pb6"])
                yield
                rsq(T7[:, hs, 0:W], p6v[:, 0:4, :], 1e-12, ["pb6"], "T7")
                yield
            stt("dve", T6[:, :, 0:W], T5[:, :, 0:W], -1.0, T7[:, :, 0:W], ALU.mult, ALU.mult, ["T5", "T7"], ["T6"])
            yield
            stt("dve", T7[:, :, 0:W], T6[:, :, 0:W], -1.0, T4[:, :, 0:W], ALU.mult, ALU.mult, ["T6"] + ALLT4, ["T7"])
            yield
            tt("pool", T0[:, :, 0:W], T4[:, :, 0:W], colb(O_KA), ALU.mult, ALLT4 + ["col", "T0"], ["T0"])
            tt("pool", T0[:, :, 0:W], T0[:, :, 0:W], colb(O_OMKA), ALU.add, ["T0", "col"], ["T0"])
            yield
            tt("dve", T5[:, :, 0:W], kS[:, :, sl], T0[:, :, 0:W], ALU.mult, ["kS", "T0"], ["T5"])
            yield
            tt("pool", T0[:, :, 0:W], rS[:, :, sl], T5[:, :, 0:W], ALU.mult, ["rS", "T5"], ["T0"])
            tt("pool", kh_[:, :, 0:W], T0[:, :, 0:W], colb(O_RK), ALU.mult, ["T0", "col"], ["kh_"])
            yield
            for hf in range(2):
                hs = slice(hf * 4, hf * 4 + 4)
                for hq in range(4):
                    hh = hf * 4 + hq
                    mm(p6[:, hq * 128:hq * 128 + W], bob[:], kh_[:, hh, 0:W], True, True, ["bob", "kh_"], ["pb6"])
                yield
                tt("dve", bonT[:, hs, 0:W], p6v[:, 0:4, :], vS[:, hs, sl], ALU.mult, ["pb6", "vS"], [kbon])
                yield
            if sample:
                return
            ph("mults")
            EG, EnG, EGe, k2, aa, bb = T1, T2, T3, T5, T6, T7
            rt, at, bt, kt = PSp["rt"], PSp["at"], PSp["bt"], PSp["kt"]
            krt, kat, kbt, kkt = "rt" + sfx, "at" + sfx, "bt" + sfx, "kt" + sfx
            tt("dve", rt, rS[:, :, sl], EG[:], ALU.mult, ["rS", "T1"], [krt])
            tt("pool", at[:, :, 1:128], aa[:, :, 1:128], EG[:, :, 0:127], ALU.mult, ["T6", "T1"], [kat])
            cp("pool", at[:, :, 0:1], aa[:, :, 0:1], ["T6"], [kat])
            yield
            tt("dve", bt, bb[:], EnG[:], ALU.mult, ["T7", "T2"], [kbt])
            tt("pool", kt, k2[:], EnG[:], ALU.mult, ["T5", "T2"], [kkt])
            yield
            tt("dve", EGe[:], EnG[:], bc(EG[:, :, 127:128], [128, 8, 128]), ALU.mult, ["T2", "T1", kat], ["T3"])
            yield
            tt("pool", bh_[:], bb[:], EGe[:], ALU.mult, ["T7", "T3"], ["bh_"])
            tt("dve", kh_[:], k2[:], EGe[:], ALU.mult, ["T5", "T3"], ["kh_"])
            yield
            ph("transp")
            for src, skey, dst, dkey in ((vS, "vS", PSp["Vt"], "Vt" + sfx), (bh_, "bh_", PSp["Bt"], "Bt" + sfx), (kh_, "kh_", PSp["Kt"], "Kt" + sfx)):
                for hh in range(8):
                    srcap = src[:, hh, sl] if src is vS else src[:, hh, :]
                    op("pe", lambda e, srcap=srcap, hh=hh: e.transpose(ptb[:, hh * 128:(hh + 1) * 128], srcap, idb[:]),
                       [skey, "idb"], ["ptb"])
                yield
                cp("act", dst[:, 0:512], ptb[:, 0:512], ["ptb"], [dkey])
                cp("dve", dst[:, 512:1024], ptb[:, 512:1024], ["ptb"], [dkey])
                yield

        def gn_gen(yT, ykey, cs, W, G, Gk, bonT, kbon):
            ph("gn")
            G1, G2, G3 = G
            k1, k2_, k3 = Gk
            sl = slice(cs, cs + W)
            sh = [128, 8, W]
            colb = lambda o: bc(col[:, o:o + 8].unsqueeze(2), sh)
            p6 = pb[6]
            p6v = p6[:].rearrange("p (a b) -> p a b", b=128)[:, :, 0:W]
            for hf in range(2):
                hs = slice(hf * 4, hf * 4 + 4)
                for hq in range(4):
                    hh = hf * 4 + hq
                    mm(p6[:, hq * 128:hq * 128 + W], C(C_BM), yT[:, hh, 0:W], True, True, ["cst"] + ykey, ["pb6"])
                yield
                tt("dve", G1[:, hs, 0:W], yT[:, hs, 0:W], p6v[:, 0:4, :], ALU.subtract, ykey + ["pb6"], [k1])
                yield
            hbv = hb[:, :].rearrange("p (a b) -> p a b", b=128)
            tt("pool", hbv[:, :, 0:W], G1[:, :, 0:W], G1[:, :, 0:W], ALU.mult, [k1], ["hb"])
            yield
            for hf in range(2):
                hs = slice(hf * 4, hf * 4 + 4)
                for hq in range(4):
                    hh = hf * 4 + hq
                    mm(p6[:, hq * 128:hq * 128 + W], bmb[:], hbv[:, hh, 0:W], True, True, ["bmb", "hb"], ["pb6"])
                yield
                rsq(G3[:, hs, 0:W], p6v[:, 0:4, :], 64e-5, ["pb6"], k3)
                yield
            tt("dve", G1[:, :, 0:W], G1[:, :, 0:W], G3[:, :, 0:W], ALU.mult, [k1, k3], [k1])
            yield
            tt("pool", G1[:, :, 0:W], G1[:, :, 0:W], colb(O_GNG), ALU.mult, [k1, "col"], [k1])
            tt("pool", G1[:, :, 0:W], G1[:, :, 0:W], colb(O_GNB), ALU.add, [k1, "col"], [k1])
            yield
            tt("dve", G1[:, :, 0:W], G1[:, :, 0:W], bonT[:, :, 0:W], ALU.add, [k1, kbon], [k1])
            yield
            tt("dve", orT[:, :, sl], G1[:, :, 0:W], zrS[:, :, sl], ALU.mult, [k1, "zrS"], ["orT"])
            yield

        def run_all(gens):
            gens = list(gens)
            while gens:
                for gq in list(gens):
                    try:
                        next(gq)
                    except StopIteration:
                        gens.remove(gq)

        gC2 = T("gC2", [128, 8])
        _fl = lambda t, i: t[:, 2 * i:2 * i + 2, :].rearrange("p a b -> p (a b)")
        _v8 = lambda ap: ap.rearrange("p (a b) -> p a b", b=128)
        PS = [
            {"sfx": "_0", "rt": rt_[:], "at": at_[:], "bt": bt_[:], "kt": kt_[:], "Vt": Vt, "Bt": Bt, "Kt": Kt, "bon": bon, "gC": gC},
            {"sfx": "_1", "rt": _v8(_fl(wb[0], 0)), "at": _v8(_fl(wb[0], 1)), "bt": _v8(_fl(wb[0], 2)), "kt": _v8(_fl(wb[0], 3)),
             "Vt": _fl(wb[1], 0), "Bt": _fl(wb[1], 1), "Kt": _fl(wb[1], 2), "bon": _v8(_fl(wb[1], 3)), "gC": gC2},
        ]
        PS1_KEYS = [k + "_1" for k in ("rt", "at", "bt", "kt", "Vt", "Bt", "Kt", "bon")]

        def scan_group(g, S, PSp, yT):
            Ak_, Nk_, Qb_, LkT_, MbT_, MkT_, Xb_, SAb_ = S["Ak"], S["Nk"], S["Qb"], S["LkT"], S["MbT"], S["MkT"], S["Xb"], S["SAb"]
            b0, b1, b2 = S["banks"]
            kb = lambda i: "pb%d" % i
            n = S["n"]
            sfx = PSp["sfx"]
            rt_, at_, bt_, kt_, Vt, Bt, Kt, gC = PSp["rt"], PSp["at"], PSp["bt"], PSp["kt"], PSp["Vt"], PSp["Bt"], PSp["Kt"], PSp["gC"]
            K = lambda nm: nm + n
            heads = [4 * g + x for x in (0, 2, 1, 3)]
            SbK = "Sb%d" % g; SfK = "Sf%d" % g; yK = "yT%d" % g

            def hp(h):
                hl, hh = h % 2, h // 2
                return slice(hl * 64, hl * 64 + 64), hh
            v4 = lambda p: p[:].rearrange("p (a b) -> p a b", b=128)
            mk = lambda o: bc(cst[:, o:o + 128].unsqueeze(1), [128, 4, 128])
            ph("scores")
            plan = [(b0, "at_", "bt_", Ak_[0], K("Ak0"), C_SL), (b1, "bt_", "at_", Nk_[0], K("Nk0"), C_SU),
                    (b2, "kt_", "at_", LkT_, K("LkT"), C_SU), (b0, "bt_", "rt_", MbT_, K("MbT"), C_UI),
                    (b1, "kt_", "rt_", MkT_, K("MkT"), C_UI)]
            tl = {"at_": at_, "bt_": bt_, "kt_": kt_, "rt_": rt_}
            kn = {"at_": "at" + sfx, "bt_": "bt" + sfx, "kt_": "kt" + sfx, "rt_": "rt" + sfx}
            first_lo = (n == "_A")
            for rnd in (plan[0:3], plan[3:5]):
                for tagsel in ((0, 1) if first_lo else (1, 0)):
                    for (bk, ln, rn, dst, dk, msk) in rnd:
                        for hi, h in enumerate(heads):
                            if (h % 2) != tagsel:
                                continue
                            pr, hh = hp(h)
                            mm(pb[bk][:, hi * 128:(hi + 1) * 128], tl[ln][pr, hh, :], tl[rn][pr, hh, :], True, True, [kn[ln], kn[rn]], [kb(bk)])
                yield
                for (bk, ln, rn, dst, dk, msk) in rnd:
                    tt("dve", dst[:], v4(pb[bk]), mk(msk), ALU.mult, [kb(bk), "cst"], [dk])
                    yield
            ph("doubling")
            tt("pool", Qb_[:], Nk_[0][:], mk(C_ID), ALU.add, [K("Nk0"), "cst"], [K("Qb")])
            yield
            mm(pb[b2][:, :], idb[:], Qb_.rearrange("p a b -> p (a b)"), True, True,
               ["idb", K("Qb")], [kb(b2)])
            yield
            cur = 0
            for lvl in range(6):
                nx = 1 - cur
                for hi in range(4):
                    mm(pb[b0][:, hi * 128:(hi + 1) * 128], Nk_[cur][:, hi, :], Ak_[cur][:, hi, :], True, True,
                       [K("Nk%d" % cur), K("Ak%d" % cur)], [kb(b0)])
                if lvl < 5:
                    for hi in range(4):
                        mm(pb[b1][:, hi * 128:(hi + 1) * 128], Ak_[cur][:, hi, :], Nk_[cur][:, hi, :], True, True,
                           [K("Nk%d" % cur), K("Ak%d" % cur)], [kb(b1)])
                yield
                cp("act", Ak_[nx][:], v4(pb[b0]), [kb(b0)], [K("Ak%d" % nx)])
                if lvl < 5:
                    cp("dve", Nk_[nx][:], v4(pb[b1]), [kb(b1)], [K("Nk%d" % nx)])
                yield
                for hi in range(4):
                    mm(pb[b2][:, hi * 128:(hi + 1) * 128], Ak_[nx][:, hi, :], Qb_[:, hi, :], False, True,
                       [K("Ak%d" % nx), K("Qb")], [kb(b2)])
                yield
                if lvl % 2 == 0 or os.environ.get("MK_QACT"):
                    cp("act", Qb_[:], v4(pb[b2]), [kb(b2)], [K("Qb")])
                else:
                    cp("dve", Qb_[:], v4(pb[b2]), [kb(b2)], [K("Qb")])
                yield
                cur = nx
            ph("seq")
            for hi, h in enumerate(heads):
                pr, hh = hp(h)
                o = pb[b0][:, hi * 64:(hi + 1) * 64]
                mm(o, at_[pr, hh, :], Sb[pr, hh, :], True, False, ["at" + sfx, SbK], [kb(b0)])
                mm(o, LkT_[:, hi, :], Vt[:, h * 64:(h + 1) * 64], False, True, [K("LkT"), "Vt" + sfx], [kb(b0)])
            yield
            cp("act", Xb_[:], pb[b0][:, 0:256], [kb(b0)], [K("Xb")])
            yield
            for hi, h in enumerate(heads):
                mm(pb[b1][:, hi * 64:(hi + 1) * 64], Qb_[:, hi, :], Xb_[:, hi * 64:(hi + 1) * 64], True, True, [K("Qb"), K("Xb")], [kb(b1)])
            yield
            cp("dve", SAb_[:], pb[b1][:, 0:256], [kb(b1)], [K("SAb")])
            yield
            for hi, h in enumerate(heads):
                pr, hh = hp(h)
                o = pb[b2][pr, (hh - 2 * g) * 128:(hh - 2 * g) * 128 + 128]
                mm(o, Sb[pr, hh, :], rt_[pr, hh, :], True, False, [SbK, "rt" + sfx], [kb(b2)])
                mm(o, SAb_[:, hi * 64:(hi + 1) * 64], MbT_[:, hi, :], False, False, [K("SAb"), K("MbT")], [kb(b2)])
                mm(o, Vt[:, h * 64:(h + 1) * 64], MkT_[:, hi, :], False, True, ["Vt" + sfx, K("MkT")], [kb(b2)])
            yield
            cp("act", yT[:, 2 * g:2 * g + 2, :], pb[b2][:, 0:256].rearrange("p (a b) -> p a b", b=128), [kb(b2)], [yK])
            for hi, h in enumerate(heads):
                pr, hh = hp(h)
                o = pb[b0][pr, (hh - 2 * g) * 64:(hh - 2 * g) * 64 + 64]
                mm(o, Bt[:, h * 64:(h + 1) * 64], SAb_[:, hi * 64:(hi + 1) * 64], True, False, ["Bt" + sfx, K("SAb")], [kb(b0)])
                mm(o, Kt[:, h * 64:(h + 1) * 64], Vt[:, h * 64:(h + 1) * 64], False, True, ["Kt" + sfx, "Vt" + sfx], [kb(b0)])
            yield
            gs = slice(2 * g, 2 * g + 2)
            tt("dve", Sf[:, gs, :], Sf[:, gs, :], bc(gC[:, gs].unsqueeze(2), [128, 2, 64]), ALU.mult, [SfK, "gC" + sfx], [SfK])
            tt("dve", Sf[:, gs, :], Sf[:, gs, :], pb[b0][:, 0:128].rearrange("p (a b) -> p a b", b=64), ALU.add, [SfK, kb(b0)], [SfK])
            yield
            cp("act", Sb[:, gs, :], Sf[:, gs, :], [SfK], [SbK])
            yield


        def tail(NT, xsrc_tiles, ydst_tiles, nrows):
            ph("tail")
            for q in range(2):
                sgr = wload(wi_v, 8, (45 + 4 * q) * 128, 512, ikeys((45 + 4 * q) * 128, 512))
                sbr = wload(wbr_v, 8, q * 512, 512, ["scr_o"])
                sgc = wload(wi_v, 8, (53 + 4 * q) * 128, 512, ikeys((53 + 4 * q) * 128, 512))
                sbc = wload(wbc_v, 4, q * 512, 512, ["scr_o"])
                for jj in range(4):
                    j = q * 4 + jj
                    od = j % 2
                    bA, bB, bC, bD = (0, 1, 2, 3) if od == 0 else (4, 5, 6, 3)
                    sg1, sg1k = (sgb, "sgb")
                    sg2, sg2k = (alb, "alb")
                    m1_ = TT[5 + od][:].rearrange("p a b -> p (a b)")
                    m1k = "T%d" % (5 + od)
                    t2_, t2k = ((tmpc, "tmpc"), (tmpb, "tmpb"))[od]
                    project(45 + j, sgr, jj, NT, bA)
                    act(sg1[:, 0:NT], pb[bA][:, 0:NT], AF.Sigmoid, ["pb%d" % bA], [sg1k])
                    for fc in range(8):
                        mm(pb[bB][:, 0:NT], wb[sbr][:, fc, jj * 128:(jj + 1) * 128], orT[:, fc, 0:NT], fc == 0, fc == 7,
                           ["wb%d" % sbr, "orT"], ["pb%d" % bB])
                    tt("dve", m1_[:, 0:NT], pb[bB][:, 0:NT], sg1[:, 0:NT], ALU.mult, ["pb%d" % bB, sg1k], [m1k])
                    project(53 + j, sgc, jj, NT, bC)
                    act(sg2[:, 0:NT], pb[bC][:, 0:NT], AF.Sigmoid, ["pb%d" % bC], [sg2k])
                    for fc in range(4):
                        mm(pb[bD][:, 0:NT], wb[sbc][:, fc, jj * 128:(jj + 1) * 128], ocT[:, fc, 0:NT], fc == 0, fc == 3,
                           ["wb%d" % sbc, "kS"], ["pb%d" % bD])
                    tt("dve", t2_[:, 0:NT], pb[bD][:, 0:NT], sg2[:, 0:NT], ALU.mult, ["pb%d" % bD, sg2k], [t2k])
                    tt("pool", mT[:, j, 0:NT], m1_[:, 0:NT], t2_[:, 0:NT], ALU.add, [m1k, t2k], ["rS"])
            so = [wload(wo_v, 8, 0, 512, ["scr_o"]), wload(wo_v, 8, 512, 512, ["scr_o"])]
            npg = TT[2][:].rearrange("p a b -> p (a b)")
            dma(npg[:, :], npg_d.partition_broadcast(128), [], ["T2"])
            for i, (xsrc, ydst) in enumerate(zip(xsrc_tiles, ydst_tiles)):
                xb = xt[i % 2]
                xk = "xt%d" % (i % 2)
                dma(xb[0:nrows, :], xsrc, [], [xk])
                tsl = slice(i * 128, i * 128 + nrows)
                bks = [(4, 5), (0, 1), (2, 3)][i % 3]
                so_ = 32 + 8 * (i % 3)
                stg_i = (0, 1, 3)[i % 3]
                stg = TT[stg_i][:].rearrange("p a b -> p (a b)")
                sk_ = "T%d" % stg_i
                for hf in range(2):
                    for fc in range(8):
                        mm(pb[bks[hf]][0:nrows, :], mT[:, fc, tsl], wb[so[hf]][:, fc, :], fc == 0, fc == 7,
                           ["rS", "wb%d" % so[hf]], ["pb%d" % bks[hf]])
                for hf in range(2):
                    act(hb[0:nrows, hf * 512:(hf + 1) * 512], pb[bks[hf]][0:nrows, :], AF.Square, ["pb%d" % bks[hf]], ["hb", "small"],
                        accum=small[0:nrows, so_ + hf:so_ + hf + 1])
                tt("dve", small[0:nrows, so_ + 2:so_ + 3], small[0:nrows, so_:so_ + 1], small[0:nrows, so_ + 1:so_ + 2], ALU.add, ["small"], ["small"])
                ts("dve", small[0:nrows, so_ + 3:so_ + 4], small[0:nrows, so_ + 2:so_ + 3], 1.0 / D, 1e-6, ALU.mult, ALU.add, ["small"], ["small"])
                rsq(small[0:nrows, so_ + 4:so_ + 5], small[0:nrows, so_ + 3:so_ + 4], 0.0, ["small"], "small")
                for hf in range(2):
                    hsl = slice(hf * 512, (hf + 1) * 512)
                    stt("dve", stg[0:nrows, hsl], pb[bks[hf]][0:nrows, :], small[0:nrows, so_ + 4:so_ + 5], npg[0:nrows, hsl], ALU.mult, ALU.mult,
                        ["pb%d" % bks[hf], "small", "T2"], [sk_])
                tt("pool", stg[0:nrows, :], stg[0:nrows, :], xb[0:nrows, :], ALU.add, [sk_, xk], [sk_])
                dma(ydst, stg[0:nrows, :], [sk_], [], q="pool")

        def rms_phase(sc):
            t0 = sc * 512
            for i in range(4):
                xb = xt[i % 2]; xk = "xt%d" % (i % 2)
                dma(xb[:], xp[t0 + i * 128:t0 + (i + 1) * 128, :], [], [xk])
                last = (sc == 3 and i == 3)

                def hout(xb=xb, xk=xk):
                    T0v = TT[0][:].rearrange("p a b -> p (a b)")
                    dma(T0v[:, :], npre_d.partition_broadcast(128), [], ["T0"])
                    ts("dve", TT[1][:].rearrange("p a b -> p (a b)"), xb[:], small[:, 2:3], None, ALU.mult, None, [xk, "small"], ["T1"])
                    tt("dve", TT[1][:].rearrange("p a b -> p (a b)"), TT[1][:].rearrange("p a b -> p (a b)"), T0v, ALU.mult, ["T1", "T0"], ["T1"])
                    dma(nsp, TT[1][:].rearrange("p a b -> p (a b)")[127:128, :], ["T1"], [])
                rmsnorm_tile(xb, xk, 128, (i * 128, (i + 1) * 128), hout if last else None)

        rms_phase(0)
        run_all([prologue_gen(1)])
        for sc in range(4):
            t0 = sc * 512
            if sc > 0:
                rms_phase(sc)
            stage(3 if sc == 0 else 11)
            if sc > 0:
                cp("pool", tmpc[:, 0:120].rearrange("p (c w) -> p c w", w=30), uex[:, :, 512:542], ["uex"], ["tmpc"])
                cp("pool", uex[:, :, 0:30], tmpc[:, 0:120].rearrange("p (c w) -> p c w", w=30), ["tmpc"], ["uex"])
            pg = proj_phase(512, False, sc % 2, (sc + 1) % 2)
            side = prologue_gen(2) if sc == 0 else None
            pr0 = None
            step = 0
            while True:
                try:
                    next(pg)
                except StopIteration:
                    break
                step += 1
                if side is not None:
                    try:
                        next(side)
                    except StopIteration:
                        side = None
                if step == 33:
                    pr0 = prep_gen(0, 128, False, PS[0])
                if pr0 is not None:
                    try:
                        next(pr0)
                    except StopIteration:
                        pr0 = None
            rest = [g_ for g_ in (side, pr0) if g_ is not None]
            run_all(rest)
            stage(4 if sc == 0 else 11)
            op("pool", lambda e: e.memset(dummy[:, 0:1], 0.0), [], ["wb3", "wb0", "wb1", "xt0", "xt1", "ua", "uaA", "uaB", "dummy", "yT0", "yT1", "yT2", "yT3"] + SETB_KEYS + PS1_KEYS)
            yTp = xt[0][:, :].rearrange("p (a b) -> p a b", b=128)
            Gp = (xt[1][:, :].rearrange("p (a b) -> p a b", b=128),
                  ua[:, 0:2, :].rearrange("p a b -> p (a b)").rearrange("p (a b) -> p a b", b=128),
                  ua[:, 2:4, :].rearrange("p a b -> p (a b)").rearrange("p (a b) -> p a b", b=128))
            Gpk = ("xt1", "uaA", "uaB")

            def chunk_scan(c4):
                PSp = PS[c4 % 2]
                for pair in ((0, 1), (2, 3)):
                    gens = [scan_group(pair[0], SETS[0], PSp, yTp), scan_group(pair[1], SETS[1], PSp, yTp)]
                    while gens:
                        for gq in list(gens):
                            try:
                                next(gq)
                                yield
                            except StopIteration:
                                gens.remove(gq)
                yield from gn_gen(yTp, ["yT0", "yT1", "yT2", "yT3"], c4 * 128, 128, Gp, Gpk, PSp["bon"], "bon" + PSp["sfx"])

            for c4 in range(4):
                main = chunk_scan(c4)
                side = prep_gen((c4 + 1) * 128, 128, False, PS[(c4 + 1) % 2]) if c4 < 3 else None
                RATIO = int(os.environ.get("MK_RATIO", "4"))
                done = False
                while not done:
                    for _ in range(RATIO):
                        try:
                            next(main)
                        except StopIteration:
                            done = True
                            break
                    if side is not None:
                        try:
                            next(side)
                        except StopIteration:
                            side = None
                if side is not None:
                    run_all([side])
            op("pool", lambda e: e.memset(dummy[:, 1:2], 0.0), [], ["wb3", "wb0", "wb1", "xt0", "xt1", "ua", "uaA", "uaB", "dummy", "yT0", "yT1", "yT2", "yT3"] + SETB_KEYS + PS1_KEYS)
            stage(8 if sc == 0 else 11)
            ph("conv")
            cp("pool", ubf[:], uex[:], ["uex"], ["ubf"])
            for c in range(4):
                for w in range(31):
                    s = (c * 31 + w) % 4
                    if w % 2 == 0:
                        act(dg[s][:], idb[:], AF.Copy, ["idb", "col"], ["dg%d" % s], scale=col[:, O_CW + c * 31 + w:O_CW + c * 31 + w + 1])
                    else:
                        ts("dve", dg[s][:], idb[:], col[:, O_CW + c * 31 + w:O_CW + c * 31 + w + 1], None, ALU.mult, None,
                           ["idb", "col"], ["dg%d" % s])
                    mm(pb[6][:, :], dg[s][:], ubf[:, c, w:w + 512], w == 0, w == 30, ["dg%d" % s, "ubf"], ["pb6"])
                act(TT[c][:].rearrange("p a b -> p (a b)")[:, 0:512], pb[6][:, :], AF.Identity, ["pb6", "col"], ["T%d" % c],
                    bias=col[:, O_CB + c:O_CB + c + 1])
            ln_conv_out(512, [TT[c][:].rearrange("p a b -> p (a b)")[:, 0:512] for c in range(4)], ["T0", "T1", "T2", "T3"])
            if sc == 3:
                for c in range(4):
                    mm(pb[6][0:30, c * 128:(c + 1) * 128], uex[:, c, 512:542], C(C_ID), True, True, ["uex", "cst"], ["pb6"])
                cp("dve", tmpc[0:30, :], pb[6][0:30, :], ["pb6"], ["tmpc"])
                dma(ncp, tmpc[0:30, :], ["tmpc"], [])
            stage(9 if sc == 0 else 11)
            tail(512, [xp[t0 + i * 128:t0 + (i + 1) * 128, :] for i in range(4)],
                 [yp[t0 + i * 128:t0 + (i + 1) * 128, :] for i in range(4)], 128)

        stage(12)
        for h in range(16):
            hl, hh = h % 2, h // 2
            pr = slice(hl * 64, hl * 64 + 64)
            mm(pb[0][pr, hh * 64:(hh + 1) * 64], Sf[pr, hh, :], cst[pr, C_ID + hl * 64:C_ID + hl * 64 + 64], True, True, ALLSF + ["cst"], ["pb0"])
        cp("dve", tmpc[:, :], pb[0][:, :], ["pb0"], ["tmpc"])
        dma(nwp.rearrange("(hh p) j -> p hh j", p=128), tmpc[:, :].rearrange("p (a b) -> p a b", b=64), ["tmpc"], [])

        stage(13)
        ph("sample")
        xb = xt[0]
        dma(xb[0:NS, :], xs, [], ["xt0"])

        def hout_s():
            T0v = TT[0][:].rearrange("p a b -> p (a b)")
            T1v = TT[1][:].rearrange("p a b -> p (a b)")
            dma(T0v[0:NS, :], npre_d.partition_broadcast(NS), [], ["T0"])
            ts("dve", T1v[0:NS, :], xb[0:NS, :], small[0:NS, 2:3], None, ALU.mult, None, ["xt0", "small"], ["T1"])
            tt("dve", T1v[0:NS, :], T1v[0:NS, :], T0v[0:NS, :], ALU.mult, ["T1", "T0"], ["T1"])
            dma(nss, T1v[0:NS, :], ["T1"], [])
        rmsnorm_tile(xb, "xt0", NS, (0, NS), hout_s)
        dma(xt[1][0:NS, :], sshift, [], ["xt1"])
        cp("dve", hb[0:NS, :], xt[1][0:NS, :], ["xt1"], ["hb"])
        for dc in range(8):
            op("pe", lambda e, dc=dc: e.transpose(ptb[:, dc * 128:dc * 128 + NS], hb[0:NS, dc * 128:(dc + 1) * 128], idb[0:NS, 0:NS]),
               ["hb", "idb"], ["ptb"])
        cp("act", hT[:, :, NS:2 * NS], ptb[:, :].rearrange("p (a b) -> p a b", b=128)[:, :, 0:NS], ["ptb"], ["hT"])
        uv = [uex[:, c, 0:NS * 31].rearrange("p (n w) -> p n w", w=31) for c in range(4)]
        for q in range(4):
            dma(xt[1][0:120, 0:512], sconv[q * 120:(q + 1) * 120, :], [], ["xt1"])
            for c in range(4):
                mm(pb[6][:, c * 120:(c + 1) * 120], xt[1][0:120, c * 128:(c + 1) * 128], cst[0:120, C_ID:C_ID + 120], True, True,
                   ["xt1", "cst"], ["pb6"])
            for c in range(4):
                cp("dve", uv[c][:, q * 4:(q + 1) * 4, 0:30], pb[6][:, c * 120:(c + 1) * 120].rearrange("p (n w) -> p n w", w=30),
                   ["pb6"], ["uex"])
        dma(ncs[:, 0:29, :], sconv.rearrange("(n w) c -> n w c", w=30)[:, 1:30, :], [], [])
        run_all([proj_phase(2 * NS, True, 0, 0)])
        stage(14)
        run_all([prep_gen(0, NS, True, PS[0])])
        EG, EnG, EGe, k2, aa, bb = TT[1], TT[2], TT[3], TT[5], TT[6], TT[7]
        SW = [ua[:, i, :].rearrange("p (a b) -> p a b", b=64) for i in range(2)]
        Dxs = [Vt[:, 0:512], tmpb[:, :], Bt[:, 0:512], Kt[:, 0:512], Vt[:, 512:1024]]
        Dxk = ["Vt_0", "tmpb", "Bt_0", "Kt_0", "Vt_0"]
        Dxo = [bob[:], C(C_BO), bob[:], bob[:], bob[:]]
        Dxok = ["bob", "cst", "bob", "bob", "bob"]
        yTs = TT[4]
        i2b = bc(cst[:, C_I2:C_I2 + 64].unsqueeze(1), [128, 8, 64])
        op("pool", lambda e: e.memset(dummy[:, 2:3], 0.0), [], ["ua", "uaA", "uaB", "dummy"])
        for n in range(NS):
            Sw = SW[n % 2]; sk = ("uaA", "uaB")[n % 2]
            dma(Sw, swkv[n].rearrange("(hh p) j -> p hh j", p=128), [], [sk])
            vecs = [(aa, "T6"), (EG, "T1"), (bb, "T7"), (k2, "T5"), (rS, "rS")]
            for vi, (vt_, vk) in enumerate(vecs):
                tt("pool", Dxs[vi].rearrange("p (a b) -> p a b", b=64), i2b, bc(vt_[:, :, n:n + 1], [128, 8, 64]), ALU.mult,
                   ["cst", vk], [Dxk[vi]])
                mm(pb[vi][:, :], Dxo[vi], Dxs[vi], True, True, [Dxok[vi], Dxk[vi]], ["pb%d" % vi])
            v8 = lambda p: p[:].rearrange("p (a b) -> p a b", b=64)
            W3 = TT[0][:, :, 0:64]
            tt("dve", W3, Sw, v8(pb[0]), ALU.mult, [sk, "pb0"], ["T0"])
            op("dve", lambda e: e.tensor_reduce(out=small[:, 16:24], in_=TT[0][:, :, 0:64], axis=AX.X, op=ALU.add), ["T0"], ["small"])
            tt("dve", Sw, Sw, v8(pb[1]), ALU.mult, [sk, "pb1"], [sk])
            tt("dve", W3, v8(pb[2]), bc(small[:, 16:24].unsqueeze(2), [128, 8, 64]), ALU.mult, ["pb2", "small"], ["T0"])
            tt("pool", Sw, Sw, W3, ALU.add, [sk, "T0"], [sk])
            cp("dve", small[:, 24:32], vS[:, :, n], ["vS"], ["small"])
            tt("dve", W3, v8(pb[3]), bc(small[:, 24:32].unsqueeze(2), [128, 8, 64]), ALU.mult, ["pb3", "small"], ["T0"])
            tt("pool", Sw, Sw, W3, ALU.add, [sk, "T0"], [sk])
            dma(nws[n].rearrange("(hh p) j -> p hh j", p=128), Sw, [sk], [])
            tt("dve", W3, Sw, v8(pb[4]), ALU.mult, [sk, "pb4"], ["T0"])
            op("dve", lambda e, n=n: e.tensor_reduce(out=yTs[:, :, n], in_=TT[0][:, :, 0:64], axis=AX.X, op=ALU.add), ["T0"], ["T4_0"])
        stage(15)
        op("pool", lambda e: e.memset(dummy[:, 3:4], 0.0), [], ["ua", "uaA", "uaB", "dummy"])
        run_all([gn_gen(yTs, ALLT4, 0, NS, (TT[1], TT[2], TT[3]), ("T1", "T2", "T3"), bon, "bon_0")])
        stage(16)
        cf = []
        for c in range(4):
            cwb = bc(col[:, O_CW + c * 31:O_CW + (c + 1) * 31].unsqueeze(1), [128, NS, 31])
            tt("dve", tmpc[:, 0:NS * 31].rearrange("p (n w) -> p n w", w=31), uv[c], cwb, ALU.mult, ["uex", "col"], ["tmpc"])
            cfc = TT[c][:].rearrange("p a b -> p (a b)")[:, 0:NS]
            op("dve", lambda e, cfc=cfc: e.tensor_reduce(out=cfc, in_=tmpc[:, 0:NS * 31].rearrange("p (n w) -> p n w", w=31),
                                                        axis=AX.X, op=ALU.add), ["tmpc"], ["T%d" % c])
            ts("dve", cfc, cfc, col[:, O_CB + c:O_CB + c + 1], None, ALU.add, None, ["T%d" % c, "col"], ["T%d" % c])
            cf.append(cfc)
        for c in range(4):
            cp("dve", tmpb[:, c * NS:(c + 1) * NS], uv[c][:, :, 30], ["uex"], ["tmpb"])
        for c in range(4):
            mm(pb[6][0:NS, c * 128:(c + 1) * 128], tmpb[:, c * NS:(c + 1) * NS], C(C_ID), True, True, ["tmpb", "cst"], ["pb6"])
        cp("dve", m1[0:NS, 0:512], pb[6][0:NS, :], ["pb6"], ["T5"])
        dma(ncs[:, 29, :], m1[0:NS, 0:512], ["T5"], [])
        ln_conv_out(NS, cf, ["T0", "T1", "T2", "T3"])
        stage(17)
        tail(NS, [xs], [ys], NS)
        P.emit()
    return nc


def _prep_consts():
    c = np.zeros((128, NCONST), np.float32)
    idx = np.arange(128)
    c[:, C_ID:C_ID + 128] = np.eye(128)
    c[:, C_SL:C_SL + 128] = (idx[None, :] < idx[:, None])
    c[:, C_SU:C_SU + 128] = (idx[:, None] < idx[None, :])
    c[:, C_UI:C_UI + 128] = (idx[:, None] <= idx[None, :])
    c[:, C_TRI:C_TRI + 128] = (idx[:, None] <= idx[None, :]) * CNEG
    c[:, C_TRE:C_TRE + 128] = (idx[:, None] < idx[None, :]) * CNEG
    blk = (idx[:, None] // 64 == idx[None, :] // 64).astype(np.float32)
    c[:, C_BM:C_BM + 128] = blk / 64.0
    c[:, C_BO:C_BO + 128] = blk
    c[:, C_AM:C_AM + 128] = 1.0 / 512.0
    c[:, C_NI:C_NI + 64] = np.eye(128)[:, :64] * CNEG
    c[:, C_I2:C_I2 + 64] = (idx[:, None] % 64 == np.arange(64)[None, :])
    return c


_NC = None


def kernel(x_prompt, x_sample, state_shift, state_wkv, state_conv, norm_pre_g, w_in, mu_shift,
           decay_w0, decay_w2, iclr_a0, iclr_a2, k_k, k_a, r_k, gn_g, gn_b, conv_glu_b, conv_w,
           conv_b, ln_c_g, ln_c_b, w_branch_r, w_branch_c, w_out, norm_post_g):
    global _NC
    f = lambda a: np.ascontiguousarray(np.asarray(a, dtype=np.float32))
    colv = lambda v, n: f(v).reshape(n, 128).T
    cols = np.zeros((128, NCOL), np.float32)
    cols[:, O_MU:O_MU + 33] = colv(mu_shift[0], 33)
    cols[:, O_KK:O_KK + 8] = colv(k_k[0], 8)
    cols[:, O_KA:O_KA + 8] = colv(k_a[0], 8)
    cols[:, O_RK:O_RK + 8] = colv(np.asarray(r_k[0]).reshape(-1), 8)
    cols[:, O_GNG:O_GNG + 8] = colv(gn_g[0], 8)
    cols[:, O_GNB:O_GNB + 8] = colv(gn_b[0], 8)
    cols[:, O_A0:O_A0 + 8] = colv(iclr_a0[0], 8)
    cols[:, O_GLUB:O_GLUB + 8] = colv(conv_glu_b[0], 8)
    cols[:, O_CB:O_CB + 4] = colv(conv_b[0], 4)
    cols[:, O_LNG:O_LNG + 4] = colv(ln_c_g[0], 4)
    cols[:, O_LNB:O_LNB + 4] = colv(ln_c_b[0], 4)
    cw = f(conv_w[0])
    cols[:, O_CW:O_CW + 124] = cw.reshape(31, 4, 128).transpose(2, 1, 0).reshape(128, 124)
    cols[:, O_GPRE:O_GPRE + 8] = colv(norm_pre_g[0], 8)
    consts = _prep_consts()
    w2ext = np.concatenate([f(decay_w2[0]), f(decay_w0[0])[None, :]], axis=0)
    shared = {
        "w_in": f(w_in[0]), "w_br": f(w_branch_r[0]), "w_bc": f(w_branch_c[0]), "w_out": f(w_out[0]),
        "cols": cols, "consts": consts, "w2ext": f(w2ext), "a2": f(iclr_a2[0]),
        "npg": f(norm_post_g[0])[None, :], "npre": f(norm_pre_g[0])[None, :],
    }
    xpf = f(x_prompt); xsf = f(x_sample).reshape(128, D); ssf = f(state_shift[0])
    swf = f(state_wkv[0]).reshape(128, 1024, 64); scf = f(state_conv[0]).reshape(128 * 30, 512)
    in_maps = []
    for c in range(8):
        m = dict(shared)
        m["xp"] = xpf[c]
        m["xs"] = xsf[c * NS:(c + 1) * NS]
        m["sshift"] = ssf[c * NS:(c + 1) * NS]
        m["swkv"] = swf[c * NS:(c + 1) * NS]
        m["sconv"] = scf[c * NS * 30:(c + 1) * NS * 30]
        in_maps.append(m)
    if _NC is None:
        _NC = build()
    res = run_bass_kernel_spmd(_NC, in_maps, core_ids=list(range(8)))
    R = res.results
    y_prompt = np.stack([R[c]["yp"] for c in range(8)]).astype(np.float32)
    y_sample = np.concatenate([R[c]["ys"] for c in range(8)]).reshape(128, 1, D).astype(np.float32)
    nsp_ = np.concatenate([R[c]["nsp"] for c in range(8)]).reshape(1, 8, D).astype(np.float32)
    nwp_ = np.stack([R[c]["nwp"] for c in range(8)]).reshape(1, 8, 16, 64, 64).astype(np.float32)
    ncp_ = np.stack([R[c]["ncp"] for c in range(8)]).reshape(1, 8, 30, 512).astype(np.float32)
    nss_ = np.concatenate([R[c]["nss"] for c in range(8)]).reshape(1, 128, D).astype(np.float32)
    nws_ = np.concatenate([R[c]["nws"] for c in range(8)]).reshape(1, 128, 16, 64, 64).astype(np.float32)
    ncs_ = np.concatenate([R[c]["ncs"] for c in range(8)]).reshape(1, 128, 30, 512).astype(np.float32)
    return (y_prompt, y_sample, nsp_, nwp_, ncp_, nss_, nws_, ncs_)
```

```python
import contextlib
import numpy as np
import concourse.bass as bass
import concourse.mybir as mybir
from concourse.bass_utils import run_bass_kernel_spmd

F32 = mybir.dt.float32
BF16 = mybir.dt.bfloat16
AF = mybir.ActivationFunctionType
ALU = mybir.AluOpType
AX = mybir.AxisListType

D = 1024
NIN = 7808
SEQ = 2048
NS = 16
NCH = 61
CNEG = -0.6065306597126334

O_MU = 0; O_KK = 33; O_KA = 41; O_RK = 49; O_GNG = 57; O_GNB = 65; O_A0 = 73; O_GLUB = 81
O_CB = 89; O_LNG = 93; O_LNB = 97; O_CW = 101; O_GPRE = 225; O_OMM = 233; O_OMKA = 266; NCOL = 274
C_ID = 0; C_SL = 128; C_SU = 256; C_UI = 384; C_TRI = 512; C_TRE = 640; C_BM = 768; C_BO = 896
C_AM = 1024; C_NI = 1152; C_I2 = 1280; NCONST = 1344


class Prog:
    ENG = ("pe", "act", "dve", "pool", "sp")

    def __init__(self, nc):
        self.nc = nc
        self.ops = {e: [] for e in self.ENG}
        self.cnt = {e: 0 for e in self.ENG}
        self.waited = {e: {} for e in self.ENG}
        self.lastw = {}
        self.readers = {}
        self.dcnt = {}
        self.dead = False
        self.phase = ""
        self.annotate = False

    def _need(self, eng, waits, tok):
        if tok is None:
            return
        kind, key, val = tok
        if kind == "e" and key == "pe" and eng == "pe":
            return
        k = (kind, key)
        if self.waited[eng].get(k, 0) >= val:
            return
        if waits.get(k, 0) < val:
            waits[k] = val

    def op(self, eng, fn, reads=(), writes=(), dma=None, tag=None):
        if self.dead:
            return None
        waits = {}
        if eng == "pe":
            prev = getattr(self, "petag", None)
            if tag is not None and prev is not None and tag != prev:
                waits[("e", "pe")] = self.cnt["pe"]
            self.petag = tag
        for r in reads:
            self._need(eng, waits, self.lastw.get(r))
        for w in writes:
            self._need(eng, waits, self.lastw.get(w))
            for rd in self.readers.get(w, ()):
                self._need(eng, waits, rd)
        for k, v in waits.items():
            self.waited[eng][k] = v
        if dma is not None:
            prevc = self.dcnt.get(dma, 0)
            if prevc > 0 and self.waited[eng].get(("d", dma), 0) < prevc:
                waits[("d", dma)] = max(waits.get(("d", dma), 0), prevc)
                self.waited[eng][("d", dma)] = prevc
            self.dcnt[dma] = prevc + 1
            tok = ("d", dma, self.dcnt[dma])
        else:
            self.cnt[eng] += 1
            tok = ("e", eng, self.cnt[eng])
        self.ops[eng].append((waits, fn, tok, self.phase))
        for r in reads:
            self.readers.setdefault(r, []).append(tok)
        for w in writes:
            self.lastw[w] = tok
            self.readers[w] = []
        return tok

    def emit(self):
        nc = self.nc
        with contextlib.ExitStack() as st:
            esem = {e: st.enter_context(nc.semaphore("s_" + e)) for e in self.ENG}
            dsem = {k: st.enter_context(nc.semaphore("d_" + str(k))) for k in self.dcnt}
            block = st.enter_context(nc.Block())

            def run(engname, e):
                for waits, fn, tok, ph in self.ops[engname]:
                    for (kind, key), val in waits.items():
                        if kind == "e":
                            e.wait_ge(esem[key], val)
                        else:
                            e.wait_ge(dsem[key], 16 * val)
                    ins = fn(e)
                    if self.annotate:
                        ins.annotate(ph)
                    if tok[0] == "e":
                        ins.then_inc(esem[tok[1]], 1)
                    else:
                        ins.then_inc(dsem[tok[1]], 16)
                if engname == "sp":
                    for k, c in self.dcnt.items():
                        e.wait_ge(dsem[k], 16 * c)

            @block.tensor
            def _(e):
                run("pe", e)

            @block.scalar
            def _(e):
                run("act", e)

            @block.vector
            def _(e):
                run("dve", e)

            @block.gpsimd
            def _(e):
                run("pool", e)

            @block.sync
            def _(e):
                run("sp", e)


def build():
    nc = bass.Bass("TRN2", target_bir_lowering=False)
    di = lambda n, s: nc.dram_tensor(n, s, F32, kind="ExternalInput").ap()
    do = lambda n, s: nc.dram_tensor(n, s, F32, kind="ExternalOutput").ap()
    xp = di("xp", [SEQ, D]); xs = di("xs", [NS, D]); sshift = di("sshift", [NS, D])
    swkv = di("swkv", [NS, 1024, 64]); sconv = di("sconv", [NS * 30, 512])
    w_in = di("w_in", [D, NIN]); w_br = di("w_br", [D, D]); w_bc = di("w_bc", [512, D]); w_out = di("w_out", [D, D])
    cols_d = di("cols", [128, NCOL]); consts_d = di("consts", [128, NCONST])
    w2ext_d = di("w2ext", [65, 1024]); a2_d = di("a2", [64, 1024])
    npg_d = di("npg", [1, D]); npre_d = di("npre", [1, D])
    yp = do("yp", [SEQ, D]); ys = do("ys", [NS, D]); nsp = do("nsp", [1, D])
    nwp = do("nwp", [1024, 64]); ncp = do("ncp", [30, 512]); nss = do("nss", [NS, D])
    nws = do("nws", [NS, 1024, 64]); ncs = do("ncs", [NS, 30, 512])
    wi_s = nc.dram_tensor("wi_s", [D, NIN], BF16).ap()
    wbr_s = nc.dram_tensor("wbr_s", [D, D], BF16).ap()
    wbc_s = nc.dram_tensor("wbc_s", [512, D], BF16).ap()
    wo_s = nc.dram_tensor("wo_s", [D, D], BF16).ap()

    with contextlib.ExitStack() as st:
        def T(n, s, d=F32):
            return st.enter_context(nc.sbuf_tensor("sb_" + n, s, d))
        P = Prog(nc)
        op = P.op
        cst = T("cst", [128, NCONST]); col = T("col", [128, NCOL])
        idb = T("idb", [128, 128], BF16)
        bob = T("bob", [128, 128], BF16)
        bmb = T("bmb", [128, 128], BF16)
        w2e = T("w2e", [65, 1024]); a2b = T("a2b", [128, 1024], BF16)
        wb = [T("wb%d" % i, [128, 8, 512], BF16) for i in range(4)]
        xt = [T("xt%d" % i, [128, D]) for i in range(2)]
        hb = T("hb", [128, D], BF16)
        hT = T("hT", [128, 8, 512], BF16)
        rS = T("rS", [128, 8, 512], BF16); kS = T("kS", [128, 8, 512], BF16)
        vS = T("vS", [128, 8, 512], BF16); zrS = T("zrS", [128, 8, 512], BF16)
        twl = T("twl", [65, 512]); alb = T("alb", [128, 512], BF16)
        ua = T("ua", [128, 4, 512]); uex = T("uex", [128, 4, 542]); ubf = T("ubf", [128, 4, 542], BF16)
        szc = T("szc", [128, 4, 512], BF16)
        orT = T("orT", [128, 8, 512], BF16)
        mT = rS
        ocT = kS
        TT = [T("T%d" % i, [128, 8, 128]) for i in range(8)]
        bon = T("bon", [128, 8, 128], BF16)
        rt_ = T("rt_", [128, 8, 128], BF16); at_ = T("at_", [128, 8, 128], BF16); bt_ = T("bt_", [128, 8, 128], BF16)
        kt_ = T("kt_", [128, 8, 128], BF16); bh_ = T("bh_", [128, 8, 128], BF16); kh_ = T("kh_", [128, 8, 128], BF16)
        Vt = T("Vt", [128, 1024], BF16); Bt = T("Bt", [128, 1024], BF16); Kt = T("Kt", [128, 1024], BF16)
        Ak = [T("Ak%d" % i, [128, 4, 128], BF16) for i in range(2)]
        Nk = [T("Nk%d" % i, [128, 4, 128], BF16) for i in range(2)]
        Qb = T("Qb", [128, 4, 128], BF16)
        LkT = T("LkT", [128, 4, 128], BF16); MbT = T("MbT", [128, 4, 128], BF16); MkT = T("MkT", [128, 4, 128], BF16)
        Xb = T("Xb", [128, 256], BF16); SAb = T("SAb", [128, 256], BF16)
        Xb2 = T("Xb2", [128, 256], BF16); SAb2 = T("SAb2", [128, 256], BF16)
        dummy = T("dummy", [128, 8])
        _w3 = lambda i: wb[3][:, i, :].rearrange("p (a b) -> p a b", b=128)
        SETS = [
            {"Ak": Ak, "Nk": Nk, "Qb": Qb, "LkT": LkT, "MbT": MbT, "MkT": MkT, "Xb": Xb, "SAb": SAb, "banks": (0, 1, 2), "n": "_A"},
            {"Ak": [_w3(0), _w3(1)], "Nk": [_w3(2), _w3(3)], "Qb": _w3(4), "LkT": _w3(5), "MbT": _w3(6), "MkT": _w3(7),
             "Xb": Xb2, "SAb": SAb2, "banks": (3, 4, 5), "n": "_B"},
        ]
        SETB_KEYS = [k + "_B" for k in ("Ak0", "Ak1", "Nk0", "Nk1", "Qb", "LkT", "MbT", "MkT")]
        ALLT4 = ["T4_0", "T4_1", "T4_2", "T4_3"]
        ALLSF = ["Sf0", "Sf1", "Sf2", "Sf3"]
        ALLSB = ["Sb0", "Sb1", "Sb2", "Sb3"]
        Sf = T("Sf", [128, 8, 64]); Sb = T("Sb", [128, 8, 64], BF16)
        gC = T("gC", [128, 8]); tmpb = T("tmpb", [128, 512]); tmpc = T("tmpc", [128, 512])
        sgb = T("sgb", [128, 512], BF16)
        m1 = TT[5][:].rearrange("p a b -> p (a b)")
        pprev = [T("pprev%d" % i, [128, 40]) for i in range(2)]
        small = T("small", [128, 64])
        dg = [T("dg%d" % i, [128, 128], BF16) for i in range(4)]
        wld = xt
        pb = [st.enter_context(nc.psum_tensor("pb%d" % i, [128, 512], F32)) for i in range(7)]
        ptb = st.enter_context(nc.psum_tensor("ptb", [128, 1024], BF16))

        cnt = {"d": 0, "e": 0}
        import os
        STOP = float(os.environ.get("MK_STOP", "1000"))

        def stage(k):
            if k > STOP:
                P.dead = True
        P.annotate = bool(os.environ.get("MK_ANN"))

        def ph(name):
            P.phase = name

        def dma(out, in_, reads, writes, q="sp"):
            cnt[q] = cnt.get(q, 0) + 1
            key = "%s%d" % (q, cnt[q] % (16 if q == "sp" else 8))
            if q == "act":
                return op("act", lambda e: e.dma_start(out=out, in_=in_), reads, writes, dma=key)
            return op(q, lambda e: e.dma_start(out=out, in_=in_), reads, writes, dma=key)

        def mm(out, lhsT, rhs, start, stop, reads, writes):
            b0 = lhsT.base_partition()
            n0 = lhsT.shape[0]
            tag = "lo" if b0 + n0 <= 64 else ("hi" if b0 >= 64 else None)
            op("pe", lambda e: e.matmul(out, lhsT=lhsT, rhs=rhs, start=start, stop=stop), reads, writes, tag=tag)

        def act(out, in_, func, reads, writes, bias=None, scale=None, accum=None):
            kw = {}
            if bias is not None: kw["bias"] = bias
            if scale is not None: kw["scale"] = scale
            if accum is not None: kw["accum_out"] = accum
            op("act", lambda e: e.activation(out=out, in_=in_, func=func, **kw), reads, writes)

        def tt(eng, out, in0, in1, o, reads, writes):
            g = {"dve": "dve", "pool": "pool"}[eng]
            op(g, lambda e: e.tensor_tensor(out=out, in0=in0, in1=in1, op=o), reads, writes)

        def ts(eng, out, in0, s1, s2, o0, o1, reads, writes):
            if s2 is None:
                op(eng, lambda e: e.tensor_scalar(out=out, in0=in0, scalar1=s1, scalar2=None, op0=o0), reads, writes)
            else:
                op(eng, lambda e: e.tensor_scalar(out=out, in0=in0, scalar1=s1, scalar2=s2, op0=o0, op1=o1), reads, writes)

        def stt(eng, out, in0, sc, in1, o0, o1, reads, writes):
            op(eng, lambda e: e.scalar_tensor_tensor(out=out, in0=in0, scalar=sc, in1=in1, op0=o0, op1=o1), reads, writes)

        def cp(eng, out, in_, reads, writes):
            if eng == "act":
                act(out, in_, AF.Copy, reads, writes)
            else:
                op(eng, lambda e: e.tensor_copy(out=out, in_=in_), reads, writes)

        def rsq(out, in_, eps, reads, wkey):
            act(out, in_, AF.Sqrt, reads, [wkey], bias=eps)
            op("dve", lambda e: e.reciprocal(out=out, in_=out), [wkey], [wkey])

        def bc(ap, shape):
            return ap.to_broadcast(shape)

        C = lambda o, n=128: cst[:, o:o + n]

        dma(cst[:], consts_d, [], ["cst"])
        dma(col[:], cols_d, [], ["col"])
        dma(w2e[:], w2ext_d, [], ["w2e"])
        dma(wld[0][64:128, 0:1024], a2_d, [], ["xt0"])
        cp("dve", a2b[64:128, :], wld[0][64:128, 0:1024], ["xt0"], ["a2b"])
        cp("dve", idb[:], C(C_ID), ["cst"], ["idb"])
        cp("dve", bob[:], C(C_BO), ["cst"], ["bob"])
        cp("dve", bmb[:], C(C_BM), ["cst"], ["bmb"])
        ts("dve", col[:, O_OMM:O_OMM + 33], col[:, O_MU:O_MU + 33], -1.0, 1.0, ALU.mult, ALU.add, ["col"], ["col"])
        ts("dve", col[:, O_OMKA:O_OMKA + 8], col[:, O_KA:O_KA + 8], -1.0, 1.0, ALU.mult, ALU.add, ["col"], ["col"])
        op("pool", lambda e: e.memset(twl[64:65, :], 1.0), [], ["twl"])
        op("pool", lambda e: e.memset(Sf[:], 0.0), [], ALLSF)
        op("pool", lambda e: e.memset(Sb[:], 0.0), [], ALLSB)
        op("pool", lambda e: e.memset(pprev[0][:], 0.0), [], ["pprev0"])
        op("pool", lambda e: e.memset(pprev[1][:], 0.0), [], ["pprev1"])
        op("pool", lambda e: e.memset(uex[:], 0.0), [], ["uex"])

        stage(1)
        ph("prologue")
        def prologue_gen(part):
            ph("prologue")
            pieces = []
            for c0 in range(0, NIN, 1024):
                for rc in range(8):
                    pieces.append((w_in, wi_s, rc, c0, min(1024, NIN - c0), "scr_i%d" % (c0 // 1024)))
            for rc in range(8):
                pieces.append((w_br, wbr_s, rc, 0, 1024, "scr_o"))
            for rc in range(4):
                pieces.append((w_bc, wbc_s, rc, 0, 1024, "scr_o"))
            for rc in range(8):
                pieces.append((w_out, wo_s, rc, 0, 1024, "scr_o"))
            fl = lambda t: t[:].rearrange("p a b -> p (a b)")
            orv = lambda i: orT[:, 2 * i:2 * i + 2, :].rearrange("p a b -> p (a b)")
            if part == 1:
                pieces = pieces[0:48]
                sf32 = [(fl(TT[i]), "T%d" % i) for i in range(8)]
                sbf = [(fl(rt_), "rt_0"), (fl(at_), "at_0"), (fl(bt_), "bt_0"), (fl(kt_), "kt_0"), (fl(bh_), "bh_"), (fl(kh_), "kh_"),
                       (Vt[:, :], "Vt_0"), (Bt[:, :], "Bt_0"), (Kt[:, :], "Kt_0")]
                DEPTH = 6
            else:
                pieces = pieces[48:]
                sf32 = [(fl(TT[i]), "T%d" % i) for i in (3, 4, 6, 7)]
                sbf = [(orv(0), "orT"), (orv(1), "orT"), (orv(2), "orT"), (orv(3), "orT"),
                       (fl(kh_), "kh_"), (Vt[:, :], "Vt_0"), (Bt[:, :], "Bt_0"), (Kt[:, :], "Kt_0")]
                DEPTH = 3
            NB = len(sf32)
            engs = ["dve", "act"]
            npc = len(pieces)
            for i in range(npc + DEPTH):
                if i < npc:
                    src, dst, rc, c0, n, skey = pieces[i]
                    bf_, kf_ = sf32[i % NB]
                    dma(bf_[:, 0:n], src[rc * 128:(rc + 1) * 128, c0:c0 + n], [], [kf_],
                        q=("act" if (os.environ.get("MK_ACTQ") and i % 2 == 1) else "sp"))
                j = i - DEPTH
                if j >= 0:
                    src, dst, rc, c0, n, skey = pieces[j]
                    bf_, kf_ = sf32[j % NB]
                    bb_, kb_ = sbf[j % len(sbf)]
                    cp(engs[j % 2], bb_[:, 0:n], bf_[:, 0:n], [kf_], [kb_])
                    dma(dst[rc * 128:(rc + 1) * 128, c0:c0 + n], bb_[:, 0:n], [kb_], [skey], q="pool")
                yield

        stage(2)
        wi_v = wi_s.rearrange("(dc p) n -> p dc n", p=128)
        wbr_v = wbr_s.rearrange("(dc p) n -> p dc n", p=128)
        wbc_v = wbc_s.rearrange("(dc p) n -> p dc n", p=128)
        wo_v = wo_s.rearrange("(dc p) n -> p dc n", p=128)
        wslot = {"i": 0}

        def wload(view, ndc, c0, n, skeys):
            s = wslot["i"] % 4
            wslot["i"] += 1
            dma(wb[s][:, 0:ndc, 0:n], view[:, :, c0:c0 + n], skeys, ["wb%d" % s])
            return s

        def ikeys(c0, n):
            return ["scr_i%d" % b for b in range(c0 // 1024, (c0 + n - 1) // 1024 + 1)]

        def rmsnorm_tile(xtile, key, npart, dst_cols, want_h_out=None):
            ph("rmsnorm")
            act(hb[0:npart, :], xtile[0:npart, :], AF.Square, [key], ["hb", "small"], accum=small[0:npart, 0:1])
            ts("dve", small[0:npart, 1:2], small[0:npart, 0:1], 1.0 / D, 1e-6, ALU.mult, ALU.add, ["small"], ["small"])
            rsq(small[0:npart, 2:3], small[0:npart, 1:2], 0.0, ["small"], "small")
            ts("dve", hb[0:npart, :], xtile[0:npart, :], small[0:npart, 2:3], None, ALU.mult, None, [key, "small"], ["hb"])
            if want_h_out is not None:
                want_h_out()
            for dc in range(8):
                op("pe", lambda e, dc=dc: e.transpose(ptb[:, dc * 128:dc * 128 + npart], hb[0:npart, dc * 128:(dc + 1) * 128], idb[0:npart, 0:npart]),
                   ["hb", "idb"], ["ptb"])
            for dc in range(8):
                act(hT[:, dc, dst_cols[0]:dst_cols[1]], ptb[:, dc * 128:dc * 128 + npart], AF.Copy, ["ptb", "col"], ["hT"],
                    scale=col[:, O_GPRE + dc:O_GPRE + dc + 1])

        def project(j, wslot_i, jj, NT, bank):
            for dc in range(8):
                mm(pb[bank][:, 0:NT], wb[wslot_i][:, dc, jj * 128:(jj + 1) * 128], hT[:, dc, 0:NT], dc == 0, dc == 7,
                   ["wb%d" % wslot_i, "hT"], ["pb%d" % bank])

        def shiftmix(j, bank, NT, dst, dkey, sample, pp_old, pp_new):
            p = pb[bank]
            mu = col[:, O_MU + j:O_MU + j + 1]
            omm = col[:, O_OMM + j:O_OMM + j + 1]
            bk = "pb%d" % bank
            if sample:
                act(tmpb[:, 0:NS], p[:, 0:NS], AF.Copy, [bk, "col"], ["tmpb"], scale=omm)
                stt("dve", dst, p[:, NS:2 * NS], mu, tmpb[:, 0:NS], ALU.mult, ALU.add, [bk, "tmpb", "col"], [dkey])
            else:
                act(tmpb[:, 0:NT], p[:, 0:NT], AF.Copy, [bk, "col"], ["tmpb"], scale=omm)
                act(pprev[pp_new][:, j:j + 1], p[:, NT - 1:NT], AF.Copy, [bk], ["pprev%d" % pp_new])
                stt("dve", dst[:, 1:NT], p[:, 0:NT - 1], mu, tmpb[:, 1:NT], ALU.mult, ALU.add, [bk, "tmpb", "col"], [dkey])
                stt("dve", dst[:, 0:1], pprev[pp_old][:, j:j + 1], mu, tmpb[:, 0:1], ALU.mult, ALU.add,
                    ["pprev%d" % pp_old, "tmpb", "col"], [dkey])

        def proj_phase(NT, sample, pp_old, pp_new):
            ph("proj")
            nb = 0
            for g0 in range(0, 45, 4):
                ng = min(4, 45 - g0)
                s = wload(wi_v, 8, g0 * 128, ng * 128, ikeys(g0 * 128, ng * 128))
                for jj in range(ng):
                    j = g0 + jj
                    bank = nb % 2
                    nb += 1
                    bk = "pb%d" % bank
                    project(j, s, jj, NT if not sample else 2 * NS, bank)
                    W = NS if sample else NT
                    if j < 8:
                        shiftmix(j, bank, NT, rS[:, j, 0:W], "rS", sample, pp_old, pp_new)
                    elif j < 16:
                        shiftmix(j, bank, NT, kS[:, j - 8, 0:W], "kS", sample, pp_old, pp_new)
                    elif j < 24:
                        shiftmix(j, bank, NT, vS[:, j - 16, 0:W], "vS", sample, pp_old, pp_new)
                    elif j < 32:
                        shiftmix(j, bank, NT, tmpc[:, 0:W], "tmpc", sample, pp_old, pp_new)
                        act(zrS[:, j - 24, 0:W], tmpc[:, 0:W], AF.Silu, ["tmpc"], ["zrS"])
                    elif j == 32:
                        shiftmix(j, bank, NT, tmpc[:, 0:W], "tmpc", sample, pp_old, pp_new)
                        act(twl[0:64, 0:W], tmpc[0:64, 0:W], AF.Tanh, ["tmpc"], ["twl"])
                        cp("pool", alb[64:128, 0:W], tmpc[64:128, 0:W], ["tmpc"], ["alb"])
                    elif j < 37:
                        c = j - 33
                        act(ua[:, c, 0:W], pb[bank][:, 0:W], AF.Identity, [bk, "col"], ["ua"],
                            bias=col[:, O_GLUB + c:O_GLUB + c + 1])
                    elif j < 41:
                        c = j - 37
                        act(tmpc[:, 0:W], pb[bank][:, 0:W], AF.Sigmoid, [bk, "col"], ["tmpc"],
                            bias=col[:, O_GLUB + 4 + c:O_GLUB + 5 + c])
                        if sample:
                            tt("dve", uex[:, c, 0:NS * 31].rearrange("p (n w) -> p n w", w=31)[:, :, 30], ua[:, c, 0:W], tmpc[:, 0:W],
                               ALU.mult, ["ua", "tmpc"], ["uex"])
                        else:
                            tt("dve", uex[:, c, 30:30 + W], ua[:, c, 0:W], tmpc[:, 0:W], ALU.mult, ["ua", "tmpc"], ["uex"])
                    else:
                        c = j - 41
                        act(szc[:, c, 0:W], pb[bank][:, 0:W], AF.Silu, [bk], ["szc"])
                    yield

        def ln_conv_out(W, cf, ck):
            ph("lnconv")
            for c in range(4):
                mm(pb[2][:, 0:W], C(C_AM), cf[c], c == 0, c == 3, ["cst", ck[c]], ["pb2"])
            for c in range(4):
                tt("dve", cf[c], cf[c], pb[2][:, 0:W], ALU.subtract, [ck[c], "pb2"], [ck[c]])
            for c in range(4):
                tt("pool", ua[:, c, 0:W], cf[c], cf[c], ALU.mult, [ck[c]], ["ua"])
            for c in range(4):
                mm(pb[3][:, 0:W], C(C_AM), ua[:, c, 0:W], c == 0, c == 3, ["cst", "ua"], ["pb3"])
            rsq(tmpc[:, 0:W], pb[3][:, 0:W], 1e-5, ["pb3"], "tmpc")
            for c in range(4):
                tt("dve", cf[c], cf[c], tmpc[:, 0:W], ALU.mult, [ck[c], "tmpc"], [ck[c]])
                act(ua[:, c, 0:W], cf[c], AF.Silu, [ck[c], "col"], ["ua"],
                    bias=col[:, O_LNB + c:O_LNB + c + 1], scale=col[:, O_LNG + c:O_LNG + c + 1])
                tt("pool", ocT[:, c, 0:W], ua[:, c, 0:W], szc[:, c, 0:W], ALU.mult, ["ua", "szc"], ["kS"])

        def prep_gen(cs, W, sample, PSp):
            T0, T1, T2, T3, T4, T5, T6, T7 = TT
            sl = slice(cs, cs + W)
            sfx = PSp["sfx"]
            bonT, gCt = PSp["bon"], PSp["gC"]
            kbon, kgc = "bon" + sfx, "gC" + sfx
            ph("prep")
            sh = [128, 8, W]
            colb = lambda o: bc(col[:, o:o + 8].unsqueeze(2), sh)
            p6 = pb[6]
            p6v = p6[:].rearrange("p (a b) -> p a b", b=128)[:, :, 0:W]
            T0v = T0[:].rearrange("p a b -> p (a b)")
            tt("dve", T5[:, :, 0:W], kS[:, :, sl], colb(O_KK), ALU.mult, ["kS", "col"], ["T5"])
            tt("pool", bh_[:, :, 0:W], T5[:, :, 0:W], T5[:, :, 0:W], ALU.mult, ["T5"], ["bh_"])
            yield
            for hf in range(2):
                mm(p6[0:W, :], twl[0:65, sl], w2e[0:65, hf * 512:(hf + 1) * 512], True, True, ["twl", "w2e"], ["pb6"])
                yield
                act(T0v[0:W, hf * 512:(hf + 1) * 512], p6[0:W, :], AF.Sigmoid, ["pb6"], ["T0"])
                yield
            tri = C(C_TRI) if not sample else cst[0:W, C_NI:C_NI + W]
            tre = C(C_TRE) if not sample else cst[0:W, C_NI + 64:C_NI + 64 + W]
            for hf in range(2):
                hs = slice(hf * 4, hf * 4 + 4)
                for hq in range(4):
                    hh = hf * 4 + hq
                    mm(p6[:, hq * 128:hq * 128 + W], T0v[0:W, hh * 128:(hh + 1) * 128], tri[0:W, 0:W], True, True, ["T0", "cst"], ["pb6"])
                yield
                act(T1[:, hs, 0:W], p6v[:, 0:4, :], AF.Exp, ["pb6"], ["T1"])
                act(T2[:, hs, 0:W], p6v[:, 0:4, :], AF.Exp, ["pb6"], ["T2"], scale=-1.0)
                yield
            cp("pool", gCt[:, :], T1[:, :, W - 1], ["T1"], [kgc])
            for hf in range(2):
                for hq in range(4):
                    hh = hf * 4 + hq
                    mm(p6[:, hq * 128:hq * 128 + W], a2b[64:128, hh * 128:(hh + 1) * 128], alb[64:128, sl], True, True, ["a2b", "alb"], ["pb6"])
                yield
                for hq in range(4):
                    hh = hf * 4 + hq
                    act(T4[:, hh, 0:W], p6[:, hq * 128:hq * 128 + W], AF.Sigmoid, ["pb6", "col"], ALLT4,
                        bias=col[:, O_A0 + hh:O_A0 + hh + 1])
                yield
            for hf in range(2):
                hs = slice(hf * 4, hf * 4 + 4)
                for hq in range(4):
                    hh = hf * 4 + hq
                    mm(p6[:, hq * 128:hq * 128 + W], bob[:], bh_[:, hh, 0:W], True, True, ["bob", "bh_"], ["pb6"])
                yield
                rsq(T7[:, hs, 0:W], p6v[:, 0:4, :], 1e-12, ["pb6"], "T7")
                yield
            stt("dve", T6[:, :, 0:W], T5[:, :, 0:W], -1.0, T7[:, :, 0:W], ALU.mult, ALU.mult, ["T5", "T7"], ["T6"])
            yield
            stt("dve", T7[:, :, 0:W], T6[:, :, 0:W], -1.0, T4[:, :, 0:W], ALU.mult, ALU.mult, ["T6"] + ALLT4, ["T7"])
            yield
            tt("pool", T0[:, :, 0:W], T4[:, :, 0:W], colb(O_KA), ALU.mult, ALLT4 + ["col", "T0"], ["T0"])
            tt("pool", T0[:, :, 0:W], T0[:, :, 0:W], colb(O_OMKA), ALU.add, ["T0", "col"], ["T0"])
            yield
            tt("dve", T5[:, :, 0:W], kS[:, :, sl], T0[:, :, 0:W], ALU.mult, ["kS", "T0"], ["T5"])
            yield
            tt("pool", T0[:, :, 0:W], rS[:, :, sl], T5[:, :, 0:W], ALU.mult, ["rS", "T5"], ["T0"])
            tt("pool", kh_[:, :, 0:W], T0[:, :, 0:W], colb(O_RK), ALU.mult, ["T0", "col"], ["kh_"])
            yield
            for hf in range(2):
                hs = slice(hf * 4, hf * 4 + 4)
                for hq in range(4):
                    hh = hf * 4 + hq
                    mm(p6[:, hq * 128:hq * 128 + W], bob[:], kh_[:, hh, 0:W], True, True, ["bob", "kh_"], ["pb6"])
                yield
                tt("dve", bonT[:, hs, 0:W], p6v[:, 0:4, :], vS[:, hs, sl], ALU.mult, ["pb6", "vS"], [kbon])
                yield
            if sample:
                return
            ph("mults")
            EG, EnG, EGe, k2, aa, bb = T1, T2, T3, T5, T6, T7
            rt, at, bt, kt = PSp["rt"], PSp["at"], PSp["bt"], PSp["kt"]
            krt, kat, kbt, kkt = "rt" + sfx, "at" + sfx, "bt" + sfx, "kt" + sfx
            tt("dve", rt, rS[:, :, sl], EG[:], ALU.mult, ["rS", "T1"], [krt])
            tt("pool", at[:, :, 1:128], aa[:, :, 1:128], EG[:, :, 0:127], ALU.mult, ["T6", "T1"], [kat])
            cp("pool", at[:, :, 0:1], aa[:, :, 0:1], ["T6"], [kat])
            yield
            tt("dve", bt, bb[:], EnG[:], ALU.mult, ["T7", "T2"], [kbt])
            tt("pool", kt, k2[:], EnG[:], ALU.mult, ["T5", "T2"], [kkt])
            yield
            tt("dve", EGe[:], EnG[:], bc(EG[:, :, 127:128], [128, 8, 128]), ALU.mult, ["T2", "T1", kat], ["T3"])
            yield
            tt("pool", bh_[:], bb[:], EGe[:], ALU.mult, ["T7", "T3"], ["bh_"])
            tt("dve", kh_[:], k2[:], EGe[:], ALU.mult, ["T5", "T3"], ["kh_"])
            yield
            ph("transp")
            for src, skey, dst, dkey in ((vS, "vS", PSp["Vt"], "Vt" + sfx), (bh_, "bh_", PSp["Bt"], "Bt" + sfx), (kh_, "kh_", PSp["Kt"], "Kt" + sfx)):
                for hh in range(8):
                    srcap = src[:, hh, sl] if src is vS else src[:, hh, :]
                    op("pe", lambda e, srcap=srcap, hh=hh: e.transpose(ptb[:, hh * 128:(hh + 1) * 128], srcap, idb[:]),
                       [skey, "idb"], ["ptb"])
                yield
                cp("act", dst[:, 0:512], ptb[:, 0:512], ["ptb"], [dkey])
                cp("dve", dst[:, 512:1024], ptb[:, 512:1024], ["ptb"], [dkey])
                yield

        def gn_gen(yT, ykey, cs, W, G, Gk, bonT, kbon):
            ph("gn")
            G1, G2, G3 = G
            k1, k2_, k3 = Gk
            sl = slice(cs, cs + W)
            sh = [128, 8, W]
            colb = lambda o: bc(col[:, o:o + 8].unsqueeze(2), sh)
            p6 = pb[6]
            p6v = p6[:].rearrange("p (a b) -> p a b", b=128)[:, :, 0:W]
            for hf in range(2):
                hs = slice(hf * 4, hf * 4 + 4)
                for hq in range(4):
                    hh = hf * 4 + hq
                    mm(p6[:, hq * 128:hq * 128 + W], C(C_BM), yT[:, hh, 0:W], True, True, ["cst"] + ykey, ["pb6"])
                yield
                tt("dve", G1[:, hs, 0:W], yT[:, hs, 0:W], p6v[:, 0:4, :], ALU.subtract, ykey + ["pb6"], [k1])
                yield
            hbv = hb[:, :].rearrange("p (a b) -> p a b", b=128)
            tt("pool", hbv[:, :, 0:W], G1[:, :, 0:W], G1[:, :, 0:W], ALU.mult, [k1], ["hb"])
            yield
            for hf in range(2):
                hs = slice(hf * 4, hf * 4 + 4)
                for hq in range(4):
                    hh = hf * 4 + hq
                    mm(p6[:, hq * 128:hq * 128 + W], bmb[:], hbv[:, hh, 0:W], True, True, ["bmb", "hb"], ["pb6"])
                yield
                rsq(G3[:, hs, 0:W], p6v[:, 0:4, :], 64e-5, ["pb6"], k3)
                yield
            tt("dve", G1[:, :, 0:W], G1[:, :, 0:W], G3[:, :, 0:W], ALU.mult, [k1, k3], [k1])
            yield
            tt("pool", G1[:, :, 0:W], G1[:, :, 0:W], colb(O_GNG), ALU.mult, [k1, "col"], [k1])
            tt("pool", G1[:, :, 0:W], G1[:, :, 0:W], colb(O_GNB), ALU.add, [k1, "col"], [k1])
            yield
            tt("dve", G1[:, :, 0:W], G1[:, :, 0:W], bonT[:, :, 0:W], ALU.add, [k1, kbon], [k1])
            yield
            tt("dve", orT[:, :, sl], G1[:, :, 0:W], zrS[:, :, sl], ALU.mult, [k1, "zrS"], ["orT"])
            yield

        def run_all(gens):
            gens = list(gens)
            while gens:
                for gq in list(gens):
                    try:
                        next(gq)
                    except StopIteration:
                        gens.remove(gq)

        gC2 = T("gC2", [128, 8])
        _fl = lambda t, i: t[:, 2 * i:2 * i + 2, :].rearrange("p a b -> p (a b)")
        _v8 = lambda ap: ap.rearrange("p (a b) -> p a b", b=128)
        PS = [
            {"sfx": "_0", "rt": rt_[:], "at": at_[:], "bt": bt_[:], "kt": kt_[:], "Vt": Vt, "Bt": Bt, "Kt": Kt, "bon": bon, "gC": gC},
            {"sfx": "_1", "rt": _v8(_fl(wb[0], 0)), "at": _v8(_fl(wb[0], 1)), "bt": _v8(_fl(wb[0], 2)), "kt": _v8(_fl(wb[0], 3)),
             "Vt": _fl(wb[1], 0), "Bt": _fl(wb[1], 1), "Kt": _fl(wb[1], 2), "bon": _v8(_fl(wb[1], 3)), "gC": gC2},
        ]
        PS1_KEYS = [k + "_1" for k in ("rt", "at", "bt", "kt", "Vt", "Bt", "Kt", "bon")]

        def scan_group(g, S, PSp, yT):
            Ak_, Nk_, Qb_, LkT_, MbT_, MkT_, Xb_, SAb_ = S["Ak"], S["Nk"], S["Qb"], S["LkT"], S["MbT"], S["MkT"], S["Xb"], S["SAb"]
            b0, b1, b2 = S["banks"]
            kb = lambda i: "pb%d" % i
            n = S["n"]
            sfx = PSp["sfx"]
            rt_, at_, bt_, kt_, Vt, Bt, Kt, gC = PSp["rt"], PSp["at"], PSp["bt"], PSp["kt"], PSp["Vt"], PSp["Bt"], PSp["Kt"], PSp["gC"]
            K = lambda nm: nm + n
            heads = [4 * g + x for x in (0, 2, 1, 3)]
            SbK = "Sb%d" % g; SfK = "Sf%d" % g; yK = "yT%d" % g

            def hp(h):
                hl, hh = h % 2, h // 2
                return slice(hl * 64, hl * 64 + 64), hh
            v4 = lambda p: p[:].rearrange("p (a b) -> p a b", b=128)
            mk = lambda o: bc(cst[:, o:o + 128].unsqueeze(1), [128, 4, 128])
            ph("scores")
            plan = [(b0, "at_", "bt_", Ak_[0], K("Ak0"), C_SL), (b1, "bt_", "at_", Nk_[0], K("Nk0"), C_SU),
                    (b2, "kt_", "at_", LkT_, K("LkT"), C_SU), (b0, "bt_", "rt_", MbT_, K("MbT"), C_UI),
                    (b1, "kt_", "rt_", MkT_, K("MkT"), C_UI)]
            tl = {"at_": at_, "bt_": bt_, "kt_": kt_, "rt_": rt_}
            kn = {"at_": "at" + sfx, "bt_": "bt" + sfx, "kt_": "kt" + sfx, "rt_": "rt" + sfx}
            first_lo = (n == "_A")
            for rnd in (plan[0:3], plan[3:5]):
                for tagsel in ((0, 1) if first_lo else (1, 0)):
                    for (bk, ln, rn, dst, dk, msk) in rnd:
                        for hi, h in enumerate(heads):
                            if (h % 2) != tagsel:
                                continue
                            pr, hh = hp(h)
                            mm(pb[bk][:, hi * 128:(hi + 1) * 128], tl[ln][pr, hh, :], tl[rn][pr, hh, :], True, True, [kn[ln], kn[rn]], [kb(bk)])
                yield
                for (bk, ln, rn, dst, dk, msk) in rnd:
                    tt("dve", dst[:], v4(pb[bk]), mk(msk), ALU.mult, [kb(bk), "cst"], [dk])
                    yield
            ph("doubling")
            tt("pool", Qb_[:], Nk_[0][:], mk(C_ID), ALU.add, [K("Nk0"), "cst"], [K("Qb")])
            yield
            mm(pb[b2][:, :], idb[:], Qb_.rearrange("p a b -> p (a b)"), True, True,
               ["idb", K("Qb")], [kb(b2)])
            yield
            cur = 0
            for lvl in range(6):
                nx = 1 - cur
                for hi in range(4):
                    mm(pb[b0][:, hi * 128:(hi + 1) * 128], Nk_[cur][:, hi, :], Ak_[cur][:, hi, :], True, True,
                       [K("Nk%d" % cur), K("Ak%d" % cur)], [kb(b0)])
                if lvl < 5:
                    for hi in range(4):
                        mm(pb[b1][:, hi * 128:(hi + 1) * 128], Ak_[cur][:, hi, :], Nk_[cur][:, hi, :], True, True,
                           [K("Nk%d" % cur), K("Ak%d" % cur)], [kb(b1)])
                yield
                cp("act", Ak_[nx][:], v4(pb[b0]), [kb(b0)], [K("Ak%d" % nx)])
                if lvl < 5:
                    cp("dve", Nk_[nx][:], v4(pb[b1]), [kb(b1)], [K("Nk%d" % nx)])
                yield
                for hi in range(4):
                    mm(pb[b2][:, hi * 128:(hi + 1) * 128], Ak_[nx][:, hi, :], Qb_[:, hi, :], False, True,
                       [K("Ak%d" % nx), K("Qb")], [kb(b2)])
                yield
                if lvl % 2 == 0 or os.environ.get("MK_QACT"):
                    cp("act", Qb_[:], v4(pb[b2]), [kb(b2)], [K("Qb")])
                else:
                    cp("dve", Qb_[:], v4(pb[b2]), [kb(b2)], [K("Qb")])
                yield
                cur = nx
            ph("seq")
            for hi, h in enumerate(heads):
                pr, hh = hp(h)
                o = pb[b0][:, hi * 64:(hi + 1) * 64]
                mm(o, at_[pr, hh, :], Sb[pr, hh, :], True, False, ["at" + sfx, SbK], [kb(b0)])
                mm(o, LkT_[:, hi, :], Vt[:, h * 64:(h + 1) * 64], False, True, [K("LkT"), "Vt" + sfx], [kb(b0)])
            yield
            cp("act", Xb_[:], pb[b0][:, 0:256], [kb(b0)], [K("Xb")])
            yield
            for hi, h in enumerate(heads):
                mm(pb[b1][:, hi * 64:(hi + 1) * 64], Qb_[:, hi, :], Xb_[:, hi * 64:(hi + 1) * 64], True, True, [K("Qb"), K("Xb")], [kb(b1)])
            yield
            cp("dve", SAb_[:], pb[b1][:, 0:256], [kb(b1)], [K("SAb")])
            yield
            for hi, h in enumerate(heads):
                pr, hh = hp(h)
                o = pb[b2][pr, (hh - 2 * g) * 128:(hh - 2 * g) * 128 + 128]
                mm(o, Sb[pr, hh, :], rt_[pr, hh, :], True, False, [SbK, "rt" + sfx], [kb(b2)])
                mm(o, SAb_[:, hi * 64:(hi + 1) * 64], MbT_[:, hi, :], False, False, [K("SAb"), K("MbT")], [kb(b2)])
                mm(o, Vt[:, h * 64:(h + 1) * 64], MkT_[:, hi, :], False, True, ["Vt" + sfx, K("MkT")], [kb(b2)])
            yield
            cp("act", yT[:, 2 * g:2 * g + 2, :], pb[b2][:, 0:256].rearrange("p (a b) -> p a b", b=128), [kb(b2)], [yK])
            for hi, h in enumerate(heads):
                pr, hh = hp(h)
                o = pb[b0][pr, (hh - 2 * g) * 64:(hh - 2 * g) * 64 + 64]
                mm(o, Bt[:, h * 64:(h + 1) * 64], SAb_[:, hi * 64:(hi + 1) * 64], True, False, ["Bt" + sfx, K("SAb")], [kb(b0)])
                mm(o, Kt[:, h * 64:(h + 1) * 64], Vt[:, h * 64:(h + 1) * 64], False, True, ["Kt" + sfx, "Vt" + sfx], [kb(b0)])
            yield
            gs = slice(2 * g, 2 * g + 2)
            tt("dve", Sf[:, gs, :], Sf[:, gs, :], bc(gC[:, gs].unsqueeze(2), [128, 2, 64]), ALU.mult, [SfK, "gC" + sfx], [SfK])
            tt("dve", Sf[:, gs, :], Sf[:, gs, :], pb[b0][:, 0:128].rearrange("p (a b) -> p a b", b=64), ALU.add, [SfK, kb(b0)], [SfK])
            yield
            cp("act", Sb[:, gs, :], Sf[:, gs, :], [SfK], [SbK])
            yield


        def tail(NT, xsrc_tiles, ydst_tiles, nrows):
            ph("tail")
            for q in range(2):
                sgr = wload(wi_v, 8, (45 + 4 * q) * 128, 512, ikeys((45 + 4 * q) * 128, 512))
                sbr = wload(wbr_v, 8, q * 512, 512, ["scr_o"])
                sgc = wload(wi_v, 8, (53 + 4 * q) * 128, 512, ikeys((53 + 4 * q) * 128, 512))
                sbc = wload(wbc_v, 4, q * 512, 512, ["scr_o"])
                for jj in range(4):
                    j = q * 4 + jj
                    od = j % 2
                    bA, bB, bC, bD = (0, 1, 2, 3) if od == 0 else (4, 5, 6, 3)
                    sg1, sg1k = (sgb, "sgb")
                    sg2, sg2k = (alb, "alb")
                    m1_ = TT[5 + od][:].rearrange("p a b -> p (a b)")
                    m1k = "T%d" % (5 + od)
                    t2_, t2k = ((tmpc, "tmpc"), (tmpb, "tmpb"))[od]
                    project(45 + j, sgr, jj, NT, bA)
                    act(sg1[:, 0:NT], pb[bA][:, 0:NT], AF.Sigmoid, ["pb%d" % bA], [sg1k])
                    for fc in range(8):
                        mm(pb[bB][:, 0:NT], wb[sbr][:, fc, jj * 128:(jj + 1) * 128], orT[:, fc, 0:NT], fc == 0, fc == 7,
                           ["wb%d" % sbr, "orT"], ["pb%d" % bB])
                    tt("dve", m1_[:, 0:NT], pb[bB][:, 0:NT], sg1[:, 0:NT], ALU.mult, ["pb%d" % bB, sg1k], [m1k])
                    project(53 + j, sgc, jj, NT, bC)
                    act(sg2[:, 0:NT], pb[bC][:, 0:NT], AF.Sigmoid, ["pb%d" % bC], [sg2k])
                    for fc in range(4):
                        mm(pb[bD][:, 0:NT], wb[sbc][:, fc, jj * 128:(jj + 1) * 128], ocT[:, fc, 0:NT], fc == 0, fc == 3,
                           ["wb%d" % sbc, "kS"], ["pb%d" % bD])
                    tt("dve", t2_[:, 0:NT], pb[bD][:, 0:NT], sg2[:, 0:NT], ALU.mult, ["pb%d" % bD, sg2k], [t2k])
                    tt("pool", mT[:, j, 0:NT], m1_[:, 0:NT], t2_[:, 0:NT], ALU.add, [m1k, t2k], ["rS"])
            so = [wload(wo_v, 8, 0, 512, ["scr_o"]), wload(wo_v, 8, 512, 512, ["scr_o"])]
            npg = TT[2][:].rearrange("p a b -> p (a b)")
            dma(npg[:, :], npg_d.partition_broadcast(128), [], ["T2"])
            for i, (xsrc, ydst) in enumerate(zip(xsrc_tiles, ydst_tiles)):
                xb = xt[i % 2]
                xk = "xt%d" % (i % 2)
                dma(xb[0:nrows, :], xsrc, [], [xk])
                tsl = slice(i * 128, i * 128 + nrows)
                bks = [(4, 5), (0, 1), (2, 3)][i % 3]
                so_ = 32 + 8 * (i % 3)
                stg_i = (0, 1, 3)[i % 3]
                stg = TT[stg_i][:].rearrange("p a b -> p (a b)")
                sk_ = "T%d" % stg_i
                for hf in range(2):
                    for fc in range(8):
                        mm(pb[bks[hf]][0:nrows, :], mT[:, fc, tsl], wb[so[hf]][:, fc, :], fc == 0, fc == 7,
                           ["rS", "wb%d" % so[hf]], ["pb%d" % bks[hf]])
                for hf in range(2):
                    act(hb[0:nrows, hf * 512:(hf + 1) * 512], pb[bks[hf]][0:nrows, :], AF.Square, ["pb%d" % bks[hf]], ["hb", "small"],
                        accum=small[0:nrows, so_ + hf:so_ + hf + 1])
                tt("dve", small[0:nrows, so_ + 2:so_ + 3], small[0:nrows, so_:so_ + 1], small[0:nrows, so_ + 1:so_ + 2], ALU.add, ["small"], ["small"])
                ts("dve", small[0:nrows, so_ + 3:so_ + 4], small[0:nrows, so_ + 2:so_ + 3], 1.0 / D, 1e-6, ALU.mult, ALU.add, ["small"], ["small"])
                rsq(small[0:nrows, so_ + 4:so_ + 5], small[0:nrows, so_ + 3:so_ + 4], 0.0, ["small"], "small")
                for hf in range(2):
                    hsl = slice(hf * 512, (hf + 1) * 512)
                    stt("dve", stg[0:nrows, hsl], pb[bks[hf]][0:nrows, :], small[0:nrows, so_ + 4:so_ + 5], npg[0:nrows, hsl], ALU.mult, ALU.mult,
                        ["pb%d" % bks[hf], "small", "T2"], [sk_])
                tt("pool", stg[0:nrows, :], stg[0:nrows, :], xb[0:nrows, :], ALU.add, [sk_, xk], [sk_])
                dma(ydst, stg[0:nrows, :], [sk_], [], q="pool")

        def rms_phase(sc):
            t0 = sc * 512
            for i in range(4):
                xb = xt[i % 2]; xk = "xt%d" % (i % 2)
                dma(xb[:], xp[t0 + i * 128:t0 + (i + 1) * 128, :], [], [xk])
                last = (sc == 3 and i == 3)

                def hout(xb=xb, xk=xk):
                    T0v = TT[0][:].rearrange("p a b -> p (a b)")
                    dma(T0v[:, :], npre_d.partition_broadcast(128), [], ["T0"])
                    ts("dve", TT[1][:].rearrange("p a b -> p (a b)"), xb[:], small[:, 2:3], None, ALU.mult, None, [xk, "small"], ["T1"])
                    tt("dve", TT[1][:].rearrange("p a b -> p (a b)"), TT[1][:].rearrange("p a b -> p (a b)"), T0v, ALU.mult, ["T1", "T0"], ["T1"])
                    dma(nsp, TT[1][:].rearrange("p a b -> p (a b)")[127:128, :], ["T1"], [])
                rmsnorm_tile(xb, xk, 128, (i * 128, (i + 1) * 128), hout if last else None)

        rms_phase(0)
        run_all([prologue_gen(1)])
        for sc in range(4):
            t0 = sc * 512
            if sc > 0:
                rms_phase(sc)
            stage(3 if sc == 0 else 11)
            if sc > 0:
                cp("pool", tmpc[:, 0:120].rearrange("p (c w) -> p c w", w=30), uex[:, :, 512:542], ["uex"], ["tmpc"])
                cp("pool", uex[:, :, 0:30], tmpc[:, 0:120].rearrange("p (c w) -> p c w", w=30), ["tmpc"], ["uex"])
            pg = proj_phase(512, False, sc % 2, (sc + 1) % 2)
            side = prologue_gen(2) if sc == 0 else None
            pr0 = None
            step = 0
            while True:
                try:
                    next(pg)
                except StopIteration:
                    break
                step += 1
                if side is not None:
                    try:
                        next(side)
                    except StopIteration:
                        side = None
                if step == 33:
                    pr0 = prep_gen(0, 128, False, PS[0])
                if pr0 is not None:
                    try:
                        next(pr0)
                    except StopIteration:
                        pr0 = None
            rest = [g_ for g_ in (side, pr0) if g_ is not None]
            run_all(rest)
            stage(4 if sc == 0 else 11)
            op("pool", lambda e: e.memset(dummy[:, 0:1], 0.0), [], ["wb3", "wb0", "wb1", "xt0", "xt1", "ua", "uaA", "uaB", "dummy", "yT0", "yT1", "yT2", "yT3"] + SETB_KEYS + PS1_KEYS)
            yTp = xt[0][:, :].rearrange("p (a b) -> p a b", b=128)
            Gp = (xt[1][:, :].rearrange("p (a b) -> p a b", b=128),
                  ua[:, 0:2, :].rearrange("p a b -> p (a b)").rearrange("p (a b) -> p a b", b=128),
                  ua[:, 2:4, :].rearrange("p a b -> p (a b)").rearrange("p (a b) -> p a b", b=128))
            Gpk = ("xt1", "uaA", "uaB")

            def chunk_scan(c4):
                PSp = PS[c4 % 2]
                for pair in ((0, 1), (2, 3)):
                    gens = [scan_group(pair[0], SETS[0], PSp, yTp), scan_group(pair[1], SETS[1], PSp, yTp)]
                    while gens:
                        for gq in list(gens):
                            try:
                                next(gq)
                                yield
                            except StopIteration:
                                gens.remove(gq)
                yield from gn_gen(yTp, ["yT0", "yT1", "yT2", "yT3"], c4 * 128, 128, Gp, Gpk, PSp["bon"], "bon" + PSp["sfx"])

            for c4 in range(4):
                main = chunk_scan(c4)
                side = prep_gen((c4 + 1) * 128, 128, False, PS[(c4 + 1) % 2]) if c4 < 3 else None
                RATIO = int(os.environ.get("MK_RATIO", "5"))
                done = False
                while not done:
                    for _ in range(RATIO):
                        try:
                            next(main)
                        except StopIteration:
                            done = True
                            break
                    if side is not None:
                        try:
                            next(side)
                        except StopIteration:
                            side = None
                if side is not None:
                    run_all([side])
            op("pool", lambda e: e.memset(dummy[:, 1:2], 0.0), [], ["wb3", "wb0", "wb1", "xt0", "xt1", "ua", "uaA", "uaB", "dummy", "yT0", "yT1", "yT2", "yT3"] + SETB_KEYS + PS1_KEYS)
            stage(8 if sc == 0 else 11)
            ph("conv")
            cp("pool", ubf[:], uex[:], ["uex"], ["ubf"])
            for c in range(4):
                for w in range(31):
                    s = (c * 31 + w) % 4
                    if w % 2 == 0:
                        act(dg[s][:], idb[:], AF.Copy, ["idb", "col"], ["dg%d" % s], scale=col[:, O_CW + c * 31 + w:O_CW + c * 31 + w + 1])
                    else:
                        ts("dve", dg[s][:], idb[:], col[:, O_CW + c * 31 + w:O_CW + c * 31 + w + 1], None, ALU.mult, None,
                           ["idb", "col"], ["dg%d" % s])
                    mm(pb[6][:, :], dg[s][:], ubf[:, c, w:w + 512], w == 0, w == 30, ["dg%d" % s, "ubf"], ["pb6"])
                act(TT[c][:].rearrange("p a b -> p (a b)")[:, 0:512], pb[6][:, :], AF.Identity, ["pb6", "col"], ["T%d" % c],
                    bias=col[:, O_CB + c:O_CB + c + 1])
            ln_conv_out(512, [TT[c][:].rearrange("p a b -> p (a b)")[:, 0:512] for c in range(4)], ["T0", "T1", "T2", "T3"])
            if sc == 3:
                for c in range(4):
                    mm(pb[6][0:30, c * 128:(c + 1) * 128], uex[:, c, 512:542], C(C_ID), True, True, ["uex", "cst"], ["pb6"])
                cp("dve", tmpc[0:30, :], pb[6][0:30, :], ["pb6"], ["tmpc"])
                dma(ncp, tmpc[0:30, :], ["tmpc"], [])
            stage(9 if sc == 0 else 11)
            tail(512, [xp[t0 + i * 128:t0 + (i + 1) * 128, :] for i in range(4)],
                 [yp[t0 + i * 128:t0 + (i + 1) * 128, :] for i in range(4)], 128)

        stage(12)
        for h in range(16):
            hl, hh = h % 2, h // 2
            pr = slice(hl * 64, hl * 64 + 64)
            mm(pb[0][pr, hh * 64:(hh + 1) * 64], Sf[pr, hh, :], cst[pr, C_ID + hl * 64:C_ID + hl * 64 + 64], True, True, ALLSF + ["cst"], ["pb0"])
        cp("dve", tmpc[:, :], pb[0][:, :], ["pb0"], ["tmpc"])
        dma(nwp.rearrange("(hh p) j -> p hh j", p=128), tmpc[:, :].rearrange("p (a b) -> p a b", b=64), ["tmpc"], [])

        stage(13)
        ph("sample")
        xb = xt[0]
        dma(xb[0:NS, :], xs, [], ["xt0"])

        def hout_s():
            T0v = TT[0][:].rearrange("p a b -> p (a b)")
            T1v = TT[1][:].rearrange("p a b -> p (a b)")
            dma(T0v[0:NS, :], npre_d.partition_broadcast(NS), [], ["T0"])
            ts("dve", T1v[0:NS, :], xb[0:NS, :], small[0:NS, 2:3], None, ALU.mult, None, ["xt0", "small"], ["T1"])
            tt("dve", T1v[0:NS, :], T1v[0:NS, :], T0v[0:NS, :], ALU.mult, ["T1", "T0"], ["T1"])
            dma(nss, T1v[0:NS, :], ["T1"], [])
        rmsnorm_tile(xb, "xt0", NS, (0, NS), hout_s)
        dma(xt[1][0:NS, :], sshift, [], ["xt1"])
        cp("dve", hb[0:NS, :], xt[1][0:NS, :], ["xt1"], ["hb"])
        for dc in range(8):
            op("pe", lambda e, dc=dc: e.transpose(ptb[:, dc * 128:dc * 128 + NS], hb[0:NS, dc * 128:(dc + 1) * 128], idb[0:NS, 0:NS]),
               ["hb", "idb"], ["ptb"])
        cp("act", hT[:, :, NS:2 * NS], ptb[:, :].rearrange("p (a b) -> p a b", b=128)[:, :, 0:NS], ["ptb"], ["hT"])
        uv = [uex[:, c, 0:NS * 31].rearrange("p (n w) -> p n w", w=31) for c in range(4)]
        for q in range(4):
            dma(xt[1][0:120, 0:512], sconv[q * 120:(q + 1) * 120, :], [], ["xt1"])
            for c in range(4):
                mm(pb[6][:, c * 120:(c + 1) * 120], xt[1][0:120, c * 128:(c + 1) * 128], cst[0:120, C_ID:C_ID + 120], True, True,
                   ["xt1", "cst"], ["pb6"])
            for c in range(4):
                cp("dve", uv[c][:, q * 4:(q + 1) * 4, 0:30], pb[6][:, c * 120:(c + 1) * 120].rearrange("p (n w) -> p n w", w=30),
                   ["pb6"], ["uex"])
        dma(ncs[:, 0:29, :], sconv.rearrange("(n w) c -> n w c", w=30)[:, 1:30, :], [], [])
        run_all([proj_phase(2 * NS, True, 0, 0)])
        stage(14)
        run_all([prep_gen(0, NS, True, PS[0])])
        EG, EnG, EGe, k2, aa, bb = TT[1], TT[2], TT[3], TT[5], TT[6], TT[7]
        SW = [ua[:, i, :].rearrange("p (a b) -> p a b", b=64) for i in range(2)]
        Dxs = [Vt[:, 0:512], tmpb[:, :], Bt[:, 0:512], Kt[:, 0:512], Vt[:, 512:1024]]
        Dxk = ["Vt_0", "tmpb", "Bt_0", "Kt_0", "Vt_0"]
        Dxo = [bob[:], C(C_BO), bob[:], bob[:], bob[:]]
        Dxok = ["bob", "cst", "bob", "bob", "bob"]
        yTs = TT[4]
        i2b = bc(cst[:, C_I2:C_I2 + 64].unsqueeze(1), [128, 8, 64])
        op("pool", lambda e: e.memset(dummy[:, 2:3], 0.0), [], ["ua", "uaA", "uaB", "dummy"])
        for n in range(NS):
            Sw = SW[n % 2]; sk = ("uaA", "uaB")[n % 2]
            dma(Sw, swkv[n].rearrange("(hh p) j -> p hh j", p=128), [], [sk])
            vecs = [(aa, "T6"), (EG, "T1"), (bb, "T7"), (k2, "T5"), (rS, "rS")]
            for vi, (vt_, vk) in enumerate(vecs):
                tt("pool", Dxs[vi].rearrange("p (a b) -> p a b", b=64), i2b, bc(vt_[:, :, n:n + 1], [128, 8, 64]), ALU.mult,
                   ["cst", vk], [Dxk[vi]])
                mm(pb[vi][:, :], Dxo[vi], Dxs[vi], True, True, [Dxok[vi], Dxk[vi]], ["pb%d" % vi])
            v8 = lambda p: p[:].rearrange("p (a b) -> p a b", b=64)
            W3 = TT[0][:, :, 0:64]
            tt("dve", W3, Sw, v8(pb[0]), ALU.mult, [sk, "pb0"], ["T0"])
            op("dve", lambda e: e.tensor_reduce(out=small[:, 16:24], in_=TT[0][:, :, 0:64], axis=AX.X, op=ALU.add), ["T0"], ["small"])
            tt("dve", Sw, Sw, v8(pb[1]), ALU.mult, [sk, "pb1"], [sk])
            tt("dve", W3, v8(pb[2]), bc(small[:, 16:24].unsqueeze(2), [128, 8, 64]), ALU.mult, ["pb2", "small"], ["T0"])
            tt("pool", Sw, Sw, W3, ALU.add, [sk, "T0"], [sk])
            cp("dve", small[:, 24:32], vS[:, :, n], ["vS"], ["small"])
            tt("dve", W3, v8(pb[3]), bc(small[:, 24:32].unsqueeze(2), [128, 8, 64]), ALU.mult, ["pb3", "small"], ["T0"])
            tt("pool", Sw, Sw, W3, ALU.add, [sk, "T0"], [sk])
            dma(nws[n].rearrange("(hh p) j -> p hh j", p=128), Sw, [sk], [])
            tt("dve", W3, Sw, v8(pb[4]), ALU.mult, [sk, "pb4"], ["T0"])
            op("dve", lambda e, n=n: e.tensor_reduce(out=yTs[:, :, n], in_=TT[0][:, :, 0:64], axis=AX.X, op=ALU.add), ["T0"], ["T4_0"])
        stage(15)
        op("pool", lambda e: e.memset(dummy[:, 3:4], 0.0), [], ["ua", "uaA", "uaB", "dummy"])
        run_all([gn_gen(yTs, ALLT4, 0, NS, (TT[1], TT[2], TT[3]), ("T1", "T2", "T3"), bon, "bon_0")])
        stage(16)
        cf = []
        for c in range(4):
            cwb = bc(col[:, O_CW + c * 31:O_CW + (c + 1) * 31].unsqueeze(1), [128, NS, 31])
            tt("dve", tmpc[:, 0:NS * 31].rearrange("p (n w) -> p n w", w=31), uv[c], cwb, ALU.mult, ["uex", "col"], ["tmpc"])
            cfc = TT[c][:].rearrange("p a b -> p (a b)")[:, 0:NS]
            op("dve", lambda e, cfc=cfc: e.tensor_reduce(out=cfc, in_=tmpc[:, 0:NS * 31].rearrange("p (n w) -> p n w", w=31),
                                                        axis=AX.X, op=ALU.add), ["tmpc"], ["T%d" % c])
            ts("dve", cfc, cfc, col[:, O_CB + c:O_CB + c + 1], None, ALU.add, None, ["T%d" % c, "col"], ["T%d" % c])
            cf.append(cfc)
        for c in range(4):
            cp("dve", tmpb[:, c * NS:(c + 1) * NS], uv[c][:, :, 30], ["uex"], ["tmpb"])
        for c in range(4):
            mm(pb[6][0:NS, c * 128:(c + 1) * 128], tmpb[:, c * NS:(c + 1) * NS], C(C_ID), True, True, ["tmpb", "cst"], ["pb6"])
        cp("dve", m1[0:NS, 0:512], pb[6][0:NS, :], ["pb6"], ["T5"])
        dma(ncs[:, 29, :], m1[0:NS, 0:512], ["T5"], [])
        ln_conv_out(NS, cf, ["T0", "T1", "T2", "T3"])
        stage(17)
        tail(NS, [xs], [ys], NS)
        P.emit()
    return nc


def _prep_consts():
    c = np.zeros((128, NCONST), np.float32)
    idx = np.arange(128)
    c[:, C_ID:C_ID + 128] = np.eye(128)
    c[:, C_SL:C_SL + 128] = (idx[None, :] < idx[:, None])
    c[:, C_SU:C_SU + 128] = (idx[:, None] < idx[None, :])
    c[:, C_UI:C_UI + 128] = (idx[:, None] <= idx[None, :])
    c[:, C_TRI:C_TRI + 128] = (idx[:, None] <= idx[None, :]) * CNEG
    c[:, C_TRE:C_TRE + 128] = (idx[:, None] < idx[None, :]) * CNEG
    blk = (idx[:, None] // 64 == idx[None, :] // 64).astype(np.float32)
    c[:, C_BM:C_BM + 128] = blk / 64.0
    c[:, C_BO:C_BO + 128] = blk
    c[:, C_AM:C_AM + 128] = 1.0 / 512.0
    c[:, C_NI:C_NI + 64] = np.eye(128)[:, :64] * CNEG
    c[:, C_I2:C_I2 + 64] = (idx[:, None] % 64 == np.arange(64)[None, :])
    return c


_NC = None


def kernel(x_prompt, x_sample, state_shift, state_wkv, state_conv, norm_pre_g, w_in, mu_shift,
           decay_w0, decay_w2, iclr_a0, iclr_a2, k_k, k_a, r_k, gn_g, gn_b, conv_glu_b, conv_w,
           conv_b, ln_c_g, ln_c_b, w_branch_r, w_branch_c, w_out, norm_post_g):
    global _NC
    f = lambda a: np.ascontiguousarray(np.asarray(a, dtype=np.float32))
    colv = lambda v, n: f(v).reshape(n, 128).T
    cols = np.zeros((128, NCOL), np.float32)
    cols[:, O_MU:O_MU + 33] = colv(mu_shift[0], 33)
    cols[:, O_KK:O_KK + 8] = colv(k_k[0], 8)
    cols[:, O_KA:O_KA + 8] = colv(k_a[0], 8)
    cols[:, O_RK:O_RK + 8] = colv(np.asarray(r_k[0]).reshape(-1), 8)
    cols[:, O_GNG:O_GNG + 8] = colv(gn_g[0], 8)
    cols[:, O_GNB:O_GNB + 8] = colv(gn_b[0], 8)
    cols[:, O_A0:O_A0 + 8] = colv(iclr_a0[0], 8)
    cols[:, O_GLUB:O_GLUB + 8] = colv(conv_glu_b[0], 8)
    cols[:, O_CB:O_CB + 4] = colv(conv_b[0], 4)
    cols[:, O_LNG:O_LNG + 4] = colv(ln_c_g[0], 4)
    cols[:, O_LNB:O_LNB + 4] = colv(ln_c_b[0], 4)
    cw = f(conv_w[0])
    cols[:, O_CW:O_CW + 124] = cw.reshape(31, 4, 128).transpose(2, 1, 0).reshape(128, 124)
    cols[:, O_GPRE:O_GPRE + 8] = colv(norm_pre_g[0], 8)
    consts = _prep_consts()
    w2ext = np.concatenate([f(decay_w2[0]), f(decay_w0[0])[None, :]], axis=0)
    shared = {
        "w_in": f(w_in[0]), "w_br": f(w_branch_r[0]), "w_bc": f(w_branch_c[0]), "w_out": f(w_out[0]),
        "cols": cols, "consts": consts, "w2ext": f(w2ext), "a2": f(iclr_a2[0]),
        "npg": f(norm_post_g[0])[None, :], "npre": f(norm_pre_g[0])[None, :],
    }
    xpf = f(x_prompt); xsf = f(x_sample).reshape(128, D); ssf = f(state_shift[0])
    swf = f(state_wkv[0]).reshape(128, 1024, 64); scf = f(state_conv[0]).reshape(128 * 30, 512)
    in_maps = []
    for c in range(8):
        m = dict(shared)
        m["xp"] = xpf[c]
        m["xs"] = xsf[c * NS:(c + 1) * NS]
        m["sshift"] = ssf[c * NS:(c + 1) * NS]
        m["swkv"] = swf[c * NS:(c + 1) * NS]
        m["sconv"] = scf[c * NS * 30:(c + 1) * NS * 30]
        in_maps.append(m)
    if _NC is None:
        _NC = build()
    res = run_bass_kernel_spmd(_NC, in_maps, core_ids=list(range(8)))
    R = res.results
    y_prompt = np.stack([R[c]["yp"] for c in range(8)]).astype(np.float32)
    y_sample = np.concatenate([R[c]["ys"] for c in range(8)]).reshape(128, 1, D).astype(np.float32)
    nsp_ = np.concatenate([R[c]["nsp"] for c in range(8)]).reshape(1, 8, D).astype(np.float32)
    nwp_ = np.stack([R[c]["nwp"] for c in range(8)]).reshape(1, 8, 16, 64, 64).astype(np.float32)
    ncp_ = np.stack([R[c]["ncp"] for c in range(8)]).reshape(1, 8, 30, 512).astype(np.float32)
    nss_ = np.concatenate([R[c]["nss"] for c in range(8)]).reshape(1, 128, D).astype(np.float32)
    nws_ = np.concatenate([R[c]["nws"] for c in range(8)]).reshape(1, 128, 16, 64, 64).astype(np.float32)
    ncs_ = np.concatenate([R[c]["ncs"] for c in range(8)]).reshape(1, 128, 30, 512).astype(np.float32)
    return (y_prompt, y_sample, nsp_, nwp_, ncp_, nss_, nws_, ncs_)
```

```python
import contextlib
import numpy as np
import concourse.bass as bass
import concourse.mybir as mybir
from concourse.bass_utils import run_bass_kernel_spmd

F32 = mybir.dt.float32
BF16 = mybir.dt.bfloat16
AF = mybir.ActivationFunctionType
ALU = mybir.AluOpType
AX = mybir.AxisListType

D = 1024
NIN = 7808
SEQ = 2048
NS = 16
NCH = 61
CNEG = -0.6065306597126334

O_MU = 0; O_KK = 33; O_KA = 41; O_RK = 49; O_GNG = 57; O_GNB = 65; O_A0 = 73; O_GLUB = 81
O_CB = 89; O_LNG = 93; O_LNB = 97; O_CW = 101; O_GPRE = 225; O_OMM = 233; O_OMKA = 266; NCOL = 274
C_ID = 0; C_SL = 128; C_SU = 256; C_UI = 384; C_TRI = 512; C_TRE = 640; C_BM = 768; C_BO = 896
C_AM = 1024; C_NI = 1152; C_I2 = 1280; NCONST = 1344


class Prog:
    ENG = ("pe", "act", "dve", "pool", "sp")

    def __init__(self, nc):
        self.nc = nc
        self.ops = {e: [] for e in self.ENG}
        self.cnt = {e: 0 for e in self.ENG}
        self.waited = {e: {} for e in self.ENG}
        self.lastw = {}
        self.readers = {}
        self.dcnt = {}
        self.dead = False
        self.phase = ""
        self.annotate = False

    def _need(self, eng, waits, tok):
        if tok is None:
            return
        kind, key, val = tok
        if kind == "e" and key == "pe" and eng == "pe":
            return
        k = (kind, key)
        if self.waited[eng].get(k, 0) >= val:
            return
        if waits.get(k, 0) < val:
            waits[k] = val

    def op(self, eng, fn, reads=(), writes=(), dma=None, tag=None):
        if self.dead:
            return None
        waits = {}
        if eng == "pe":
            prev = getattr(self, "petag", None)
            if tag is not None and prev is not None and tag != prev:
                waits[("e", "pe")] = self.cnt["pe"]
            self.petag = tag
        for r in reads:
            self._need(eng, waits, self.lastw.get(r))
        for w in writes:
            self._need(eng, waits, self.lastw.get(w))
            for rd in self.readers.get(w, ()):
                self._need(eng, waits, rd)
        for k, v in waits.items():
            self.waited[eng][k] = v
        if dma is not None:
            prevc = self.dcnt.get(dma, 0)
            if prevc > 0 and self.waited[eng].get(("d", dma), 0) < prevc:
                waits[("d", dma)] = max(waits.get(("d", dma), 0), prevc)
                self.waited[eng][("d", dma)] = prevc
            self.dcnt[dma] = prevc + 1
            tok = ("d", dma, self.dcnt[dma])
        else:
            self.cnt[eng] += 1
            tok = ("e", eng, self.cnt[eng])
        self.ops[eng].append((waits, fn, tok, self.phase))
        for r in reads:
            self.readers.setdefault(r, []).append(tok)
        for w in writes:
            self.lastw[w] = tok
            self.readers[w] = []
        return tok

    def emit(self):
        nc = self.nc
        with contextlib.ExitStack() as st:
            esem = {e: st.enter_context(nc.semaphore("s_" + e)) for e in self.ENG}
            dsem = {k: st.enter_context(nc.semaphore("d_" + str(k))) for k in self.dcnt}
            block = st.enter_context(nc.Block())

            def run(engname, e):
                for waits, fn, tok, ph in self.ops[engname]:
                    for (kind, key), val in waits.items():
                        if kind == "e":
                            e.wait_ge(esem[key], val)
                        else:
                            e.wait_ge(dsem[key], 16 * val)
                    ins = fn(e)
                    if self.annotate:
                        ins.annotate(ph)
                    if tok[0] == "e":
                        ins.then_inc(esem[tok[1]], 1)
                    else:
                        ins.then_inc(dsem[tok[1]], 16)
                if engname == "sp":
                    for k, c in self.dcnt.items():
                        e.wait_ge(dsem[k], 16 * c)

            @block.tensor
            def _(e):
                run("pe", e)

            @block.scalar
            def _(e):
                run("act", e)

            @block.vector
            def _(e):
                run("dve", e)

            @block.gpsimd
            def _(e):
                run("pool", e)

            @block.sync
            def _(e):
                run("sp", e)


def build():
    nc = bass.Bass("TRN2", target_bir_lowering=False)
    di = lambda n, s: nc.dram_tensor(n, s, F32, kind="ExternalInput").ap()
    do = lambda n, s: nc.dram_tensor(n, s, F32, kind="ExternalOutput").ap()
    xp = di("xp", [SEQ, D]); xs = di("xs", [NS, D]); sshift = di("sshift", [NS, D])
    swkv = di("swkv", [NS, 1024, 64]); sconv = di("sconv", [NS * 30, 512])
    w_in = di("w_in", [D, NIN]); w_br = di("w_br", [D, D]); w_bc = di("w_bc", [512, D]); w_out = di("w_out", [D, D])
    cols_d = di("cols", [128, NCOL]); consts_d = di("consts", [128, NCONST])
    w2ext_d = di("w2ext", [65, 1024]); a2_d = di("a2", [64, 1024])
    npg_d = di("npg", [1, D]); npre_d = di("npre", [1, D])
    yp = do("yp", [SEQ, D]); ys = do("ys", [NS, D]); nsp = do("nsp", [1, D])
    nwp = do("nwp", [1024, 64]); ncp = do("ncp", [30, 512]); nss = do("nss", [NS, D])
    nws = do("nws", [NS, 1024, 64]); ncs = do("ncs", [NS, 30, 512])
    wi_s = nc.dram_tensor("wi_s", [D, NIN], BF16).ap()
    wbr_s = nc.dram_tensor("wbr_s", [D, D], BF16).ap()
    wbc_s = nc.dram_tensor("wbc_s", [512, D], BF16).ap()
    wo_s = nc.dram_tensor("wo_s", [D, D], BF16).ap()

    with contextlib.ExitStack() as st:
        def T(n, s, d=F32):
            return st.enter_context(nc.sbuf_tensor("sb_" + n, s, d))
        P = Prog(nc)
        op = P.op
        cst = T("cst", [128, NCONST]); col = T("col", [128, NCOL])
        idb = T("idb", [128, 128], BF16)
        bob = T("bob", [128, 128], BF16)
        bmb = T("bmb", [128, 128], BF16)
        w2e = T("w2e", [65, 1024]); a2b = T("a2b", [128, 1024], BF16)
        wb = [T("wb%d" % i, [128, 8, 512], BF16) for i in range(4)]
        xt = [T("xt%d" % i, [128, D]) for i in range(2)]
        hb = T("hb", [128, D], BF16)
        hT = T("hT", [128, 8, 512], BF16)
        rS = T("rS", [128, 8, 512], BF16); kS = T("kS", [128, 8, 512], BF16)
        vS = T("vS", [128, 8, 512], BF16); zrS = T("zrS", [128, 8, 512], BF16)
        twl = T("twl", [65, 512]); alb = T("alb", [128, 512], BF16)
        ua = T("ua", [128, 4, 512]); uex = T("uex", [128, 4, 542]); ubf = T("ubf", [128, 4, 542], BF16)
        szc = T("szc", [128, 4, 512], BF16)
        orT = T("orT", [128, 8, 512], BF16)
        mT = rS
        ocT = kS
        TT = [T("T%d" % i, [128, 8, 128]) for i in range(8)]
        bon = T("bon", [128, 8, 128], BF16)
        rt_ = T("rt_", [128, 8, 128], BF16); at_ = T("at_", [128, 8, 128], BF16); bt_ = T("bt_", [128, 8, 128], BF16)
        kt_ = T("kt_", [128, 8, 128], BF16); bh_ = T("bh_", [128, 8, 128], BF16); kh_ = T("kh_", [128, 8, 128], BF16)
        Vt = T("Vt", [128, 1024], BF16); Bt = T("Bt", [128, 1024], BF16); Kt = T("Kt", [128, 1024], BF16)
        Ak = [T("Ak%d" % i, [128, 4, 128], BF16) for i in range(2)]
        Nk = [T("Nk%d" % i, [128, 4, 128], BF16) for i in range(2)]
        Qb = T("Qb", [128, 4, 128], BF16)
        LkT = T("LkT", [128, 4, 128], BF16); MbT = T("MbT", [128, 4, 128], BF16); MkT = T("MkT", [128, 4, 128], BF16)
        Xb = T("Xb", [128, 256], BF16); SAb = T("SAb", [128, 256], BF16)
        Xb2 = T("Xb2", [128, 256], BF16); SAb2 = T("SAb2", [128, 256], BF16)
        dummy = T("dummy", [128, 8])
        _w3 = lambda i: wb[3][:, i, :].rearrange("p (a b) -> p a b", b=128)
        SETS = [
            {"Ak": Ak, "Nk": Nk, "Qb": Qb, "LkT": LkT, "MbT": MbT, "MkT": MkT, "Xb": Xb, "SAb": SAb, "banks": (0, 1, 2), "n": "_A"},
            {"Ak": [_w3(0), _w3(1)], "Nk": [_w3(2), _w3(3)], "Qb": _w3(4), "LkT": _w3(5), "MbT": _w3(6), "MkT": _w3(7),
             "Xb": Xb2, "SAb": SAb2, "banks": (3, 4, 5), "n": "_B"},
        ]
        SETB_KEYS = [k + "_B" for k in ("Ak0", "Ak1", "Nk0", "Nk1", "Qb", "LkT", "MbT", "MkT")]
        ALLT4 = ["T4_0", "T4_1", "T4_2", "T4_3"]
        ALLSF = ["Sf0", "Sf1", "Sf2", "Sf3"]
        ALLSB = ["Sb0", "Sb1", "Sb2", "Sb3"]
        Sf = T("Sf", [128, 8, 64]); Sb = T("Sb", [128, 8, 64], BF16)
        gC = T("gC", [128, 8]); tmpb = T("tmpb", [128, 512]); tmpc = T("tmpc", [128, 512])
        sgb = T("sgb", [128, 512], BF16)
        m1 = TT[5][:].rearrange("p a b -> p (a b)")
        pprev = [T("pprev%d" % i, [128, 40]) for i in range(2)]
        small = T("small", [128, 64])
        dg = [T("dg%d" % i, [128, 128], BF16) for i in range(4)]
        wld = xt
        pb = [st.enter_context(nc.psum_tensor("pb%d" % i, [128, 512], F32)) for i in range(7)]
        ptb = st.enter_context(nc.psum_tensor("ptb", [128, 1024], BF16))

        cnt = {"d": 0, "e": 0}
        import os
        STOP = float(os.environ.get("MK_STOP", "1000"))
        DMACAST = bool(int(os.environ.get("MK_DMACAST", "1")))

        def stage(k):
            if k > STOP:
                P.dead = True
        P.annotate = bool(os.environ.get("MK_ANN"))

        def ph(name):
            P.phase = name

        def dma(out, in_, reads, writes, q="sp"):
            cnt[q] = cnt.get(q, 0) + 1
            key = "%s%d" % (q, cnt[q] % (16 if q == "sp" else 8))
            if q == "act":
                return op("act", lambda e: e.dma_start(out=out, in_=in_), reads, writes, dma=key)
            return op(q, lambda e: e.dma_start(out=out, in_=in_), reads, writes, dma=key)

        def mm(out, lhsT, rhs, start, stop, reads, writes):
            b0 = lhsT.base_partition()
            n0 = lhsT.shape[0]
            tag = "lo" if b0 + n0 <= 64 else ("hi" if b0 >= 64 else None)
            op("pe", lambda e: e.matmul(out, lhsT=lhsT, rhs=rhs, start=start, stop=stop), reads, writes, tag=tag)

        def act(out, in_, func, reads, writes, bias=None, scale=None, accum=None):
            kw = {}
            if bias is not None: kw["bias"] = bias
            if scale is not None: kw["scale"] = scale
            if accum is not None: kw["accum_out"] = accum
            op("act", lambda e: e.activation(out=out, in_=in_, func=func, **kw), reads, writes)

        def tt(eng, out, in0, in1, o, reads, writes):
            g = {"dve": "dve", "pool": "pool"}[eng]
            op(g, lambda e: e.tensor_tensor(out=out, in0=in0, in1=in1, op=o), reads, writes)

        def ts(eng, out, in0, s1, s2, o0, o1, reads, writes):
            if s2 is None:
                op(eng, lambda e: e.tensor_scalar(out=out, in0=in0, scalar1=s1, scalar2=None, op0=o0), reads, writes)
            else:
                op(eng, lambda e: e.tensor_scalar(out=out, in0=in0, scalar1=s1, scalar2=s2, op0=o0, op1=o1), reads, writes)

        def stt(eng, out, in0, sc, in1, o0, o1, reads, writes):
            op(eng, lambda e: e.scalar_tensor_tensor(out=out, in0=in0, scalar=sc, in1=in1, op0=o0, op1=o1), reads, writes)

        def cp(eng, out, in_, reads, writes):
            if eng == "act":
                act(out, in_, AF.Copy, reads, writes)
            else:
                op(eng, lambda e: e.tensor_copy(out=out, in_=in_), reads, writes)

        def rsq(out, in_, eps, reads, wkey):
            act(out, in_, AF.Sqrt, reads, [wkey], bias=eps)
            op("dve", lambda e: e.reciprocal(out=out, in_=out), [wkey], [wkey])

        def bc(ap, shape):
            return ap.to_broadcast(shape)

        C = lambda o, n=128: cst[:, o:o + n]

        dma(cst[:], consts_d, [], ["cst"])
        dma(col[:], cols_d, [], ["col"])
        dma(w2e[:], w2ext_d, [], ["w2e"])
        dma(wld[0][64:128, 0:1024], a2_d, [], ["xt0"])
        cp("dve", a2b[64:128, :], wld[0][64:128, 0:1024], ["xt0"], ["a2b"])
        cp("dve", idb[:], C(C_ID), ["cst"], ["idb"])
        cp("dve", bob[:], C(C_BO), ["cst"], ["bob"])
        cp("dve", bmb[:], C(C_BM), ["cst"], ["bmb"])
        ts("dve", col[:, O_OMM:O_OMM + 33], col[:, O_MU:O_MU + 33], -1.0, 1.0, ALU.mult, ALU.add, ["col"], ["col"])
        ts("dve", col[:, O_OMKA:O_OMKA + 8], col[:, O_KA:O_KA + 8], -1.0, 1.0, ALU.mult, ALU.add, ["col"], ["col"])
        op("pool", lambda e: e.memset(twl[64:65, :], 1.0), [], ["twl"])
        op("pool", lambda e: e.memset(Sf[:], 0.0), [], ALLSF)
        op("pool", lambda e: e.memset(Sb[:], 0.0), [], ALLSB)
        op("pool", lambda e: e.memset(pprev[0][:], 0.0), [], ["pprev0"])
        op("pool", lambda e: e.memset(pprev[1][:], 0.0), [], ["pprev1"])
        op("pool", lambda e: e.memset(uex[:], 0.0), [], ["uex"])

        stage(1)
        ph("prologue")
        def prologue_gen(part):
            ph("prologue")
            if DMACAST:
                if part == 1:
                    blocks = [(w_in, wi_s, c0, min(1024, NIN - c0), "scr_i%d" % (c0 // 1024)) for c0 in range(0, 6 * 1024, 1024)]
                else:
                    blocks = [(w_in, wi_s, c0, min(1024, NIN - c0), "scr_i%d" % (c0 // 1024)) for c0 in range(6 * 1024, NIN, 1024)]
                    blocks += [(w_br, wbr_s, 0, 1024, "scr_o"), (w_bc, wbc_s, 0, 1024, "scr_o"), (w_out, wo_s, 0, 1024, "scr_o")]
                for (src, dst, c0, n, skey) in blocks:
                    nr = src.shape[0]
                    for r0 in range(0, nr, 256):
                        dma(dst[r0:r0 + 256, c0:c0 + n], src[r0:r0 + 256, c0:c0 + n], [], [skey], q="pool")
                        yield
                return
            pieces = []
            for c0 in range(0, NIN, 1024):
                for rc in range(8):
                    pieces.append((w_in, wi_s, rc, c0, min(1024, NIN - c0), "scr_i%d" % (c0 // 1024)))
            for rc in range(8):
                pieces.append((w_br, wbr_s, rc, 0, 1024, "scr_o"))
            for rc in range(4):
                pieces.append((w_bc, wbc_s, rc, 0, 1024, "scr_o"))
            for rc in range(8):
                pieces.append((w_out, wo_s, rc, 0, 1024, "scr_o"))
            fl = lambda t: t[:].rearrange("p a b -> p (a b)")
            orv = lambda i: orT[:, 2 * i:2 * i + 2, :].rearrange("p a b -> p (a b)")
            if part == 1:
                pieces = pieces[0:48]
                sf32 = [(fl(TT[i]), "T%d" % i) for i in range(8)]
                sbf = [(fl(rt_), "rt_0"), (fl(at_), "at_0"), (fl(bt_), "bt_0"), (fl(kt_), "kt_0"), (fl(bh_), "bh_"), (fl(kh_), "kh_"),
                       (Vt[:, :], "Vt_0"), (Bt[:, :], "Bt_0"), (Kt[:, :], "Kt_0")]
                DEPTH = 6
            else:
                pieces = pieces[48:]
                sf32 = [(fl(TT[i]), "T%d" % i) for i in (3, 4, 6, 7)]
                sbf = [(orv(0), "orT"), (orv(1), "orT"), (orv(2), "orT"), (orv(3), "orT"),
                       (fl(kh_), "kh_"), (Vt[:, :], "Vt_0"), (Bt[:, :], "Bt_0"), (Kt[:, :], "Kt_0")]
                DEPTH = 3
            NB = len(sf32)
            engs = ["dve", "act"]
            npc = len(pieces)
            for i in range(npc + DEPTH):
                if i < npc:
                    src, dst, rc, c0, n, skey = pieces[i]
                    bf_, kf_ = sf32[i % NB]
                    dma(bf_[:, 0:n], src[rc * 128:(rc + 1) * 128, c0:c0 + n], [], [kf_],
                        q=("act" if (os.environ.get("MK_ACTQ") and i % 2 == 1) else "sp"))
                j = i - DEPTH
                if j >= 0:
                    src, dst, rc, c0, n, skey = pieces[j]
                    bf_, kf_ = sf32[j % NB]
                    bb_, kb_ = sbf[j % len(sbf)]
                    cp(engs[j % 2], bb_[:, 0:n], bf_[:, 0:n], [kf_], [kb_])
                    dma(dst[rc * 128:(rc + 1) * 128, c0:c0 + n], bb_[:, 0:n], [kb_], [skey], q="pool")
                yield

        stage(2)
        wi_v = wi_s.rearrange("(dc p) n -> p dc n", p=128)
        wbr_v = wbr_s.rearrange("(dc p) n -> p dc n", p=128)
        wbc_v = wbc_s.rearrange("(dc p) n -> p dc n", p=128)
        wo_v = wo_s.rearrange("(dc p) n -> p dc n", p=128)
        wslot = {"i": 0}

        def wload(view, ndc, c0, n, skeys):
            s = wslot["i"] % 4
            wslot["i"] += 1
            dma(wb[s][:, 0:ndc, 0:n], view[:, :, c0:c0 + n], skeys, ["wb%d" % s])
            return s

        def ikeys(c0, n):
            return ["scr_i%d" % b for b in range(c0 // 1024, (c0 + n - 1) // 1024 + 1)]

        def rmsnorm_tile(xtile, key, npart, dst_cols, want_h_out=None):
            ph("rmsnorm")
            act(hb[0:npart, :], xtile[0:npart, :], AF.Square, [key], ["hb", "small"], accum=small[0:npart, 0:1])
            ts("dve", small[0:npart, 1:2], small[0:npart, 0:1], 1.0 / D, 1e-6, ALU.mult, ALU.add, ["small"], ["small"])
            rsq(small[0:npart, 2:3], small[0:npart, 1:2], 0.0, ["small"], "small")
            ts("dve", hb[0:npart, :], xtile[0:npart, :], small[0:npart, 2:3], None, ALU.mult, None, [key, "small"], ["hb"])
            if want_h_out is not None:
                want_h_out()
            for dc in range(8):
                op("pe", lambda e, dc=dc: e.transpose(ptb[:, dc * 128:dc * 128 + npart], hb[0:npart, dc * 128:(dc + 1) * 128], idb[0:npart, 0:npart]),
                   ["hb", "idb"], ["ptb"])
            for dc in range(8):
                act(hT[:, dc, dst_cols[0]:dst_cols[1]], ptb[:, dc * 128:dc * 128 + npart], AF.Copy, ["ptb", "col"], ["hT"],
                    scale=col[:, O_GPRE + dc:O_GPRE + dc + 1])

        def project(j, wslot_i, jj, NT, bank):
            for dc in range(8):
                mm(pb[bank][:, 0:NT], wb[wslot_i][:, dc, jj * 128:(jj + 1) * 128], hT[:, dc, 0:NT], dc == 0, dc == 7,
                   ["wb%d" % wslot_i, "hT"], ["pb%d" % bank])

        def shiftmix(j, bank, NT, dst, dkey, sample, pp_old, pp_new):
            p = pb[bank]
            mu = col[:, O_MU + j:O_MU + j + 1]
            omm = col[:, O_OMM + j:O_OMM + j + 1]
            bk = "pb%d" % bank
            if sample:
                act(tmpb[:, 0:NS], p[:, 0:NS], AF.Copy, [bk, "col"], ["tmpb"], scale=omm)
                stt("dve", dst, p[:, NS:2 * NS], mu, tmpb[:, 0:NS], ALU.mult, ALU.add, [bk, "tmpb", "col"], [dkey])
            else:
                act(tmpb[:, 0:NT], p[:, 0:NT], AF.Copy, [bk, "col"], ["tmpb"], scale=omm)
                act(pprev[pp_new][:, j:j + 1], p[:, NT - 1:NT], AF.Copy, [bk], ["pprev%d" % pp_new])
                stt("dve", dst[:, 1:NT], p[:, 0:NT - 1], mu, tmpb[:, 1:NT], ALU.mult, ALU.add, [bk, "tmpb", "col"], [dkey])
                stt("dve", dst[:, 0:1], pprev[pp_old][:, j:j + 1], mu, tmpb[:, 0:1], ALU.mult, ALU.add,
                    ["pprev%d" % pp_old, "tmpb", "col"], [dkey])

        def proj_phase(NT, sample, pp_old, pp_new):
            ph("proj")
            nb = 0
            for g0 in range(0, 45, 4):
                ng = min(4, 45 - g0)
                s = wload(wi_v, 8, g0 * 128, ng * 128, ikeys(g0 * 128, ng * 128))
                for jj in range(ng):
                    j = g0 + jj
                    bank = nb % 2
                    nb += 1
                    bk = "pb%d" % bank
                    project(j, s, jj, NT if not sample else 2 * NS, bank)
                    W = NS if sample else NT
                    if j < 8:
                        shiftmix(j, bank, NT, rS[:, j, 0:W], "rS", sample, pp_old, pp_new)
                    elif j < 16:
                        shiftmix(j, bank, NT, kS[:, j - 8, 0:W], "kS", sample, pp_old, pp_new)
                    elif j < 24:
                        shiftmix(j, bank, NT, vS[:, j - 16, 0:W], "vS", sample, pp_old, pp_new)
                    elif j < 32:
                        shiftmix(j, bank, NT, tmpc[:, 0:W], "tmpc", sample, pp_old, pp_new)
                        act(zrS[:, j - 24, 0:W], tmpc[:, 0:W], AF.Silu, ["tmpc"], ["zrS"])
                    elif j == 32:
                        shiftmix(j, bank, NT, tmpc[:, 0:W], "tmpc", sample, pp_old, pp_new)
                        act(twl[0:64, 0:W], tmpc[0:64, 0:W], AF.Tanh, ["tmpc"], ["twl"])
                        cp("pool", alb[64:128, 0:W], tmpc[64:128, 0:W], ["tmpc"], ["alb"])
                    elif j < 37:
                        c = j - 33
                        act(ua[:, c, 0:W], pb[bank][:, 0:W], AF.Identity, [bk, "col"], ["ua"],
                            bias=col[:, O_GLUB + c:O_GLUB + c + 1])
                    elif j < 41:
                        c = j - 37
                        act(tmpc[:, 0:W], pb[bank][:, 0:W], AF.Sigmoid, [bk, "col"], ["tmpc"],
                            bias=col[:, O_GLUB + 4 + c:O_GLUB + 5 + c])
                        if sample:
                            tt("dve", uex[:, c, 0:NS * 31].rearrange("p (n w) -> p n w", w=31)[:, :, 30], ua[:, c, 0:W], tmpc[:, 0:W],
                               ALU.mult, ["ua", "tmpc"], ["uex"])
                        else:
                            tt("dve", uex[:, c, 30:30 + W], ua[:, c, 0:W], tmpc[:, 0:W], ALU.mult, ["ua", "tmpc"], ["uex"])
                    else:
                        c = j - 41
                        act(szc[:, c, 0:W], pb[bank][:, 0:W], AF.Silu, [bk], ["szc"])
                    yield

        def ln_conv_out(W, cf, ck):
            ph("lnconv")
            for c in range(4):
                mm(pb[2][:, 0:W], C(C_AM), cf[c], c == 0, c == 3, ["cst", ck[c]], ["pb2"])
            for c in range(4):
                tt("dve", cf[c], cf[c], pb[2][:, 0:W], ALU.subtract, [ck[c], "pb2"], [ck[c]])
            for c in range(4):
                tt("pool", ua[:, c, 0:W], cf[c], cf[c], ALU.mult, [ck[c]], ["ua"])
            for c in range(4):
                mm(pb[3][:, 0:W], C(C_AM), ua[:, c, 0:W], c == 0, c == 3, ["cst", "ua"], ["pb3"])
            rsq(tmpc[:, 0:W], pb[3][:, 0:W], 1e-5, ["pb3"], "tmpc")
            for c in range(4):
                tt("dve", cf[c], cf[c], tmpc[:, 0:W], ALU.mult, [ck[c], "tmpc"], [ck[c]])
                act(ua[:, c, 0:W], cf[c], AF.Silu, [ck[c], "col"], ["ua"],
                    bias=col[:, O_LNB + c:O_LNB + c + 1], scale=col[:, O_LNG + c:O_LNG + c + 1])
                tt("pool", ocT[:, c, 0:W], ua[:, c, 0:W], szc[:, c, 0:W], ALU.mult, ["ua", "szc"], ["kS"])

        def prep_gen(cs, W, sample, PSp):
            T0, T1, T2, T3, T4, T5, T6, T7 = TT
            sl = slice(cs, cs + W)
            sfx = PSp["sfx"]
            bonT, gCt = PSp["bon"], PSp["gC"]
            kbon, kgc = "bon" + sfx, "gC" + sfx
            ph("prep")
            sh = [128, 8, W]
            colb = lambda o: bc(col[:, o:o + 8].unsqueeze(2), sh)
            p6 = pb[6]
            p6v = p6[:].rearrange("p (a b) -> p a b", b=128)[:, :, 0:W]
            T0v = T0[:].rearrange("p a b -> p (a b)")
            tt("dve", T5[:, :, 0:W], kS[:, :, sl], colb(O_KK), ALU.mult, ["kS", "col"], ["T5"])
            tt("pool", bh_[:, :, 0:W], T5[:, :, 0:W], T5[:, :, 0:W], ALU.mult, ["T5"], ["bh_"])
            yield
            for hf in range(2):
                mm(p6[0:W, :], twl[0:65, sl], w2e[0:65, hf * 512:(hf + 1) * 512], True, True, ["twl", "w2e"], ["pb6"])
                yield
                act(T0v[0:W, hf * 512:(hf + 1) * 512], p6[0:W, :], AF.Sigmoid, ["pb6"], ["T0"])
                yield
            tri = C(C_TRI) if not sample else cst[0:W, C_NI:C_NI + W]
            tre = C(C_TRE) if not sample else cst[0:W, C_NI + 64:C_NI + 64 + W]
            for hf in range(2):
                hs = slice(hf * 4, hf * 4 + 4)
                for hq in range(4):
                    hh = hf * 4 + hq
                    mm(p6[:, hq * 128:hq * 128 + W], T0v[0:W, hh * 128:(hh + 1) * 128], tri[0:W, 0:W], True, True, ["T0", "cst"], ["pb6"])
                yield
                act(T1[:, hs, 0:W], p6v[:, 0:4, :], AF.Exp, ["pb6"], ["T1"])
                act(T2[:, hs, 0:W], p6v[:, 0:4, :], AF.Exp, ["pb6"], ["T2"], scale=-1.0)
                yield
            cp("pool", gCt[:, :], T1[:, :, W - 1], ["T1"], [kgc])
            for hf in range(2):
                for hq in range(4):
                    hh = hf * 4 + hq
                    mm(p6[:, hq * 128:hq * 128 + W], a2b[64:128, hh * 128:(hh + 1) * 128], alb[64:128, sl], True, True, ["a2b", "alb"], ["pb6"])
                yield
                for hq in range(4):
                    hh = hf * 4 + hq
                    act(T4[:, hh, 0:W], p6[:, hq * 128:hq * 128 + W], AF.Sigmoid, ["pb6", "col"], ALLT4,
                        bias=col[:, O_A0 + hh:O_A0 + hh + 1])
                yield
            for hf in range(2):
                hs = slice(hf * 4, hf * 4 + 4)
                for hq in range(4):
                    hh = hf * 4 + hq
                    mm(p6[:, hq * 128:hq * 128 + W], bob[:], bh_[:, hh, 0:W], True, True, ["bob", "bh_"], ["pb6"])
                yield
                rsq(T7[:, hs, 0:W], p6v[:, 0:4, :], 1e-12, ["pb6"], "T7")
                yield
            stt("dve", T6[:, :, 0:W], T5[:, :, 0:W], -1.0, T7[:, :, 0:W], ALU.mult, ALU.mult, ["T5", "T7"], ["T6"])
            yield
            stt("dve", T7[:, :, 0:W], T6[:, :, 0:W], -1.0, T4[:, :, 0:W], ALU.mult, ALU.mult, ["T6"] + ALLT4, ["T7"])
            yield
            tt("pool", T0[:, :, 0:W], T4[:, :, 0:W], colb(O_KA), ALU.mult, ALLT4 + ["col", "T0"], ["T0"])
            tt("pool", T0[:, :, 0:W], T0[:, :, 0:W], colb(O_OMKA), ALU.add, ["T0", "col"], ["T0"])
            yield
            tt("dve", T5[:, :, 0:W], kS[:, :, sl], T0[:, :, 0:W], ALU.mult, ["kS", "T0"], ["T5"])
            yield
            tt("pool", T0[:, :, 0:W], rS[:, :, sl], T5[:, :, 0:W], ALU.mult, ["rS", "T5"], ["T0"])
            tt("pool", kh_[:, :, 0:W], T0[:, :, 0:W], colb(O_RK), ALU.mult, ["T0", "col"], ["kh_"])
            yield
            for hf in range(2):
                hs = slice(hf * 4, hf * 4 + 4)
                for hq in range(4):
                    hh = hf * 4 + hq
                    mm(p6[:, hq * 128:hq * 128 + W], bob[:], kh_[:, hh, 0:W], True, True, ["bob", "kh_"], ["pb6"])
                yield
                tt("dve", bonT[:, hs, 0:W], p6v[:, 0:4, :], vS[:, hs, sl], ALU.mult, ["pb6", "vS"], [kbon])
                yield
            if sample:
                return
            ph("mults")
            EG, EnG, EGe, k2, aa, bb = T1, T2, T3, T5, T6, T7
            rt, at, bt, kt = PSp["rt"], PSp["at"], PSp["bt"], PSp["kt"]
            krt, kat, kbt, kkt = "rt" + sfx, "at" + sfx, "bt" + sfx, "kt" + sfx
            tt("dve", rt, rS[:, :, sl], EG[:], ALU.mult, ["rS", "T1"], [krt])
            tt("pool", at[:, :, 1:128], aa[:, :, 1:128], EG[:, :, 0:127], ALU.mult, ["T6", "T1"], [kat])
            cp("pool", at[:, :, 0:1], aa[:, :, 0:1], ["T6"], [kat])
            yield
            tt("dve", bt, bb[:], EnG[:], ALU.mult, ["T7", "T2"], [kbt])
            tt("pool", kt, k2[:], EnG[:], ALU.mult, ["T5", "T2"], [kkt])
            yield
            tt("dve", EGe[:], EnG[:], bc(EG[:, :, 127:128], [128, 8, 128]), ALU.mult, ["T2", "T1", kat], ["T3"])
            yield
            tt("pool", bh_[:], bb[:], EGe[:], ALU.mult, ["T7", "T3"], ["bh_"])
            tt("dve", kh_[:], k2[:], EGe[:], ALU.mult, ["T5", "T3"], ["kh_"])
            yield
            ph("transp")
            for src, skey, dst, dkey in ((vS, "vS", PSp["Vt"], "Vt" + sfx), (bh_, "bh_", PSp["Bt"], "Bt" + sfx), (kh_, "kh_", PSp["Kt"], "Kt" + sfx)):
                for hh in range(8):
                    srcap = src[:, hh, sl] if src is vS else src[:, hh, :]
                    op("pe", lambda e, srcap=srcap, hh=hh: e.transpose(ptb[:, hh * 128:(hh + 1) * 128], srcap, idb[:]),
                       [skey, "idb"], ["ptb"])
                yield
                cp("act", dst[:, 0:512], ptb[:, 0:512], ["ptb"], [dkey])
                cp("dve", dst[:, 512:1024], ptb[:, 512:1024], ["ptb"], [dkey])
                yield

        def gn_gen(yT, ykey, cs, W, G, Gk, bonT, kbon):
            ph("gn")
            G1, G2, G3 = G
            k1, k2_, k3 = Gk
            sl = slice(cs, cs + W)
            sh = [128, 8, W]
            colb = lambda o: bc(col[:, o:o + 8].unsqueeze(2), sh)
            p6 = pb[6]
            p6v = p6[:].rearrange("p (a b) -> p a b", b=128)[:, :, 0:W]
            for hf in range(2):
                hs = slice(hf * 4, hf * 4 + 4)
                for hq in range(4):
                    hh = hf * 4 + hq
                    mm(p6[:, hq * 128:hq * 128 + W], C(C_BM), yT[:, hh, 0:W], True, True, ["cst"] + ykey, ["pb6"])
                yield
                tt("dve", G1[:, hs, 0:W], yT[:, hs, 0:W], p6v[:, 0:4, :], ALU.subtract, ykey + ["pb6"], [k1])
                yield
            hbv = hb[:, :].rearrange("p (a b) -> p a b", b=128)
            tt("pool", hbv[:, :, 0:W], G1[:, :, 0:W], G1[:, :, 0:W], ALU.mult, [k1], ["hb"])
            yield
            for hf in range(2):
                hs = slice(hf * 4, hf * 4 + 4)
                for hq in range(4):
                    hh = hf * 4 + hq
                    mm(p6[:, hq * 128:hq * 128 + W], bmb[:], hbv[:, hh, 0:W], True, True, ["bmb", "hb"], ["pb6"])
                yield
                rsq(G3[:, hs, 0:W], p6v[:, 0:4, :], 64e-5, ["pb6"], k3)
                yield
            tt("dve", G1[:, :, 0:W], G1[:, :, 0:W], G3[:, :, 0:W], ALU.mult, [k1, k3], [k1])
            yield
            tt("pool", G1[:, :, 0:W], G1[:, :, 0:W], colb(O_GNG), ALU.mult, [k1, "col"], [k1])
            tt("pool", G1[:, :, 0:W], G1[:, :, 0:W], colb(O_GNB), ALU.add, [k1, "col"], [k1])
            yield
            tt("dve", G1[:, :, 0:W], G1[:, :, 0:W], bonT[:, :, 0:W], ALU.add, [k1, kbon], [k1])
            yield
            tt("dve", orT[:, :, sl], G1[:, :, 0:W], zrS[:, :, sl], ALU.mult, [k1, "zrS"], ["orT"])
            yield

        def run_all(gens):
            gens = list(gens)
            while gens:
                for gq in list(gens):
                    try:
                        next(gq)
                    except StopIteration:
                        gens.remove(gq)

        gC2 = T("gC2", [128, 8])
        _fl = lambda t, i: t[:, 2 * i:2 * i + 2, :].rearrange("p a b -> p (a b)")
        _v8 = lambda ap: ap.rearrange("p (a b) -> p a b", b=128)
        PS = [
            {"sfx": "_0", "rt": rt_[:], "at": at_[:], "bt": bt_[:], "kt": kt_[:], "Vt": Vt, "Bt": Bt, "Kt": Kt, "bon": bon, "gC": gC},
            {"sfx": "_1", "rt": _v8(_fl(wb[0], 0)), "at": _v8(_fl(wb[0], 1)), "bt": _v8(_fl(wb[0], 2)), "kt": _v8(_fl(wb[0], 3)),
             "Vt": _fl(wb[1], 0), "Bt": _fl(wb[1], 1), "Kt": _fl(wb[1], 2), "bon": _v8(_fl(wb[1], 3)), "gC": gC2},
        ]
        PS1_KEYS = [k + "_1" for k in ("rt", "at", "bt", "kt", "Vt", "Bt", "Kt", "bon")]

        def scan_group(g, S, PSp, yT):
            Ak_, Nk_, Qb_, LkT_, MbT_, MkT_, Xb_, SAb_ = S["Ak"], S["Nk"], S["Qb"], S["LkT"], S["MbT"], S["MkT"], S["Xb"], S["SAb"]
            b0, b1, b2 = S["banks"]
            kb = lambda i: "pb%d" % i
            n = S["n"]
            sfx = PSp["sfx"]
            rt_, at_, bt_, kt_, Vt, Bt, Kt, gC = PSp["rt"], PSp["at"], PSp["bt"], PSp["kt"], PSp["Vt"], PSp["Bt"], PSp["Kt"], PSp["gC"]
            K = lambda nm: nm + n
            heads = [4 * g + x for x in (0, 2, 1, 3)]
            SbK = "Sb%d" % g; SfK = "Sf%d" % g; yK = "yT%d" % g

            def hp(h):
                hl, hh = h % 2, h // 2
                return slice(hl * 64, hl * 64 + 64), hh
            v4 = lambda p: p[:].rearrange("p (a b) -> p a b", b=128)
            mk = lambda o: bc(cst[:, o:o + 128].unsqueeze(1), [128, 4, 128])
            ph("scores")
            plan = [(b0, "at_", "bt_", Ak_[0], K("Ak0"), C_SL), (b1, "bt_", "at_", Nk_[0], K("Nk0"), C_SU),
                    (b2, "kt_", "at_", LkT_, K("LkT"), C_SU), (b0, "bt_", "rt_", MbT_, K("MbT"), C_UI),
                    (b1, "kt_", "rt_", MkT_, K("MkT"), C_UI)]
            tl = {"at_": at_, "bt_": bt_, "kt_": kt_, "rt_": rt_}
            kn = {"at_": "at" + sfx, "bt_": "bt" + sfx, "kt_": "kt" + sfx, "rt_": "rt" + sfx}
            first_lo = (n == "_A")
            for rnd in (plan[0:3], plan[3:5]):
                for tagsel in ((0, 1) if first_lo else (1, 0)):
                    for (bk, ln, rn, dst, dk, msk) in rnd:
                        for hi, h in enumerate(heads):
                            if (h % 2) != tagsel:
                                continue
                            pr, hh = hp(h)
                            mm(pb[bk][:, hi * 128:(hi + 1) * 128], tl[ln][pr, hh, :], tl[rn][pr, hh, :], True, True, [kn[ln], kn[rn]], [kb(bk)])
                yield
                for (bk, ln, rn, dst, dk, msk) in rnd:
                    tt("dve", dst[:], v4(pb[bk]), mk(msk), ALU.mult, [kb(bk), "cst"], [dk])
                    yield
            ph("doubling")
            tt("pool", Qb_[:], Nk_[0][:], mk(C_ID), ALU.add, [K("Nk0"), "cst"], [K("Qb")])
            yield
            mm(pb[b2][:, :], idb[:], Qb_.rearrange("p a b -> p (a b)"), True, True,
               ["idb", K("Qb")], [kb(b2)])
            yield
            cur = 0
            for lvl in range(6):
                nx = 1 - cur
                for hi in range(4):
                    mm(pb[b0][:, hi * 128:(hi + 1) * 128], Nk_[cur][:, hi, :], Ak_[cur][:, hi, :], True, True,
                       [K("Nk%d" % cur), K("Ak%d" % cur)], [kb(b0)])
                if lvl < 5:
                    for hi in range(4):
                        mm(pb[b1][:, hi * 128:(hi + 1) * 128], Ak_[cur][:, hi, :], Nk_[cur][:, hi, :], True, True,
                           [K("Nk%d" % cur), K("Ak%d" % cur)], [kb(b1)])
                yield
                cp("act", Ak_[nx][:], v4(pb[b0]), [kb(b0)], [K("Ak%d" % nx)])
                if lvl < 5:
                    cp("dve", Nk_[nx][:], v4(pb[b1]), [kb(b1)], [K("Nk%d" % nx)])
                yield
                for hi in range(4):
                    mm(pb[b2][:, hi * 128:(hi + 1) * 128], Ak_[nx][:, hi, :], Qb_[:, hi, :], False, True,
                       [K("Ak%d" % nx), K("Qb")], [kb(b2)])
                yield
                if lvl % 2 == 0 or os.environ.get("MK_QACT"):
                    cp("act", Qb_[:], v4(pb[b2]), [kb(b2)], [K("Qb")])
                else:
                    cp("dve", Qb_[:], v4(pb[b2]), [kb(b2)], [K("Qb")])
                yield
                cur = nx
            ph("seq")
            for hi, h in enumerate(heads):
                pr, hh = hp(h)
                o = pb[b0][:, hi * 64:(hi + 1) * 64]
                mm(o, at_[pr, hh, :], Sb[pr, hh, :], True, False, ["at" + sfx, SbK], [kb(b0)])
                mm(o, LkT_[:, hi, :], Vt[:, h * 64:(h + 1) * 64], False, True, [K("LkT"), "Vt" + sfx], [kb(b0)])
            yield
            cp("act", Xb_[:], pb[b0][:, 0:256], [kb(b0)], [K("Xb")])
            yield
            for hi, h in enumerate(heads):
                mm(pb[b1][:, hi * 64:(hi + 1) * 64], Qb_[:, hi, :], Xb_[:, hi * 64:(hi + 1) * 64], True, True, [K("Qb"), K("Xb")], [kb(b1)])
            yield
            cp("dve", SAb_[:], pb[b1][:, 0:256], [kb(b1)], [K("SAb")])
            yield
            for hi, h in enumerate(heads):
                pr, hh = hp(h)
                o = pb[b2][pr, (hh - 2 * g) * 128:(hh - 2 * g) * 128 + 128]
                mm(o, Sb[pr, hh, :], rt_[pr, hh, :], True, False, [SbK, "rt" + sfx], [kb(b2)])
                mm(o, SAb_[:, hi * 64:(hi + 1) * 64], MbT_[:, hi, :], False, False, [K("SAb"), K("MbT")], [kb(b2)])
                mm(o, Vt[:, h * 64:(h + 1) * 64], MkT_[:, hi, :], False, True, ["Vt" + sfx, K("MkT")], [kb(b2)])
            yield
            cp("act", yT[:, 2 * g:2 * g + 2, :], pb[b2][:, 0:256].rearrange("p (a b) -> p a b", b=128), [kb(b2)], [yK])
            for hi, h in enumerate(heads):
                pr, hh = hp(h)
                o = pb[b0][pr, (hh - 2 * g) * 64:(hh - 2 * g) * 64 + 64]
                mm(o, Bt[:, h * 64:(h + 1) * 64], SAb_[:, hi * 64:(hi + 1) * 64], True, False, ["Bt" + sfx, K("SAb")], [kb(b0)])
                mm(o, Kt[:, h * 64:(h + 1) * 64], Vt[:, h * 64:(h + 1) * 64], False, True, ["Kt" + sfx, "Vt" + sfx], [kb(b0)])
            yield
            gs = slice(2 * g, 2 * g + 2)
            tt("dve", Sf[:, gs, :], Sf[:, gs, :], bc(gC[:, gs].unsqueeze(2), [128, 2, 64]), ALU.mult, [SfK, "gC" + sfx], [SfK])
            tt("dve", Sf[:, gs, :], Sf[:, gs, :], pb[b0][:, 0:128].rearrange("p (a b) -> p a b", b=64), ALU.add, [SfK, kb(b0)], [SfK])
            yield
            cp("act", Sb[:, gs, :], Sf[:, gs, :], [SfK], [SbK])
            yield


        def tail(NT, xsrc_tiles, ydst_tiles, nrows):
            ph("tail")
            for q in range(2):
                sgr = wload(wi_v, 8, (45 + 4 * q) * 128, 512, ikeys((45 + 4 * q) * 128, 512))
                sbr = wload(wbr_v, 8, q * 512, 512, ["scr_o"])
                sgc = wload(wi_v, 8, (53 + 4 * q) * 128, 512, ikeys((53 + 4 * q) * 128, 512))
                sbc = wload(wbc_v, 4, q * 512, 512, ["scr_o"])
                for jj in range(4):
                    j = q * 4 + jj
                    od = j % 2
                    bA, bB, bC, bD = (0, 1, 2, 3) if od == 0 else (4, 5, 6, 3)
                    sg1, sg1k = (sgb, "sgb")
                    sg2, sg2k = (alb, "alb")
                    m1_ = TT[5 + od][:].rearrange("p a b -> p (a b)")
                    m1k = "T%d" % (5 + od)
                    t2_, t2k = ((tmpc, "tmpc"), (tmpb, "tmpb"))[od]
                    project(45 + j, sgr, jj, NT, bA)
                    act(sg1[:, 0:NT], pb[bA][:, 0:NT], AF.Sigmoid, ["pb%d" % bA], [sg1k])
                    for fc in range(8):
                        mm(pb[bB][:, 0:NT], wb[sbr][:, fc, jj * 128:(jj + 1) * 128], orT[:, fc, 0:NT], fc == 0, fc == 7,
                           ["wb%d" % sbr, "orT"], ["pb%d" % bB])
                    tt("dve", m1_[:, 0:NT], pb[bB][:, 0:NT], sg1[:, 0:NT], ALU.mult, ["pb%d" % bB, sg1k], [m1k])
                    project(53 + j, sgc, jj, NT, bC)
                    act(sg2[:, 0:NT], pb[bC][:, 0:NT], AF.Sigmoid, ["pb%d" % bC], [sg2k])
                    for fc in range(4):
                        mm(pb[bD][:, 0:NT], wb[sbc][:, fc, jj * 128:(jj + 1) * 128], ocT[:, fc, 0:NT], fc == 0, fc == 3,
                           ["wb%d" % sbc, "kS"], ["pb%d" % bD])
                    tt("dve", t2_[:, 0:NT], pb[bD][:, 0:NT], sg2[:, 0:NT], ALU.mult, ["pb%d" % bD, sg2k], [t2k])
                    tt("pool", mT[:, j, 0:NT], m1_[:, 0:NT], t2_[:, 0:NT], ALU.add, [m1k, t2k], ["rS"])
            so = [wload(wo_v, 8, 0, 512, ["scr_o"]), wload(wo_v, 8, 512, 512, ["scr_o"])]
            npg = TT[2][:].rearrange("p a b -> p (a b)")
            dma(npg[:, :], npg_d.partition_broadcast(128), [], ["T2"])
            for i, (xsrc, ydst) in enumerate(zip(xsrc_tiles, ydst_tiles)):
                xb = xt[i % 2]
                xk = "xt%d" % (i % 2)
                dma(xb[0:nrows, :], xsrc, [], [xk])
                tsl = slice(i * 128, i * 128 + nrows)
                bks = [(4, 5), (0, 1), (2, 3)][i % 3]
                so_ = 32 + 8 * (i % 3)
                stg_i = (0, 1, 3)[i % 3]
                stg = TT[stg_i][:].rearrange("p a b -> p (a b)")
                sk_ = "T%d" % stg_i
                for hf in range(2):
                    for fc in range(8):
                        mm(pb[bks[hf]][0:nrows, :], mT[:, fc, tsl], wb[so[hf]][:, fc, :], fc == 0, fc == 7,
                           ["rS", "wb%d" % so[hf]], ["pb%d" % bks[hf]])
                for hf in range(2):
                    act(hb[0:nrows, hf * 512:(hf + 1) * 512], pb[bks[hf]][0:nrows, :], AF.Square, ["pb%d" % bks[hf]], ["hb", "small"],
                        accum=small[0:nrows, so_ + hf:so_ + hf + 1])
                tt("dve", small[0:nrows, so_ + 2:so_ + 3], small[0:nrows, so_:so_ + 1], small[0:nrows, so_ + 1:so_ + 2], ALU.add, ["small"], ["small"])
                ts("dve", small[0:nrows, so_ + 3:so_ + 4], small[0:nrows, so_ + 2:so_ + 3], 1.0 / D, 1e-6, ALU.mult, ALU.add, ["small"], ["small"])
                rsq(small[0:nrows, so_ + 4:so_ + 5], small[0:nrows, so_ + 3:so_ + 4], 0.0, ["small"], "small")
                for hf in range(2):
                    hsl = slice(hf * 512, (hf + 1) * 512)
                    stt("dve", stg[0:nrows, hsl], pb[bks[hf]][0:nrows, :], small[0:nrows, so_ + 4:so_ + 5], npg[0:nrows, hsl], ALU.mult, ALU.mult,
                        ["pb%d" % bks[hf], "small", "T2"], [sk_])
                tt("pool", stg[0:nrows, :], stg[0:nrows, :], xb[0:nrows, :], ALU.add, [sk_, xk], [sk_])
                dma(ydst, stg[0:nrows, :], [sk_], [], q="pool")

        def rms_phase(sc):
            t0 = sc * 512
            for i in range(4):
                xb = xt[i % 2]; xk = "xt%d" % (i % 2)
                dma(xb[:], xp[t0 + i * 128:t0 + (i + 1) * 128, :], [], [xk])
                last = (sc == 3 and i == 3)

                def hout(xb=xb, xk=xk):
                    T0v = TT[0][:].rearrange("p a b -> p (a b)")
                    dma(T0v[:, :], npre_d.partition_broadcast(128), [], ["T0"])
                    ts("dve", TT[1][:].rearrange("p a b -> p (a b)"), xb[:], small[:, 2:3], None, ALU.mult, None, [xk, "small"], ["T1"])
                    tt("dve", TT[1][:].rearrange("p a b -> p (a b)"), TT[1][:].rearrange("p a b -> p (a b)"), T0v, ALU.mult, ["T1", "T0"], ["T1"])
                    dma(nsp, TT[1][:].rearrange("p a b -> p (a b)")[127:128, :], ["T1"], [])
                rmsnorm_tile(xb, xk, 128, (i * 128, (i + 1) * 128), hout if last else None)

        rms_phase(0)
        run_all([prologue_gen(1)])
        for sc in range(4):
            t0 = sc * 512
            if sc > 0:
                rms_phase(sc)
            stage(3 if sc == 0 else 11)
            if sc > 0:
                cp("pool", tmpc[:, 0:120].rearrange("p (c w) -> p c w", w=30), uex[:, :, 512:542], ["uex"], ["tmpc"])
                cp("pool", uex[:, :, 0:30], tmpc[:, 0:120].rearrange("p (c w) -> p c w", w=30), ["tmpc"], ["uex"])
            pg = proj_phase(512, False, sc % 2, (sc + 1) % 2)
            side = prologue_gen(2) if sc == 0 else None
            pr0 = None
            step = 0
            while True:
                try:
                    next(pg)
                except StopIteration:
                    break
                step += 1
                if side is not None:
                    try:
                        next(side)
                    except StopIteration:
                        side = None
                if step == 33:
                    pr0 = prep_gen(0, 128, False, PS[0])
                if pr0 is not None:
                    try:
                        next(pr0)
                    except StopIteration:
                        pr0 = None
            rest = [g_ for g_ in (side, pr0) if g_ is not None]
            run_all(rest)
            stage(4 if sc == 0 else 11)
            op("pool", lambda e: e.memset(dummy[:, 0:1], 0.0), [], ["wb3", "wb0", "wb1", "xt0", "xt1", "ua", "uaA", "uaB", "dummy", "yT0", "yT1", "yT2", "yT3"] + SETB_KEYS + PS1_KEYS)
            yTp = xt[0][:, :].rearrange("p (a b) -> p a b", b=128)
            Gp = (xt[1][:, :].rearrange("p (a b) -> p a b", b=128),
                  ua[:, 0:2, :].rearrange("p a b -> p (a b)").rearrange("p (a b) -> p a b", b=128),
                  ua[:, 2:4, :].rearrange("p a b -> p (a b)").rearrange("p (a b) -> p a b", b=128))
            Gpk = ("xt1", "uaA", "uaB")

            def chunk_scan(c4):
                PSp = PS[c4 % 2]
                for pair in ((0, 1), (2, 3)):
                    gens = [scan_group(pair[0], SETS[0], PSp, yTp), scan_group(pair[1], SETS[1], PSp, yTp)]
                    while gens:
                        for gq in list(gens):
                            try:
                                next(gq)
                                yield
                            except StopIteration:
                                gens.remove(gq)
                yield from gn_gen(yTp, ["yT0", "yT1", "yT2", "yT3"], c4 * 128, 128, Gp, Gpk, PSp["bon"], "bon" + PSp["sfx"])

            for c4 in range(4):
                main = chunk_scan(c4)
                side = prep_gen((c4 + 1) * 128, 128, False, PS[(c4 + 1) % 2]) if c4 < 3 else None
                RATIO = int(os.environ.get("MK_RATIO", "5"))
                done = False
                while not done:
                    for _ in range(RATIO):
                        try:
                            next(main)
                        except StopIteration:
                            done = True
                            break
                    if side is not None:
                        try:
                            next(side)
                        except StopIteration:
                            side = None
                if side is not None:
                    run_all([side])
            op("pool", lambda e: e.memset(dummy[:, 1:2], 0.0), [], ["wb3", "wb0", "wb1", "xt0", "xt1", "ua", "uaA", "uaB", "dummy", "yT0", "yT1", "yT2", "yT3"] + SETB_KEYS + PS1_KEYS)
            stage(8 if sc == 0 else 11)
            ph("conv")
            cp("pool", ubf[:], uex[:], ["uex"], ["ubf"])
            for c in range(4):
                for w in range(31):
                    s = (c * 31 + w) % 4
                    if w % 2 == 0:
                        act(dg[s][:], idb[:], AF.Copy, ["idb", "col"], ["dg%d" % s], scale=col[:, O_CW + c * 31 + w:O_CW + c * 31 + w + 1])
                    else:
                        ts("dve", dg[s][:], idb[:], col[:, O_CW + c * 31 + w:O_CW + c * 31 + w + 1], None, ALU.mult, None,
                           ["idb", "col"], ["dg%d" % s])
                    mm(pb[6][:, :], dg[s][:], ubf[:, c, w:w + 512], w == 0, w == 30, ["dg%d" % s, "ubf"], ["pb6"])
                act(TT[c][:].rearrange("p a b -> p (a b)")[:, 0:512], pb[6][:, :], AF.Identity, ["pb6", "col"], ["T%d" % c],
                    bias=col[:, O_CB + c:O_CB + c + 1])
            ln_conv_out(512, [TT[c][:].rearrange("p a b -> p (a b)")[:, 0:512] for c in range(4)], ["T0", "T1", "T2", "T3"])
            if sc == 3:
                for c in range(4):
                    mm(pb[6][0:30, c * 128:(c + 1) * 128], uex[:, c, 512:542], C(C_ID), True, True, ["uex", "cst"], ["pb6"])
                cp("dve", tmpc[0:30, :], pb[6][0:30, :], ["pb6"], ["tmpc"])
                dma(ncp, tmpc[0:30, :], ["tmpc"], [])
            stage(9 if sc == 0 else 11)
            tail(512, [xp[t0 + i * 128:t0 + (i + 1) * 128, :] for i in range(4)],
                 [yp[t0 + i * 128:t0 + (i + 1) * 128, :] for i in range(4)], 128)

        stage(12)
        for h in range(16):
            hl, hh = h % 2, h // 2
            pr = slice(hl * 64, hl * 64 + 64)
            mm(pb[0][pr, hh * 64:(hh + 1) * 64], Sf[pr, hh, :], cst[pr, C_ID + hl * 64:C_ID + hl * 64 + 64], True, True, ALLSF + ["cst"], ["pb0"])
        cp("dve", tmpc[:, :], pb[0][:, :], ["pb0"], ["tmpc"])
        dma(nwp.rearrange("(hh p) j -> p hh j", p=128), tmpc[:, :].rearrange("p (a b) -> p a b", b=64), ["tmpc"], [])

        stage(13)
        ph("sample")
        xb = xt[0]
        dma(xb[0:NS, :], xs, [], ["xt0"])

        def hout_s():
            T0v = TT[0][:].rearrange("p a b -> p (a b)")
            T1v = TT[1][:].rearrange("p a b -> p (a b)")
            dma(T0v[0:NS, :], npre_d.partition_broadcast(NS), [], ["T0"])
            ts("dve", T1v[0:NS, :], xb[0:NS, :], small[0:NS, 2:3], None, ALU.mult, None, ["xt0", "small"], ["T1"])
            tt("dve", T1v[0:NS, :], T1v[0:NS, :], T0v[0:NS, :], ALU.mult, ["T1", "T0"], ["T1"])
            dma(nss, T1v[0:NS, :], ["T1"], [])
        rmsnorm_tile(xb, "xt0", NS, (0, NS), hout_s)
        dma(xt[1][0:NS, :], sshift, [], ["xt1"])
        cp("dve", hb[0:NS, :], xt[1][0:NS, :], ["xt1"], ["hb"])
        for dc in range(8):
            op("pe", lambda e, dc=dc: e.transpose(ptb[:, dc * 128:dc * 128 + NS], hb[0:NS, dc * 128:(dc + 1) * 128], idb[0:NS, 0:NS]),
               ["hb", "idb"], ["ptb"])
        cp("act", hT[:, :, NS:2 * NS], ptb[:, :].rearrange("p (a b) -> p a b", b=128)[:, :, 0:NS], ["ptb"], ["hT"])
        uv = [uex[:, c, 0:NS * 31].rearrange("p (n w) -> p n w", w=31) for c in range(4)]
        for q in range(4):
            dma(xt[1][0:120, 0:512], sconv[q * 120:(q + 1) * 120, :], [], ["xt1"])
            for c in range(4):
                mm(pb[6][:, c * 120:(c + 1) * 120], xt[1][0:120, c * 128:(c + 1) * 128], cst[0:120, C_ID:C_ID + 120], True, True,
                   ["xt1", "cst"], ["pb6"])
            for c in range(4):
                cp("dve", uv[c][:, q * 4:(q + 1) * 4, 0:30], pb[6][:, c * 120:(c + 1) * 120].rearrange("p (n w) -> p n w", w=30),
                   ["pb6"], ["uex"])
        dma(ncs[:, 0:29, :], sconv.rearrange("(n w) c -> n w c", w=30)[:, 1:30, :], [], [])
        run_all([proj_phase(2 * NS, True, 0, 0)])
        stage(14)
        run_all([prep_gen(0, NS, True, PS[0])])
        EG, EnG, EGe, k2, aa, bb = TT[1], TT[2], TT[3], TT[5], TT[6], TT[7]
        SW = [ua[:, i, :].rearrange("p (a b) -> p a b", b=64) for i in range(2)]
        Dxs = [Vt[:, 0:512], tmpb[:, :], Bt[:, 0:512], Kt[:, 0:512], Vt[:, 512:1024]]
        Dxk = ["Vt_0", "tmpb", "Bt_0", "Kt_0", "Vt_0"]
        Dxo = [bob[:], C(C_BO), bob[:], bob[:], bob[:]]
        Dxok = ["bob", "cst", "bob", "bob", "bob"]
        yTs = TT[4]
        i2b = bc(cst[:, C_I2:C_I2 + 64].unsqueeze(1), [128, 8, 64])
        op("pool", lambda e: e.memset(dummy[:, 2:3], 0.0), [], ["ua", "uaA", "uaB", "dummy"])
        for n in range(NS):
            Sw = SW[n % 2]; sk = ("uaA", "uaB")[n % 2]
            dma(Sw, swkv[n].rearrange("(hh p) j -> p hh j", p=128), [], [sk])
            vecs = [(aa, "T6"), (EG, "T1"), (bb, "T7"), (k2, "T5"), (rS, "rS")]
            for vi, (vt_, vk) in enumerate(vecs):
                tt("pool", Dxs[vi].rearrange("p (a b) -> p a b", b=64), i2b, bc(vt_[:, :, n:n + 1], [128, 8, 64]), ALU.mult,
                   ["cst", vk], [Dxk[vi]])
                mm(pb[vi][:, :], Dxo[vi], Dxs[vi], True, True, [Dxok[vi], Dxk[vi]], ["pb%d" % vi])
            v8 = lambda p: p[:].rearrange("p (a b) -> p a b", b=64)
            W3 = TT[0][:, :, 0:64]
            tt("dve", W3, Sw, v8(pb[0]), ALU.mult, [sk, "pb0"], ["T0"])
            op("dve", lambda e: e.tensor_reduce(out=small[:, 16:24], in_=TT[0][:, :, 0:64], axis=AX.X, op=ALU.add), ["T0"], ["small"])
            tt("dve", Sw, Sw, v8(pb[1]), ALU.mult, [sk, "pb1"], [sk])
            tt("dve", W3, v8(pb[2]), bc(small[:, 16:24].unsqueeze(2), [128, 8, 64]), ALU.mult, ["pb2", "small"], ["T0"])
            tt("pool", Sw, Sw, W3, ALU.add, [sk, "T0"], [sk])
            cp("dve", small[:, 24:32], vS[:, :, n], ["vS"], ["small"])
            tt("dve", W3, v8(pb[3]), bc(small[:, 24:32].unsqueeze(2), [128, 8, 64]), ALU.mult, ["pb3", "small"], ["T0"])
            tt("pool", Sw, Sw, W3, ALU.add, [sk, "T0"], [sk])
            dma(nws[n].rearrange("(hh p) j -> p hh j", p=128), Sw, [sk], [])
            tt("dve", W3, Sw, v8(pb[4]), ALU.mult, [sk, "pb4"], ["T0"])
            op("dve", lambda e, n=n: e.tensor_reduce(out=yTs[:, :, n], in_=TT[0][:, :, 0:64], axis=AX.X, op=ALU.add), ["T0"], ["T4_0"])
        stage(15)
        op("pool", lambda e: e.memset(dummy[:, 3:4], 0.0), [], ["ua", "uaA", "uaB", "dummy"])
        run_all([gn_gen(yTs, ALLT4, 0, NS, (TT[1], TT[2], TT[3]), ("T1", "T2", "T3"), bon, "bon_0")])
        stage(16)
        cf = []
        for c in range(4):
            cwb = bc(col[:, O_CW + c * 31:O_CW + (c + 1) * 31].unsqueeze(1), [128, NS, 31])
            tt("dve", tmpc[:, 0:NS * 31].rearrange("p (n w) -> p n w", w=31), uv[c], cwb, ALU.mult, ["uex", "col"], ["tmpc"])
            cfc = TT[c][:].rearrange("p a b -> p (a b)")[:, 0:NS]
            op("dve", lambda e, cfc=cfc: e.tensor_reduce(out=cfc, in_=tmpc[:, 0:NS * 31].rearrange("p (n w) -> p n w", w=31),
                                                        axis=AX.X, op=ALU.add), ["tmpc"], ["T%d" % c])
            ts("dve", cfc, cfc, col[:, O_CB + c:O_CB + c + 1], None, ALU.add, None, ["T%d" % c, "col"], ["T%d" % c])
            cf.append(cfc)
        for c in range(4):
            cp("dve", tmpb[:, c * NS:(c + 1) * NS], uv[c][:, :, 30], ["uex"], ["tmpb"])
        for c in range(4):
            mm(pb[6][0:NS, c * 128:(c + 1) * 128], tmpb[:, c * NS:(c + 1) * NS], C(C_ID), True, True, ["tmpb", "cst"], ["pb6"])
        cp("dve", m1[0:NS, 0:512], pb[6][0:NS, :], ["pb6"], ["T5"])
        dma(ncs[:, 29, :], m1[0:NS, 0:512], ["T5"], [])
        ln_conv_out(NS, cf, ["T0", "T1", "T2", "T3"])
        stage(17)
        tail(NS, [xs], [ys], NS)
        P.emit()
    return nc


def _prep_consts():
    c = np.zeros((128, NCONST), np.float32)
    idx = np.arange(128)
    c[:, C_ID:C_ID + 128] = np.eye(128)
    c[:, C_SL:C_SL + 128] = (idx[None, :] < idx[:, None])
    c[:, C_SU:C_SU + 128] = (idx[:, None] < idx[None, :])
    c[:, C_UI:C_UI + 128] = (idx[:, None] <= idx[None, :])
    c[:, C_TRI:C_TRI + 128] = (idx[:, None] <= idx[None, :]) * CNEG
    c[:, C_TRE:C_TRE + 128] = (idx[:, None] < idx[None, :]) * CNEG
    blk = (idx[:, None] // 64 == idx[None, :] // 64).astype(np.float32)
    c[:, C_BM:C_BM + 128] = blk / 64.0
    c[:, C_BO:C_BO + 128] = blk
    c[:, C_AM:C_AM + 128] = 1.0 / 512.0
    c[:, C_NI:C_NI + 64] = np.eye(128)[:, :64] * CNEG
    c[:, C_I2:C_I2 + 64] = (idx[:, None] % 64 == np.arange(64)[None, :])
    return c


_NC = None


def kernel(x_prompt, x_sample, state_shift, state_wkv, state_conv, norm_pre_g, w_in, mu_shift,
           decay_w0, decay_w2, iclr_a0, iclr_a2, k_k, k_a, r_k, gn_g, gn_b, conv_glu_b, conv_w,
           conv_b, ln_c_g, ln_c_b, w_branch_r, w_branch_c, w_out, norm_post_g):
    global _NC
    f = lambda a: np.ascontiguousarray(np.asarray(a, dtype=np.float32))
    colv = lambda v, n: f(v).reshape(n, 128).T
    cols = np.zeros((128, NCOL), np.float32)
    cols[:, O_MU:O_MU + 33] = colv(mu_shift[0], 33)
    cols[:, O_KK:O_KK + 8] = colv(k_k[0], 8)
    cols[:, O_KA:O_KA + 8] = colv(k_a[0], 8)
    cols[:, O_RK:O_RK + 8] = colv(np.asarray(r_k[0]).reshape(-1), 8)
    cols[:, O_GNG:O_GNG + 8] = colv(gn_g[0], 8)
    cols[:, O_GNB:O_GNB + 8] = colv(gn_b[0], 8)
    cols[:, O_A0:O_A0 + 8] = colv(iclr_a0[0], 8)
    cols[:, O_GLUB:O_GLUB + 8] = colv(conv_glu_b[0], 8)
    cols[:, O_CB:O_CB + 4] = colv(conv_b[0], 4)
    cols[:, O_LNG:O_LNG + 4] = colv(ln_c_g[0], 4)
    cols[:, O_LNB:O_LNB + 4] = colv(ln_c_b[0], 4)
    cw = f(conv_w[0])
    cols[:, O_CW:O_CW + 124] = cw.reshape(31, 4, 128).transpose(2, 1, 0).reshape(128, 124)
    cols[:, O_GPRE:O_GPRE + 8] = colv(norm_pre_g[0], 8)
    consts = _prep_consts()
    w2ext = np.concatenate([f(decay_w2[0]), f(decay_w0[0])[None, :]], axis=0)
    shared = {
        "w_in": f(w_in[0]), "w_br": f(w_branch_r[0]), "w_bc": f(w_branch_c[0]), "w_out": f(w_out[0]),
        "cols": cols, "consts": consts, "w2ext": f(w2ext), "a2": f(iclr_a2[0]),
        "npg": f(norm_post_g[0])[None, :], "npre": f(norm_pre_g[0])[None, :],
    }
    xpf = f(x_prompt); xsf = f(x_sample).reshape(128, D); ssf = f(state_shift[0])
    swf = f(state_wkv[0]).reshape(128, 1024, 64); scf = f(state_conv[0]).reshape(128 * 30, 512)
    in_maps = []
    for c in range(8):
        m = dict(shared)
        m["xp"] = xpf[c]
        m["xs"] = xsf[c * NS:(c + 1) * NS]
        m["sshift"] = ssf[c * NS:(c + 1) * NS]
        m["swkv"] = swf[c * NS:(c + 1) * NS]
        m["sconv"] = scf[c * NS * 30:(c + 1) * NS * 30]
        in_maps.append(m)
    if _NC is None:
        _NC = build()
    res = run_bass_kernel_spmd(_NC, in_maps, core_ids=list(range(8)))
    R = res.results
    y_prompt = np.stack([R[c]["yp"] for c in range(8)]).astype(np.float32)
    y_sample = np.concatenate([R[c]["ys"] for c in range(8)]).reshape(128, 1, D).astype(np.float32)
    nsp_ = np.concatenate([R[c]["nsp"] for c in range(8)]).reshape(1, 8, D).astype(np.float32)
    nwp_ = np.stack([R[c]["nwp"] for c in range(8)]).reshape(1, 8, 16, 64, 64).astype(np.float32)
    ncp_ = np.stack([R[c]["ncp"] for c in range(8)]).reshape(1, 8, 30, 512).astype(np.float32)
    nss_ = np.concatenate([R[c]["nss"] for c in range(8)]).reshape(1, 128, D).astype(np.float32)
    nws_ = np.concatenate([R[c]["nws"] for c in range(8)]).reshape(1, 128, 16, 64, 64).astype(np.float32)
    ncs_ = np.concatenate([R[c]["ncs"] for c in range(8)]).reshape(1, 128, 30, 512).astype(np.float32)
    return (y_prompt, y_sample, nsp_, nwp_, ncp_, nss_, nws_, ncs_)
```

```python
import contextlib
import numpy as np
import concourse.bass as bass
import concourse.mybir as mybir
from concourse.bass_utils import run_bass_kernel_spmd

F32 = mybir.dt.float32
BF16 = mybir.dt.bfloat16
AF = mybir.ActivationFunctionType
ALU = mybir.AluOpType
AX = mybir.AxisListType

D = 1024
NIN = 7808
SEQ = 2048
NS = 16
NCH = 61
CNEG = -0.6065306597126334

O_MU = 0; O_KK = 33; O_KA = 41; O_RK = 49; O_GNG = 57; O_GNB = 65; O_A0 = 73; O_GLUB = 81
O_CB = 89; O_LNG = 93; O_LNB = 97; O_CW = 101; O_GPRE = 225; O_OMM = 233; O_OMKA = 266; NCOL = 274
C_ID = 0; C_SL = 128; C_SU = 256; C_UI = 384; C_TRI = 512; C_TRE = 640; C_BM = 768; C_BO = 896
C_AM = 1024; C_NI = 1152; C_I2 = 1280; NCONST = 1344


class Prog:
    ENG = ("pe", "act", "dve", "pool", "sp")

    def __init__(self, nc):
        self.nc = nc
        self.ops = {e: [] for e in self.ENG}
        self.cnt = {e: 0 for e in self.ENG}
        self.waited = {e: {} for e in self.ENG}
        self.lastw = {}
        self.readers = {}
        self.dcnt = {}
        self.dead = False
        self.phase = ""
        self.annotate = False

    def _need(self, eng, waits, tok):
        if tok is None:
            return
        kind, key, val = tok
        if kind == "e" and key == "pe" and eng == "pe":
            return
        k = (kind, key)
        if self.waited[eng].get(k, 0) >= val:
            return
        if waits.get(k, 0) < val:
            waits[k] = val

    def op(self, eng, fn, reads=(), writes=(), dma=None, tag=None):
        if self.dead:
            return None
        waits = {}
        if eng == "pe":
            prev = getattr(self, "petag", None)
            if tag is not None and prev is not None and tag != prev:
                waits[("e", "pe")] = self.cnt["pe"]
            self.petag = tag
        for r in reads:
            self._need(eng, waits, self.lastw.get(r))
        for w in writes:
            self._need(eng, waits, self.lastw.get(w))
            for rd in self.readers.get(w, ()):
                self._need(eng, waits, rd)
        for k, v in waits.items():
            self.waited[eng][k] = v
        if dma is not None:
            prevc = self.dcnt.get(dma, 0)
            if prevc > 0 and self.waited[eng].get(("d", dma), 0) < prevc:
                waits[("d", dma)] = max(waits.get(("d", dma), 0), prevc)
                self.waited[eng][("d", dma)] = prevc
            self.dcnt[dma] = prevc + 1
            tok = ("d", dma, self.dcnt[dma])
        else:
            self.cnt[eng] += 1
            tok = ("e", eng, self.cnt[eng])
        self.ops[eng].append((waits, fn, tok, self.phase))
        for r in reads:
            self.readers.setdefault(r, []).append(tok)
        for w in writes:
            self.lastw[w] = tok
            self.readers[w] = []
        return tok

    def emit(self):
        nc = self.nc
        with contextlib.ExitStack() as st:
            esem = {e: st.enter_context(nc.semaphore("s_" + e)) for e in self.ENG}
            dsem = {k: st.enter_context(nc.semaphore("d_" + str(k))) for k in self.dcnt}
            block = st.enter_context(nc.Block())

            def run(engname, e):
                for waits, fn, tok, ph in self.ops[engname]:
                    for (kind, key), val in waits.items():
                        if kind == "e":
                            e.wait_ge(esem[key], val)
                        else:
                            e.wait_ge(dsem[key], 16 * val)
                    ins = fn(e)
                    if self.annotate:
                        ins.annotate(ph)
                    if tok[0] == "e":
                        ins.then_inc(esem[tok[1]], 1)
                    else:
                        ins.then_inc(dsem[tok[1]], 16)
                if engname == "sp":
                    for k, c in self.dcnt.items():
                        e.wait_ge(dsem[k], 16 * c)

            @block.tensor
            def _(e):
                run("pe", e)

            @block.scalar
            def _(e):
                run("act", e)

            @block.vector
            def _(e):
                run("dve", e)

            @block.gpsimd
            def _(e):
                run("pool", e)

            @block.sync
            def _(e):
                run("sp", e)


def build():
    nc = bass.Bass("TRN2", target_bir_lowering=False)
    di = lambda n, s: nc.dram_tensor(n, s, F32, kind="ExternalInput").ap()
    do = lambda n, s: nc.dram_tensor(n, s, F32, kind="ExternalOutput").ap()
    xp = di("xp", [SEQ, D]); xs = di("xs", [NS, D]); sshift = di("sshift", [NS, D])
    swkv = di("swkv", [NS, 1024, 64]); sconv = di("sconv", [NS * 30, 512])
    w_in = di("w_in", [D, NIN]); w_br = di("w_br", [D, D]); w_bc = di("w_bc", [512, D]); w_out = di("w_out", [D, D])
    cols_d = di("cols", [128, NCOL]); consts_d = di("consts", [128, NCONST])
    w2ext_d = di("w2ext", [65, 1024]); a2_d = di("a2", [64, 1024])
    npg_d = di("npg", [1, D]); npre_d = di("npre", [1, D])
    yp = do("yp", [SEQ, D]); ys = do("ys", [NS, D]); nsp = do("nsp", [1, D])
    nwp = do("nwp", [1024, 64]); ncp = do("ncp", [30, 512]); nss = do("nss", [NS, D])
    nws = do("nws", [NS, 1024, 64]); ncs = do("ncs", [NS, 30, 512])
    wi_s = nc.dram_tensor("wi_s", [D, NIN], BF16).ap()
    wbr_s = nc.dram_tensor("wbr_s", [D, D], BF16).ap()
    wbc_s = nc.dram_tensor("wbc_s", [512, D], BF16).ap()
    wo_s = nc.dram_tensor("wo_s", [D, D], BF16).ap()

    with contextlib.ExitStack() as st:
        def T(n, s, d=F32):
            return st.enter_context(nc.sbuf_tensor("sb_" + n, s, d))
        P = Prog(nc)
        op = P.op
        cst = T("cst", [128, NCONST]); col = T("col", [128, NCOL])
        idb = T("idb", [128, 128], BF16)
        bob = T("bob", [128, 128], BF16)
        bmb = T("bmb", [128, 128], BF16)
        w2e = T("w2e", [65, 1024]); a2b = T("a2b", [128, 1024], BF16)
        wb = [T("wb%d" % i, [128, 8, 512], BF16) for i in range(4)]
        xt = [T("xt%d" % i, [128, D]) for i in range(2)]
        hb = T("hb", [128, D], BF16)
        hT = T("hT", [128, 8, 512], BF16)
        rS = T("rS", [128, 8, 512], BF16); kS = T("kS", [128, 8, 512], BF16)
        vS = T("vS", [128, 8, 512], BF16); zrS = T("zrS", [128, 8, 512], BF16)
        twl = T("twl", [65, 512]); alb = T("alb", [128, 512], BF16)
        ua = T("ua", [128, 4, 512]); uex = T("uex", [128, 4, 542]); ubf = T("ubf", [128, 4, 542], BF16)
        szc = T("szc", [128, 4, 512], BF16)
        orT = T("orT", [128, 8, 512], BF16)
        mT = rS
        ocT = kS
        TT = [T("T%d" % i, [128, 8, 128]) for i in range(8)]
        bon = T("bon", [128, 8, 128], BF16)
        rt_ = T("rt_", [128, 8, 128], BF16); at_ = T("at_", [128, 8, 128], BF16); bt_ = T("bt_", [128, 8, 128], BF16)
        kt_ = T("kt_", [128, 8, 128], BF16); bh_ = T("bh_", [128, 8, 128], BF16); kh_ = T("kh_", [128, 8, 128], BF16)
        Vt = T("Vt", [128, 1024], BF16); Bt = T("Bt", [128, 1024], BF16); Kt = T("Kt", [128, 1024], BF16)
        Ak = [T("Ak%d" % i, [128, 4, 128], BF16) for i in range(2)]
        Nk = [T("Nk%d" % i, [128, 4, 128], BF16) for i in range(2)]
        Qb = T("Qb", [128, 4, 128], BF16)
        LkT = T("LkT", [128, 4, 128], BF16); MbT = T("MbT", [128, 4, 128], BF16); MkT = T("MkT", [128, 4, 128], BF16)
        Xb = T("Xb", [128, 256], BF16); SAb = T("SAb", [128, 256], BF16)
        Xb2 = T("Xb2", [128, 256], BF16); SAb2 = T("SAb2", [128, 256], BF16)
        dummy = T("dummy", [128, 8])
        _w3 = lambda i: wb[3][:, i, :].rearrange("p (a b) -> p a b", b=128)
        SETS = [
            {"Ak": Ak, "Nk": Nk, "Qb": Qb, "LkT": LkT, "MbT": MbT, "MkT": MkT, "Xb": Xb, "SAb": SAb, "banks": (0, 1, 2), "n": "_A"},
            {"Ak": [_w3(0), _w3(1)], "Nk": [_w3(2), _w3(3)], "Qb": _w3(4), "LkT": _w3(5), "MbT": _w3(6), "MkT": _w3(7),
             "Xb": Xb2, "SAb": SAb2, "banks": (3, 4, 5), "n": "_B"},
        ]
        SETB_KEYS = [k + "_B" for k in ("Ak0", "Ak1", "Nk0", "Nk1", "Qb", "LkT", "MbT", "MkT")]
        ALLT4 = ["T4_0", "T4_1", "T4_2", "T4_3"]
        ALLSF = ["Sf0", "Sf1", "Sf2", "Sf3"]
        ALLSB = ["Sb0", "Sb1", "Sb2", "Sb3"]
        Sf = T("Sf", [128, 8, 64]); Sb = T("Sb", [128, 8, 64], BF16)
        gC = T("gC", [128, 8]); tmpb = T("tmpb", [128, 512]); tmpc = T("tmpc", [128, 512])
        sgb = T("sgb", [128, 512], BF16)
        m1 = TT[5][:].rearrange("p a b -> p (a b)")
        pprev = [T("pprev%d" % i, [128, 40]) for i in range(2)]
        small = T("small", [128, 64])
        dg = [T("dg%d" % i, [128, 128], BF16) for i in range(4)]
        wld = xt
        pb = [st.enter_context(nc.psum_tensor("pb%d" % i, [128, 512], F32)) for i in range(7)]
        ptb = st.enter_context(nc.psum_tensor("ptb", [128, 1024], BF16))

        cnt = {"d": 0, "e": 0}
        import os
        STOP = float(os.environ.get("MK_STOP", "1000"))
        DMACAST = bool(int(os.environ.get("MK_DMACAST", "1")))

        def stage(k):
            if k > STOP:
                P.dead = True
        P.annotate = bool(os.environ.get("MK_ANN"))

        def ph(name):
            P.phase = name

        def dma(out, in_, reads, writes, q="sp"):
            cnt[q] = cnt.get(q, 0) + 1
            key = "%s%d" % (q, cnt[q] % (16 if q == "sp" else 8))
            if q == "act":
                return op("act", lambda e: e.dma_start(out=out, in_=in_), reads, writes, dma=key)
            return op(q, lambda e: e.dma_start(out=out, in_=in_), reads, writes, dma=key)

        def mm(out, lhsT, rhs, start, stop, reads, writes):
            b0 = lhsT.base_partition()
            n0 = lhsT.shape[0]
            tag = "lo" if b0 + n0 <= 64 else ("hi" if b0 >= 64 else None)
            op("pe", lambda e: e.matmul(out, lhsT=lhsT, rhs=rhs, start=start, stop=stop), reads, writes, tag=tag)

        def act(out, in_, func, reads, writes, bias=None, scale=None, accum=None):
            kw = {}
            if bias is not None: kw["bias"] = bias
            if scale is not None: kw["scale"] = scale
            if accum is not None: kw["accum_out"] = accum
            op("act", lambda e: e.activation(out=out, in_=in_, func=func, **kw), reads, writes)

        def tt(eng, out, in0, in1, o, reads, writes):
            g = {"dve": "dve", "pool": "pool"}[eng]
            op(g, lambda e: e.tensor_tensor(out=out, in0=in0, in1=in1, op=o), reads, writes)

        def ts(eng, out, in0, s1, s2, o0, o1, reads, writes):
            if s2 is None:
                op(eng, lambda e: e.tensor_scalar(out=out, in0=in0, scalar1=s1, scalar2=None, op0=o0), reads, writes)
            else:
                op(eng, lambda e: e.tensor_scalar(out=out, in0=in0, scalar1=s1, scalar2=s2, op0=o0, op1=o1), reads, writes)

        def stt(eng, out, in0, sc, in1, o0, o1, reads, writes):
            op(eng, lambda e: e.scalar_tensor_tensor(out=out, in0=in0, scalar=sc, in1=in1, op0=o0, op1=o1), reads, writes)

        def cp(eng, out, in_, reads, writes):
            if eng == "act":
                act(out, in_, AF.Copy, reads, writes)
            else:
                op(eng, lambda e: e.tensor_copy(out=out, in_=in_), reads, writes)

        def rsq(out, in_, eps, reads, wkey):
            act(out, in_, AF.Sqrt, reads, [wkey], bias=eps)
            op("dve", lambda e: e.reciprocal(out=out, in_=out), [wkey], [wkey])

        def bc(ap, shape):
            return ap.to_broadcast(shape)

        C = lambda o, n=128: cst[:, o:o + n]

        dma(cst[:], consts_d, [], ["cst"])
        dma(col[:], cols_d, [], ["col"])
        dma(w2e[:], w2ext_d, [], ["w2e"])
        dma(wld[0][64:128, 0:1024], a2_d, [], ["xt0"])
        cp("dve", a2b[64:128, :], wld[0][64:128, 0:1024], ["xt0"], ["a2b"])
        cp("dve", idb[:], C(C_ID), ["cst"], ["idb"])
        cp("dve", bob[:], C(C_BO), ["cst"], ["bob"])
        cp("dve", bmb[:], C(C_BM), ["cst"], ["bmb"])
        ts("dve", col[:, O_OMM:O_OMM + 33], col[:, O_MU:O_MU + 33], -1.0, 1.0, ALU.mult, ALU.add, ["col"], ["col"])
        ts("dve", col[:, O_OMKA:O_OMKA + 8], col[:, O_KA:O_KA + 8], -1.0, 1.0, ALU.mult, ALU.add, ["col"], ["col"])
        op("pool", lambda e: e.memset(twl[64:65, :], 1.0), [], ["twl"])
        op("pool", lambda e: e.memset(Sf[:], 0.0), [], ALLSF)
        op("pool", lambda e: e.memset(Sb[:], 0.0), [], ALLSB)
        op("pool", lambda e: e.memset(pprev[0][:], 0.0), [], ["pprev0"])
        op("pool", lambda e: e.memset(pprev[1][:], 0.0), [], ["pprev1"])
        op("pool", lambda e: e.memset(uex[:], 0.0), [], ["uex"])

        stage(1)
        ph("prologue")
        def prologue_gen(part):
            ph("prologue")
            if DMACAST:
                if part == 1:
                    blocks = [(w_in, wi_s, c0, min(1024, NIN - c0), "scr_i%d" % (c0 // 1024)) for c0 in range(0, 6 * 1024, 1024)]
                else:
                    blocks = [(w_in, wi_s, c0, min(1024, NIN - c0), "scr_i%d" % (c0 // 1024)) for c0 in range(6 * 1024, NIN, 1024)]
                    blocks += [(w_br, wbr_s, 0, 1024, "scr_o"), (w_bc, wbc_s, 0, 1024, "scr_o"), (w_out, wo_s, 0, 1024, "scr_o")]
                for (src, dst, c0, n, skey) in blocks:
                    nr = src.shape[0]
                    RS = int(os.environ.get("MK_RS", "512"))
                    for r0 in range(0, nr, RS):
                        r1 = min(nr, r0 + RS)
                        dma(dst[r0:r1, c0:c0 + n], src[r0:r1, c0:c0 + n], [], [skey], q="pool")
                        yield
                return
            pieces = []
            for c0 in range(0, NIN, 1024):
                for rc in range(8):
                    pieces.append((w_in, wi_s, rc, c0, min(1024, NIN - c0), "scr_i%d" % (c0 // 1024)))
            for rc in range(8):
                pieces.append((w_br, wbr_s, rc, 0, 1024, "scr_o"))
            for rc in range(4):
                pieces.append((w_bc, wbc_s, rc, 0, 1024, "scr_o"))
            for rc in range(8):
                pieces.append((w_out, wo_s, rc, 0, 1024, "scr_o"))
            fl = lambda t: t[:].rearrange("p a b -> p (a b)")
            orv = lambda i: orT[:, 2 * i:2 * i + 2, :].rearrange("p a b -> p (a b)")
            if part == 1:
                pieces = pieces[0:48]
                sf32 = [(fl(TT[i]), "T%d" % i) for i in range(8)]
                sbf = [(fl(rt_), "rt_0"), (fl(at_), "at_0"), (fl(bt_), "bt_0"), (fl(kt_), "kt_0"), (fl(bh_), "bh_"), (fl(kh_), "kh_"),
                       (Vt[:, :], "Vt_0"), (Bt[:, :], "Bt_0"), (Kt[:, :], "Kt_0")]
                DEPTH = 6
            else:
                pieces = pieces[48:]
                sf32 = [(fl(TT[i]), "T%d" % i) for i in (3, 4, 6, 7)]
                sbf = [(orv(0), "orT"), (orv(1), "orT"), (orv(2), "orT"), (orv(3), "orT"),
                       (fl(kh_), "kh_"), (Vt[:, :], "Vt_0"), (Bt[:, :], "Bt_0"), (Kt[:, :], "Kt_0")]
                DEPTH = 3
            NB = len(sf32)
            engs = ["dve", "act"]
            npc = len(pieces)
            for i in range(npc + DEPTH):
                if i < npc:
                    src, dst, rc, c0, n, skey = pieces[i]
                    bf_, kf_ = sf32[i % NB]
                    dma(bf_[:, 0:n], src[rc * 128:(rc + 1) * 128, c0:c0 + n], [], [kf_],
                        q=("act" if (os.environ.get("MK_ACTQ") and i % 2 == 1) else "sp"))
                j = i - DEPTH
                if j >= 0:
                    src, dst, rc, c0, n, skey = pieces[j]
                    bf_, kf_ = sf32[j % NB]
                    bb_, kb_ = sbf[j % len(sbf)]
                    cp(engs[j % 2], bb_[:, 0:n], bf_[:, 0:n], [kf_], [kb_])
                    dma(dst[rc * 128:(rc + 1) * 128, c0:c0 + n], bb_[:, 0:n], [kb_], [skey], q="pool")
                yield

        stage(2)
        wi_v = wi_s.rearrange("(dc p) n -> p dc n", p=128)
        wbr_v = wbr_s.rearrange("(dc p) n -> p dc n", p=128)
        wbc_v = wbc_s.rearrange("(dc p) n -> p dc n", p=128)
        wo_v = wo_s.rearrange("(dc p) n -> p dc n", p=128)
        wslot = {"i": 0}

        def wload(view, ndc, c0, n, skeys):
            s = wslot["i"] % 4
            wslot["i"] += 1
            dma(wb[s][:, 0:ndc, 0:n], view[:, :, c0:c0 + n], skeys, ["wb%d" % s])
            return s

        def ikeys(c0, n):
            return ["scr_i%d" % b for b in range(c0 // 1024, (c0 + n - 1) // 1024 + 1)]

        def rmsnorm_tile(xtile, key, npart, dst_cols, want_h_out=None):
            ph("rmsnorm")
            act(hb[0:npart, :], xtile[0:npart, :], AF.Square, [key], ["hb", "small"], accum=small[0:npart, 0:1])
            ts("dve", small[0:npart, 1:2], small[0:npart, 0:1], 1.0 / D, 1e-6, ALU.mult, ALU.add, ["small"], ["small"])
            rsq(small[0:npart, 2:3], small[0:npart, 1:2], 0.0, ["small"], "small")
            ts("dve", hb[0:npart, :], xtile[0:npart, :], small[0:npart, 2:3], None, ALU.mult, None, [key, "small"], ["hb"])
            if want_h_out is not None:
                want_h_out()
            for dc in range(8):
                op("pe", lambda e, dc=dc: e.transpose(ptb[:, dc * 128:dc * 128 + npart], hb[0:npart, dc * 128:(dc + 1) * 128], idb[0:npart, 0:npart]),
                   ["hb", "idb"], ["ptb"])
            for dc in range(8):
                act(hT[:, dc, dst_cols[0]:dst_cols[1]], ptb[:, dc * 128:dc * 128 + npart], AF.Copy, ["ptb", "col"], ["hT"],
                    scale=col[:, O_GPRE + dc:O_GPRE + dc + 1])

        def project(j, wslot_i, jj, NT, bank):
            for dc in range(8):
                mm(pb[bank][:, 0:NT], wb[wslot_i][:, dc, jj * 128:(jj + 1) * 128], hT[:, dc, 0:NT], dc == 0, dc == 7,
                   ["wb%d" % wslot_i, "hT"], ["pb%d" % bank])

        def shiftmix(j, bank, NT, dst, dkey, sample, pp_old, pp_new):
            p = pb[bank]
            mu = col[:, O_MU + j:O_MU + j + 1]
            omm = col[:, O_OMM + j:O_OMM + j + 1]
            bk = "pb%d" % bank
            if sample:
                act(tmpb[:, 0:NS], p[:, 0:NS], AF.Copy, [bk, "col"], ["tmpb"], scale=omm)
                stt("dve", dst, p[:, NS:2 * NS], mu, tmpb[:, 0:NS], ALU.mult, ALU.add, [bk, "tmpb", "col"], [dkey])
            else:
                act(tmpb[:, 0:NT], p[:, 0:NT], AF.Copy, [bk, "col"], ["tmpb"], scale=omm)
                act(pprev[pp_new][:, j:j + 1], p[:, NT - 1:NT], AF.Copy, [bk], ["pprev%d" % pp_new])
                stt("dve", dst[:, 1:NT], p[:, 0:NT - 1], mu, tmpb[:, 1:NT], ALU.mult, ALU.add, [bk, "tmpb", "col"], [dkey])
                stt("dve", dst[:, 0:1], pprev[pp_old][:, j:j + 1], mu, tmpb[:, 0:1], ALU.mult, ALU.add,
                    ["pprev%d" % pp_old, "tmpb", "col"], [dkey])

        def proj_phase(NT, sample, pp_old, pp_new):
            ph("proj")
            nb = 0
            for g0 in range(0, 45, 4):
                ng = min(4, 45 - g0)
                s = wload(wi_v, 8, g0 * 128, ng * 128, ikeys(g0 * 128, ng * 128))
                for jj in range(ng):
                    j = g0 + jj
                    bank = nb % 2
                    nb += 1
                    bk = "pb%d" % bank
                    project(j, s, jj, NT if not sample else 2 * NS, bank)
                    W = NS if sample else NT
                    if j < 8:
                        shiftmix(j, bank, NT, rS[:, j, 0:W], "rS", sample, pp_old, pp_new)
                    elif j < 16:
                        shiftmix(j, bank, NT, kS[:, j - 8, 0:W], "kS", sample, pp_old, pp_new)
                    elif j < 24:
                        shiftmix(j, bank, NT, vS[:, j - 16, 0:W], "vS", sample, pp_old, pp_new)
                    elif j < 32:
                        shiftmix(j, bank, NT, tmpc[:, 0:W], "tmpc", sample, pp_old, pp_new)
                        act(zrS[:, j - 24, 0:W], tmpc[:, 0:W], AF.Silu, ["tmpc"], ["zrS"])
                    elif j == 32:
                        shiftmix(j, bank, NT, tmpc[:, 0:W], "tmpc", sample, pp_old, pp_new)
                        act(twl[0:64, 0:W], tmpc[0:64, 0:W], AF.Tanh, ["tmpc"], ["twl"])
                        cp("pool", alb[64:128, 0:W], tmpc[64:128, 0:W], ["tmpc"], ["alb"])
                    elif j < 37:
                        c = j - 33
                        act(ua[:, c, 0:W], pb[bank][:, 0:W], AF.Identity, [bk, "col"], ["ua"],
                            bias=col[:, O_GLUB + c:O_GLUB + c + 1])
                    elif j < 41:
                        c = j - 37
                        act(tmpc[:, 0:W], pb[bank][:, 0:W], AF.Sigmoid, [bk, "col"], ["tmpc"],
                            bias=col[:, O_GLUB + 4 + c:O_GLUB + 5 + c])
                        if sample:
                            tt("dve", uex[:, c, 0:NS * 31].rearrange("p (n w) -> p n w", w=31)[:, :, 30], ua[:, c, 0:W], tmpc[:, 0:W],
                               ALU.mult, ["ua", "tmpc"], ["uex"])
                        else:
                            tt("dve", uex[:, c, 30:30 + W], ua[:, c, 0:W], tmpc[:, 0:W], ALU.mult, ["ua", "tmpc"], ["uex"])
                    else:
                        c = j - 41
                        act(szc[:, c, 0:W], pb[bank][:, 0:W], AF.Silu, [bk], ["szc"])
                    yield

        def ln_conv_out(W, cf, ck):
            ph("lnconv")
            for c in range(4):
                mm(pb[2][:, 0:W], C(C_AM), cf[c], c == 0, c == 3, ["cst", ck[c]], ["pb2"])
            for c in range(4):
                tt("dve", cf[c], cf[c], pb[2][:, 0:W], ALU.subtract, [ck[c], "pb2"], [ck[c]])
            for c in range(4):
                tt("pool", ua[:, c, 0:W], cf[c], cf[c], ALU.mult, [ck[c]], ["ua"])
            for c in range(4):
                mm(pb[3][:, 0:W], C(C_AM), ua[:, c, 0:W], c == 0, c == 3, ["cst", "ua"], ["pb3"])
            rsq(tmpc[:, 0:W], pb[3][:, 0:W], 1e-5, ["pb3"], "tmpc")
            for c in range(4):
                tt("dve", cf[c], cf[c], tmpc[:, 0:W], ALU.mult, [ck[c], "tmpc"], [ck[c]])
                act(ua[:, c, 0:W], cf[c], AF.Silu, [ck[c], "col"], ["ua"],
                    bias=col[:, O_LNB + c:O_LNB + c + 1], scale=col[:, O_LNG + c:O_LNG + c + 1])
                tt("pool", ocT[:, c, 0:W], ua[:, c, 0:W], szc[:, c, 0:W], ALU.mult, ["ua", "szc"], ["kS"])

        def prep_gen(cs, W, sample, PSp):
            T0, T1, T2, T3, T4, T5, T6, T7 = TT
            sl = slice(cs, cs + W)
            sfx = PSp["sfx"]
            bonT, gCt = PSp["bon"], PSp["gC"]
            kbon, kgc = "bon" + sfx, "gC" + sfx
            ph("prep")
            sh = [128, 8, W]
            colb = lambda o: bc(col[:, o:o + 8].unsqueeze(2), sh)
            p6 = pb[6]
            p6v = p6[:].rearrange("p (a b) -> p a b", b=128)[:, :, 0:W]
            T0v = T0[:].rearrange("p a b -> p (a b)")
            tt("dve", T5[:, :, 0:W], kS[:, :, sl], colb(O_KK), ALU.mult, ["kS", "col"], ["T5"])
            tt("pool", bh_[:, :, 0:W], T5[:, :, 0:W], T5[:, :, 0:W], ALU.mult, ["T5"], ["bh_"])
            yield
            for hf in range(2):
                mm(p6[0:W, :], twl[0:65, sl], w2e[0:65, hf * 512:(hf + 1) * 512], True, True, ["twl", "w2e"], ["pb6"])
                yield
                act(T0v[0:W, hf * 512:(hf + 1) * 512], p6[0:W, :], AF.Sigmoid, ["pb6"], ["T0"])
                yield
            tri = C(C_TRI) if not sample else cst[0:W, C_NI:C_NI + W]
            tre = C(C_TRE) if not sample else cst[0:W, C_NI + 64:C_NI + 64 + W]
            for hf in range(2):
                hs = slice(hf * 4, hf * 4 + 4)
                for hq in range(4):
                    hh = hf * 4 + hq
                    mm(p6[:, hq * 128:hq * 128 + W], T0v[0:W, hh * 128:(hh + 1) * 128], tri[0:W, 0:W], True, True, ["T0", "cst"], ["pb6"])
                yield
                act(T1[:, hs, 0:W], p6v[:, 0:4, :], AF.Exp, ["pb6"], ["T1"])
                act(T2[:, hs, 0:W], p6v[:, 0:4, :], AF.Exp, ["pb6"], ["T2"], scale=-1.0)
                yield
            cp("pool", gCt[:, :], T1[:, :, W - 1], ["T1"], [kgc])
            for hf in range(2):
                for hq in range(4):
                    hh = hf * 4 + hq
                    mm(p6[:, hq * 128:hq * 128 + W], a2b[64:128, hh * 128:(hh + 1) * 128], alb[64:128, sl], True, True, ["a2b", "alb"], ["pb6"])
                yield
                for hq in range(4):
                    hh = hf * 4 + hq
                    act(T4[:, hh, 0:W], p6[:, hq * 128:hq * 128 + W], AF.Sigmoid, ["pb6", "col"], ALLT4,
                        bias=col[:, O_A0 + hh:O_A0 + hh + 1])
                yield
            for hf in range(2):
                hs = slice(hf * 4, hf * 4 + 4)
                for hq in range(4):
                    hh = hf * 4 + hq
                    mm(p6[:, hq * 128:hq * 128 + W], bob[:], bh_[:, hh, 0:W], True, True, ["bob", "bh_"], ["pb6"])
                yield
                rsq(T7[:, hs, 0:W], p6v[:, 0:4, :], 1e-12, ["pb6"], "T7")
                yield
            stt("dve", T6[:, :, 0:W], T5[:, :, 0:W], -1.0, T7[:, :, 0:W], ALU.mult, ALU.mult, ["T5", "T7"], ["T6"])
            yield
            stt("dve", T7[:, :, 0:W], T6[:, :, 0:W], -1.0, T4[:, :, 0:W], ALU.mult, ALU.mult, ["T6"] + ALLT4, ["T7"])
            yield
            tt("pool", T0[:, :, 0:W], T4[:, :, 0:W], colb(O_KA), ALU.mult, ALLT4 + ["col", "T0"], ["T0"])
            tt("pool", T0[:, :, 0:W], T0[:, :, 0:W], colb(O_OMKA), ALU.add, ["T0", "col"], ["T0"])
            yield
            tt("dve", T5[:, :, 0:W], kS[:, :, sl], T0[:, :, 0:W], ALU.mult, ["kS", "T0"], ["T5"])
            yield
            tt("pool", T0[:, :, 0:W], rS[:, :, sl], T5[:, :, 0:W], ALU.mult, ["rS", "T5"], ["T0"])
            tt("pool", kh_[:, :, 0:W], T0[:, :, 0:W], colb(O_RK), ALU.mult, ["T0", "col"], ["kh_"])
            yield
            for hf in range(2):
                hs = slice(hf * 4, hf * 4 + 4)
                for hq in range(4):
                    hh = hf * 4 + hq
                    mm(p6[:, hq * 128:hq * 128 + W], bob[:], kh_[:, hh, 0:W], True, True, ["bob", "kh_"], ["pb6"])
                yield
                tt("dve", bonT[:, hs, 0:W], p6v[:, 0:4, :], vS[:, hs, sl], ALU.mult, ["pb6", "vS"], [kbon])
                yield
            if sample:
                return
            ph("mults")
            EG, EnG, EGe, k2, aa, bb = T1, T2, T3, T5, T6, T7
            rt, at, bt, kt = PSp["rt"], PSp["at"], PSp["bt"], PSp["kt"]
            krt, kat, kbt, kkt = "rt" + sfx, "at" + sfx, "bt" + sfx, "kt" + sfx
            tt("dve", rt, rS[:, :, sl], EG[:], ALU.mult, ["rS", "T1"], [krt])
            tt("pool", at[:, :, 1:128], aa[:, :, 1:128], EG[:, :, 0:127], ALU.mult, ["T6", "T1"], [kat])
            cp("pool", at[:, :, 0:1], aa[:, :, 0:1], ["T6"], [kat])
            yield
            tt("dve", bt, bb[:], EnG[:], ALU.mult, ["T7", "T2"], [kbt])
            tt("pool", kt, k2[:], EnG[:], ALU.mult, ["T5", "T2"], [kkt])
            yield
            tt("dve", EGe[:], EnG[:], bc(EG[:, :, 127:128], [128, 8, 128]), ALU.mult, ["T2", "T1", kat], ["T3"])
            yield
            tt("pool", bh_[:], bb[:], EGe[:], ALU.mult, ["T7", "T3"], ["bh_"])
            tt("dve", kh_[:], k2[:], EGe[:], ALU.mult, ["T5", "T3"], ["kh_"])
            yield
            ph("transp")
            for src, skey, dst, dkey in ((vS, "vS", PSp["Vt"], "Vt" + sfx), (bh_, "bh_", PSp["Bt"], "Bt" + sfx), (kh_, "kh_", PSp["Kt"], "Kt" + sfx)):
                for hh in range(8):
                    srcap = src[:, hh, sl] if src is vS else src[:, hh, :]
                    op("pe", lambda e, srcap=srcap, hh=hh: e.transpose(ptb[:, hh * 128:(hh + 1) * 128], srcap, idb[:]),
                       [skey, "idb"], ["ptb"])
                yield
                cp("act", dst[:, 0:512], ptb[:, 0:512], ["ptb"], [dkey])
                cp("dve", dst[:, 512:1024], ptb[:, 512:1024], ["ptb"], [dkey])
                yield

        def gn_gen(yT, ykey, cs, W, G, Gk, bonT, kbon):
            ph("gn")
            G1, G2, G3 = G
            k1, k2_, k3 = Gk
            sl = slice(cs, cs + W)
            sh = [128, 8, W]
            colb = lambda o: bc(col[:, o:o + 8].unsqueeze(2), sh)
            p6 = pb[6]
            p6v = p6[:].rearrange("p (a b) -> p a b", b=128)[:, :, 0:W]
            for hf in range(2):
                hs = slice(hf * 4, hf * 4 + 4)
                for hq in range(4):
                    hh = hf * 4 + hq
                    mm(p6[:, hq * 128:hq * 128 + W], C(C_BM), yT[:, hh, 0:W], True, True, ["cst"] + ykey, ["pb6"])
                yield
                tt("dve", G1[:, hs, 0:W], yT[:, hs, 0:W], p6v[:, 0:4, :], ALU.subtract, ykey + ["pb6"], [k1])
                yield
            hbv = hb[:, :].rearrange("p (a b) -> p a b", b=128)
            tt("pool", hbv[:, :, 0:W], G1[:, :, 0:W], G1[:, :, 0:W], ALU.mult, [k1], ["hb"])
            yield
            for hf in range(2):
                hs = slice(hf * 4, hf * 4 + 4)
                for hq in range(4):
                    hh = hf * 4 + hq
                    mm(p6[:, hq * 128:hq * 128 + W], bmb[:], hbv[:, hh, 0:W], True, True, ["bmb", "hb"], ["pb6"])
                yield
                rsq(G3[:, hs, 0:W], p6v[:, 0:4, :], 64e-5, ["pb6"], k3)
                yield
            tt("dve", G1[:, :, 0:W], G1[:, :, 0:W], G3[:, :, 0:W], ALU.mult, [k1, k3], [k1])
            yield
            tt("pool", G1[:, :, 0:W], G1[:, :, 0:W], colb(O_GNG), ALU.mult, [k1, "col"], [k1])
            tt("pool", G1[:, :, 0:W], G1[:, :, 0:W], colb(O_GNB), ALU.add, [k1, "col"], [k1])
            yield
            tt("dve", G1[:, :, 0:W], G1[:, :, 0:W], bonT[:, :, 0:W], ALU.add, [k1, kbon], [k1])
            yield
            tt("dve", orT[:, :, sl], G1[:, :, 0:W], zrS[:, :, sl], ALU.mult, [k1, "zrS"], ["orT"])
            yield

        def run_all(gens):
            gens = list(gens)
            while gens:
                for gq in list(gens):
                    try:
                        next(gq)
                    except StopIteration:
                        gens.remove(gq)

        gC2 = T("gC2", [128, 8])
        _fl = lambda t, i: t[:, 2 * i:2 * i + 2, :].rearrange("p a b -> p (a b)")
        _v8 = lambda ap: ap.rearrange("p (a b) -> p a b", b=128)
        PS = [
            {"sfx": "_0", "rt": rt_[:], "at": at_[:], "bt": bt_[:], "kt": kt_[:], "Vt": Vt, "Bt": Bt, "Kt": Kt, "bon": bon, "gC": gC},
            {"sfx": "_1", "rt": _v8(_fl(wb[0], 0)), "at": _v8(_fl(wb[0], 1)), "bt": _v8(_fl(wb[0], 2)), "kt": _v8(_fl(wb[0], 3)),
             "Vt": _fl(wb[1], 0), "Bt": _fl(wb[1], 1), "Kt": _fl(wb[1], 2), "bon": _v8(_fl(wb[1], 3)), "gC": gC2},
        ]
        PS1_KEYS = [k + "_1" for k in ("rt", "at", "bt", "kt", "Vt", "Bt", "Kt", "bon")]

        def scan_group(g, S, PSp, yT):
            Ak_, Nk_, Qb_, LkT_, MbT_, MkT_, Xb_, SAb_ = S["Ak"], S["Nk"], S["Qb"], S["LkT"], S["MbT"], S["MkT"], S["Xb"], S["SAb"]
            b0, b1, b2 = S["banks"]
            kb = lambda i: "pb%d" % i
            n = S["n"]
            sfx = PSp["sfx"]
            rt_, at_, bt_, kt_, Vt, Bt, Kt, gC = PSp["rt"], PSp["at"], PSp["bt"], PSp["kt"], PSp["Vt"], PSp["Bt"], PSp["Kt"], PSp["gC"]
            K = lambda nm: nm + n
            heads = [4 * g + x for x in (0, 2, 1, 3)]
            SbK = "Sb%d" % g; SfK = "Sf%d" % g; yK = "yT%d" % g

            def hp(h):
                hl, hh = h % 2, h // 2
                return slice(hl * 64, hl * 64 + 64), hh
            v4 = lambda p: p[:].rearrange("p (a b) -> p a b", b=128)
            mk = lambda o: bc(cst[:, o:o + 128].unsqueeze(1), [128, 4, 128])
            ph("scores")
            plan = [(b0, "at_", "bt_", Ak_[0], K("Ak0"), C_SL), (b1, "bt_", "at_", Nk_[0], K("Nk0"), C_SU),
                    (b2, "kt_", "at_", LkT_, K("LkT"), C_SU), (b0, "bt_", "rt_", MbT_, K("MbT"), C_UI),
                    (b1, "kt_", "rt_", MkT_, K("MkT"), C_UI)]
            tl = {"at_": at_, "bt_": bt_, "kt_": kt_, "rt_": rt_}
            kn = {"at_": "at" + sfx, "bt_": "bt" + sfx, "kt_": "kt" + sfx, "rt_": "rt" + sfx}
            first_lo = (n == "_A")
            for rnd in (plan[0:3], plan[3:5]):
                for tagsel in ((0, 1) if first_lo else (1, 0)):
                    for (bk, ln, rn, dst, dk, msk) in rnd:
                        for hi, h in enumerate(heads):
                            if (h % 2) != tagsel:
                                continue
                            pr, hh = hp(h)
                            mm(pb[bk][:, hi * 128:(hi + 1) * 128], tl[ln][pr, hh, :], tl[rn][pr, hh, :], True, True, [kn[ln], kn[rn]], [kb(bk)])
                yield
                for (bk, ln, rn, dst, dk, msk) in rnd:
                    tt("dve", dst[:], v4(pb[bk]), mk(msk), ALU.mult, [kb(bk), "cst"], [dk])
                    yield
            ph("doubling")
            tt("pool", Qb_[:], Nk_[0][:], mk(C_ID), ALU.add, [K("Nk0"), "cst"], [K("Qb")])
            yield
            mm(pb[b2][:, :], idb[:], Qb_.rearrange("p a b -> p (a b)"), True, True,
               ["idb", K("Qb")], [kb(b2)])
            yield
            cur = 0
            for lvl in range(6):
                nx = 1 - cur
                for hi in range(4):
                    mm(pb[b0][:, hi * 128:(hi + 1) * 128], Nk_[cur][:, hi, :], Ak_[cur][:, hi, :], True, True,
                       [K("Nk%d" % cur), K("Ak%d" % cur)], [kb(b0)])
                if lvl < 5:
                    for hi in range(4):
                        mm(pb[b1][:, hi * 128:(hi + 1) * 128], Ak_[cur][:, hi, :], Nk_[cur][:, hi, :], True, True,
                           [K("Nk%d" % cur), K("Ak%d" % cur)], [kb(b1)])
                yield
                cp("act", Ak_[nx][:], v4(pb[b0]), [kb(b0)], [K("Ak%d" % nx)])
                if lvl < 5:
                    cp("dve", Nk_[nx][:], v4(pb[b1]), [kb(b1)], [K("Nk%d" % nx)])
                yield
                for hi in range(4):
                    mm(pb[b2][:, hi * 128:(hi + 1) * 128], Ak_[nx][:, hi, :], Qb_[:, hi, :], False, True,
                       [K("Ak%d" % nx), K("Qb")], [kb(b2)])
                yield
                if lvl % 2 == 0 or os.environ.get("MK_QACT"):
                    cp("act", Qb_[:], v4(pb[b2]), [kb(b2)], [K("Qb")])
                else:
                    cp("dve", Qb_[:], v4(pb[b2]), [kb(b2)], [K("Qb")])
                yield
                cur = nx
            ph("seq")
            for hi, h in enumerate(heads):
                pr, hh = hp(h)
                o = pb[b0][:, hi * 64:(hi + 1) * 64]
                mm(o, at_[pr, hh, :], Sb[pr, hh, :], True, False, ["at" + sfx, SbK], [kb(b0)])
                mm(o, LkT_[:, hi, :], Vt[:, h * 64:(h + 1) * 64], False, True, [K("LkT"), "Vt" + sfx], [kb(b0)])
            yield
            cp("act", Xb_[:], pb[b0][:, 0:256], [kb(b0)], [K("Xb")])
            yield
            for hi, h in enumerate(heads):
                mm(pb[b1][:, hi * 64:(hi + 1) * 64], Qb_[:, hi, :], Xb_[:, hi * 64:(hi + 1) * 64], True, True, [K("Qb"), K("Xb")], [kb(b1)])
            yield
            cp("dve", SAb_[:], pb[b1][:, 0:256], [kb(b1)], [K("SAb")])
            yield
            for hi, h in enumerate(heads):
                pr, hh = hp(h)
                o = pb[b2][pr, (hh - 2 * g) * 128:(hh - 2 * g) * 128 + 128]
                mm(o, Sb[pr, hh, :], rt_[pr, hh, :], True, False, [SbK, "rt" + sfx], [kb(b2)])
                mm(o, SAb_[:, hi * 64:(hi + 1) * 64], MbT_[:, hi, :], False, False, [K("SAb"), K("MbT")], [kb(b2)])
                mm(o, Vt[:, h * 64:(h + 1) * 64], MkT_[:, hi, :], False, True, ["Vt" + sfx, K("MkT")], [kb(b2)])
            yield
            cp("act", yT[:, 2 * g:2 * g + 2, :], pb[b2][:, 0:256].rearrange("p (a b) -> p a b", b=128), [kb(b2)], [yK])
            for hi, h in enumerate(heads):
                pr, hh = hp(h)
                o = pb[b0][pr, (hh - 2 * g) * 64:(hh - 2 * g) * 64 + 64]
                mm(o, Bt[:, h * 64:(h + 1) * 64], SAb_[:, hi * 64:(hi + 1) * 64], True, False, ["Bt" + sfx, K("SAb")], [kb(b0)])
                mm(o, Kt[:, h * 64:(h + 1) * 64], Vt[:, h * 64:(h + 1) * 64], False, True, ["Kt" + sfx, "Vt" + sfx], [kb(b0)])
            yield
            gs = slice(2 * g, 2 * g + 2)
            tt("dve", Sf[:, gs, :], Sf[:, gs, :], bc(gC[:, gs].unsqueeze(2), [128, 2, 64]), ALU.mult, [SfK, "gC" + sfx], [SfK])
            tt("dve", Sf[:, gs, :], Sf[:, gs, :], pb[b0][:, 0:128].rearrange("p (a b) -> p a b", b=64), ALU.add, [SfK, kb(b0)], [SfK])
            yield
            cp("act", Sb[:, gs, :], Sf[:, gs, :], [SfK], [SbK])
            yield


        def tail(NT, xsrc_tiles, ydst_tiles, nrows):
            ph("tail")
            for q in range(2):
                sgr = wload(wi_v, 8, (45 + 4 * q) * 128, 512, ikeys((45 + 4 * q) * 128, 512))
                sbr = wload(wbr_v, 8, q * 512, 512, ["scr_o"])
                sgc = wload(wi_v, 8, (53 + 4 * q) * 128, 512, ikeys((53 + 4 * q) * 128, 512))
                sbc = wload(wbc_v, 4, q * 512, 512, ["scr_o"])
                for jj in range(4):
                    j = q * 4 + jj
                    od = j % 2
                    bA, bB, bC, bD = (0, 1, 2, 3) if od == 0 else (4, 5, 6, 3)
                    sg1, sg1k = (sgb, "sgb")
                    sg2, sg2k = (alb, "alb")
                    m1_ = TT[5 + od][:].rearrange("p a b -> p (a b)")
                    m1k = "T%d" % (5 + od)
                    t2_, t2k = ((tmpc, "tmpc"), (tmpb, "tmpb"))[od]
                    project(45 + j, sgr, jj, NT, bA)
                    act(sg1[:, 0:NT], pb[bA][:, 0:NT], AF.Sigmoid, ["pb%d" % bA], [sg1k])
                    for fc in range(8):
                        mm(pb[bB][:, 0:NT], wb[sbr][:, fc, jj * 128:(jj + 1) * 128], orT[:, fc, 0:NT], fc == 0, fc == 7,
                           ["wb%d" % sbr, "orT"], ["pb%d" % bB])
                    tt("dve", m1_[:, 0:NT], pb[bB][:, 0:NT], sg1[:, 0:NT], ALU.mult, ["pb%d" % bB, sg1k], [m1k])
                    project(53 + j, sgc, jj, NT, bC)
                    act(sg2[:, 0:NT], pb[bC][:, 0:NT], AF.Sigmoid, ["pb%d" % bC], [sg2k])
                    for fc in range(4):
                        mm(pb[bD][:, 0:NT], wb[sbc][:, fc, jj * 128:(jj + 1) * 128], ocT[:, fc, 0:NT], fc == 0, fc == 3,
                           ["wb%d" % sbc, "kS"], ["pb%d" % bD])
                    tt("dve", t2_[:, 0:NT], pb[bD][:, 0:NT], sg2[:, 0:NT], ALU.mult, ["pb%d" % bD, sg2k], [t2k])
                    tt("pool", mT[:, j, 0:NT], m1_[:, 0:NT], t2_[:, 0:NT], ALU.add, [m1k, t2k], ["rS"])
            so = [wload(wo_v, 8, 0, 512, ["scr_o"]), wload(wo_v, 8, 512, 512, ["scr_o"])]
            npg = TT[2][:].rearrange("p a b -> p (a b)")
            dma(npg[:, :], npg_d.partition_broadcast(128), [], ["T2"])
            for i, (xsrc, ydst) in enumerate(zip(xsrc_tiles, ydst_tiles)):
                xb = xt[i % 2]
                xk = "xt%d" % (i % 2)
                dma(xb[0:nrows, :], xsrc, [], [xk])
                tsl = slice(i * 128, i * 128 + nrows)
                bks = [(4, 5), (0, 1), (2, 3)][i % 3]
                so_ = 32 + 8 * (i % 3)
                stg_i = (0, 1, 3)[i % 3]
                stg = TT[stg_i][:].rearrange("p a b -> p (a b)")
                sk_ = "T%d" % stg_i
                for hf in range(2):
                    for fc in range(8):
                        mm(pb[bks[hf]][0:nrows, :], mT[:, fc, tsl], wb[so[hf]][:, fc, :], fc == 0, fc == 7,
                           ["rS", "wb%d" % so[hf]], ["pb%d" % bks[hf]])
                for hf in range(2):
                    act(hb[0:nrows, hf * 512:(hf + 1) * 512], pb[bks[hf]][0:nrows, :], AF.Square, ["pb%d" % bks[hf]], ["hb", "small"],
                        accum=small[0:nrows, so_ + hf:so_ + hf + 1])
                tt("dve", small[0:nrows, so_ + 2:so_ + 3], small[0:nrows, so_:so_ + 1], small[0:nrows, so_ + 1:so_ + 2], ALU.add, ["small"], ["small"])
                ts("dve", small[0:nrows, so_ + 3:so_ + 4], small[0:nrows, so_ + 2:so_ + 3], 1.0 / D, 1e-6, ALU.mult, ALU.add, ["small"], ["small"])
                rsq(small[0:nrows, so_ + 4:so_ + 5], small[0:nrows, so_ + 3:so_ + 4], 0.0, ["small"], "small")
                for hf in range(2):
                    hsl = slice(hf * 512, (hf + 1) * 512)
                    stt("dve", stg[0:nrows, hsl], pb[bks[hf]][0:nrows, :], small[0:nrows, so_ + 4:so_ + 5], npg[0:nrows, hsl], ALU.mult, ALU.mult,
                        ["pb%d" % bks[hf], "small", "T2"], [sk_])
                tt("pool", stg[0:nrows, :], stg[0:nrows, :], xb[0:nrows, :], ALU.add, [sk_, xk], [sk_])
                dma(ydst, stg[0:nrows, :], [sk_], [], q="pool")

        def rms_phase(sc):
            t0 = sc * 512
            for i in range(4):
                xb = xt[i % 2]; xk = "xt%d" % (i % 2)
                dma(xb[:], xp[t0 + i * 128:t0 + (i + 1) * 128, :], [], [xk])
                last = (sc == 3 and i == 3)

                def hout(xb=xb, xk=xk):
                    T0v = TT[0][:].rearrange("p a b -> p (a b)")
                    dma(T0v[:, :], npre_d.partition_broadcast(128), [], ["T0"])
                    ts("dve", TT[1][:].rearrange("p a b -> p (a b)"), xb[:], small[:, 2:3], None, ALU.mult, None, [xk, "small"], ["T1"])
                    tt("dve", TT[1][:].rearrange("p a b -> p (a b)"), TT[1][:].rearrange("p a b -> p (a b)"), T0v, ALU.mult, ["T1", "T0"], ["T1"])
                    dma(nsp, TT[1][:].rearrange("p a b -> p (a b)")[127:128, :], ["T1"], [])
                rmsnorm_tile(xb, xk, 128, (i * 128, (i + 1) * 128), hout if last else None)

        rms_phase(0)
        run_all([prologue_gen(1)])
        for sc in range(4):
            t0 = sc * 512
            if sc > 0:
                rms_phase(sc)
            stage(3 if sc == 0 else 11)
            if sc > 0:
                cp("pool", tmpc[:, 0:120].rearrange("p (c w) -> p c w", w=30), uex[:, :, 512:542], ["uex"], ["tmpc"])
                cp("pool", uex[:, :, 0:30], tmpc[:, 0:120].rearrange("p (c w) -> p c w", w=30), ["tmpc"], ["uex"])
            pg = proj_phase(512, False, sc % 2, (sc + 1) % 2)
            side = prologue_gen(2) if sc == 0 else None
            pr0 = None
            step = 0
            while True:
                try:
                    next(pg)
                except StopIteration:
                    break
                step += 1
                if side is not None:
                    try:
                        next(side)
                    except StopIteration:
                        side = None
                if step == 33:
                    pr0 = prep_gen(0, 128, False, PS[0])
                if pr0 is not None:
                    try:
                        next(pr0)
                    except StopIteration:
                        pr0 = None
            rest = [g_ for g_ in (side, pr0) if g_ is not None]
            run_all(rest)
            stage(4 if sc == 0 else 11)
            op("pool", lambda e: e.memset(dummy[:, 0:1], 0.0), [], ["wb3", "wb0", "wb1", "xt0", "xt1", "ua", "uaA", "uaB", "dummy", "yT0", "yT1", "yT2", "yT3"] + SETB_KEYS + PS1_KEYS)
            yTp = xt[0][:, :].rearrange("p (a b) -> p a b", b=128)
            Gp = (xt[1][:, :].rearrange("p (a b) -> p a b", b=128),
                  ua[:, 0:2, :].rearrange("p a b -> p (a b)").rearrange("p (a b) -> p a b", b=128),
                  ua[:, 2:4, :].rearrange("p a b -> p (a b)").rearrange("p (a b) -> p a b", b=128))
            Gpk = ("xt1", "uaA", "uaB")

            def chunk_scan(c4):
                PSp = PS[c4 % 2]
                for pair in ((0, 1), (2, 3)):
                    gens = [scan_group(pair[0], SETS[0], PSp, yTp), scan_group(pair[1], SETS[1], PSp, yTp)]
                    while gens:
                        for gq in list(gens):
                            try:
                                next(gq)
                                yield
                            except StopIteration:
                                gens.remove(gq)
                yield from gn_gen(yTp, ["yT0", "yT1", "yT2", "yT3"], c4 * 128, 128, Gp, Gpk, PSp["bon"], "bon" + PSp["sfx"])

            for c4 in range(4):
                main = chunk_scan(c4)
                side = prep_gen((c4 + 1) * 128, 128, False, PS[(c4 + 1) % 2]) if c4 < 3 else None
                RATIO = int(os.environ.get("MK_RATIO", "5"))
                done = False
                while not done:
                    for _ in range(RATIO):
                        try:
                            next(main)
                        except StopIteration:
                            done = True
                            break
                    if side is not None:
                        try:
                            next(side)
                        except StopIteration:
                            side = None
                if side is not None:
                    run_all([side])
            op("pool", lambda e: e.memset(dummy[:, 1:2], 0.0), [], ["wb3", "wb0", "wb1", "xt0", "xt1", "ua", "uaA", "uaB", "dummy", "yT0", "yT1", "yT2", "yT3"] + SETB_KEYS + PS1_KEYS)
            stage(8 if sc == 0 else 11)
            ph("conv")
            cp("pool", ubf[:], uex[:], ["uex"], ["ubf"])
            for c in range(4):
                for w in range(31):
                    s = (c * 31 + w) % 4
                    if w % 2 == 0:
                        act(dg[s][:], idb[:], AF.Copy, ["idb", "col"], ["dg%d" % s], scale=col[:, O_CW + c * 31 + w:O_CW + c * 31 + w + 1])
                    else:
                        ts("dve", dg[s][:], idb[:], col[:, O_CW + c * 31 + w:O_CW + c * 31 + w + 1], None, ALU.mult, None,
                           ["idb", "col"], ["dg%d" % s])
                    mm(pb[6][:, :], dg[s][:], ubf[:, c, w:w + 512], w == 0, w == 30, ["dg%d" % s, "ubf"], ["pb6"])
                act(TT[c][:].rearrange("p a b -> p (a b)")[:, 0:512], pb[6][:, :], AF.Identity, ["pb6", "col"], ["T%d" % c],
                    bias=col[:, O_CB + c:O_CB + c + 1])
            ln_conv_out(512, [TT[c][:].rearrange("p a b -> p (a b)")[:, 0:512] for c in range(4)], ["T0", "T1", "T2", "T3"])
            if sc == 3:
                for c in range(4):
                    mm(pb[6][0:30, c * 128:(c + 1) * 128], uex[:, c, 512:542], C(C_ID), True, True, ["uex", "cst"], ["pb6"])
                cp("dve", tmpc[0:30, :], pb[6][0:30, :], ["pb6"], ["tmpc"])
                dma(ncp, tmpc[0:30, :], ["tmpc"], [])
            stage(9 if sc == 0 else 11)
            tail(512, [xp[t0 + i * 128:t0 + (i + 1) * 128, :] for i in range(4)],
                 [yp[t0 + i * 128:t0 + (i + 1) * 128, :] for i in range(4)], 128)

        stage(12)
        for h in range(16):
            hl, hh = h % 2, h // 2
            pr = slice(hl * 64, hl * 64 + 64)
            mm(pb[0][pr, hh * 64:(hh + 1) * 64], Sf[pr, hh, :], cst[pr, C_ID + hl * 64:C_ID + hl * 64 + 64], True, True, ALLSF + ["cst"], ["pb0"])
        cp("dve", tmpc[:, :], pb[0][:, :], ["pb0"], ["tmpc"])
        dma(nwp.rearrange("(hh p) j -> p hh j", p=128), tmpc[:, :].rearrange("p (a b) -> p a b", b=64), ["tmpc"], [])

        stage(13)
        ph("sample")
        xb = xt[0]
        dma(xb[0:NS, :], xs, [], ["xt0"])

        def hout_s():
            T0v = TT[0][:].rearrange("p a b -> p (a b)")
            T1v = TT[1][:].rearrange("p a b -> p (a b)")
            dma(T0v[0:NS, :], npre_d.partition_broadcast(NS), [], ["T0"])
            ts("dve", T1v[0:NS, :], xb[0:NS, :], small[0:NS, 2:3], None, ALU.mult, None, ["xt0", "small"], ["T1"])
            tt("dve", T1v[0:NS, :], T1v[0:NS, :], T0v[0:NS, :], ALU.mult, ["T1", "T0"], ["T1"])
            dma(nss, T1v[0:NS, :], ["T1"], [])
        rmsnorm_tile(xb, "xt0", NS, (0, NS), hout_s)
        dma(xt[1][0:NS, :], sshift, [], ["xt1"])
        cp("dve", hb[0:NS, :], xt[1][0:NS, :], ["xt1"], ["hb"])
        for dc in range(8):
            op("pe", lambda e, dc=dc: e.transpose(ptb[:, dc * 128:dc * 128 + NS], hb[0:NS, dc * 128:(dc + 1) * 128], idb[0:NS, 0:NS]),
               ["hb", "idb"], ["ptb"])
        cp("act", hT[:, :, NS:2 * NS], ptb[:, :].rearrange("p (a b) -> p a b", b=128)[:, :, 0:NS], ["ptb"], ["hT"])
        uv = [uex[:, c, 0:NS * 31].rearrange("p (n w) -> p n w", w=31) for c in range(4)]
        for q in range(4):
            dma(xt[1][0:120, 0:512], sconv[q * 120:(q + 1) * 120, :], [], ["xt1"])
            for c in range(4):
                mm(pb[6][:, c * 120:(c + 1) * 120], xt[1][0:120, c * 128:(c + 1) * 128], cst[0:120, C_ID:C_ID + 120], True, True,
                   ["xt1", "cst"], ["pb6"])
            for c in range(4):
                cp("dve", uv[c][:, q * 4:(q + 1) * 4, 0:30], pb[6][:, c * 120:(c + 1) * 120].rearrange("p (n w) -> p n w", w=30),
                   ["pb6"], ["uex"])
        dma(ncs[:, 0:29, :], sconv.rearrange("(n w) c -> n w c", w=30)[:, 1:30, :], [], [])
        run_all([proj_phase(2 * NS, True, 0, 0)])
        stage(14)
        run_all([prep_gen(0, NS, True, PS[0])])
        EG, EnG, EGe, k2, aa, bb = TT[1], TT[2], TT[3], TT[5], TT[6], TT[7]
        SW = [ua[:, i, :].rearrange("p (a b) -> p a b", b=64) for i in range(2)]
        Dxs = [Vt[:, 0:512], tmpb[:, :], Bt[:, 0:512], Kt[:, 0:512], Vt[:, 512:1024]]
        Dxk = ["Vt_0", "tmpb", "Bt_0", "Kt_0", "Vt_0"]
        Dxo = [bob[:], C(C_BO), bob[:], bob[:], bob[:]]
        Dxok = ["bob", "cst", "bob", "bob", "bob"]
        yTs = TT[4]
        i2b = bc(cst[:, C_I2:C_I2 + 64].unsqueeze(1), [128, 8, 64])
        op("pool", lambda e: e.memset(dummy[:, 2:3], 0.0), [], ["ua", "uaA", "uaB", "dummy"])
        for n in range(NS):
            Sw = SW[n % 2]; sk = ("uaA", "uaB")[n % 2]
            dma(Sw, swkv[n].rearrange("(hh p) j -> p hh j", p=128), [], [sk])
            vecs = [(aa, "T6"), (EG, "T1"), (bb, "T7"), (k2, "T5"), (rS, "rS")]
            for vi, (vt_, vk) in enumerate(vecs):
                tt("pool", Dxs[vi].rearrange("p (a b) -> p a b", b=64), i2b, bc(vt_[:, :, n:n + 1], [128, 8, 64]), ALU.mult,
                   ["cst", vk], [Dxk[vi]])
                mm(pb[vi][:, :], Dxo[vi], Dxs[vi], True, True, [Dxok[vi], Dxk[vi]], ["pb%d" % vi])
            v8 = lambda p: p[:].rearrange("p (a b) -> p a b", b=64)
            W3 = TT[0][:, :, 0:64]
            tt("dve", W3, Sw, v8(pb[0]), ALU.mult, [sk, "pb0"], ["T0"])
            op("dve", lambda e: e.tensor_reduce(out=small[:, 16:24], in_=TT[0][:, :, 0:64], axis=AX.X, op=ALU.add), ["T0"], ["small"])
            tt("dve", Sw, Sw, v8(pb[1]), ALU.mult, [sk, "pb1"], [sk])
            tt("dve", W3, v8(pb[2]), bc(small[:, 16:24].unsqueeze(2), [128, 8, 64]), ALU.mult, ["pb2", "small"], ["T0"])
            tt("pool", Sw, Sw, W3, ALU.add, [sk, "T0"], [sk])
            cp("dve", small[:, 24:32], vS[:, :, n], ["vS"], ["small"])
            tt("dve", W3, v8(pb[3]), bc(small[:, 24:32].unsqueeze(2), [128, 8, 64]), ALU.mult, ["pb3", "small"], ["T0"])
            tt("pool", Sw, Sw, W3, ALU.add, [sk, "T0"], [sk])
            dma(nws[n].rearrange("(hh p) j -> p hh j", p=128), Sw, [sk], [])
            tt("dve", W3, Sw, v8(pb[4]), ALU.mult, [sk, "pb4"], ["T0"])
            op("dve", lambda e, n=n: e.tensor_reduce(out=yTs[:, :, n], in_=TT[0][:, :, 0:64], axis=AX.X, op=ALU.add), ["T0"], ["T4_0"])
        stage(15)
        op("pool", lambda e: e.memset(dummy[:, 3:4], 0.0), [], ["ua", "uaA", "uaB", "dummy"])
        run_all([gn_gen(yTs, ALLT4, 0, NS, (TT[1], TT[2], TT[3]), ("T1", "T2", "T3"), bon, "bon_0")])
        stage(16)
        cf = []
        for c in range(4):
            cwb = bc(col[:, O_CW + c * 31:O_CW + (c + 1) * 31].unsqueeze(1), [128, NS, 31])
            tt("dve", tmpc[:, 0:NS * 31].rearrange("p (n w) -> p n w", w=31), uv[c], cwb, ALU.mult, ["uex", "col"], ["tmpc"])
            cfc = TT[c][:].rearrange("p a b -> p (a b)")[:, 0:NS]
            op("dve", lambda e, cfc=cfc: e.tensor_reduce(out=cfc, in_=tmpc[:, 0:NS * 31].rearrange("p (n w) -> p n w", w=31),
                                                        axis=AX.X, op=ALU.add), ["tmpc"], ["T%d" % c])
            ts("dve", cfc, cfc, col[:, O_CB + c:O_CB + c + 1], None, ALU.add, None, ["T%d" % c, "col"], ["T%d" % c])
            cf.append(cfc)
        for c in range(4):
            cp("dve", tmpb[:, c * NS:(c + 1) * NS], uv[c][:, :, 30], ["uex"], ["tmpb"])
        for c in range(4):
            mm(pb[6][0:NS, c * 128:(c + 1) * 128], tmpb[:, c * NS:(c + 1) * NS], C(C_ID), True, True, ["tmpb", "cst"], ["pb6"])
        cp("dve", m1[0:NS, 0:512], pb[6][0:NS, :], ["pb6"], ["T5"])
        dma(ncs[:, 29, :], m1[0:NS, 0:512], ["T5"], [])
        ln_conv_out(NS, cf, ["T0", "T1", "T2", "T3"])
        stage(17)
        tail(NS, [xs], [ys], NS)
        P.emit()
    return nc


def _prep_consts():
    c = np.zeros((128, NCONST), np.float32)
    idx = np.arange(128)
    c[:, C_ID:C_ID + 128] = np.eye(128)
    c[:, C_SL:C_SL + 128] = (idx[None, :] < idx[:, None])
    c[:, C_SU:C_SU + 128] = (idx[:, None] < idx[None, :])
    c[:, C_UI:C_UI + 128] = (idx[:, None] <= idx[None, :])
    c[:, C_TRI:C_TRI + 128] = (idx[:, None] <= idx[None, :]) * CNEG
    c[:, C_TRE:C_TRE + 128] = (idx[:, None] < idx[None, :]) * CNEG
    blk = (idx[:, None] // 64 == idx[None, :] // 64).astype(np.float32)
    c[:, C_BM:C_BM + 128] = blk / 64.0
    c[:, C_BO:C_BO + 128] = blk
    c[:, C_AM:C_AM + 128] = 1.0 / 512.0
    c[:, C_NI:C_NI + 64] = np.eye(128)[:, :64] * CNEG
    c[:, C_I2:C_I2 + 64] = (idx[:, None] % 64 == np.arange(64)[None, :])
    return c


_NC = None


def kernel(x_prompt, x_sample, state_shift, state_wkv, state_conv, norm_pre_g, w_in, mu_shift,
           decay_w0, decay_w2, iclr_a0, iclr_a2, k_k, k_a, r_k, gn_g, gn_b, conv_glu_b, conv_w,
           conv_b, ln_c_g, ln_c_b, w_branch_r, w_branch_c, w_out, norm_post_g):
    global _NC
    f = lambda a: np.ascontiguousarray(np.asarray(a, dtype=np.float32))
    colv = lambda v, n: f(v).reshape(n, 128).T
    cols = np.zeros((128, NCOL), np.float32)
    cols[:, O_MU:O_MU + 33] = colv(mu_shift[0], 33)
    cols[:, O_KK:O_KK + 8] = colv(k_k[0], 8)
    cols[:, O_KA:O_KA + 8] = colv(k_a[0], 8)
    cols[:, O_RK:O_RK + 8] = colv(np.asarray(r_k[0]).reshape(-1), 8)
    cols[:, O_GNG:O_GNG + 8] = colv(gn_g[0], 8)
    cols[:, O_GNB:O_GNB + 8] = colv(gn_b[0], 8)
    cols[:, O_A0:O_A0 + 8] = colv(iclr_a0[0], 8)
    cols[:, O_GLUB:O_GLUB + 8] = colv(conv_glu_b[0], 8)
    cols[:, O_CB:O_CB + 4] = colv(conv_b[0], 4)
    cols[:, O_LNG:O_LNG + 4] = colv(ln_c_g[0], 4)
    cols[:, O_LNB:O_LNB + 4] = colv(ln_c_b[0], 4)
    cw = f(conv_w[0])
    cols[:, O_CW:O_CW + 124] = cw.reshape(31, 4, 128).transpose(2, 1, 0).reshape(128, 124)
    cols[:, O_GPRE:O_GPRE + 8] = colv(norm_pre_g[0], 8)
    consts = _prep_consts()
    w2ext = np.concatenate([f(decay_w2[0]), f(decay_w0[0])[None, :]], axis=0)
    shared = {
        "w_in": f(w_in[0]), "w_br": f(w_branch_r[0]), "w_bc": f(w_branch_c[0]), "w_out": f(w_out[0]),
        "cols": cols, "consts": consts, "w2ext": f(w2ext), "a2": f(iclr_a2[0]),
        "npg": f(norm_post_g[0])[None, :], "npre": f(norm_pre_g[0])[None, :],
    }
    xpf = f(x_prompt); xsf = f(x_sample).reshape(128, D); ssf = f(state_shift[0])
    swf = f(state_wkv[0]).reshape(128, 1024, 64); scf = f(state_conv[0]).reshape(128 * 30, 512)
    in_maps = []
    for c in range(8):
        m = dict(shared)
        m["xp"] = xpf[c]
        m["xs"] = xsf[c * NS:(c + 1) * NS]
        m["sshift"] = ssf[c * NS:(c + 1) * NS]
        m["swkv"] = swf[c * NS:(c + 1) * NS]
        m["sconv"] = scf[c * NS * 30:(c + 1) * NS * 30]
        in_maps.append(m)
    if _NC is None:
        _NC = build()
    res = run_bass_kernel_spmd(_NC, in_maps, core_ids=list(range(8)))
    R = res.results
    y_prompt = np.stack([R[c]["yp"] for c in range(8)]).astype(np.float32)
    y_sample = np.concatenate([R[c]["ys"] for c in range(8)]).reshape(128, 1, D).astype(np.float32)
    nsp_ = np.concatenate([R[c]["nsp"] for c in range(8)]).reshape(1, 8, D).astype(np.float32)
    nwp_ = np.stack([R[c]["nwp"] for c in range(8)]).reshape(1, 8, 16, 64, 64).astype(np.float32)
    ncp_ = np.stack([R[c]["ncp"] for c in range(8)]).reshape(1, 8, 30, 512).astype(np.float32)
    nss_ = np.concatenate([R[c]["nss"] for c in range(8)]).reshape(1, 128, D).astype(np.float32)
    nws_ = np.concatenate([R[c]["nws"] for c in range(8)]).reshape(1, 128, 16, 64, 64).astype(np.float32)
    ncs_ = np.concatenate([R[c]["ncs"] for c in range(8)]).reshape(1, 128, 30, 512).astype(np.float32)
    return (y_prompt, y_sample, nsp_, nwp_, ncp_, nss_, nws_, ncs_)
```
